# Optimizing a Trainium2 kernel written in Bass

```python
import math
import jax
import jax.numpy as jnp
from jax import lax
import numpy as np

D_MODEL = 1024
BATCH = 8
SEQ = 2048
DEPTH = 1

CHUNK = 64
Q_BLOCK = 128
GDN_HEADS = D_MODEL // 256
GDN_DK = 128
GDN_DV = 128
GDN_WIDTH = GDN_HEADS * GDN_DV
FOX_HEADS = D_MODEL // 128
FOX_DH = 64
FOX_WIDTH = FOX_HEADS * FOX_DH
CONV_W = 4
D_FF = 4 * D_MODEL
D_PLE = 256
LN_EPS = 1e-5
NORM_EPS = 1e-6
ALPHA = (2.0 * DEPTH) ** 0.25
BETA_INIT = (8.0 * DEPTH) ** -0.25

GDN_QK = GDN_HEADS * GDN_DK
GDN_QKV = 2 * GDN_QK + GDN_WIDTH
OFF_Z = GDN_QKV
OFF_BETA = OFF_Z + GDN_WIDTH
OFF_A = OFF_BETA + GDN_HEADS
OFF_FOX = OFF_A + GDN_HEADS
OFF_F = OFF_FOX + 3 * FOX_WIDTH
D_IN = OFF_F + FOX_HEADS

kernel_name = 'hybrid_gdn_fox_deepnorm_block'


def _layer_norm(x, g, b):
    xf = x.astype(jnp.float32)
    mu = jnp.mean(xf, -1, keepdims=True)
    var = jnp.mean(jnp.square(xf - mu), -1, keepdims=True)
    return ((xf - mu) * lax.rsqrt(var + LN_EPS) * g.astype(jnp.float32) + b.astype(jnp.float32)).astype(x.dtype)


def _rms_norm(x, g):
    xf = x.astype(jnp.float32)
    return (xf * lax.rsqrt(jnp.mean(xf * xf, -1, keepdims=True) + NORM_EPS) * g.astype(jnp.float32)).astype(x.dtype)


def _l2norm(x):
    xf = x.astype(jnp.float32)
    return xf * lax.rsqrt(jnp.sum(xf * xf, -1, keepdims=True) + NORM_EPS)


def _causal_conv(x, w):
    c = x.shape[-1]
    return lax.conv_general_dilated(x, w[:, None, :], window_strides=(1,), padding=[(CONV_W - 1, 0)],
                                    dimension_numbers=('NWC', 'WIO', 'NWC'), feature_group_count=c)


def _gated_delta_rule(q, k, v, beta, log_g):
    B, T, H, dk = q.shape
    dv = v.shape[-1]
    n = T // CHUNK
    f32 = jnp.float32

    def to_chunks(a):
        a = a.reshape((B, n, CHUNK) + a.shape[2:])
        return jnp.moveaxis(a, 3, 1)

    q = to_chunks(q.astype(f32)) * (dk ** -0.5)
    k = to_chunks(k.astype(f32))
    v = to_chunks(v.astype(f32))
    beta = to_chunks(beta)
    gam = jnp.cumsum(to_chunks(log_g), axis=-1)
    idx = jnp.arange(CHUNK)
    causal = idx[:, None] >= idx[None, :]
    strict = idx[:, None] > idx[None, :]
    decay = jnp.exp(jnp.where(causal, gam[..., :, None] - gam[..., None, :], -jnp.inf))

    kk = jnp.einsum('bhnid,bhnjd->bhnij', k, k)
    a_mat = jnp.where(strict, kk * beta[..., :, None] * decay, 0.0) + jnp.eye(CHUNK, dtype=f32)
    rhs = jnp.concatenate([v * beta[..., None], k * (beta * jnp.exp(gam))[..., None]], axis=-1)
    sol = lax.linalg.triangular_solve(a_mat, rhs, left_side=True, lower=True, unit_diagonal=True)
    u, w = sol[..., :dv], sol[..., dv:]

    qk_intra = jnp.where(causal, jnp.einsum('bhnid,bhnjd->bhnij', q, k) * decay, 0.0)
    q_dec = q * jnp.exp(gam)[..., None]
    k_dec = k * jnp.exp(gam[..., -1:] - gam)[..., None]
    g_last = jnp.exp(gam[..., -1])

    xs = tuple(jnp.moveaxis(a, 2, 0) for a in (q_dec, k_dec, u, w, qk_intra, g_last))

    def step(S, inp):
        qd, kd, u_c, w_c, a_c, gl = inp
        v_new = u_c - jnp.einsum('bhck,bhkv->bhcv', w_c, S)
        o = jnp.einsum('bhck,bhkv->bhcv', qd, S) + jnp.einsum('bhij,bhjv->bhiv', a_c, v_new)
        S = S * gl[..., None, None] + jnp.einsum('bhck,bhcv->bhkv', kd, v_new)
        return S, o

    s0 = jnp.zeros((B, H, dk, dv), f32)
    _, o = lax.scan(step, s0, xs)
    o = jnp.moveaxis(o, 0, 2)
    return jnp.moveaxis(o, 1, 3).reshape(B, T, H, dv)


def _forgetting_attention(q, k, v, log_f):
    B, T, H, d = q.shape
    nb = T // Q_BLOCK
    scale = d ** -0.5
    c_all = jnp.transpose(jnp.cumsum(log_f, axis=1), (0, 2, 1))
    qb = jnp.moveaxis(q.reshape(B, nb, Q_BLOCK, H, d), 1, 0)
    cb = jnp.moveaxis(c_all.reshape(B, H, nb, Q_BLOCK), 2, 0)
    k_pos = jnp.arange(T)

    def block(args):
        i, q_i, c_i = args
        s = jnp.einsum('bqhd,bkhd->bhqk', q_i, k).astype(jnp.float32) * scale
        s = s + c_i[..., :, None] - c_all[..., None, :]
        q_pos = i * Q_BLOCK + jnp.arange(Q_BLOCK)
        s = jnp.where(k_pos[None, :] <= q_pos[:, None], s, -jnp.inf)
        attn = jax.nn.softmax(s, axis=-1)
        return jnp.einsum('bhqk,bkhd->bqhd', attn.astype(v.dtype), v)

    out = lax.map(block, (jnp.arange(nb), qb, cb))
    return jnp.moveaxis(out, 0, 1).reshape(B, T, H, d)


def setup_inputs(seed: int = 0) -> dict:
    key = jax.random.key(seed)
    ks = jax.random.split(key, 24)
    f32 = jnp.float32

    def nrm(k, shape, s):
        return jax.random.normal(k, shape, f32) * s

    x = nrm(ks[0], (BATCH, SEQ, D_MODEL), 1.0)
    p = nrm(ks[1], (DEPTH, BATCH, SEQ, D_PLE), 1.0)
    ln_in_g = 1.0 + nrm(ks[2], (D_MODEL,), 0.02)
    ln_in_b = nrm(ks[3], (D_MODEL,), 0.02)
    w_in = nrm(ks[4], (DEPTH, D_MODEL, D_IN), D_MODEL ** -0.5)
    conv_w = nrm(ks[5], (DEPTH, CONV_W, GDN_QKV), CONV_W ** -0.5)
    a_log = jnp.log(jax.random.uniform(ks[6], (DEPTH, GDN_HEADS), f32, 1.0, 16.0))
    dt = jnp.exp(jax.random.uniform(ks[7], (DEPTH, GDN_HEADS), f32, math.log(1e-3), math.log(1e-1)))
    dt_bias = dt + jnp.log(-jnp.expm1(-dt))
    gdn_norm_g = 1.0 + nrm(ks[8], (DEPTH, GDN_DV), 0.02)
    b_f = jnp.linspace(1.0, 5.0, FOX_HEADS, dtype=f32)[None, :] + nrm(ks[9], (DEPTH, FOX_HEADS), 0.1)
    fox_norm_g = 1.0 + nrm(ks[10], (DEPTH, FOX_DH), 0.02)
    w_out = nrm(ks[11], (DEPTH, D_MODEL, D_MODEL), BETA_INIT * D_MODEL ** -0.5)
    ln1_g = 1.0 + nrm(ks[12], (DEPTH, D_MODEL), 0.02)
    ln1_b = nrm(ks[13], (DEPTH, D_MODEL), 0.02)
    w_up = nrm(ks[14], (DEPTH, D_MODEL, D_FF), D_MODEL ** -0.5)
    w_down = nrm(ks[15], (DEPTH, D_FF, D_MODEL), BETA_INIT * D_FF ** -0.5)
    w_ple = nrm(ks[16], (DEPTH, D_PLE, D_MODEL), BETA_INIT * D_PLE ** -0.5)
    w_ple_gate = nrm(ks[17], (DEPTH, D_MODEL, D_MODEL), D_MODEL ** -0.5)
    b_ple_gate = nrm(ks[18], (DEPTH, D_MODEL), 0.02)
    ln2_g = 1.0 + nrm(ks[19], (DEPTH, D_MODEL), 0.02)
    ln2_b = nrm(ks[20], (DEPTH, D_MODEL), 0.02)
    return {'x': x, 'p': p, 'ln_in_g': ln_in_g, 'ln_in_b': ln_in_b, 'w_in': w_in, 'conv_w': conv_w,
            'a_log': a_log, 'dt_bias': dt_bias, 'gdn_norm_g': gdn_norm_g, 'b_f': b_f,
            'fox_norm_g': fox_norm_g, 'w_out': w_out, 'ln1_g': ln1_g, 'ln1_b': ln1_b, 'w_up': w_up,
            'w_down': w_down, 'w_ple': w_ple, 'w_ple_gate': w_ple_gate, 'b_ple_gate': b_ple_gate,
            'ln2_g': ln2_g, 'ln2_b': ln2_b}


def reference(x, p, ln_in_g, ln_in_b, w_in, conv_w, a_log, dt_bias, gdn_norm_g, b_f, fox_norm_g,
              w_out, ln1_g, ln1_b, w_up, w_down, w_ple, w_ple_gate, b_ple_gate, ln2_g, ln2_b):
    B, T, _ = x.shape
    f32 = jnp.float32
    h = _layer_norm(x, ln_in_g, ln_in_b)
    for i in range(DEPTH):
        proj = h @ w_in[i]

        qkv = jax.nn.silu(_causal_conv(proj[..., :GDN_QKV], conv_w[i]))
        gq = _l2norm(qkv[..., :GDN_QK].reshape(B, T, GDN_HEADS, GDN_DK))
        gk = _l2norm(qkv[..., GDN_QK:2 * GDN_QK].reshape(B, T, GDN_HEADS, GDN_DK))
        gv = qkv[..., 2 * GDN_QK:].reshape(B, T, GDN_HEADS, GDN_DV)
        z = proj[..., OFF_Z:OFF_BETA].reshape(B, T, GDN_HEADS, GDN_DV)
        beta = jax.nn.sigmoid(proj[..., OFF_BETA:OFF_A].astype(f32))
        log_g = -jnp.exp(a_log[i].astype(f32)) * jax.nn.softplus(proj[..., OFF_A:OFF_FOX].astype(f32) + dt_bias[i].astype(f32))
        o_gdn = _gated_delta_rule(gq, gk, gv, beta, log_g).astype(x.dtype)
        o_gdn = (_rms_norm(o_gdn, gdn_norm_g[i]) * jax.nn.silu(z)).reshape(B, T, GDN_WIDTH)

        fqkv = proj[..., OFF_FOX:OFF_F].reshape(B, T, 3, FOX_HEADS, FOX_DH)
        log_f = jax.nn.log_sigmoid(proj[..., OFF_F:].astype(f32) + b_f[i].astype(f32))
        o_fox = _forgetting_attention(fqkv[:, :, 0], fqkv[:, :, 1], fqkv[:, :, 2], log_f)
        o_fox = _rms_norm(o_fox, fox_norm_g[i]).reshape(B, T, FOX_WIDTH)

        mix = jnp.concatenate([o_gdn, o_fox], axis=-1) @ w_out[i]
        h = _layer_norm(ALPHA * h + mix, ln1_g[i], ln1_b[i])

        ff = jnp.square(jax.nn.relu(h @ w_up[i])) @ w_down[i]
        ple = (p[i] @ w_ple[i]) * jax.nn.sigmoid(h @ w_ple_gate[i] + b_ple_gate[i])
        h = _layer_norm(ALPHA * h + ff + ple, ln2_g[i], ln2_b[i])
    return h
```

```python
import numpy as np
from contextlib import ExitStack
import concourse.bass as bass
import concourse.mybir as mybir
from concourse.bass_utils import run_bass_kernel_spmd

F32 = mybir.dt.float32
BF16 = mybir.dt.bfloat16
F32R = mybir.dt.float32r
AF = mybir.ActivationFunctionType
ALU = mybir.AluOpType

T, D, NT, NBLK = 2048, 1024, 16, 4
DFF = 4096
ALPHA = 2.0 ** 0.25
LN_EPS = 1e-5
NORM_EPS = 1e-6
NEG = -30000.0
DSZ = {F32: 4, BF16: 2, F32R: 4}

C_ID, C_ONE, C_TRI, C_BLK, C_NEGM, C_STR, C_IND, C_OAUG, C_M05, NCST = 0, 128, 256, 384, 512, 640, 768, 770, 898, 900
P_G0, P_B0, P_CW, P_FOXG, P_BF, P_DTB, P_ALOG, P_G1, P_B1, NPAR = 0, 8, 16, 64, 65, 66, 130, 194, 202, 210


class Buf:
    __slots__ = ("w", "r", "lock")

    def __init__(self, lock=None):
        self.w = None
        self.r = {}
        self.lock = lock


class Ticket:
    __slots__ = ("sem", "val", "key", "eng")

    def __init__(self, sem, val, key, eng):
        self.sem, self.val, self.key, self.eng = sem, val, key, eng


class Eng:
    def __init__(self, name, h, sem):
        self.name, self.h, self.sem = name, h, sem
        self.count = 0
        self.waited = {}


class FW:
    def __init__(self, nc, es):
        self.nc, self.es = nc, es
        self.E = {}
        for n in ("tensor", "vector", "scalar", "gpsimd", "sync"):
            sem = es.enter_context(nc.semaphore("s_" + n))
            self.E[n] = Eng(n, getattr(nc, n), sem)
        self.dsems = []

    def _wait(self, e, t):
        if e.waited.get(t.key, 0) >= t.val:
            return
        e.h.wait_ge(t.sem, t.val)
        e.waited[t.key] = t.val

    def _deps(self, e, reads, writes):
        for b in reads:
            t = b.w
            if t is not None and not (t.eng == e.name and e.name == "tensor"):
                self._wait(e, t)
        for b in writes:
            t = b.w
            if t is not None and (t.eng != e.name or (e.name != "tensor" and t.val <= e.count)):
                self._wait(e, t)
            for t in b.r.values():
                if t.eng != e.name:
                    self._wait(e, t)

    @staticmethod
    def _mark(t, reads, writes):
        for b in reads:
            b.r[t.key] = t
        for b in writes:
            b.w = t
            b.r = {}

    def op(self, eng, fn, r=(), w=(), inc=True):
        e = self.E[eng]
        locks = []
        for b in list(r) + list(w):
            if b.lock is not None and b.lock not in locks:
                locks.append(b.lock)
        for lk in locks:
            t = lk.w
            if t is not None and t.eng != e.name:
                self._wait(e, t)
        self._deps(e, r, w)
        ins = fn(e.h)
        if inc:
            e.count += 1
            ins.then_inc(e.sem, 1)
            t = Ticket(e.sem, e.count, "e_" + eng, eng)
        else:
            t = Ticket(e.sem, e.count + 1, "e_" + eng, eng)
        self._mark(t, r, w)
        for lk in locks:
            lk.w = t
        return t

    def dsem(self, name):
        sem = self.es.enter_context(self.nc.semaphore(name))
        d = [sem, 0, name]
        self.dsems.append(d)
        return d

    def dma(self, q, d, out, in_, r=(), w=()):
        e = self.E[q]
        self._deps(e, r, w)
        ins = e.h.dma_start(out=out, in_=in_)
        d[1] += 16
        ins.then_inc(d[0], 16)
        t = Ticket(d[0], d[1], "d_" + d[2], "dma")
        self._mark(t, r, w)
        return t

    def barrier(self):
        tl = [Ticket(e.sem, e.count, "e_" + n, n) for n, e in self.E.items() if e.count > 0]
        tl += [Ticket(d[0], d[1], "d_" + d[2], "dma") for d in self.dsems if d[1] > 0]
        for e in self.E.values():
            for t in tl:
                if t.eng != e.name:
                    self._wait(e, t)


class Arena:
    def __init__(self, ap, nbytes):
        self.ap = ap
        self.free = [(0, nbytes)]
        self.live = {}

    def alloc(self, name, nbytes):
        nbytes = (nbytes + 63) // 64 * 64
        for i, (o, s) in enumerate(self.free):
            if s >= nbytes:
                self.free[i] = (o + nbytes, s - nbytes)
                self.live[name] = (o, nbytes)
                return o
        raise RuntimeError("arena full for %s (%d) free=%s" % (name, nbytes, self.free))

    def release(self, *names):
        for name in names:
            o, s = self.live.pop(name)
            self.free.append((o, s))
        self.free.sort()
        m = []
        for o, s in self.free:
            if s == 0:
                continue
            if m and m[-1][0] + m[-1][1] == o:
                m[-1] = (m[-1][0], m[-1][1] + s)
            else:
                m.append((o, s))
        self.free = m

    def view(self, name, shape, dt):
        n = 1
        for s in shape:
            n *= s
        nb = n * DSZ[dt]
        o = self.alloc(name, nb)
        v = self.ap[:, o // 4:(o + nb) // 4]
        if dt != F32:
            v = v.bitcast(dt)
        if len(shape) == 2:
            v = v.rearrange("p (a b) -> p a b", a=shape[0])
        elif len(shape) == 3:
            v = v.rearrange("p (a b c) -> p a b c", a=shape[0], b=shape[1])
        elif len(shape) == 4:
            v = v.rearrange("p (a b c d) -> p a b c d", a=shape[0], b=shape[1], c=shape[2])
        return v


def build(stop_after=None, dumps=()):
    nc = bass.Bass("TRN2", target_bir_lowering=False)

    def din(name, shape):
        return nc.dram_tensor(name, list(shape), F32, kind="ExternalInput").ap()

    x = din("x", [T, D])
    p_in = din("p", [T, 256])
    w_in = din("w_in", [D, 3600])
    w_out = din("w_out", [D, D])
    w_up = din("w_up", [D, DFF])
    w_down = din("w_down", [DFF, D])
    w_ple = din("w_ple", [256, D])
    w_gate = din("w_gate", [D, D])
    cst_d = din("cst", [128, NCST])
    par_d = din("par", [128, NPAR])
    vec_d = din("vecs", [8, D])
    ones_d = din("ones_rows", [3, 8 * T])
    out = nc.dram_tensor("out", [T, D], F32, kind="ExternalOutput").ap()
    dump_out = {}
    for (nm, shp, dt) in dumps:
        dump_out[nm] = nc.dram_tensor("dbg_" + nm, list(shp), dt, kind="ExternalOutput").ap()

    w_in_v = w_in.rearrange("(kc p) c -> p kc c", p=128)

    with ExitStack() as es:
        fw = FW(nc, es)
        ARENA_BYTES = 204 * 1024
        arena_t = es.enter_context(nc.sbuf_tensor("arena", [128, ARENA_BYTES // 4], F32))
        ar = Arena(arena_t[:, :], ARENA_BYTES)
        banks = [es.enter_context(nc.psum_tensor("bank%d" % i, [128, 512], F32)) for i in range(8)]
        LK = [Buf() for _ in range(8)]
        BK = [Buf(LK[i]) for i in range(8)]
        rot = [0]

        def nextbank(lo=0, hi=8):
            b = lo + rot[0] % (hi - lo)
            rot[0] += 1
            return b

        def V(eng, fn, r=(), w=(), inc=True):
            return fw.op(eng, fn, r, w, inc)

        dctr = [0]

        def newd():
            dctr[0] += 1
            return fw.dsem("d%d" % dctr[0])

        final_tickets = []

        def do_dumps(stage):
            for (nm, shp, dt) in dumps:
                if nm in dumpsrc and dumpsrc[nm][2] == stage:
                    src, bufs, _ = dumpsrc[nm]
                    final_tickets.append(fw.dma("sync", newd(), dump_out[nm], src, r=bufs))

        dumpsrc = {}

        def finish():
            for t in final_tickets:
                fw._wait(fw.E["sync"], t)

        cst = ar.view("cst", [NCST], F32)
        par = ar.view("par", [NPAR], F32)
        CST, PAR = Buf(), Buf()
        fw.dma("sync", newd(), cst, cst_d, w=[CST])
        fw.dma("sync", newd(), par, par_d, w=[PAR])
        identf = cst[:, C_ID:C_ID + 128]
        onesf = cst[:, C_ONE:C_ONE + 128]
        m05 = cst[:, C_M05:C_M05 + 1]
        eps_col = cst[:, C_M05 + 1:C_M05 + 2]
        EPSB = CST
        identb = ar.view("identb", [128], BF16)
        IDB = Buf()
        V("vector", lambda h: h.tensor_copy(out=identb, in_=identf), r=[CST], w=[IDB])
        mv = ar.view("mv", [NT, 2], F32)
        MV = [Buf() for _ in range(NT)]

        hT = ar.view("hT", [8, T], BF16)
        HT = [[Buf() for _ in range(NBLK)] for _ in range(8)]
        xt = ar.view("xt", [8, D], F32)
        XT = [Buf() for _ in range(8)]
        xsem = [newd() for _ in range(8)]
        st = ar.view("st", [8, 12], F32)
        STB = [Buf() for _ in range(8)]

        def ln_stats(src, SRC, j, mvap, MVB):
            V("vector", lambda h: h.bn_stats(out=st[:, j, 0:6], in_=src[:, 0:512]), r=[SRC], w=[STB[j]])
            V("vector", lambda h: h.bn_stats(out=st[:, j, 6:12], in_=src[:, 512:1024]), r=[SRC], w=[STB[j]])
            V("vector", lambda h: h.bn_aggr(out=mvap[:, 0:2], in_=st[:, j, 0:12]), r=[STB[j]], w=[MVB])
            V("vector", lambda h: h.tensor_scalar_add(out=mvap[:, 1:2], in0=mvap[:, 1:2], scalar1=LN_EPS), r=[MVB], w=[MVB])
            V("gpsimd", lambda h: h.tensor_tensor(out=mvap[:, 1:2], in0=mvap[:, 1:2], in1=m05, op=ALU.pow), r=[MVB, CST], w=[MVB])

        for g in range(NBLK):
            for j4 in range(4):
                m = 4 * g + j4
                j = m % 8
                fw.dma("sync", xsem[j], xt[:, j, :], x[m * 128:(m + 1) * 128, :], w=[XT[j]])
                ln_stats(xt[:, j, :], XT[j], j, mv[:, m, :], MV[m])
                V("vector", lambda h: h.tensor_scalar(out=xt[:, j, :], in0=xt[:, j, :], scalar1=mv[:, m, 0:1], scalar2=mv[:, m, 1:2],
                                                      op0=ALU.subtract, op1=ALU.mult), r=[XT[j], MV[m]], w=[XT[j]])
            for c in range(8):
                bk = nextbank()
                for j4 in range(4):
                    j = (4 * g + j4) % 8
                    V("tensor", lambda h: h.transpose(banks[bk][:, j4 * 128:(j4 + 1) * 128], xt[:, j, c * 128:(c + 1) * 128], identf),
                      r=[XT[j], CST], w=[BK[bk]], inc=(j4 == 3))
                V("scalar", lambda h: h.activation(out=hT[:, c, g * 512:(g + 1) * 512], in_=banks[bk][:, :], func=AF.Identity,
                                                   scale=par[:, P_G0 + c:P_G0 + c + 1], bias=par[:, P_B0 + c:P_B0 + c + 1]),
                  r=[BK[bk], PAR], w=[HT[c][g]])
        dumpsrc["hT"] = (hT, [b for row in HT for b in row], "A1")
        do_dumps("A1")
        if stop_after == "A1":
            finish()
            return nc

        fw.barrier()
        ar.release("xt", "st")
        wsl = [ar.view("wsl%d" % i, [8, 512], BF16) for i in range(3)]
        WSL = [Buf() for _ in range(3)]
        wsem = [newd() for _ in range(3)]
        wsmall = ar.view("wsmall", [8, 8], BF16)
        WSM = Buf()
        wsm_sem = newd()
        fq = ar.view("fq", [8, T], BF16)
        fk = ar.view("fk", [8, T], BF16)
        FQ = [Buf() for _ in range(8)]
        FK = [Buf() for _ in range(8)]
        vaug = ar.view("vaug", [NT, 8, 65], BF16)
        VAUG = Buf()
        OFF_FOX = 2056
        fw.dma("gpsimd", wsem[0], wsl[0], w_in_v[:, :, OFF_FOX:OFF_FOX + 512], w=[WSL[0]])
        fw.dma("gpsimd", wsem[1], wsl[1], w_in_v[:, :, OFF_FOX + 512:OFF_FOX + 1024], w=[WSL[1]])
        fw.dma("gpsimd", wsem[2], wsl[2], w_in_v[:, :, OFF_FOX + 1024:OFF_FOX + 1536], w=[WSL[2]])
        fw.dma("gpsimd", wsm_sem, wsmall, w_in_v[:, :, 3592:3600], w=[WSM])
        for h8 in range(8):
            pass
        V("gpsimd", lambda h: h.memset(fq[64:128, :, :], 0.0), w=FQ)
        V("vector", lambda h: h.memset(fk[64:128, :, :], 0.0), w=FK)
        fw.dma("gpsimd", newd(), fq[67:70, :, :], ones_d.rearrange("r (h t) -> r h t", h=8), w=FQ)
        fw.dma("gpsimd", newd(), fk[64:67, :, :], ones_d.rearrange("r (h t) -> r h t", h=8), w=FK)
        V("gpsimd", lambda h: h.memset(vaug[:, :, :, 64:65], 1.0), w=[VAUG])

        rowA = ar.view("rowA", [T], F32)
        rowC = ar.view("rowC", [T], F32)
        rparts = ar.view("rparts", [6, T], BF16)
        nbf = ar.view("nbf", [1], F32)
        RA, RC, RP, NBF = Buf(), Buf(), Buf(), Buf()
        V("vector", lambda h: h.tensor_scalar_mul(out=nbf[0:8, :], in0=par[0:8, P_BF:P_BF + 1], scalar1=-1.0), r=[PAR], w=[NBF])
        for blk in range(NBLK):
            bk = nextbank()
            for kc in range(8):
                V("tensor", lambda h: h.matmul(banks[bk][0:8, :], wsmall[:, kc, :], hT[:, kc, blk * 512:(blk + 1) * 512],
                                               start=(kc == 0), stop=(kc == 7)), r=[WSM, HT[kc][blk]], w=[BK[bk]], inc=(kc == 7))
            V("scalar", lambda h: h.activation(out=rowA[0:8, blk * 512:(blk + 1) * 512], in_=banks[bk][0:8, :], func=AF.Exp,
                                               scale=-1.0, bias=nbf[0:8, :]), r=[BK[bk], NBF], w=[RA])
        V("scalar", lambda h: h.activation(out=rowA[0:8, :], in_=rowA[0:8, :], func=AF.Ln, bias=1.0), r=[RA], w=[RA])
        V("vector", lambda h: h.tensor_tensor_scan(out=rowC[0:8, :], data0=cst[0:8, C_ONE:C_ONE + 1].to_broadcast([8, T]),
                                                   data1=rowA[0:8, :], initial=0.0, op0=ALU.mult, op1=ALU.subtract),
          r=[RA, CST], w=[RC])
        dumpsrc["crow"] = (rowC[0:8, :], [RC], "AF")
        V("vector", lambda h: h.tensor_copy(out=rparts[0:8, 0, :], in_=rowC[0:8, :]), r=[RC], w=[RP])
        V("vector", lambda h: h.tensor_tensor(out=rowA[0:8, :], in0=rowC[0:8, :], in1=rparts[0:8, 0, :], op=ALU.subtract), r=[RC, RP], w=[RA])
        V("vector", lambda h: h.tensor_copy(out=rparts[0:8, 1, :], in_=rowA[0:8, :]), r=[RA], w=[RP])
        V("vector", lambda h: h.tensor_tensor(out=rowA[0:8, :], in0=rowA[0:8, :], in1=rparts[0:8, 1, :], op=ALU.subtract), r=[RA, RP], w=[RA])
        V("vector", lambda h: h.tensor_copy(out=rparts[0:8, 2, :], in_=rowA[0:8, :]), r=[RA], w=[RP])
        V("vector", lambda h: h.tensor_scalar_mul(out=rparts[0:8, 3:6, :], in0=rparts[0:8, 0:3, :], scalar1=-1.0), r=[RP], w=[RP])
        for h8 in range(8):
            dq = newd()
            fw.dma("sync", dq, fq[64:67, h8, :], rparts[h8:h8 + 1, 0:3, :], r=[RP], w=[FQ[h8]])
            dk_ = newd()
            fw.dma("sync", dk_, fk[67:70, h8, :], rparts[h8:h8 + 1, 3:6, :], r=[RP], w=[FK[h8]])

        stg = ar.view("stg", [4, 512], BF16)
        STG = [Buf() for _ in range(4)]
        stgsem = [newd() for _ in range(4)]
        cnt = 0
        for (dst, DST, ws, WS, sc) in ((fq, FQ, wsl[0], WSL[0], 0.125), (fk, FK, wsl[1], WSL[1], 1.0)):
            for c4 in range(4):
                for blk in range(NBLK):
                    bk = nextbank()
                    for kc in range(8):
                        V("tensor", lambda h: h.matmul(banks[bk][:, :], ws[:, kc, c4 * 128:(c4 + 1) * 128], hT[:, kc, blk * 512:(blk + 1) * 512],
                                                       start=(kc == 0), stop=(kc == 7)), r=[WS, HT[kc][blk]], w=[BK[bk]], inc=(kc == 7))
                    he, ho = 2 * c4, 2 * c4 + 1
                    sl = cnt % 4
                    cnt += 1
                    cs = slice(blk * 512, (blk + 1) * 512)
                    V("scalar", lambda h: h.activation(out=dst[0:64, he, cs], in_=banks[bk][0:64, :], func=AF.Copy, scale=sc), r=[BK[bk]], w=[DST[he]])
                    V("vector", lambda h: h.tensor_scalar_mul(out=stg[64:128, sl, :], in0=banks[bk][64:128, :], scalar1=sc), r=[BK[bk]], w=[STG[sl]])
                    fw.dma("sync", stgsem[sl], dst[0:64, ho, cs], stg[64:128, sl, :], r=[STG[sl]], w=[DST[ho]])
        for m in range(NT):
            bk = nextbank()
            for kc in range(8):
                V("tensor", lambda h: h.matmul(banks[bk][:, :], hT[:, kc, m * 128:(m + 1) * 128], wsl[2][:, kc, :],
                                               start=(kc == 0), stop=(kc == 7)), r=[WSL[2], HT[kc][m // 4]], w=[BK[bk]], inc=(kc == 7))
            src = banks[bk][:, :].rearrange("p (h d) -> p h d", h=8)
            if m % 2 == 0:
                V("scalar", lambda h: h.copy(out=vaug[:, m, :, 0:64], in_=src), r=[BK[bk]], w=[VAUG])
            else:
                V("vector", lambda h: h.tensor_copy(out=vaug[:, m, :, 0:64], in_=src), r=[BK[bk]], w=[VAUG])
        dumpsrc["fq"] = (fq, FQ, "AF")
        dumpsrc["fk"] = (fk, FK, "AF")
        dumpsrc["vaug"] = (vaug, [VAUG], "AF")
        do_dumps("AF")
        if stop_after == "AF":
            finish()
            return nc

        fw.barrier()
        ar.release("wsl0", "wsl1", "wsl2", "rowA", "rowC", "rparts", "wsmall", "nbf", "stg")
        ofox = ar.view("ofox", [8, T], BF16)
        OFOX = [Buf() for _ in range(8)]
        pT = [ar.view("pT%d" % i, [512], BF16) for i in range(4)]
        PT = [Buf() for _ in range(4)]
        sq = [ar.view("sq%d" % i, [512], F32) for i in range(2)]
        rr = [ar.view("rr%d" % i, [512], F32) for i in range(2)]
        oaug = cst[:, C_OAUG:C_OAUG + 128]
        SQ, RR, OAUG = [Buf(), Buf()], [Buf(), Buf()], CST
        def pe_warm(n, bk_, lhs, rhs, RB_):
            for _ in range(n):
                V("tensor", lambda h: h.matmul(banks[bk_][:, :], lhs, rhs, start=True, stop=True), r=RB_, w=[BK[bk_]], inc=False)

        pe_warm(24, 5, fk[0:64, 0, 0:128], fq[0:64, 0, 0:512], [FK[0], FQ[0]])
        NFF, NFW = 0, 256
        pslot = [0]
        gp = [0]
        pend = []
        fsl = [0]

        def fin_step(it):
            stp, h8_, qb_, po_ = it[1], it[2], it[3], it[4]
            if stp == 0:
                k_ = fsl[0] % 2
                fsl[0] += 1
                it.append(k_)
                V("scalar", lambda h: h.activation(out=sq[k_][0:65, :], in_=banks[po_][0:65, :], func=AF.Square), r=[BK[po_]], w=[SQ[k_]])
                V("tensor", lambda h: h.matmul(banks[5][:, :], oaug[0:65, :], sq[k_][0:65, :], start=True, stop=True),
                  r=[OAUG, SQ[k_]], w=[BK[5]])
                it[0] = gp[0] + 2
                it[1] = 1
            else:
                k_ = it[5]
                V("scalar", lambda h: h.activation(out=rr[k_][0:64, :], in_=banks[5][0:64, :], func=AF.Ln, scale=1.0 / 64.0), r=[BK[5]], w=[RR[k_]])
                V("scalar", lambda h: h.activation(out=rr[k_][0:64, :], in_=rr[k_][0:64, :], func=AF.Exp, scale=-0.5), r=[RR[k_]], w=[RR[k_]])
                V("vector", lambda h: h.scalar_tensor_tensor(out=ofox[0:64, h8_, qb_ * 512:(qb_ + 1) * 512], in0=banks[po_][0:64, :],
                                                             scalar=par[0:64, P_FOXG:P_FOXG + 1], in1=rr[k_][0:64, :],
                                                             op0=ALU.mult, op1=ALU.mult), r=[BK[po_], RR[k_], PAR], w=[OFOX[h8_]])
                pend.remove(it)

        for h8 in range(8):
            pairs = []
            for qb in range(NBLK):
                for kb in range(4 * (qb + 1)):
                    pairs.append((qb, kb))
            sbank = {}

            def emit_qk(i):
                qb, kb = pairs[i]
                j = kb - 4 * qb
                qlo = max(j, 0) * 128
                bk = nextbank(0, 4)
                sbank[i] = bk
                V("tensor", lambda h: h.matmul(banks[bk][:, qlo:512], fk[:, h8, kb * 128:(kb + 1) * 128],
                                               fq[:, h8, qb * 512 + qlo:(qb + 1) * 512], start=True, stop=True),
                  r=[FK[h8], FQ[h8]], w=[BK[bk]])

            LA = 3
            for i0 in range(min(LA, len(pairs))):
                emit_qk(i0)
            for i, (qb, kb) in enumerate(pairs):
                if i + LA < len(pairs):
                    emit_qk(i + LA)
                for _ in range(NFF):
                    V("tensor", lambda h: h.matmul(banks[4][:, 0:NFW], identb, fq[:, 0, 0:NFW], start=True, stop=True), r=[IDB], w=[BK[4]], inc=False)
                j = kb - 4 * qb
                qlo = max(j, 0) * 128
                bk = sbank.pop(i)
                s = pslot[0] % 4
                pslot[0] += 1
                po = 6 + (h8 * NBLK + qb) % 2
                V("scalar", lambda h: h.activation(out=pT[s][:, qlo:512], in_=banks[bk][:, qlo:512], func=AF.Exp), r=[BK[bk]], w=[PT[s]])
                if j >= 0:
                    V("gpsimd", lambda h: h.affine_select(out=pT[s][:, qlo:qlo + 128], in_=pT[s][:, qlo:qlo + 128], pattern=[[1, 128]],
                                                          compare_op=ALU.is_ge, fill=0.0, base=0, channel_multiplier=-1),
                      r=[PT[s]], w=[PT[s]])
                last = (kb == 4 * (qb + 1) - 1)
                V("tensor", lambda h: h.matmul(banks[po][0:65, qlo:512], vaug[:, kb, h8, :], pT[s][:, qlo:512],
                                               start=(kb == 0), stop=last), r=[VAUG, PT[s]], w=[BK[po]], inc=last)
                if last:
                    pend.append([gp[0] + 2, 0, h8, qb, po])
                gp[0] += 1
                for it in list(pend):
                    if it[0] <= gp[0]:
                        fin_step(it)
        while pend:
            for it in list(pend):
                fin_step(it)
        dumpsrc["ofox"] = (ofox, OFOX, "F")
        do_dumps("F")
        if stop_after == "F":
            finish()
            return nc
        fw.barrier()
        ar.release("fq", "fk", "vaug", "pT0", "pT1", "pT2", "pT3", "sq0", "sq1", "rr0", "rr1")

        wsl = [ar.view("wsl%d" % i, [8, 512], BF16) for i in range(3)]
        WSL = [Buf() for _ in range(3)]
        wg = ar.view("wg", [8, 8], BF16)
        WG = Buf()
        qkvT = ar.view("qkvT", [12, T], BF16)
        QKV = [Buf() for _ in range(12)]
        sz = ar.view("sz", [4, T], BF16)
        SZ = [Buf() for _ in range(4)]
        pcb = ar.view("pcb", [4, 516], F32)
        PC = [Buf() for _ in range(4)]
        accb = ar.view("accb", [4, 512], F32)
        ACC = [Buf() for _ in range(4)]
        gpre = ar.view("gpre", [NT, 8], F32)
        GPRE = Buf()
        beta = ar.view("beta", [NT, 4], F32)
        lg = ar.view("lg", [NT, 4], F32)
        gt1 = ar.view("gt1", [NT, 4], F32)
        gt2 = ar.view("gt2", [NT, 4], F32)
        BETA, LG, GT1, GT2 = Buf(), Buf(), Buf(), Buf()
        for i in range(3):
            fw.dma("gpsimd", wsem[i], wsl[i], w_in_v[:, :, i * 512:(i + 1) * 512], w=[WSL[i]])
        fw.dma("gpsimd", wsm_sem, wg, w_in_v[:, :, 2048:2056], w=[WG])
        cw = par[:, P_CW:P_CW + 48].rearrange("p (c j) -> p c j", j=4)

        def ag_E(u):
            ch, blk = u // 4, u % 4
            grp, hc = ch // 4, ch % 4
            ws, WS = wsl[grp], WSL[grp]
            sl = u % 4
            if u == 16:
                pass
            bk = nextbank()
            for kc in range(8):
                V("tensor", lambda h: h.matmul(banks[bk][:, :], ws[:, kc, hc * 128:(hc + 1) * 128], hT[:, kc, blk * 512:(blk + 1) * 512],
                                               start=(kc == 0), stop=(kc == 7)), r=[WS, HT[kc][blk]], w=[BK[bk]], inc=(kc == 7))
            if blk == 0:
                V("gpsimd", lambda h: h.memset(pcb[:, sl, 0:3], 0.0), w=[PC[sl]])
            else:
                V("gpsimd", lambda h: h.tensor_copy(out=pcb[:, sl, 0:3], in_=pcb[:, (u - 1) % 4, 512:515]), r=[PC[(u - 1) % 4]], w=[PC[sl]])
            V("scalar", lambda h: h.copy(out=pcb[:, sl, 3:515], in_=banks[bk][:, :]), r=[BK[bk]], w=[PC[sl]])

        def ag_T(u):
            ch = u // 4
            sl = u % 4
            V("scalar", lambda h: h.activation(out=accb[:, sl, :], in_=pcb[:, sl, 0:512], func=AF.Copy, scale=cw[:, ch, 0:1]), r=[PC[sl], PAR], w=[ACC[sl]])
            for j in range(1, 4):
                V("vector", lambda h: h.scalar_tensor_tensor(out=accb[:, sl, :], in0=pcb[:, sl, j:j + 512], scalar=cw[:, ch, j:j + 1], in1=accb[:, sl, :],
                                                             op0=ALU.mult, op1=ALU.add), r=[PC[sl], PAR, ACC[sl]], w=[ACC[sl]])

        def ag_S(u):
            ch, blk = u // 4, u % 4
            sl = u % 4
            V("scalar", lambda h: h.activation(out=qkvT[:, ch, blk * 512:(blk + 1) * 512], in_=accb[:, sl, :], func=AF.Silu), r=[ACC[sl]], w=[QKV[ch]])

        NU = 48
        for u in range(NU + 2):
            if u < NU:
                ag_E(u)
                if u == 16:
                    fw.dma("gpsimd", wsem[0], wsl[0], w_in_v[:, :, 1536:2048], w=[WSL[0]])
            if 0 <= u - 1 < NU:
                ag_T(u - 1)
            if 0 <= u - 2 < NU:
                ag_S(u - 2)
        for hc in range(4):
            for blk in range(NBLK):
                bk = nextbank()
                for kc in range(8):
                    V("tensor", lambda h: h.matmul(banks[bk][:, :], wsl[0][:, kc, hc * 128:(hc + 1) * 128], hT[:, kc, blk * 512:(blk + 1) * 512],
                                                   start=(kc == 0), stop=(kc == 7)), r=[WSL[0], HT[kc][blk]], w=[BK[bk]], inc=(kc == 7))
                V("scalar", lambda h: h.activation(out=sz[:, hc, blk * 512:(blk + 1) * 512], in_=banks[bk][:, :], func=AF.Silu), r=[BK[bk]], w=[SZ[hc]])
        for m in range(NT):
            bk = nextbank()
            for kc in range(8):
                V("tensor", lambda h: h.matmul(banks[bk][:, 0:8], hT[:, kc, m * 128:(m + 1) * 128], wg[:, kc, :],
                                               start=(kc == 0), stop=(kc == 7)), r=[WG, HT[kc][m // 4]], w=[BK[bk]], inc=(kc == 7))
            V("scalar", lambda h: h.copy(out=gpre[:, m, :], in_=banks[bk][:, 0:8]), r=[BK[bk]], w=[GPRE])
        dtb = par[:, P_DTB:P_DTB + 64].rearrange("p (m h) -> p m h", h=4)
        alog = par[:, P_ALOG:P_ALOG + 64].rearrange("p (m h) -> p m h", h=4)
        V("scalar", lambda h: h.activation(out=gt1, in_=gpre[:, :, 0:4], func=AF.Exp, scale=-1.0), r=[GPRE], w=[GT1])
        V("vector", lambda h: h.tensor_scalar_add(out=gt1, in0=gt1, scalar1=1.0), r=[GT1], w=[GT1])
        V("vector", lambda h: h.reciprocal(out=beta, in_=gt1), r=[GT1], w=[BETA])
        V("vector", lambda h: h.tensor_tensor(out=gt2, in0=gpre[:, :, 4:8], in1=dtb, op=ALU.add), r=[GPRE, PAR], w=[GT2])
        V("scalar", lambda h: h.activation(out=gt2, in_=gt2, func=AF.Exp), r=[GT2], w=[GT2])
        V("scalar", lambda h: h.activation(out=gt2, in_=gt2, func=AF.Ln, bias=1.0), r=[GT2], w=[GT2])
        V("scalar", lambda h: h.activation(out=gt1, in_=alog, func=AF.Exp), r=[PAR, GT1, BETA], w=[GT1])
        V("vector", lambda h: h.scalar_tensor_tensor(out=lg, in0=gt2, scalar=-1.0, in1=gt1, op0=ALU.mult, op1=ALU.mult), r=[GT1, GT2], w=[LG])
        dumpsrc["qkvT"] = (qkvT, QKV, "AG")
        dumpsrc["sz"] = (sz, SZ, "AG")
        dumpsrc["beta"] = (beta, [BETA], "AG")
        dumpsrc["lg"] = (lg, [LG], "AG")
        do_dumps("AG")
        if stop_after == "AG":
            finish()
            return nc
        fw.barrier()
        ar.release("hT", "wsl0", "wsl1", "wsl2", "wg", "pcb", "accb", "gpre", "gt1", "gt2")

        ogT = ar.view("ogT", [4, T], BF16)
        OGT = [Buf() for _ in range(4)]
        BQ = [[Buf(LK[i]) for _ in range(4)] for i in range(8)]

        def qap(b, q):
            return banks[b][:, q * 128:(q + 1) * 128]

        def flat(v):
            return v.rearrange("p h d -> p (h d)")

        def hb():
            return [Buf() for _ in range(4)]

        gam = ar.view("gam", [NT, 4], F32)
        egam = ar.view("egam", [NT, 4], F32)
        edec = ar.view("edec", [NT, 4], F32)
        glb = ar.view("glb", [2, 64], F32)
        X2 = ar.view("X2", [2, 64], F32)
        GAM, EGAM, EDEC, GLB, X2B = Buf(), Buf(), Buf(), Buf(), Buf()
        gbc = ar.view("gbc", [128], F32)
        GBC = Buf()
        fw.dma("sync", newd(), gbc, vec_d[7:8, 0:128].partition_broadcast(128), w=[GBC])
        negm = cst[:, C_NEGM:C_NEGM + 128]
        negmb = ar.view("negmb", [128], BF16)
        NEGMB = Buf()
        V("vector", lambda h: h.tensor_copy(out=negmb, in_=negm), r=[CST], w=[NEGMB])
        strict = cst[:, C_STR:C_STR + 128]
        lgf = lg.rearrange("p m h -> p (m h)")
        b0 = nextbank()
        V("tensor", lambda h: h.matmul(banks[b0][:, 0:64], cst[:, C_TRI:C_TRI + 128], lgf, start=True, stop=True), r=[CST, LG], w=[BQ[b0][0]])
        V("tensor", lambda h: h.matmul(banks[b0][:, 128:192], cst[:, C_BLK:C_BLK + 128], lgf, start=True, stop=True), r=[CST, LG], w=[BQ[b0][1]])
        V("vector", lambda h: h.tensor_copy(out=gam.rearrange("p m h -> p (m h)"), in_=banks[b0][:, 0:64]), r=[BQ[b0][0]], w=[GAM])
        V("scalar", lambda h: h.activation(out=egam.rearrange("p m h -> p (m h)"), in_=banks[b0][:, 0:64], func=AF.Exp), r=[BQ[b0][0]], w=[EGAM])
        V("vector", lambda h: h.tensor_tensor(out=edec.rearrange("p m h -> p (m h)"), in0=banks[b0][:, 128:192],
                                              in1=gam.rearrange("p m h -> p (m h)"), op=ALU.subtract), r=[BQ[b0][1], GAM], w=[EDEC])
        V("scalar", lambda h: h.activation(out=edec, in_=edec, func=AF.Exp), r=[EDEC], w=[EDEC])
        for hf in range(2):
            V("vector", lambda h: h.tensor_scalar_mul(out=X2[:, hf, :], in0=lgf, scalar1=cst[:, C_IND + hf:C_IND + hf + 1]), r=[LG, CST], w=[X2B])
        V("tensor", lambda h: h.matmul(banks[b0][:, 256:384], onesf, X2.rearrange("p a b -> p (a b)"), start=True, stop=True), r=[CST, X2B], w=[BQ[b0][2]])
        V("scalar", lambda h: h.activation(out=glb.rearrange("p a b -> p (a b)"), in_=banks[b0][:, 256:384], func=AF.Exp), r=[BQ[b0][2]], w=[GLB])

        dumpsrc["gam"] = (gam, [GAM], "G0")
        dumpsrc["glb"] = (glb, [GLB], "G0")
        if stop_after == "G0":
            do_dumps("G0")
            finish()
            return nc
        def hb3(n):
            return [[Buf() for _ in range(4)] for _ in range(n)]

        def v2(name, n, dt):
            return [ar.view("%s%d" % (name, i), [4, 128], dt) for i in range(n)]

        kbt = v2("kbt", 2, BF16); KBT = hb3(2)
        vbt = v2("vbt", 2, BF16); VBT = hb3(2)
        diagc = v2("diagc", 2, F32); DIAGC = hb3(2)
        Dm = v2("Dm", 2, F32); DM = hb3(2)
        DmS = v2("DmS", 2, F32); DMS = hb3(2)
        Am = [v2("Am%d_" % i, 2, BF16) for i in range(2)]; AMB = [hb3(2), hb3(2)]
        Bm = [v2("Bm%d_" % i, 2, BF16) for i in range(2)]; BMB = [hb3(2), hb3(2)]
        ac = v2("ac", 2, BF16); AC = hb3(2)
        Tt32 = v2("Tt32_", 2, F32); TT32 = hb3(2)
        Ttb = [v2("Ttb%d_" % i, 2, BF16) for i in range(2)]; TTB = [hb3(2), hb3(2)]
        kdt = v2("kdt", 3, BF16); KDT = hb3(3)
        acT = v2("acT", 3, BF16); ACTB = hb3(3)
        u_sb = v2("u_sb", 3, F32); US = hb3(3)
        wT_sb = v2("wT_sb", 3, BF16); WT = hb3(3)
        vnew = ar.view("vnew", [4, 128], BF16); VN = hb()
        tmpo = ar.view("tmpo", [4, 128], F32); TMPO = hb()
        otok = ar.view("otok", [4, 128], F32); OTOK = hb()
        on = ar.view("on", [4, 128], BF16); ON = hb()
        junkt = ar.view("junk", [8, 128], F32); JUNKS = [Buf() for _ in range(8)]
        junk2t = ar.view("junk2", [4, 128], F32); JUNK2S = [Buf() for _ in range(4)]
        jctr = [0]

        def njunk():
            jctr[0] += 1
            return jctr[0] % 8
        S32 = ar.view("S32", [4, 128], F32); S32B = hb()
        Sp = ar.view("Sp", [4, 128], F32); SPB = hb()
        Sbf = ar.view("Sbf", [4, 128], BF16); SBF = hb()
        sm = ar.view("sm", [3, 12, 4], F32)
        SM = [Buf(), Buf(), Buf()]
        SM2 = [Buf(), Buf(), Buf()]
        V("vector", lambda h: h.memset(flat(S32), 0.0), w=S32B)
        V("vector", lambda h: h.memset(flat(Sbf), 0.0), w=SBF)
        bWS, bOI, bO2, bSD, bOT = 4, 5, 6, 7, 4

        def bq(b, hh):
            return banks[b][:, :].bitcast(BF16)[:, hh * 256:hh * 256 + 128]

        def bqall(b):
            return banks[b][:, :].bitcast(BF16).rearrange("p (h c) -> p h c", h=4)[:, :, 0:128]

        def gen_prep(m):
            tok = slice(m * 128, (m + 1) * 128)
            pi = m % 2
            p3 = m % 3
            X, Y = 2 * pi, 2 * pi + 1
            rk, rq, bkk, skb, skd, sa, so, lnk, crow, sso = [sm[:, p3, i, :] for i in range(10)]
            S_ = [SM[p3]]
            t1 = banks[X][:, :].bitcast(BF16)
            for hh in range(4):
                kq = t1[:, hh * 256:hh * 256 + 128]
                vq = t1[:, hh * 256 + 128:hh * 256 + 256]
                V("tensor", lambda h: h.transpose(kq, qkvT[:, 4 + hh, tok], identb), r=[QKV[4 + hh], IDB], w=[BQ[X][hh]], inc=False)
                V("tensor", lambda h: h.transpose(vq, qkvT[:, 8 + hh, tok], identb), r=[QKV[8 + hh], IDB], w=[BQ[X][hh]], inc=False)
                V("tensor", lambda h: h.transpose(bq(Y, hh), qkvT[:, hh, tok], identb), r=[QKV[hh], IDB], w=[BQ[Y][hh]])
            for hh in range(4):
                kq = t1[:, hh * 256:hh * 256 + 128]
                j1, j2 = njunk(), njunk()
                V("scalar", lambda h: h.activation(out=junkt[:, j1, :], in_=kq, func=AF.Square, accum_out=rk[:, hh:hh + 1]), r=[BQ[X][hh]], w=[JUNKS[j1], SM[p3]])
                V("scalar", lambda h: h.activation(out=junkt[:, j2, :], in_=bq(Y, hh), func=AF.Square, accum_out=rq[:, hh:hh + 1]), r=[BQ[Y][hh]], w=[JUNKS[j2], SM[p3]])
            yield
            rkq = sm[:, p3, 0:2, :]
            V("scalar", lambda h: h.activation(out=rkq, in_=rkq, func=AF.Ln, bias=eps_col[:, 0:1]), r=S_ + [EPSB], w=S_)
            V("vector", lambda h: h.scalar_tensor_tensor(out=crow, in0=rk, scalar=-0.5, in1=gam[:, m, :], op0=ALU.mult, op1=ALU.subtract), r=S_ + [GAM], w=S_)
            V("scalar", lambda h: h.activation(out=rkq, in_=rkq, func=AF.Exp, scale=-0.5), r=S_, w=S_)
            V("vector", lambda h: h.tensor_tensor(out=bkk, in0=rk, in1=beta[:, m, :], op=ALU.mult), r=S_ + [BETA], w=S_)
            V("vector", lambda h: h.tensor_tensor(out=skb, in0=bkk, in1=egam[:, m, :], op=ALU.mult), r=S_ + [EGAM], w=S_)
            V("vector", lambda h: h.tensor_tensor(out=skd, in0=rk, in1=edec[:, m, :], op=ALU.mult), r=S_ + [EDEC], w=S_)
            V("vector", lambda h: h.tensor_scalar_mul(out=sa, in0=rq, scalar1=128.0 ** -0.5), r=S_, w=S_)
            V("vector", lambda h: h.tensor_tensor(out=so, in0=sa, in1=egam[:, m, :], op=ALU.mult), r=S_ + [EGAM], w=S_)
            yield
            for hh in range(4):
                kq = t1[:, hh * 256:hh * 256 + 128]
                vq = t1[:, hh * 256 + 128:hh * 256 + 256]
                V("scalar", lambda h: h.activation(out=diagc[pi][:, hh, :], in_=identf, func=AF.Copy, scale=crow[:, hh:hh + 1]), r=[CST] + S_, w=[DIAGC[pi][hh]])
                V("scalar", lambda h: h.activation(out=kbt[pi][:, hh, :], in_=kq, func=AF.Copy, scale=skb[:, hh:hh + 1]), r=[BQ[X][hh]] + S_, w=[KBT[pi][hh]])
                V("scalar", lambda h: h.activation(out=kdt[p3][:, hh, :], in_=kq, func=AF.Copy, scale=skd[:, hh:hh + 1]), r=[BQ[X][hh]] + S_, w=[KDT[p3][hh]])
                V("vector", lambda h: h.tensor_scalar_mul(out=vbt[pi][:, hh, :], in0=vq, scalar1=beta[:, m, hh:hh + 1]), r=[BQ[X][hh], BETA], w=[VBT[pi][hh]])
            yield
            for hh in range(4):
                V("tensor", lambda h: h.matmul(qap(Y, hh), onesf, diagc[pi][:, hh, :], start=True, stop=False), r=[CST, DIAGC[pi][hh]], w=[BQ[Y][hh]], inc=False)
                V("tensor", lambda h: h.matmul(qap(Y, hh), identb, negmb, start=False, stop=True), r=[IDB, NEGMB], w=[BQ[Y][hh]], inc=(hh == 3))
            for hh in range(4):
                V("tensor", lambda h: h.matmul(qap(X, hh), qkvT[:, 4 + hh, tok], qkvT[:, 4 + hh, tok], start=True, stop=True), r=[QKV[4 + hh]], w=[BQ[X][hh]], inc=(hh == 3))
            yield
            for hh in range(4):
                V("scalar", lambda h: h.activation(out=Dm[pi][:, hh, :], in_=qap(Y, hh), func=AF.Exp, bias=gam[:, m, hh:hh + 1]), r=[BQ[Y][hh], GAM], w=[DM[pi][hh]])
            for hh in range(4):
                V("gpsimd", lambda h: h.tensor_tensor(out=DmS[pi][:, hh, :], in0=Dm[pi][:, hh, :], in1=strict, op=ALU.mult), r=[DM[pi][hh], CST], w=[DMS[pi][hh]])
            for hh in range(4):
                V("tensor", lambda h: h.matmul(qap(Y, hh), qkvT[:, hh, tok], qkvT[:, 4 + hh, tok], start=True, stop=True), r=[QKV[hh], QKV[4 + hh]], w=[BQ[Y][hh]], inc=(hh == 3))
            yield
            for hh in range(4):
                V("vector", lambda h: h.scalar_tensor_tensor(out=Am[0][pi][:, hh, :], in0=qap(X, hh), scalar=bkk[:, hh:hh + 1], in1=DmS[pi][:, hh, :],
                                                             op0=ALU.mult, op1=ALU.mult), r=[BQ[X][hh], DMS[pi][hh]] + S_, w=[AMB[0][pi][hh]])
            for hh in range(4):
                V("vector", lambda h: h.scalar_tensor_tensor(out=ac[pi][:, hh, :], in0=qap(Y, hh), scalar=sa[:, hh:hh + 1], in1=Dm[pi][:, hh, :],
                                                             op0=ALU.mult, op1=ALU.mult), r=[BQ[Y][hh], DM[pi][hh]] + S_, w=[AC[pi][hh]])
            yield
            for hh in range(4):
                V("tensor", lambda h: h.transpose(bq(X, hh), Am[0][pi][:, hh, :], identb), r=[AMB[0][pi][hh], IDB], w=[BQ[X][hh]], inc=(hh == 3))
            for hh in range(4):
                V("tensor", lambda h: h.transpose(bq(Y, hh), ac[pi][:, hh, :], identb), r=[AC[pi][hh], IDB], w=[BQ[Y][hh]], inc=(hh == 3))
            V("scalar", lambda h: h.copy(out=Bm[0][pi], in_=bqall(X)), r=BQ[X], w=BMB[0][pi])
            for hh in range(4):
                V("vector", lambda h: h.tensor_tensor(out=Ttb[0][pi][:, hh, :], in0=identf, in1=bq(X, hh), op=ALU.subtract), r=[CST, BQ[X][hh]], w=[TTB[0][pi][hh]])
            for hh in range(4):
                V("vector", lambda h: h.tensor_tensor(out=Tt32[pi][:, hh, :], in0=identf, in1=bq(X, hh), op=ALU.subtract), r=[CST, BQ[X][hh]], w=[TT32[pi][hh]])
            V("scalar", lambda h: h.copy(out=acT[p3], in_=bqall(Y)), r=BQ[Y], w=ACTB[p3])
            yield
            cur = 0
            for s_ in range(5):
                for hh in range(4):
                    V("tensor", lambda h: h.matmul(qap(X, hh), Bm[cur][pi][:, hh, :], Am[cur][pi][:, hh, :], start=True, stop=True),
                      r=[BMB[cur][pi][hh], AMB[cur][pi][hh]], w=[BQ[X][hh]], inc=(hh == 3))
                if s_ < 4:
                    for hh in range(4):
                        V("tensor", lambda h: h.matmul(qap(Y, hh), Am[cur][pi][:, hh, :], Bm[cur][pi][:, hh, :], start=True, stop=True),
                          r=[BMB[cur][pi][hh], AMB[cur][pi][hh]], w=[BQ[Y][hh]], inc=(hh == 3))
                V("scalar", lambda h: h.copy(out=flat(Am[1 - cur][pi]), in_=banks[X][:, :]), r=BQ[X], w=AMB[1 - cur][pi])
                if s_ < 4:
                    V("scalar", lambda h: h.copy(out=flat(Bm[1 - cur][pi]), in_=banks[Y][:, :]), r=BQ[Y], w=BMB[1 - cur][pi])
                yield
                for hh in range(4):
                    V("tensor", lambda h: h.matmul(qap(X, hh), Am[1 - cur][pi][:, hh, :], Ttb[cur][pi][:, hh, :], start=True, stop=True),
                      r=[AMB[1 - cur][pi][hh], TTB[cur][pi][hh]], w=[BQ[X][hh]], inc=(hh == 3))
                V("vector", lambda h: h.tensor_tensor(out=flat(Ttb[1 - cur][pi]), in0=flat(Tt32[pi]), in1=banks[X][:, :], op=ALU.add),
                  r=TT32[pi] + BQ[X], w=TTB[1 - cur][pi])
                V("vector", lambda h: h.tensor_tensor(out=flat(Tt32[pi]), in0=flat(Tt32[pi]), in1=banks[X][:, :], op=ALU.add),
                  r=TT32[pi] + BQ[X], w=TT32[pi])
                cur = 1 - cur
                yield
            for hh in range(4):
                V("tensor", lambda h: h.matmul(qap(X, hh), Ttb[cur][pi][:, hh, :], vbt[pi][:, hh, :], start=True, stop=True),
                  r=[TTB[cur][pi][hh], VBT[pi][hh]], w=[BQ[X][hh]], inc=(hh == 3))
            for hh in range(4):
                V("tensor", lambda h: h.matmul(qap(Y, hh), kbt[pi][:, hh, :], Ttb[cur][pi][:, hh, :], start=True, stop=True),
                  r=[TTB[cur][pi][hh], KBT[pi][hh]], w=[BQ[Y][hh]], inc=(hh == 3))
            V("scalar", lambda h: h.copy(out=flat(u_sb[p3]), in_=banks[X][:, :]), r=BQ[X], w=US[p3])
            V("vector", lambda h: h.tensor_copy(out=flat(wT_sb[p3]), in_=banks[Y][:, :]), r=BQ[Y], w=WT[p3])
            yield

        def gen_scan(m):
            tok = slice(m * 128, (m + 1) * 128)
            p3 = m % 3
            so = sm[:, p3, 6, :]
            sso = sm[:, p3, 9, :]
            S_ = [SM[p3]]
            for half in range(2):
                r0 = half * 64
                c0 = m * 128 + r0
                for hh in range(4):
                    V("tensor", lambda h: h.matmul(qap(bWS, hh)[r0:r0 + 64, :], wT_sb[p3][:, hh, r0:r0 + 64], Sbf[:, hh, :], start=True, stop=True),
                      r=[WT[p3][hh], SBF[hh]], w=[BQ[bWS][hh]], inc=(hh == 3))
                for hh in range(4):
                    V("tensor", lambda h: h.matmul(qap(bOI, hh)[r0:r0 + 64, :], qkvT[:, hh, c0:c0 + 64], Sbf[:, hh, :], start=True, stop=True),
                      r=[QKV[hh], SBF[hh]], w=[BQ[bOI][hh]], inc=(hh == 3))
                for hh in range(4):
                    gcol = glb[:, half, m * 4 + hh:m * 4 + hh + 1]
                    V("gpsimd", lambda h: h.tensor_scalar(out=Sp[:, hh, :], in0=S32[:, hh, :], scalar1=gcol, scalar2=0.0, op0=ALU.mult, op1=ALU.add),
                      r=[S32B[hh], GLB], w=[SPB[hh]])
                yield
                V("vector", lambda h: h.tensor_tensor(out=flat(vnew[r0:r0 + 64, :, :]), in0=flat(u_sb[p3][r0:r0 + 64, :, :]),
                                                      in1=banks[bWS][r0:r0 + 64, :], op=ALU.subtract), r=US[p3] + BQ[bWS], w=VN)
                yield
                for hh in range(4):
                    V("tensor", lambda h: h.matmul(qap(bSD, hh), kdt[p3][r0:r0 + 64, hh, :], vnew[r0:r0 + 64, hh, :], start=True, stop=True),
                      r=[KDT[p3][hh], VN[hh]], w=[BQ[bSD][hh]], inc=(hh == 3))
                for hh in range(4):
                    V("tensor", lambda h: h.matmul(qap(bO2, hh)[r0:r0 + 64, :], acT[p3][r0:r0 + 64, hh, r0:r0 + 64], vnew[r0:r0 + 64, hh, :],
                                                   start=True, stop=True), r=[ACTB[p3][hh], VN[hh]], w=[BQ[bO2][hh]], inc=(hh == 3))
                yield
                V("vector", lambda h: h.tensor_tensor(out=flat(S32), in0=flat(Sp), in1=banks[bSD][:, :], op=ALU.add), r=SPB + BQ[bSD], w=S32B)
                V("scalar", lambda h: h.copy(out=flat(Sbf), in_=flat(S32)), r=S32B, w=SBF)
                yield
            for hh in range(4):
                V("scalar", lambda h: h.activation(out=tmpo[:, hh, :], in_=qap(bOI, hh), func=AF.Copy, scale=so[:, hh:hh + 1]), r=[BQ[bOI][hh]] + S_, w=[TMPO[hh]])
            V("vector", lambda h: h.tensor_tensor(out=flat(otok), in0=flat(tmpo), in1=banks[bO2][:, :], op=ALU.add), r=TMPO + BQ[bO2], w=OTOK)
            yield
            for hh in range(4):
                V("scalar", lambda h: h.activation(out=junk2t[:, hh, :], in_=otok[:, hh, :], func=AF.Square, accum_out=sso[:, hh:hh + 1]), r=[OTOK[hh]], w=[JUNK2S[hh], SM2[p3]])
            V("scalar", lambda h: h.activation(out=sso, in_=sso, func=AF.Ln, scale=1.0 / 128.0, bias=eps_col[:, 0:1]), r=[SM2[p3], EPSB], w=[SM2[p3]])
            V("scalar", lambda h: h.activation(out=sso, in_=sso, func=AF.Exp, scale=-0.5), r=[SM2[p3]], w=[SM2[p3]])
            yield
            for hh in range(4):
                V("vector", lambda h: h.scalar_tensor_tensor(out=on[:, hh, :], in0=otok[:, hh, :], scalar=sso[:, hh:hh + 1], in1=gbc,
                                                             op0=ALU.mult, op1=ALU.mult), r=[OTOK[hh], SM2[p3], GBC], w=[ON[hh]])
            for hh in range(4):
                V("tensor", lambda h: h.transpose(bq(bOT, hh), on[:, hh, :], identb), r=[ON[hh], IDB], w=[BQ[bOT][hh]], inc=(hh == 3))
            yield
            V("vector", lambda h: h.tensor_tensor(out=ogT[:, :, tok], in0=bqall(bOT), in1=sz[:, :, tok], op=ALU.mult), r=BQ[bOT] + SZ, w=OGT)
            yield

        preps = {}
        prep_done = set()
        scan_done = set()
        next_prep = 0
        scan_m = 0
        scan_g = None
        while len(scan_done) < NT:
            while len(preps) < 2 and next_prep < NT and (next_prep < 3 or (next_prep - 3) in scan_done) \
                    and (next_prep < 2 or (next_prep - 2) in prep_done):
                preps[next_prep] = gen_prep(next_prep)
                next_prep += 1
            if scan_g is None and scan_m < NT and scan_m in prep_done:
                scan_g = gen_scan(scan_m)
            progressed = False
            for _rep in range(2):
                if scan_g is None and scan_m < NT and scan_m in prep_done:
                    scan_g = gen_scan(scan_m)
                if scan_g is not None:
                    try:
                        next(scan_g)
                    except StopIteration:
                        scan_done.add(scan_m)
                        scan_m += 1
                        scan_g = None
                    progressed = True
            for t_ in sorted(preps):
                try:
                    next(preps[t_])
                except StopIteration:
                    prep_done.add(t_)
                    del preps[t_]
                progressed = True
            assert progressed
        dumpsrc["ogT"] = (ogT, OGT, "G")
        do_dumps("G")
        if stop_after == "G":
            finish()
            return nc
        fw.barrier()
        ar.release("gam", "egam", "edec", "glb", "X2", "gbc", "negmb", "kbt0", "kbt1", "vbt0", "vbt1", "diagc0", "diagc1", "Dm0", "Dm1",
                   "DmS0", "DmS1", "Am0_0", "Am0_1", "Am1_0", "Am1_1", "Bm0_0", "Bm0_1", "Bm1_0", "Bm1_1", "ac0", "ac1", "Tt32_0", "Tt32_1",
                   "Ttb0_0", "Ttb0_1", "Ttb1_0", "Ttb1_1", "kdt0", "kdt1", "kdt2", "acT0", "acT1", "acT2", "u_sb0", "u_sb1", "u_sb2",
                   "wT_sb0", "wT_sb1", "wT_sb2", "vnew", "tmpo", "otok", "on", "junk", "junk2", "S32", "Sp", "Sbf", "sm",
                   "qkvT", "sz", "beta", "lg")

        Rr = ar.view("R", [NT, D], F32)
        RB = [[Buf(), Buf()] for _ in range(NT)]
        h1T = ar.view("h1T", [8, T], BF16)
        H1T = [[Buf() for _ in range(NT)] for _ in range(8)]
        wo_g = ar.view("wo_g", [4, 512], BF16)
        wo_f = ar.view("wo_f", [8, 512], BF16)
        WOG, WOF = Buf(), Buf()
        wog_sem, wof_sem = newd(), newd()
        xts = [ar.view("xt%d" % i, [D], F32) for i in range(4)]
        XT = [Buf() for _ in range(4)]
        xsem2 = [newd() for _ in range(2)]
        gA = ar.view("gA", [D], F32)
        bA = ar.view("bA", [D], F32)
        GA, BA = Buf(), Buf()
        st = ar.view("st", [4, 12], F32)
        STB = [Buf() for _ in range(4)]
        mv1 = ar.view("mv1", [NT, 2], F32)
        MV1 = [Buf() for _ in range(NT)]
        w_out_g = w_out[0:512, :].rearrange("(kc p) c -> p kc c", p=128)
        w_out_f = w_out[512:1024, :].rearrange("(h p) c -> p h c", p=64)
        fw.dma("gpsimd", wog_sem, wo_g, w_out_g[:, :, 0:512], w=[WOG])
        fw.dma("gpsimd", wof_sem, wo_f[0:64, :, :], w_out_f[:, :, 0:512], w=[WOF])
        bcsem = [newd(), newd()]

        def load_bc(rowg, rowb):
            fw.dma("sync", bcsem[0], gA, vec_d[rowg:rowg + 1, :].partition_broadcast(128), w=[GA])
            fw.dma("sync", bcsem[1], bA, vec_d[rowb:rowb + 1, :].partition_broadcast(128), w=[BA])
            V("scalar", lambda h: h.activation(out=gA, in_=gA, func=AF.Copy, scale=ALPHA), r=[GA], w=[GA])
            V("scalar", lambda h: h.activation(out=bA, in_=bA, func=AF.Copy, scale=ALPHA), r=[BA], w=[BA])

        load_bc(0, 1)
        nmr = ar.view("nmr", [NT], F32)
        NMR = Buf()
        V("vector", lambda h: h.scalar_tensor_tensor(out=nmr, in0=mv[:, :, 0], scalar=-1.0, in1=mv[:, :, 1], op0=ALU.mult, op1=ALU.mult), r=MV, w=[NMR])
        for m in range(NT):
            j = m % 2
            fw.dma("sync", xsem2[j], xts[j], x[m * 128:(m + 1) * 128, :], w=[XT[j]])
            V("scalar", lambda h: h.activation(out=xts[j], in_=xts[j], func=AF.Identity, scale=mv[:, m, 1:2], bias=nmr[:, m:m + 1]),
              r=[XT[j], MV[m], NMR], w=[XT[j]])
            V("vector", lambda h: h.tensor_tensor(out=Rr[:, m, :], in0=xts[j], in1=gA, op=ALU.mult), r=[XT[j], GA], w=RB[m])
            V("vector", lambda h: h.tensor_tensor(out=Rr[:, m, :], in0=Rr[:, m, :], in1=bA, op=ALU.add), r=RB[m] + [BA], w=RB[m])
        for half in range(2):
            if half == 1:
                fw.dma("gpsimd", wog_sem, wo_g, w_out_g[:, :, 512:1024], w=[WOG])
                fw.dma("gpsimd", wof_sem, wo_f[0:64, :, :], w_out_f[:, :, 512:1024], w=[WOF])
            for m in range(NT):
                tok = slice(m * 128, (m + 1) * 128)
                bk = nextbank()
                for kc in range(4):
                    V("tensor", lambda h: h.matmul(banks[bk][:, :], ogT[:, kc, tok], wo_g[:, kc, :], start=(kc == 0), stop=False),
                      r=[OGT[kc], WOG], w=[BK[bk]], inc=False)
                for h8 in range(8):
                    V("tensor", lambda h: h.matmul(banks[bk][:, :], ofox[0:64, h8, tok], wo_f[0:64, h8, :], start=False, stop=(h8 == 7)),
                      r=[OFOX[h8], WOF], w=[BK[bk]], inc=(h8 == 7))
                V("vector", lambda h: h.tensor_tensor(out=Rr[:, m, half * 512:(half + 1) * 512], in0=Rr[:, m, half * 512:(half + 1) * 512],
                                                      in1=banks[bk][:, :], op=ALU.add), r=[RB[m][half], BK[bk]], w=[RB[m][half]])
        load_bc(2, 3)
        for g in range(NBLK):
            for j in range(4):
                m = 4 * g + j
                V("vector", lambda h: h.bn_stats(out=st[:, j, 0:6], in_=Rr[:, m, 0:512]), r=[RB[m][0]], w=[STB[j]])
                V("vector", lambda h: h.bn_stats(out=st[:, j, 6:12], in_=Rr[:, m, 512:1024]), r=[RB[m][1]], w=[STB[j]])
                V("vector", lambda h: h.bn_aggr(out=mv1[:, m, 0:2], in_=st[:, j, 0:12]), r=[STB[j]], w=[MV1[m]])
                V("vector", lambda h: h.tensor_scalar_add(out=mv1[:, m, 1:2], in0=mv1[:, m, 1:2], scalar1=LN_EPS), r=[MV1[m]], w=[MV1[m]])
                V("gpsimd", lambda h: h.tensor_tensor(out=mv1[:, m, 1:2], in0=mv1[:, m, 1:2], in1=m05, op=ALU.pow), r=[MV1[m], CST], w=[MV1[m]])
                V("vector", lambda h: h.scalar_tensor_tensor(out=mv1[:, m, 0:1], in0=mv1[:, m, 0:1], scalar=-1.0, in1=mv1[:, m, 1:2],
                                                             op0=ALU.mult, op1=ALU.mult), r=[MV1[m]], w=[MV1[m]])
                V("scalar", lambda h: h.activation(out=xts[j], in_=Rr[:, m, :], func=AF.Identity, scale=mv1[:, m, 1:2], bias=mv1[:, m, 0:1]),
                  r=RB[m] + [MV1[m]], w=[XT[j]])
                V("vector", lambda h: h.tensor_tensor(out=Rr[:, m, :], in0=xts[j], in1=gA, op=ALU.mult), r=[XT[j], GA], w=RB[m])
                V("vector", lambda h: h.tensor_tensor(out=Rr[:, m, :], in0=Rr[:, m, :], in1=bA, op=ALU.add), r=RB[m] + [BA], w=RB[m])
            for c in range(8):
                bk = nextbank()
                for j in range(4):
                    V("tensor", lambda h: h.transpose(banks[bk][:, j * 128:(j + 1) * 128], xts[j][:, c * 128:(c + 1) * 128], identf),
                      r=[XT[j], CST], w=[BK[bk]], inc=(j == 3))
                V("scalar", lambda h: h.activation(out=h1T[:, c, g * 512:(g + 1) * 512], in_=banks[bk][:, :], func=AF.Identity,
                                                   scale=par[:, P_G1 + c:P_G1 + c + 1], bias=par[:, P_B1 + c:P_B1 + c + 1]),
                  r=[BK[bk], PAR], w=[H1T[c][4 * g + jj] for jj in range(4)])
        dumpsrc["h1T"] = (h1T, [b for row in H1T for b in row], "C")
        dumpsrc["R"] = (Rr, [b for p_ in RB for b in p_], "C")
        do_dumps("C")
        if stop_after == "C":
            finish()
            return nc
        fw.barrier()
        ar.release("ofox", "ogT", "wo_g", "wo_f", "xt0", "xt1", "xt2", "xt3", "gA", "bA", "nmr")

        NG = DFF // 512
        wu = [ar.view("wu%d" % i, [8, 512], BF16) for i in range(2)]
        wd = [ar.view("wd%d" % i, [4, D], BF16) for i in range(2)]
        WU = [Buf(), Buf()]
        WD = [Buf(), Buf()]
        wusem = [newd(), newd()]
        wdsem = [newd(), newd()]
        w_up_v = w_up.rearrange("(kc p) f -> p kc f", p=128)
        w_down_v = w_down.rearrange("(fc p) c -> p fc c", p=128)
        def load_ffw(g):
            s_ = g % 2
            fw.dma("gpsimd", wusem[s_], wu[s_], w_up_v[:, :, g * 512:(g + 1) * 512], w=[WU[s_]])
            fw.dma("gpsimd", wdsem[s_], wd[s_], w_down_v[:, g * 4:(g + 1) * 4, :], w=[WD[s_]])

        load_ffw(0)
        pTt = ar.view("pTt", [2, T], BF16)
        PTB = [Buf() for _ in range(NT)]
        ptile = ar.view("ptile", [2, 256], F32)
        PTL = [Buf() for _ in range(2)]
        psem = [newd(), newd()]
        wple = ar.view("wple", [2, D], BF16)
        wgt = ar.view("wgt", [8, D], BF16)
        WPLE, WGT = Buf(), Buf()
        bgf = ar.view("bgf", [D], F32)
        bgh = ar.view("bgh", [D], BF16)
        bgl = ar.view("bgl", [D], BF16)
        ones_b = ar.view("ones_b", [128], BF16)
        BGF, BGH, BGL, ONB = Buf(), Buf(), Buf(), Buf()
        sg = ar.view("sg", [2, 512], F32)
        SG = [Buf(), Buf()]
        fw.dma("gpsimd", newd(), wple, w_ple.rearrange("(kc p) c -> p kc c", p=128), w=[WPLE])
        fw.dma("gpsimd", newd(), wgt, w_gate.rearrange("(kc p) c -> p kc c", p=128), w=[WGT])
        fw.dma("sync", newd(), bgf[0:1, :], vec_d[6:7, :], w=[BGF])
        V("vector", lambda h: h.tensor_copy(out=bgh[0:1, :], in_=bgf[0:1, :]), r=[BGF], w=[BGH])
        V("vector", lambda h: h.tensor_tensor(out=bgf[0:1, :], in0=bgf[0:1, :], in1=bgh[0:1, :], op=ALU.subtract), r=[BGF, BGH], w=[BGF])
        V("vector", lambda h: h.tensor_copy(out=bgl[0:1, :], in_=bgf[0:1, :]), r=[BGF], w=[BGL])
        V("vector", lambda h: h.memset(ones_b, 1.0), w=[ONB])
        for m in range(NT):
            j = m % 2
            tok = slice(m * 128, (m + 1) * 128)
            fw.dma("sync", psem[j], ptile[:, j, :], p_in[tok, :], w=[PTL[j]])
            bk = nextbank()
            for kc in range(2):
                V("tensor", lambda h: h.transpose(banks[bk][:, kc * 128:(kc + 1) * 128], ptile[:, j, kc * 128:(kc + 1) * 128], identf),
                  r=[PTL[j], CST], w=[BK[bk]], inc=(kc == 1))
            V("scalar", lambda h: h.copy(out=pTt[:, :, tok], in_=banks[bk][:, 0:256].rearrange("p (k t) -> p k t", k=2)), r=[BK[bk]], w=[PTB[m]])
            for half in range(2):
                cs = slice(half * 512, (half + 1) * 512)
                bp, bg_ = nextbank(), nextbank()
                for kc in range(2):
                    V("tensor", lambda h: h.matmul(banks[bp][:, :], pTt[:, kc, tok], wple[:, kc, cs], start=(kc == 0), stop=(kc == 1)),
                      r=[PTB[m], WPLE], w=[BK[bp]], inc=(kc == 1))
                for kc in range(8):
                    V("tensor", lambda h: h.matmul(banks[bg_][:, :], h1T[:, kc, tok], wgt[:, kc, cs], start=(kc == 0), stop=False),
                      r=[H1T[kc][m], WGT], w=[BK[bg_]], inc=False)
                V("tensor", lambda h: h.matmul(banks[bg_][:, :], ones_b[0:1, :], bgh[0:1, cs], start=False, stop=False), r=[ONB, BGH], w=[BK[bg_]], inc=False)
                V("tensor", lambda h: h.matmul(banks[bg_][:, :], ones_b[0:1, :], bgl[0:1, cs], start=False, stop=True), r=[ONB, BGL], w=[BK[bg_]])
                V("scalar", lambda h: h.activation(out=sg[:, half, :], in_=banks[bg_][:, :], func=AF.Sigmoid), r=[BK[bg_]], w=[SG[half]])
                V("vector", lambda h: h.tensor_tensor(out=sg[:, half, :], in0=sg[:, half, :], in1=banks[bp][:, :], op=ALU.mult), r=[SG[half], BK[bp]], w=[SG[half]])
                V("gpsimd", lambda h: h.tensor_tensor(out=Rr[:, m, cs], in0=Rr[:, m, cs], in1=sg[:, half, :], op=ALU.add), r=[RB[m][half], SG[half]], w=[RB[m][half]])
        if stop_after == "D1":
            dumpsrc["R1"] = (Rr, [b for p_ in RB for b in p_], "D1")
            do_dumps("D1")
            finish()
            return nc
        fw.barrier()
        ar.release("pTt", "ptile", "wple", "wgt", "bgf", "bgh", "bgl", "ones_b", "sg")

        actT = [ar.view("actT%d" % i, [4, T], BF16) for i in range(2)]
        ACTT = [[[Buf() for _ in range(NBLK)] for _ in range(4)] for _ in range(2)]
        rl = [ar.view("rl%d" % i, [512], F32) for i in range(2)]
        RL = [Buf(), Buf()]

        gA = ar.view("gA", [D], F32)
        bA = ar.view("bA", [D], F32)
        GA, BA = Buf(), Buf()
        fw.dma("sync", bcsem[0], gA, vec_d[4:5, :].partition_broadcast(128), w=[GA])
        fw.dma("sync", bcsem[1], bA, vec_d[5:6, :].partition_broadcast(128), w=[BA])
        yo = ar.view("yo", [2, D], F32)
        YO = [Buf(), Buf()]
        osem = [newd(), newd()]

        def emit_E1(m):
            j = m % 4
            V("vector", lambda h: h.bn_stats(out=st[:, j, 0:6], in_=Rr[:, m, 0:512]), r=[RB[m][0]], w=[STB[j]])
            V("vector", lambda h: h.bn_stats(out=st[:, j, 6:12], in_=Rr[:, m, 512:1024]), r=[RB[m][1]], w=[STB[j]])
            V("vector", lambda h: h.bn_aggr(out=mv1[:, m, 0:2], in_=st[:, j, 0:12]), r=[STB[j]], w=[MV1[m]])
            V("vector", lambda h: h.tensor_scalar_add(out=mv1[:, m, 1:2], in0=mv1[:, m, 1:2], scalar1=LN_EPS), r=[MV1[m]], w=[MV1[m]])
            V("gpsimd", lambda h: h.tensor_tensor(out=mv1[:, m, 1:2], in0=mv1[:, m, 1:2], in1=m05, op=ALU.pow), r=[MV1[m], CST], w=[MV1[m]])

        def emit_E2(m):
            j = m % 2
            V("vector", lambda h: h.scalar_tensor_tensor(out=mv1[:, m, 0:1], in0=mv1[:, m, 0:1], scalar=-1.0, in1=mv1[:, m, 1:2],
                                                         op0=ALU.mult, op1=ALU.mult), r=[MV1[m]], w=[MV1[m]])
            V("scalar", lambda h: h.activation(out=yo[:, j, :], in_=Rr[:, m, :], func=AF.Identity, scale=mv1[:, m, 1:2], bias=mv1[:, m, 0:1]),
              r=RB[m] + [MV1[m]], w=[YO[j]])

        def emit_E3(m):
            j = m % 2
            V("vector", lambda h: h.tensor_tensor(out=yo[:, j, :], in0=yo[:, j, :], in1=gA, op=ALU.mult), r=[YO[j], GA], w=[YO[j]])
            V("vector", lambda h: h.tensor_tensor(out=yo[:, j, :], in0=yo[:, j, :], in1=bA, op=ALU.add), r=[YO[j], BA], w=[YO[j]])
            final_tickets.append(fw.dma("sync", osem[j], out[m * 128:(m + 1) * 128, :], yo[:, j, :], r=[YO[j]]))

        def emit_E_step(k):
            if 0 <= k < NT:
                emit_E1(k)
            if 0 <= k - 1 < NT:
                emit_E2(k - 1)
            if 0 <= k - 2 < NT:
                emit_E3(k - 2)

        rcnt = 0
        for g in range(NG):
            s_ = g % 2
            if g + 1 < NG:
                load_ffw(g + 1)
            for blk in range(NBLK):
                for fc in range(4):
                    bk = nextbank()
                    for kc in range(8):
                        V("tensor", lambda h: h.matmul(banks[bk][:, :], wu[s_][:, kc, fc * 128:(fc + 1) * 128], h1T[:, kc, blk * 512:(blk + 1) * 512],
                                                       start=(kc == 0), stop=(kc == 7)),
                          r=[WU[s_]] + [H1T[kc][mm] for mm in range(blk * 4, blk * 4 + 4)], w=[BK[bk]], inc=(kc == 7))
                    q = rcnt % 2
                    rcnt += 1
                    V("scalar", lambda h: h.activation(out=rl[q], in_=banks[bk][:, :], func=AF.Relu), r=[BK[bk]], w=[RL[q]])
                    V("gpsimd", lambda h: h.tensor_tensor(out=actT[s_][:, fc, blk * 512:(blk + 1) * 512], in0=rl[q], in1=rl[q], op=ALU.mult),
                      r=[RL[q]], w=[ACTT[s_][fc][blk]])
            for m in range(NT):
                tok = slice(m * 128, (m + 1) * 128)
                for half in range(2):
                    cs = slice(half * 512, (half + 1) * 512)
                    bk = nextbank()
                    for fc in range(4):
                        V("tensor", lambda h: h.matmul(banks[bk][:, :], actT[s_][:, fc, tok], wd[s_][:, fc, cs], start=(fc == 0), stop=(fc == 3)),
                          r=[ACTT[s_][fc][m // 4], WD[s_]], w=[BK[bk]], inc=(fc == 3))
                    V("vector", lambda h: h.tensor_tensor(out=Rr[:, m, cs], in0=Rr[:, m, cs], in1=banks[bk][:, :], op=ALU.add), r=[RB[m][half], BK[bk]], w=[RB[m][half]])
                if g == NG - 1:
                    emit_E_step(m - 1)
        for k in range(NT - 1, NT + 2):
            emit_E_step(k)

        finish()
    return nc


_CACHE = {}


def _host_consts():
    c = np.zeros((128, NCST), np.float32)
    i = np.arange(128)
    c[:, C_ID:C_ID + 128] = np.eye(128, dtype=np.float32)
    c[:, C_ONE:C_ONE + 128] = 1.0
    same = (i[:, None] // 64) == (i[None, :] // 64)
    c[:, C_TRI:C_TRI + 128] = (same & (i[:, None] <= i[None, :])).astype(np.float32)
    c[:, C_BLK:C_BLK + 128] = same.astype(np.float32)
    c[:, C_NEGM:C_NEGM + 128] = np.where(same & (i[:, None] >= i[None, :]), 0.0, NEG)
    c[:, C_STR:C_STR + 128] = (same & (i[:, None] > i[None, :])).astype(np.float32)
    c[:, C_IND] = (i < 64)
    c[:, C_IND + 1] = (i >= 64)
    c[0:64, C_OAUG:C_OAUG + 128] = 1.0
    c[64, C_OAUG:C_OAUG + 128] = 64.0 * NORM_EPS
    c[:, C_M05] = -0.5
    c[:, C_M05 + 1] = NORM_EPS
    return c


def _prep_shared(inp):
    par = np.zeros((128, NPAR), np.float32)
    par[:, P_G0:P_G0 + 8] = inp["ln_in_g"].reshape(8, 128).T
    par[:, P_B0:P_B0 + 8] = inp["ln_in_b"].reshape(8, 128).T
    cw = inp["conv_w"][0]
    par[:, P_CW:P_CW + 48] = cw.T.reshape(12, 128, 4).transpose(1, 0, 2).reshape(128, 48)
    par[0:64, P_FOXG] = inp["fox_norm_g"][0]
    par[0:8, P_BF] = inp["b_f"][0]
    par[:, P_DTB:P_DTB + 64] = np.tile(inp["dt_bias"][0], 16)[None, :]
    par[:, P_ALOG:P_ALOG + 64] = np.tile(inp["a_log"][0], 16)[None, :]
    par[:, P_G1:P_G1 + 8] = inp["ln1_g"][0].reshape(8, 128).T
    par[:, P_B1:P_B1 + 8] = inp["ln1_b"][0].reshape(8, 128).T
    vecs = np.zeros((8, D), np.float32)
    vecs[0] = inp["ln_in_g"]
    vecs[1] = inp["ln_in_b"]
    vecs[2] = inp["ln1_g"][0]
    vecs[3] = inp["ln1_b"][0]
    vecs[4] = inp["ln2_g"][0]
    vecs[5] = inp["ln2_b"][0]
    vecs[6] = inp["b_ple_gate"][0]
    vecs[7, 0:128] = inp["gdn_norm_g"][0]
    return {
        "w_in": np.ascontiguousarray(inp["w_in"][0]), "w_out": np.ascontiguousarray(inp["w_out"][0]),
        "w_up": np.ascontiguousarray(inp["w_up"][0]), "w_down": np.ascontiguousarray(inp["w_down"][0]),
        "w_ple": np.ascontiguousarray(inp["w_ple"][0]), "w_gate": np.ascontiguousarray(inp["w_ple_gate"][0]),
        "cst": _host_consts(), "par": par, "vecs": vecs, "ones_rows": np.ones((3, 8 * T), np.float32),
    }


def run(inp, stop_after=None, dumps=(), cores=8):
    key = (stop_after, tuple(dumps))
    if key not in _CACHE:
        _CACHE[key] = build(stop_after, dumps)
    nc = _CACHE[key]
    inp = {k: np.asarray(v, dtype=np.float32) for k, v in inp.items()}
    shared = _prep_shared(inp)
    in_maps = []
    for b in range(cores):
        m = dict(shared)
        m["x"] = np.ascontiguousarray(inp["x"][b])
        m["p"] = np.ascontiguousarray(inp["p"][0, b])
        in_maps.append(m)
    res = run_bass_kernel_spmd(nc, in_maps, core_ids=list(range(cores)))
    return res.results


def kernel(**inputs):
    results = run(inputs)
    return np.stack([r["out"] for r in results], axis=0).astype(np.float32)
```

```python
import numpy as np
from contextlib import ExitStack
import concourse.bass as bass
import concourse.mybir as mybir
from concourse.bass_utils import run_bass_kernel_spmd

F32 = mybir.dt.float32
BF16 = mybir.dt.bfloat16
F32R = mybir.dt.float32r
AF = mybir.ActivationFunctionType
ALU = mybir.AluOpType

T, D, NT, NBLK = 2048, 1024, 16, 4
DFF = 4096
ALPHA = 2.0 ** 0.25
LN_EPS = 1e-5
NORM_EPS = 1e-6
NEG = -30000.0
DSZ = {F32: 4, BF16: 2, F32R: 4}

C_ID, C_ONE, C_TRI, C_BLK, C_NEGM, C_STR, C_IND, C_OAUG, C_M05, NCST = 0, 128, 256, 384, 512, 640, 768, 770, 898, 900
P_G0, P_B0, P_CW, P_FOXG, P_BF, P_DTB, P_ALOG, P_G1, P_B1, NPAR = 0, 8, 16, 64, 65, 66, 130, 194, 202, 210


class Buf:
    __slots__ = ("w", "r", "lock")

    def __init__(self, lock=None):
        self.w = None
        self.r = {}
        self.lock = lock


class Ticket:
    __slots__ = ("sem", "val", "key", "eng")

    def __init__(self, sem, val, key, eng):
        self.sem, self.val, self.key, self.eng = sem, val, key, eng


class Eng:
    def __init__(self, name, h, sem):
        self.name, self.h, self.sem = name, h, sem
        self.count = 0
        self.waited = {}


class FW:
    def __init__(self, nc, es):
        self.nc, self.es = nc, es
        self.E = {}
        for n in ("tensor", "vector", "scalar", "gpsimd", "sync"):
            sem = es.enter_context(nc.semaphore("s_" + n))
            self.E[n] = Eng(n, getattr(nc, n), sem)
        self.dsems = []

    def _wait(self, e, t):
        if e.waited.get(t.key, 0) >= t.val:
            return
        e.h.wait_ge(t.sem, t.val)
        e.waited[t.key] = t.val

    def _deps(self, e, reads, writes):
        for b in reads:
            t = b.w
            if t is not None and not (t.eng == e.name and e.name == "tensor"):
                self._wait(e, t)
        for b in writes:
            t = b.w
            if t is not None and (t.eng != e.name or (e.name != "tensor" and t.val <= e.count)):
                self._wait(e, t)
            for t in b.r.values():
                if t.eng != e.name:
                    self._wait(e, t)

    @staticmethod
    def _mark(t, reads, writes):
        for b in reads:
            b.r[t.key] = t
        for b in writes:
            b.w = t
            b.r = {}

    def op(self, eng, fn, r=(), w=(), inc=True):
        e = self.E[eng]
        locks = []
        for b in list(r) + list(w):
            if b.lock is not None and b.lock not in locks:
                locks.append(b.lock)
        for lk in locks:
            t = lk.w
            if t is not None and t.eng != e.name:
                self._wait(e, t)
        self._deps(e, r, w)
        ins = fn(e.h)
        if inc:
            e.count += 1
            ins.then_inc(e.sem, 1)
            t = Ticket(e.sem, e.count, "e_" + eng, eng)
        else:
            t = Ticket(e.sem, e.count + 1, "e_" + eng, eng)
        self._mark(t, r, w)
        for lk in locks:
            lk.w = t
        return t

    def dsem(self, name):
        sem = self.es.enter_context(self.nc.semaphore(name))
        d = [sem, 0, name]
        self.dsems.append(d)
        return d

    def dma(self, q, d, out, in_, r=(), w=()):
        e = self.E[q]
        self._deps(e, r, w)
        ins = e.h.dma_start(out=out, in_=in_)
        d[1] += 16
        ins.then_inc(d[0], 16)
        t = Ticket(d[0], d[1], "d_" + d[2], "dma")
        self._mark(t, r, w)
        return t

    def barrier(self):
        tl = [Ticket(e.sem, e.count, "e_" + n, n) for n, e in self.E.items() if e.count > 0]
        tl += [Ticket(d[0], d[1], "d_" + d[2], "dma") for d in self.dsems if d[1] > 0]
        for e in self.E.values():
            for t in tl:
                if t.eng != e.name:
                    self._wait(e, t)


class Arena:
    def __init__(self, ap, nbytes):
        self.ap = ap
        self.free = [(0, nbytes)]
        self.live = {}

    def alloc(self, name, nbytes):
        nbytes = (nbytes + 63) // 64 * 64
        for i, (o, s) in enumerate(self.free):
            if s >= nbytes:
                self.free[i] = (o + nbytes, s - nbytes)
                self.live[name] = (o, nbytes)
                return o
        raise RuntimeError("arena full for %s (%d) free=%s" % (name, nbytes, self.free))

    def release(self, *names):
        for name in names:
            o, s = self.live.pop(name)
            self.free.append((o, s))
        self.free.sort()
        m = []
        for o, s in self.free:
            if s == 0:
                continue
            if m and m[-1][0] + m[-1][1] == o:
                m[-1] = (m[-1][0], m[-1][1] + s)
            else:
                m.append((o, s))
        self.free = m

    def view(self, name, shape, dt):
        n = 1
        for s in shape:
            n *= s
        nb = n * DSZ[dt]
        o = self.alloc(name, nb)
        v = self.ap[:, o // 4:(o + nb) // 4]
        if dt != F32:
            v = v.bitcast(dt)
        if len(shape) == 2:
            v = v.rearrange("p (a b) -> p a b", a=shape[0])
        elif len(shape) == 3:
            v = v.rearrange("p (a b c) -> p a b c", a=shape[0], b=shape[1])
        elif len(shape) == 4:
            v = v.rearrange("p (a b c d) -> p a b c d", a=shape[0], b=shape[1], c=shape[2])
        return v


def build(stop_after=None, dumps=()):
    nc = bass.Bass("TRN2", target_bir_lowering=False)

    def din(name, shape):
        return nc.dram_tensor(name, list(shape), F32, kind="ExternalInput").ap()

    x = din("x", [T, D])
    p_in = din("p", [T, 256])
    w_in = din("w_in", [D, 3600])
    w_out = din("w_out", [D, D])
    w_up = din("w_up", [D, DFF])
    w_down = din("w_down", [DFF, D])
    w_ple = din("w_ple", [256, D])
    w_gate = din("w_gate", [D, D])
    cst_d = din("cst", [128, NCST])
    par_d = din("par", [128, NPAR])
    vec_d = din("vecs", [8, D])
    ones_d = din("ones_rows", [3, 8 * T])
    out = nc.dram_tensor("out", [T, D], F32, kind="ExternalOutput").ap()
    dump_out = {}
    for (nm, shp, dt) in dumps:
        dump_out[nm] = nc.dram_tensor("dbg_" + nm, list(shp), dt, kind="ExternalOutput").ap()

    w_in_v = w_in.rearrange("(kc p) c -> p kc c", p=128)

    with ExitStack() as es:
        fw = FW(nc, es)
        ARENA_BYTES = 204 * 1024
        arena_t = es.enter_context(nc.sbuf_tensor("arena", [128, ARENA_BYTES // 4], F32))
        ar = Arena(arena_t[:, :], ARENA_BYTES)
        banks = [es.enter_context(nc.psum_tensor("bank%d" % i, [128, 512], F32)) for i in range(8)]
        LK = [Buf() for _ in range(8)]
        BK = [Buf(LK[i]) for i in range(8)]
        rot = [0]

        def nextbank(lo=0, hi=8):
            b = lo + rot[0] % (hi - lo)
            rot[0] += 1
            return b

        def V(eng, fn, r=(), w=(), inc=True):
            return fw.op(eng, fn, r, w, inc)

        dctr = [0]

        def newd():
            dctr[0] += 1
            return fw.dsem("d%d" % dctr[0])

        final_tickets = []

        def do_dumps(stage):
            for (nm, shp, dt) in dumps:
                if nm in dumpsrc and dumpsrc[nm][2] == stage:
                    src, bufs, _ = dumpsrc[nm]
                    final_tickets.append(fw.dma("sync", newd(), dump_out[nm], src, r=bufs))

        dumpsrc = {}

        def finish():
            for t in final_tickets:
                fw._wait(fw.E["sync"], t)

        cst = ar.view("cst", [NCST], F32)
        par = ar.view("par", [NPAR], F32)
        CST, PAR = Buf(), Buf()
        fw.dma("sync", newd(), cst, cst_d, w=[CST])
        fw.dma("sync", newd(), par, par_d, w=[PAR])
        identf = cst[:, C_ID:C_ID + 128]
        onesf = cst[:, C_ONE:C_ONE + 128]
        m05 = cst[:, C_M05:C_M05 + 1]
        eps_col = cst[:, C_M05 + 1:C_M05 + 2]
        EPSB = CST
        identb = ar.view("identb", [128], BF16)
        IDB = Buf()
        V("vector", lambda h: h.tensor_copy(out=identb, in_=identf), r=[CST], w=[IDB])
        mv = ar.view("mv", [NT, 2], F32)
        MV = [Buf() for _ in range(NT)]

        hT = ar.view("hT", [8, T], BF16)
        HT = [[Buf() for _ in range(NBLK)] for _ in range(8)]
        xt = ar.view("xt", [8, D], F32)
        XT = [Buf() for _ in range(8)]
        xsem = [newd() for _ in range(8)]
        st = ar.view("st", [8, 12], F32)
        STB = [Buf() for _ in range(8)]

        def ln_stats(src, SRC, j, mvap, MVB):
            V("vector", lambda h: h.bn_stats(out=st[:, j, 0:6], in_=src[:, 0:512]), r=[SRC], w=[STB[j]])
            V("vector", lambda h: h.bn_stats(out=st[:, j, 6:12], in_=src[:, 512:1024]), r=[SRC], w=[STB[j]])
            V("vector", lambda h: h.bn_aggr(out=mvap[:, 0:2], in_=st[:, j, 0:12]), r=[STB[j]], w=[MVB])
            V("vector", lambda h: h.tensor_scalar_add(out=mvap[:, 1:2], in0=mvap[:, 1:2], scalar1=LN_EPS), r=[MVB], w=[MVB])
            V("gpsimd", lambda h: h.tensor_tensor(out=mvap[:, 1:2], in0=mvap[:, 1:2], in1=m05, op=ALU.pow), r=[MVB, CST], w=[MVB])

        for g in range(NBLK):
            for j4 in range(4):
                m = 4 * g + j4
                j = m % 8
                fw.dma("sync", xsem[j], xt[:, j, :], x[m * 128:(m + 1) * 128, :], w=[XT[j]])
                ln_stats(xt[:, j, :], XT[j], j, mv[:, m, :], MV[m])
                V("vector", lambda h: h.tensor_scalar(out=xt[:, j, :], in0=xt[:, j, :], scalar1=mv[:, m, 0:1], scalar2=mv[:, m, 1:2],
                                                      op0=ALU.subtract, op1=ALU.mult), r=[XT[j], MV[m]], w=[XT[j]])
            for c in range(8):
                bk = nextbank()
                for j4 in range(4):
                    j = (4 * g + j4) % 8
                    V("tensor", lambda h: h.transpose(banks[bk][:, j4 * 128:(j4 + 1) * 128], xt[:, j, c * 128:(c + 1) * 128], identf),
                      r=[XT[j], CST], w=[BK[bk]], inc=(j4 == 3))
                V("scalar", lambda h: h.activation(out=hT[:, c, g * 512:(g + 1) * 512], in_=banks[bk][:, :], func=AF.Identity,
                                                   scale=par[:, P_G0 + c:P_G0 + c + 1], bias=par[:, P_B0 + c:P_B0 + c + 1]),
                  r=[BK[bk], PAR], w=[HT[c][g]])
        dumpsrc["hT"] = (hT, [b for row in HT for b in row], "A1")
        do_dumps("A1")
        if stop_after == "A1":
            finish()
            return nc

        fw.barrier()
        ar.release("xt", "st")
        wsl = [ar.view("wsl%d" % i, [8, 512], BF16) for i in range(3)]
        WSL = [Buf() for _ in range(3)]
        wsem = [newd() for _ in range(3)]
        wsmall = ar.view("wsmall", [8, 8], BF16)
        WSM = Buf()
        wsm_sem = newd()
        fq = ar.view("fq", [8, T], BF16)
        fk = ar.view("fk", [8, T], BF16)
        FQ = [Buf() for _ in range(8)]
        FK = [Buf() for _ in range(8)]
        vaug = ar.view("vaug", [NT, 8, 65], BF16)
        VAUG = Buf()
        OFF_FOX = 2056
        fw.dma("gpsimd", wsem[0], wsl[0], w_in_v[:, :, OFF_FOX:OFF_FOX + 512], w=[WSL[0]])
        fw.dma("gpsimd", wsem[1], wsl[1], w_in_v[:, :, OFF_FOX + 512:OFF_FOX + 1024], w=[WSL[1]])
        fw.dma("gpsimd", wsem[2], wsl[2], w_in_v[:, :, OFF_FOX + 1024:OFF_FOX + 1536], w=[WSL[2]])
        fw.dma("gpsimd", wsm_sem, wsmall, w_in_v[:, :, 3592:3600], w=[WSM])
        for h8 in range(8):
            pass
        V("gpsimd", lambda h: h.memset(fq[64:128, :, :], 0.0), w=FQ)
        V("vector", lambda h: h.memset(fk[64:128, :, :], 0.0), w=FK)
        fw.dma("gpsimd", newd(), fq[67:70, :, :], ones_d.rearrange("r (h t) -> r h t", h=8), w=FQ)
        fw.dma("gpsimd", newd(), fk[64:67, :, :], ones_d.rearrange("r (h t) -> r h t", h=8), w=FK)
        V("gpsimd", lambda h: h.memset(vaug[:, :, :, 64:65], 1.0), w=[VAUG])

        rowA = ar.view("rowA", [T], F32)
        rowC = ar.view("rowC", [T], F32)
        rparts = ar.view("rparts", [6, T], BF16)
        nbf = ar.view("nbf", [1], F32)
        RA, RC, RP, NBF = Buf(), Buf(), Buf(), Buf()
        V("vector", lambda h: h.tensor_scalar_mul(out=nbf[0:8, :], in0=par[0:8, P_BF:P_BF + 1], scalar1=-1.0), r=[PAR], w=[NBF])
        for blk in range(NBLK):
            bk = nextbank()
            for kc in range(8):
                V("tensor", lambda h: h.matmul(banks[bk][0:8, :], wsmall[:, kc, :], hT[:, kc, blk * 512:(blk + 1) * 512],
                                               start=(kc == 0), stop=(kc == 7)), r=[WSM, HT[kc][blk]], w=[BK[bk]], inc=(kc == 7))
            V("scalar", lambda h: h.activation(out=rowA[0:8, blk * 512:(blk + 1) * 512], in_=banks[bk][0:8, :], func=AF.Exp,
                                               scale=-1.0, bias=nbf[0:8, :]), r=[BK[bk], NBF], w=[RA])
        V("scalar", lambda h: h.activation(out=rowA[0:8, :], in_=rowA[0:8, :], func=AF.Ln, bias=1.0), r=[RA], w=[RA])
        V("vector", lambda h: h.tensor_tensor_scan(out=rowC[0:8, :], data0=cst[0:8, C_ONE:C_ONE + 1].to_broadcast([8, T]),
                                                   data1=rowA[0:8, :], initial=0.0, op0=ALU.mult, op1=ALU.subtract),
          r=[RA, CST], w=[RC])
        dumpsrc["crow"] = (rowC[0:8, :], [RC], "AF")
        V("vector", lambda h: h.tensor_copy(out=rparts[0:8, 0, :], in_=rowC[0:8, :]), r=[RC], w=[RP])
        V("vector", lambda h: h.tensor_tensor(out=rowA[0:8, :], in0=rowC[0:8, :], in1=rparts[0:8, 0, :], op=ALU.subtract), r=[RC, RP], w=[RA])
        V("vector", lambda h: h.tensor_copy(out=rparts[0:8, 1, :], in_=rowA[0:8, :]), r=[RA], w=[RP])
        V("vector", lambda h: h.tensor_tensor(out=rowA[0:8, :], in0=rowA[0:8, :], in1=rparts[0:8, 1, :], op=ALU.subtract), r=[RA, RP], w=[RA])
        V("vector", lambda h: h.tensor_copy(out=rparts[0:8, 2, :], in_=rowA[0:8, :]), r=[RA], w=[RP])
        V("vector", lambda h: h.tensor_scalar_mul(out=rparts[0:8, 3:6, :], in0=rparts[0:8, 0:3, :], scalar1=-1.0), r=[RP], w=[RP])
        for h8 in range(8):
            dq = newd()
            fw.dma("sync", dq, fq[64:67, h8, :], rparts[h8:h8 + 1, 0:3, :], r=[RP], w=[FQ[h8]])
            dk_ = newd()
            fw.dma("sync", dk_, fk[67:70, h8, :], rparts[h8:h8 + 1, 3:6, :], r=[RP], w=[FK[h8]])

        stg = ar.view("stg", [4, 512], BF16)
        STG = [Buf() for _ in range(4)]
        stgsem = [newd() for _ in range(4)]
        cnt = 0
        for (dst, DST, ws, WS, sc) in ((fq, FQ, wsl[0], WSL[0], 0.125), (fk, FK, wsl[1], WSL[1], 1.0)):
            for c4 in range(4):
                for blk in range(NBLK):
                    bk = nextbank()
                    for kc in range(8):
                        V("tensor", lambda h: h.matmul(banks[bk][:, :], ws[:, kc, c4 * 128:(c4 + 1) * 128], hT[:, kc, blk * 512:(blk + 1) * 512],
                                                       start=(kc == 0), stop=(kc == 7)), r=[WS, HT[kc][blk]], w=[BK[bk]], inc=(kc == 7))
                    he, ho = 2 * c4, 2 * c4 + 1
                    sl = cnt % 4
                    cnt += 1
                    cs = slice(blk * 512, (blk + 1) * 512)
                    V("scalar", lambda h: h.activation(out=dst[0:64, he, cs], in_=banks[bk][0:64, :], func=AF.Copy, scale=sc), r=[BK[bk]], w=[DST[he]])
                    V("vector", lambda h: h.tensor_scalar_mul(out=stg[64:128, sl, :], in0=banks[bk][64:128, :], scalar1=sc), r=[BK[bk]], w=[STG[sl]])
                    fw.dma("sync", stgsem[sl], dst[0:64, ho, cs], stg[64:128, sl, :], r=[STG[sl]], w=[DST[ho]])
        for m in range(NT):
            bk = nextbank()
            for kc in range(8):
                V("tensor", lambda h: h.matmul(banks[bk][:, :], hT[:, kc, m * 128:(m + 1) * 128], wsl[2][:, kc, :],
                                               start=(kc == 0), stop=(kc == 7)), r=[WSL[2], HT[kc][m // 4]], w=[BK[bk]], inc=(kc == 7))
            src = banks[bk][:, :].rearrange("p (h d) -> p h d", h=8)
            if m % 2 == 0:
                V("scalar", lambda h: h.copy(out=vaug[:, m, :, 0:64], in_=src), r=[BK[bk]], w=[VAUG])
            else:
                V("vector", lambda h: h.tensor_copy(out=vaug[:, m, :, 0:64], in_=src), r=[BK[bk]], w=[VAUG])
        dumpsrc["fq"] = (fq, FQ, "AF")
        dumpsrc["fk"] = (fk, FK, "AF")
        dumpsrc["vaug"] = (vaug, [VAUG], "AF")
        do_dumps("AF")
        if stop_after == "AF":
            finish()
            return nc

        fw.barrier()
        ar.release("wsl0", "wsl1", "wsl2", "rowA", "rowC", "rparts", "wsmall", "nbf", "stg")
        ofox = ar.view("ofox", [4, T], BF16)
        ostg = ar.view("ostg", [2, 512], BF16)
        OSTG = [Buf(), Buf()]
        ostg_sem = [newd(), newd()]
        ostg_ctr = [0]
        OFOX = [Buf() for _ in range(8)]
        pT = [ar.view("pT%d" % i, [512], BF16) for i in range(4)]
        PT = [Buf() for _ in range(4)]
        sq = [ar.view("sq%d" % i, [512], F32) for i in range(2)]
        rr = [ar.view("rr%d" % i, [512], F32) for i in range(2)]
        oaug = cst[:, C_OAUG:C_OAUG + 128]
        SQ, RR, OAUG = [Buf(), Buf()], [Buf(), Buf()], CST
        def pe_warm(n, bk_, lhs, rhs, RB_):
            for _ in range(n):
                V("tensor", lambda h: h.matmul(banks[bk_][:, :], lhs, rhs, start=True, stop=True), r=RB_, w=[BK[bk_]], inc=False)

        pe_warm(24, 5, fk[0:64, 0, 0:128], fq[0:64, 0, 0:512], [FK[0], FQ[0]])
        NFF, NFW = 0, 256
        pslot = [0]
        gp = [0]
        pend = []
        fsl = [0]

        def fin_step(it):
            stp, h8_, qb_, po_ = it[1], it[2], it[3], it[4]
            if stp == 0:
                k_ = fsl[0] % 2
                fsl[0] += 1
                it.append(k_)
                V("scalar", lambda h: h.activation(out=sq[k_][0:65, :], in_=banks[po_][0:65, :], func=AF.Square), r=[BK[po_]], w=[SQ[k_]])
                V("tensor", lambda h: h.matmul(banks[5][:, :], oaug[0:65, :], sq[k_][0:65, :], start=True, stop=True),
                  r=[OAUG, SQ[k_]], w=[BK[5]])
                it[0] = gp[0] + 2
                it[1] = 1
            else:
                k_ = it[5]
                V("scalar", lambda h: h.activation(out=rr[k_][0:64, :], in_=banks[5][0:64, :], func=AF.Ln, scale=1.0 / 64.0), r=[BK[5]], w=[RR[k_]])
                V("scalar", lambda h: h.activation(out=rr[k_][0:64, :], in_=rr[k_][0:64, :], func=AF.Exp, scale=-0.5), r=[RR[k_]], w=[RR[k_]])
                cs_ = slice(qb_ * 512, (qb_ + 1) * 512)
                if h8_ % 2 == 0:
                    V("vector", lambda h: h.scalar_tensor_tensor(out=ofox[0:64, h8_ // 2, cs_], in0=banks[po_][0:64, :],
                                                                 scalar=par[0:64, P_FOXG:P_FOXG + 1], in1=rr[k_][0:64, :],
                                                                 op0=ALU.mult, op1=ALU.mult), r=[BK[po_], RR[k_], PAR], w=[OFOX[h8_]])
                else:
                    sl_ = ostg_ctr[0] % 2
                    ostg_ctr[0] += 1
                    V("vector", lambda h: h.scalar_tensor_tensor(out=ostg[0:64, sl_, :], in0=banks[po_][0:64, :],
                                                                 scalar=par[0:64, P_FOXG:P_FOXG + 1], in1=rr[k_][0:64, :],
                                                                 op0=ALU.mult, op1=ALU.mult), r=[BK[po_], RR[k_], PAR], w=[OSTG[sl_]])
                    fw.dma("sync", ostg_sem[sl_], ofox[64:128, h8_ // 2, cs_], ostg[0:64, sl_, :], r=[OSTG[sl_]], w=[OFOX[h8_]])
                pend.remove(it)

        for h8 in range(8):
            pairs = []
            for qb in range(NBLK):
                for kb in range(4 * (qb + 1)):
                    pairs.append((qb, kb))
            sbank = {}

            def emit_qk(i):
                qb, kb = pairs[i]
                j = kb - 4 * qb
                qlo = max(j, 0) * 128
                bk = nextbank(0, 4)
                sbank[i] = bk
                V("tensor", lambda h: h.matmul(banks[bk][:, qlo:512], fk[:, h8, kb * 128:(kb + 1) * 128],
                                               fq[:, h8, qb * 512 + qlo:(qb + 1) * 512], start=True, stop=True),
                  r=[FK[h8], FQ[h8]], w=[BK[bk]])

            LA = 3
            for i0 in range(min(LA, len(pairs))):
                emit_qk(i0)
            for i, (qb, kb) in enumerate(pairs):
                if i + LA < len(pairs):
                    emit_qk(i + LA)
                for _ in range(NFF):
                    V("tensor", lambda h: h.matmul(banks[4][:, 0:NFW], identb, fq[:, 0, 0:NFW], start=True, stop=True), r=[IDB], w=[BK[4]], inc=False)
                j = kb - 4 * qb
                qlo = max(j, 0) * 128
                bk = sbank.pop(i)
                s = pslot[0] % 4
                pslot[0] += 1
                po = 6 + (h8 * NBLK + qb) % 2
                V("scalar", lambda h: h.activation(out=pT[s][:, qlo:512], in_=banks[bk][:, qlo:512], func=AF.Exp), r=[BK[bk]], w=[PT[s]])
                if j >= 0:
                    V("gpsimd", lambda h: h.affine_select(out=pT[s][:, qlo:qlo + 128], in_=pT[s][:, qlo:qlo + 128], pattern=[[1, 128]],
                                                          compare_op=ALU.is_ge, fill=0.0, base=0, channel_multiplier=-1),
                      r=[PT[s]], w=[PT[s]])
                last = (kb == 4 * (qb + 1) - 1)
                V("tensor", lambda h: h.matmul(banks[po][0:65, qlo:512], vaug[:, kb, h8, :], pT[s][:, qlo:512],
                                               start=(kb == 0), stop=last), r=[VAUG, PT[s]], w=[BK[po]], inc=last)
                if last:
                    pend.append([gp[0] + 2, 0, h8, qb, po])
                gp[0] += 1
                for it in list(pend):
                    if it[0] <= gp[0]:
                        fin_step(it)
        while pend:
            for it in list(pend):
                fin_step(it)
        dumpsrc["ofox"] = (ofox, OFOX, "F")
        do_dumps("F")
        if stop_after == "F":
            finish()
            return nc
        fw.barrier()
        ar.release("fq", "fk", "vaug", "pT0", "pT1", "pT2", "pT3", "sq0", "sq1", "rr0", "rr1", "ostg")

        wsl = [ar.view("wsl%d" % i, [8, 512], BF16) for i in range(3)]
        WSL = [Buf() for _ in range(3)]
        wg = ar.view("wg", [8, 8], BF16)
        WG = Buf()
        qkvT = ar.view("qkvT", [12, T], BF16)
        QKV = [Buf() for _ in range(12)]
        sz = ar.view("sz", [4, T], BF16)
        SZ = [Buf() for _ in range(4)]
        pcb = ar.view("pcb", [4, 516], F32)
        PC = [Buf() for _ in range(4)]
        accb = ar.view("accb", [4, 512], F32)
        ACC = [Buf() for _ in range(4)]
        gpre = ar.view("gpre", [NT, 8], F32)
        GPRE = Buf()
        beta = ar.view("beta", [NT, 4], F32)
        lg = ar.view("lg", [NT, 4], F32)
        gt1 = ar.view("gt1", [NT, 4], F32)
        gt2 = ar.view("gt2", [NT, 4], F32)
        BETA, LG, GT1, GT2 = Buf(), Buf(), Buf(), Buf()
        for i in range(3):
            fw.dma("gpsimd", wsem[i], wsl[i], w_in_v[:, :, i * 512:(i + 1) * 512], w=[WSL[i]])
        fw.dma("gpsimd", wsm_sem, wg, w_in_v[:, :, 2048:2056], w=[WG])
        cw = par[:, P_CW:P_CW + 48].rearrange("p (c j) -> p c j", j=4)

        def ag_E(u):
            ch, blk = u // 4, u % 4
            grp, hc = ch // 4, ch % 4
            ws, WS = wsl[grp], WSL[grp]
            sl = u % 4
            if u == 16:
                pass
            bk = nextbank()
            for kc in range(8):
                V("tensor", lambda h: h.matmul(banks[bk][:, :], ws[:, kc, hc * 128:(hc + 1) * 128], hT[:, kc, blk * 512:(blk + 1) * 512],
                                               start=(kc == 0), stop=(kc == 7)), r=[WS, HT[kc][blk]], w=[BK[bk]], inc=(kc == 7))
            if blk == 0:
                V("gpsimd", lambda h: h.memset(pcb[:, sl, 0:3], 0.0), w=[PC[sl]])
            else:
                V("gpsimd", lambda h: h.tensor_copy(out=pcb[:, sl, 0:3], in_=pcb[:, (u - 1) % 4, 512:515]), r=[PC[(u - 1) % 4]], w=[PC[sl]])
            V("scalar", lambda h: h.copy(out=pcb[:, sl, 3:515], in_=banks[bk][:, :]), r=[BK[bk]], w=[PC[sl]])

        def ag_T(u):
            ch = u // 4
            sl = u % 4
            V("scalar", lambda h: h.activation(out=accb[:, sl, :], in_=pcb[:, sl, 0:512], func=AF.Copy, scale=cw[:, ch, 0:1]), r=[PC[sl], PAR], w=[ACC[sl]])
            for j in range(1, 4):
                V("vector", lambda h: h.scalar_tensor_tensor(out=accb[:, sl, :], in0=pcb[:, sl, j:j + 512], scalar=cw[:, ch, j:j + 1], in1=accb[:, sl, :],
                                                             op0=ALU.mult, op1=ALU.add), r=[PC[sl], PAR, ACC[sl]], w=[ACC[sl]])

        def ag_S(u):
            ch, blk = u // 4, u % 4
            sl = u % 4
            V("scalar", lambda h: h.activation(out=qkvT[:, ch, blk * 512:(blk + 1) * 512], in_=accb[:, sl, :], func=AF.Silu), r=[ACC[sl]], w=[QKV[ch]])

        NU = 48
        for u in range(NU + 2):
            if u < NU:
                ag_E(u)
                if u == 16:
                    fw.dma("gpsimd", wsem[0], wsl[0], w_in_v[:, :, 1536:2048], w=[WSL[0]])
            if 0 <= u - 1 < NU:
                ag_T(u - 1)
            if 0 <= u - 2 < NU:
                ag_S(u - 2)
        for hc in range(4):
            for blk in range(NBLK):
                bk = nextbank()
                for kc in range(8):
                    V("tensor", lambda h: h.matmul(banks[bk][:, :], wsl[0][:, kc, hc * 128:(hc + 1) * 128], hT[:, kc, blk * 512:(blk + 1) * 512],
                                                   start=(kc == 0), stop=(kc == 7)), r=[WSL[0], HT[kc][blk]], w=[BK[bk]], inc=(kc == 7))
                V("scalar", lambda h: h.activation(out=sz[:, hc, blk * 512:(blk + 1) * 512], in_=banks[bk][:, :], func=AF.Silu), r=[BK[bk]], w=[SZ[hc]])
        for m in range(NT):
            bk = nextbank()
            for kc in range(8):
                V("tensor", lambda h: h.matmul(banks[bk][:, 0:8], hT[:, kc, m * 128:(m + 1) * 128], wg[:, kc, :],
                                               start=(kc == 0), stop=(kc == 7)), r=[WG, HT[kc][m // 4]], w=[BK[bk]], inc=(kc == 7))
            V("scalar", lambda h: h.copy(out=gpre[:, m, :], in_=banks[bk][:, 0:8]), r=[BK[bk]], w=[GPRE])
        dtb = par[:, P_DTB:P_DTB + 64].rearrange("p (m h) -> p m h", h=4)
        alog = par[:, P_ALOG:P_ALOG + 64].rearrange("p (m h) -> p m h", h=4)
        V("scalar", lambda h: h.activation(out=gt1, in_=gpre[:, :, 0:4], func=AF.Exp, scale=-1.0), r=[GPRE], w=[GT1])
        V("vector", lambda h: h.tensor_scalar_add(out=gt1, in0=gt1, scalar1=1.0), r=[GT1], w=[GT1])
        V("vector", lambda h: h.reciprocal(out=beta, in_=gt1), r=[GT1], w=[BETA])
        V("vector", lambda h: h.tensor_tensor(out=gt2, in0=gpre[:, :, 4:8], in1=dtb, op=ALU.add), r=[GPRE, PAR], w=[GT2])
        V("scalar", lambda h: h.activation(out=gt2, in_=gt2, func=AF.Exp), r=[GT2], w=[GT2])
        V("scalar", lambda h: h.activation(out=gt2, in_=gt2, func=AF.Ln, bias=1.0), r=[GT2], w=[GT2])
        V("scalar", lambda h: h.activation(out=gt1, in_=alog, func=AF.Exp), r=[PAR, GT1, BETA], w=[GT1])
        V("vector", lambda h: h.scalar_tensor_tensor(out=lg, in0=gt2, scalar=-1.0, in1=gt1, op0=ALU.mult, op1=ALU.mult), r=[GT1, GT2], w=[LG])
        dumpsrc["qkvT"] = (qkvT, QKV, "AG")
        dumpsrc["sz"] = (sz, SZ, "AG")
        dumpsrc["beta"] = (beta, [BETA], "AG")
        dumpsrc["lg"] = (lg, [LG], "AG")
        do_dumps("AG")
        if stop_after == "AG":
            finish()
            return nc
        fw.barrier()
        ar.release("hT", "wsl0", "wsl1", "wsl2", "wg", "pcb", "accb", "gpre", "gt1", "gt2")

        ogT = ar.view("ogT", [4, T], BF16)
        OGT = [Buf() for _ in range(4)]
        BQ = [[Buf(LK[i]) for _ in range(4)] for i in range(8)]

        def qap(b, q):
            return banks[b][:, q * 128:(q + 1) * 128]

        def flat(v):
            return v.rearrange("p h d -> p (h d)")

        def hb():
            return [Buf() for _ in range(4)]

        gam = ar.view("gam", [NT, 4], F32)
        egam = ar.view("egam", [NT, 4], F32)
        edec = ar.view("edec", [NT, 4], F32)
        glb = ar.view("glb", [2, 64], F32)
        X2 = ar.view("X2", [2, 64], F32)
        GAM, EGAM, EDEC, GLB, X2B = Buf(), Buf(), Buf(), Buf(), Buf()
        gbc = ar.view("gbc", [128], F32)
        GBC = Buf()
        fw.dma("sync", newd(), gbc, vec_d[7:8, 0:128].partition_broadcast(128), w=[GBC])
        negm = cst[:, C_NEGM:C_NEGM + 128]
        negmb = ar.view("negmb", [128], BF16)
        NEGMB = Buf()
        V("vector", lambda h: h.tensor_copy(out=negmb, in_=negm), r=[CST], w=[NEGMB])
        strict = cst[:, C_STR:C_STR + 128]
        lgf = lg.rearrange("p m h -> p (m h)")
        b0 = nextbank()
        V("tensor", lambda h: h.matmul(banks[b0][:, 0:64], cst[:, C_TRI:C_TRI + 128], lgf, start=True, stop=True), r=[CST, LG], w=[BQ[b0][0]])
        V("tensor", lambda h: h.matmul(banks[b0][:, 128:192], cst[:, C_BLK:C_BLK + 128], lgf, start=True, stop=True), r=[CST, LG], w=[BQ[b0][1]])
        V("vector", lambda h: h.tensor_copy(out=gam.rearrange("p m h -> p (m h)"), in_=banks[b0][:, 0:64]), r=[BQ[b0][0]], w=[GAM])
        V("scalar", lambda h: h.activation(out=egam.rearrange("p m h -> p (m h)"), in_=banks[b0][:, 0:64], func=AF.Exp), r=[BQ[b0][0]], w=[EGAM])
        V("vector", lambda h: h.tensor_tensor(out=edec.rearrange("p m h -> p (m h)"), in0=banks[b0][:, 128:192],
                                              in1=gam.rearrange("p m h -> p (m h)"), op=ALU.subtract), r=[BQ[b0][1], GAM], w=[EDEC])
        V("scalar", lambda h: h.activation(out=edec, in_=edec, func=AF.Exp), r=[EDEC], w=[EDEC])
        for hf in range(2):
            V("vector", lambda h: h.tensor_scalar_mul(out=X2[:, hf, :], in0=lgf, scalar1=cst[:, C_IND + hf:C_IND + hf + 1]), r=[LG, CST], w=[X2B])
        V("tensor", lambda h: h.matmul(banks[b0][:, 256:384], onesf, X2.rearrange("p a b -> p (a b)"), start=True, stop=True), r=[CST, X2B], w=[BQ[b0][2]])
        V("scalar", lambda h: h.activation(out=glb.rearrange("p a b -> p (a b)"), in_=banks[b0][:, 256:384], func=AF.Exp), r=[BQ[b0][2]], w=[GLB])

        dumpsrc["gam"] = (gam, [GAM], "G0")
        dumpsrc["glb"] = (glb, [GLB], "G0")
        if stop_after == "G0":
            do_dumps("G0")
            finish()
            return nc
        def hb3(n):
            return [[Buf() for _ in range(4)] for _ in range(n)]

        def v2(name, n, dt):
            return [ar.view("%s%d" % (name, i), [4, 128], dt) for i in range(n)]

        kbt = v2("kbt", 2, BF16); KBT = hb3(2)
        vbt = v2("vbt", 2, BF16); VBT = hb3(2)
        diagc = v2("diagc", 2, F32); DIAGC = hb3(2)
        Dm = v2("Dm", 2, F32); DM = hb3(2)
        DmS = v2("DmS", 2, F32); DMS = hb3(2)
        Am = [v2("Am%d_" % i, 2, BF16) for i in range(2)]; AMB = [hb3(2), hb3(2)]
        Bm = [v2("Bm%d_" % i, 2, BF16) for i in range(2)]; BMB = [hb3(2), hb3(2)]
        ac = v2("ac", 2, BF16); AC = hb3(2)
        Tt32 = v2("Tt32_", 2, F32); TT32 = hb3(2)
        Ttb = [v2("Ttb%d_" % i, 2, BF16) for i in range(2)]; TTB = [hb3(2), hb3(2)]
        kdt = v2("kdt", 3, BF16); KDT = hb3(3)
        acT = v2("acT", 3, BF16); ACTB = hb3(3)
        u_sb = v2("u_sb", 3, F32); US = hb3(3)
        wT_sb = v2("wT_sb", 3, BF16); WT = hb3(3)
        vnew = ar.view("vnew", [4, 128], BF16); VN = hb()
        tmpo = ar.view("tmpo", [4, 128], F32); TMPO = hb()
        otok = ar.view("otok", [4, 128], F32); OTOK = hb()
        on = ar.view("on", [4, 128], BF16); ON = hb()
        junkt = ar.view("junk", [8, 128], F32); JUNKS = [Buf() for _ in range(8)]
        junk2t = ar.view("junk2", [4, 128], F32); JUNK2S = [Buf() for _ in range(4)]
        jctr = [0]

        def njunk():
            jctr[0] += 1
            return jctr[0] % 8
        S32 = ar.view("S32", [4, 128], F32); S32B = hb()
        Sp = ar.view("Sp", [4, 128], F32); SPB = hb()
        Sbf = ar.view("Sbf", [4, 128], BF16); SBF = hb()
        sm = ar.view("sm", [3, 12, 4], F32)
        SM = [Buf(), Buf(), Buf()]
        SM2 = [Buf(), Buf(), Buf()]
        V("vector", lambda h: h.memset(flat(S32), 0.0), w=S32B)
        V("vector", lambda h: h.memset(flat(Sbf), 0.0), w=SBF)
        bWS, bOI, bO2, bSD, bOT = 4, 5, 6, 7, 4

        def bq(b, hh):
            return banks[b][:, :].bitcast(BF16)[:, hh * 256:hh * 256 + 128]

        def bqall(b):
            return banks[b][:, :].bitcast(BF16).rearrange("p (h c) -> p h c", h=4)[:, :, 0:128]

        def gen_prep(m):
            tok = slice(m * 128, (m + 1) * 128)
            pi = m % 2
            p3 = m % 3
            X, Y = 2 * pi, 2 * pi + 1
            rk, rq, bkk, skb, skd, sa, so, lnk, crow, sso = [sm[:, p3, i, :] for i in range(10)]
            S_ = [SM[p3]]
            t1 = banks[X][:, :].bitcast(BF16)
            for hh in range(4):
                kq = t1[:, hh * 256:hh * 256 + 128]
                vq = t1[:, hh * 256 + 128:hh * 256 + 256]
                V("tensor", lambda h: h.transpose(kq, qkvT[:, 4 + hh, tok], identb), r=[QKV[4 + hh], IDB], w=[BQ[X][hh]], inc=False)
                V("tensor", lambda h: h.transpose(vq, qkvT[:, 8 + hh, tok], identb), r=[QKV[8 + hh], IDB], w=[BQ[X][hh]], inc=False)
                V("tensor", lambda h: h.transpose(bq(Y, hh), qkvT[:, hh, tok], identb), r=[QKV[hh], IDB], w=[BQ[Y][hh]])
            for hh in range(4):
                kq = t1[:, hh * 256:hh * 256 + 128]
                j1, j2 = njunk(), njunk()
                V("scalar", lambda h: h.activation(out=junkt[:, j1, :], in_=kq, func=AF.Square, accum_out=rk[:, hh:hh + 1]), r=[BQ[X][hh]], w=[JUNKS[j1], SM[p3]])
                V("scalar", lambda h: h.activation(out=junkt[:, j2, :], in_=bq(Y, hh), func=AF.Square, accum_out=rq[:, hh:hh + 1]), r=[BQ[Y][hh]], w=[JUNKS[j2], SM[p3]])
            yield
            rkq = sm[:, p3, 0:2, :]
            V("scalar", lambda h: h.activation(out=rkq, in_=rkq, func=AF.Ln, bias=eps_col[:, 0:1]), r=S_ + [EPSB], w=S_)
            V("vector", lambda h: h.scalar_tensor_tensor(out=crow, in0=rk, scalar=-0.5, in1=gam[:, m, :], op0=ALU.mult, op1=ALU.subtract), r=S_ + [GAM], w=S_)
            V("scalar", lambda h: h.activation(out=rkq, in_=rkq, func=AF.Exp, scale=-0.5), r=S_, w=S_)
            V("vector", lambda h: h.tensor_tensor(out=bkk, in0=rk, in1=beta[:, m, :], op=ALU.mult), r=S_ + [BETA], w=S_)
            V("vector", lambda h: h.tensor_tensor(out=skb, in0=bkk, in1=egam[:, m, :], op=ALU.mult), r=S_ + [EGAM], w=S_)
            V("vector", lambda h: h.tensor_tensor(out=skd, in0=rk, in1=edec[:, m, :], op=ALU.mult), r=S_ + [EDEC], w=S_)
            V("vector", lambda h: h.tensor_scalar_mul(out=sa, in0=rq, scalar1=128.0 ** -0.5), r=S_, w=S_)
            V("vector", lambda h: h.tensor_tensor(out=so, in0=sa, in1=egam[:, m, :], op=ALU.mult), r=S_ + [EGAM], w=S_)
            yield
            for hh in range(4):
                kq = t1[:, hh * 256:hh * 256 + 128]
                vq = t1[:, hh * 256 + 128:hh * 256 + 256]
                V("scalar", lambda h: h.activation(out=diagc[pi][:, hh, :], in_=identf, func=AF.Copy, scale=crow[:, hh:hh + 1]), r=[CST] + S_, w=[DIAGC[pi][hh]])
                V("scalar", lambda h: h.activation(out=kbt[pi][:, hh, :], in_=kq, func=AF.Copy, scale=skb[:, hh:hh + 1]), r=[BQ[X][hh]] + S_, w=[KBT[pi][hh]])
                V("scalar", lambda h: h.activation(out=kdt[p3][:, hh, :], in_=kq, func=AF.Copy, scale=skd[:, hh:hh + 1]), r=[BQ[X][hh]] + S_, w=[KDT[p3][hh]])
                V("vector", lambda h: h.tensor_scalar_mul(out=vbt[pi][:, hh, :], in0=vq, scalar1=beta[:, m, hh:hh + 1]), r=[BQ[X][hh], BETA], w=[VBT[pi][hh]])
            yield
            for hh in range(4):
                V("tensor", lambda h: h.matmul(qap(Y, hh), onesf, diagc[pi][:, hh, :], start=True, stop=False), r=[CST, DIAGC[pi][hh]], w=[BQ[Y][hh]], inc=False)
                V("tensor", lambda h: h.matmul(qap(Y, hh), identb, negmb, start=False, stop=True), r=[IDB, NEGMB], w=[BQ[Y][hh]], inc=(hh == 3))
            for hh in range(4):
                V("tensor", lambda h: h.matmul(qap(X, hh), qkvT[:, 4 + hh, tok], qkvT[:, 4 + hh, tok], start=True, stop=True), r=[QKV[4 + hh]], w=[BQ[X][hh]], inc=(hh == 3))
            yield
            for hh in range(4):
                V("scalar", lambda h: h.activation(out=Dm[pi][:, hh, :], in_=qap(Y, hh), func=AF.Exp, bias=gam[:, m, hh:hh + 1]), r=[BQ[Y][hh], GAM], w=[DM[pi][hh]])
            for hh in range(4):
                V("gpsimd", lambda h: h.tensor_tensor(out=DmS[pi][:, hh, :], in0=Dm[pi][:, hh, :], in1=strict, op=ALU.mult), r=[DM[pi][hh], CST], w=[DMS[pi][hh]])
            for hh in range(4):
                V("tensor", lambda h: h.matmul(qap(Y, hh), qkvT[:, hh, tok], qkvT[:, 4 + hh, tok], start=True, stop=True), r=[QKV[hh], QKV[4 + hh]], w=[BQ[Y][hh]], inc=(hh == 3))
            yield
            for hh in range(4):
                V("vector", lambda h: h.scalar_tensor_tensor(out=Am[0][pi][:, hh, :], in0=qap(X, hh), scalar=bkk[:, hh:hh + 1], in1=DmS[pi][:, hh, :],
                                                             op0=ALU.mult, op1=ALU.mult), r=[BQ[X][hh], DMS[pi][hh]] + S_, w=[AMB[0][pi][hh]])
            for hh in range(4):
                V("vector", lambda h: h.scalar_tensor_tensor(out=ac[pi][:, hh, :], in0=qap(Y, hh), scalar=sa[:, hh:hh + 1], in1=Dm[pi][:, hh, :],
                                                             op0=ALU.mult, op1=ALU.mult), r=[BQ[Y][hh], DM[pi][hh]] + S_, w=[AC[pi][hh]])
            yield
            for hh in range(4):
                V("tensor", lambda h: h.transpose(bq(X, hh), Am[0][pi][:, hh, :], identb), r=[AMB[0][pi][hh], IDB], w=[BQ[X][hh]], inc=(hh == 3))
            for hh in range(4):
                V("tensor", lambda h: h.transpose(bq(Y, hh), ac[pi][:, hh, :], identb), r=[AC[pi][hh], IDB], w=[BQ[Y][hh]], inc=(hh == 3))
            V("scalar", lambda h: h.copy(out=Bm[0][pi], in_=bqall(X)), r=BQ[X], w=BMB[0][pi])
            for hh in range(4):
                V("vector", lambda h: h.tensor_tensor(out=Ttb[0][pi][:, hh, :], in0=identf, in1=bq(X, hh), op=ALU.subtract), r=[CST, BQ[X][hh]], w=[TTB[0][pi][hh]])
            for hh in range(4):
                V("vector", lambda h: h.tensor_tensor(out=Tt32[pi][:, hh, :], in0=identf, in1=bq(X, hh), op=ALU.subtract), r=[CST, BQ[X][hh]], w=[TT32[pi][hh]])
            V("scalar", lambda h: h.copy(out=acT[p3], in_=bqall(Y)), r=BQ[Y], w=ACTB[p3])
            yield
            cur = 0
            for s_ in range(5):
                for hh in range(4):
                    V("tensor", lambda h: h.matmul(qap(X, hh), Bm[cur][pi][:, hh, :], Am[cur][pi][:, hh, :], start=True, stop=True),
                      r=[BMB[cur][pi][hh], AMB[cur][pi][hh]], w=[BQ[X][hh]], inc=(hh == 3))
                if s_ < 4:
                    for hh in range(4):
                        V("tensor", lambda h: h.matmul(qap(Y, hh), Am[cur][pi][:, hh, :], Bm[cur][pi][:, hh, :], start=True, stop=True),
                          r=[BMB[cur][pi][hh], AMB[cur][pi][hh]], w=[BQ[Y][hh]], inc=(hh == 3))
                V("scalar", lambda h: h.copy(out=flat(Am[1 - cur][pi]), in_=banks[X][:, :]), r=BQ[X], w=AMB[1 - cur][pi])
                if s_ < 4:
                    V("scalar", lambda h: h.copy(out=flat(Bm[1 - cur][pi]), in_=banks[Y][:, :]), r=BQ[Y], w=BMB[1 - cur][pi])
                yield
                for hh in range(4):
                    V("tensor", lambda h: h.matmul(qap(X, hh), Am[1 - cur][pi][:, hh, :], Ttb[cur][pi][:, hh, :], start=True, stop=True),
                      r=[AMB[1 - cur][pi][hh], TTB[cur][pi][hh]], w=[BQ[X][hh]], inc=(hh == 3))
                V("vector", lambda h: h.tensor_tensor(out=flat(Ttb[1 - cur][pi]), in0=flat(Tt32[pi]), in1=banks[X][:, :], op=ALU.add),
                  r=TT32[pi] + BQ[X], w=TTB[1 - cur][pi])
                V("vector", lambda h: h.tensor_tensor(out=flat(Tt32[pi]), in0=flat(Tt32[pi]), in1=banks[X][:, :], op=ALU.add),
                  r=TT32[pi] + BQ[X], w=TT32[pi])
                cur = 1 - cur
                yield
            for hh in range(4):
                V("tensor", lambda h: h.matmul(qap(X, hh), Ttb[cur][pi][:, hh, :], vbt[pi][:, hh, :], start=True, stop=True),
                  r=[TTB[cur][pi][hh], VBT[pi][hh]], w=[BQ[X][hh]], inc=(hh == 3))
            for hh in range(4):
                V("tensor", lambda h: h.matmul(qap(Y, hh), kbt[pi][:, hh, :], Ttb[cur][pi][:, hh, :], start=True, stop=True),
                  r=[TTB[cur][pi][hh], KBT[pi][hh]], w=[BQ[Y][hh]], inc=(hh == 3))
            V("scalar", lambda h: h.copy(out=flat(u_sb[p3]), in_=banks[X][:, :]), r=BQ[X], w=US[p3])
            V("vector", lambda h: h.tensor_copy(out=flat(wT_sb[p3]), in_=banks[Y][:, :]), r=BQ[Y], w=WT[p3])
            yield

        def gen_scan(m):
            tok = slice(m * 128, (m + 1) * 128)
            p3 = m % 3
            so = sm[:, p3, 6, :]
            sso = sm[:, p3, 9, :]
            S_ = [SM[p3]]
            for half in range(2):
                r0 = half * 64
                c0 = m * 128 + r0
                for hh in range(4):
                    V("tensor", lambda h: h.matmul(qap(bWS, hh)[r0:r0 + 64, :], wT_sb[p3][:, hh, r0:r0 + 64], Sbf[:, hh, :], start=True, stop=True),
                      r=[WT[p3][hh], SBF[hh]], w=[BQ[bWS][hh]], inc=(hh == 3))
                for hh in range(4):
                    V("tensor", lambda h: h.matmul(qap(bOI, hh)[r0:r0 + 64, :], qkvT[:, hh, c0:c0 + 64], Sbf[:, hh, :], start=True, stop=True),
                      r=[QKV[hh], SBF[hh]], w=[BQ[bOI][hh]], inc=(hh == 3))
                for hh in range(4):
                    gcol = glb[:, half, m * 4 + hh:m * 4 + hh + 1]
                    V("gpsimd", lambda h: h.tensor_scalar(out=Sp[:, hh, :], in0=S32[:, hh, :], scalar1=gcol, scalar2=0.0, op0=ALU.mult, op1=ALU.add),
                      r=[S32B[hh], GLB], w=[SPB[hh]])
                yield
                V("vector", lambda h: h.tensor_tensor(out=flat(vnew[r0:r0 + 64, :, :]), in0=flat(u_sb[p3][r0:r0 + 64, :, :]),
                                                      in1=banks[bWS][r0:r0 + 64, :], op=ALU.subtract), r=US[p3] + BQ[bWS], w=VN)
                yield
                for hh in range(4):
                    V("tensor", lambda h: h.matmul(qap(bSD, hh), kdt[p3][r0:r0 + 64, hh, :], vnew[r0:r0 + 64, hh, :], start=True, stop=True),
                      r=[KDT[p3][hh], VN[hh]], w=[BQ[bSD][hh]], inc=(hh == 3))
                for hh in range(4):
                    V("tensor", lambda h: h.matmul(qap(bO2, hh)[r0:r0 + 64, :], acT[p3][r0:r0 + 64, hh, r0:r0 + 64], vnew[r0:r0 + 64, hh, :],
                                                   start=True, stop=True), r=[ACTB[p3][hh], VN[hh]], w=[BQ[bO2][hh]], inc=(hh == 3))
                yield
                V("vector", lambda h: h.tensor_tensor(out=flat(S32), in0=flat(Sp), in1=banks[bSD][:, :], op=ALU.add), r=SPB + BQ[bSD], w=S32B)
                V("scalar", lambda h: h.copy(out=flat(Sbf), in_=flat(S32)), r=S32B, w=SBF)
                yield
            for hh in range(4):
                V("scalar", lambda h: h.activation(out=tmpo[:, hh, :], in_=qap(bOI, hh), func=AF.Copy, scale=so[:, hh:hh + 1]), r=[BQ[bOI][hh]] + S_, w=[TMPO[hh]])
            V("vector", lambda h: h.tensor_tensor(out=flat(otok), in0=flat(tmpo), in1=banks[bO2][:, :], op=ALU.add), r=TMPO + BQ[bO2], w=OTOK)
            yield
            for hh in range(4):
                V("scalar", lambda h: h.activation(out=junk2t[:, hh, :], in_=otok[:, hh, :], func=AF.Square, accum_out=sso[:, hh:hh + 1]), r=[OTOK[hh]], w=[JUNK2S[hh], SM2[p3]])
            V("scalar", lambda h: h.activation(out=sso, in_=sso, func=AF.Ln, scale=1.0 / 128.0, bias=eps_col[:, 0:1]), r=[SM2[p3], EPSB], w=[SM2[p3]])
            V("scalar", lambda h: h.activation(out=sso, in_=sso, func=AF.Exp, scale=-0.5), r=[SM2[p3]], w=[SM2[p3]])
            yield
            for hh in range(4):
                V("vector", lambda h: h.scalar_tensor_tensor(out=on[:, hh, :], in0=otok[:, hh, :], scalar=sso[:, hh:hh + 1], in1=gbc,
                                                             op0=ALU.mult, op1=ALU.mult), r=[OTOK[hh], SM2[p3], GBC], w=[ON[hh]])
            for hh in range(4):
                V("tensor", lambda h: h.transpose(bq(bOT, hh), on[:, hh, :], identb), r=[ON[hh], IDB], w=[BQ[bOT][hh]], inc=(hh == 3))
            yield
            V("vector", lambda h: h.tensor_tensor(out=ogT[:, :, tok], in0=bqall(bOT), in1=sz[:, :, tok], op=ALU.mult), r=BQ[bOT] + SZ, w=OGT)
            yield

        preps = {}
        prep_done = set()
        scan_done = set()
        next_prep = 0
        scan_m = 0
        scan_g = None
        while len(scan_done) < NT:
            while len(preps) < 2 and next_prep < NT and (next_prep < 3 or (next_prep - 3) in scan_done) \
                    and (next_prep < 2 or (next_prep - 2) in prep_done):
                preps[next_prep] = gen_prep(next_prep)
                next_prep += 1
            if scan_g is None and scan_m < NT and scan_m in prep_done:
                scan_g = gen_scan(scan_m)
            progressed = False
            for _rep in range(2):
                if scan_g is None and scan_m < NT and scan_m in prep_done:
                    scan_g = gen_scan(scan_m)
                if scan_g is not None:
                    try:
                        next(scan_g)
                    except StopIteration:
                        scan_done.add(scan_m)
                        scan_m += 1
                        scan_g = None
                    progressed = True
            for t_ in sorted(preps):
                try:
                    next(preps[t_])
                except StopIteration:
                    prep_done.add(t_)
                    del preps[t_]
                progressed = True
            assert progressed
        dumpsrc["ogT"] = (ogT, OGT, "G")
        do_dumps("G")
        if stop_after == "G":
            finish()
            return nc
        fw.barrier()
        ar.release("gam", "egam", "edec", "glb", "X2", "gbc", "negmb", "kbt0", "kbt1", "vbt0", "vbt1", "diagc0", "diagc1", "Dm0", "Dm1",
                   "DmS0", "DmS1", "Am0_0", "Am0_1", "Am1_0", "Am1_1", "Bm0_0", "Bm0_1", "Bm1_0", "Bm1_1", "ac0", "ac1", "Tt32_0", "Tt32_1",
                   "Ttb0_0", "Ttb0_1", "Ttb1_0", "Ttb1_1", "kdt0", "kdt1", "kdt2", "acT0", "acT1", "acT2", "u_sb0", "u_sb1", "u_sb2",
                   "wT_sb0", "wT_sb1", "wT_sb2", "vnew", "tmpo", "otok", "on", "junk", "junk2", "S32", "Sp", "Sbf", "sm",
                   "qkvT", "sz", "beta", "lg")

        Rr = ar.view("R", [NT, D], F32)
        RB = [[Buf(), Buf()] for _ in range(NT)]
        h1T = ar.view("h1T", [8, T], BF16)
        H1T = [[Buf() for _ in range(NT)] for _ in range(8)]
        wo_g = ar.view("wo_g", [4, 512], BF16)
        wo_f = ar.view("wo_f", [4, 512], BF16)
        WOG, WOF = Buf(), Buf()
        wog_sem, wof_sem = newd(), newd()
        xts = [ar.view("xt%d" % i, [D], F32) for i in range(4)]
        XT = [Buf() for _ in range(4)]
        xsem2 = [newd() for _ in range(2)]
        gA = ar.view("gA", [D], F32)
        bA = ar.view("bA", [D], F32)
        GA, BA = Buf(), Buf()
        st = ar.view("st", [4, 12], F32)
        STB = [Buf() for _ in range(4)]
        mv1 = ar.view("mv1", [NT, 2], F32)
        MV1 = [Buf() for _ in range(NT)]
        w_out_g = w_out[0:512, :].rearrange("(kc p) c -> p kc c", p=128)
        w_out_f = w_out[512:1024, :].rearrange("(j p) c -> p j c", p=128)
        fw.dma("gpsimd", wog_sem, wo_g, w_out_g[:, :, 0:512], w=[WOG])
        fw.dma("gpsimd", wof_sem, wo_f, w_out_f[:, :, 0:512], w=[WOF])
        bcsem = [newd(), newd()]

        def load_bc(rowg, rowb):
            fw.dma("sync", bcsem[0], gA, vec_d[rowg:rowg + 1, :].partition_broadcast(128), w=[GA])
            fw.dma("sync", bcsem[1], bA, vec_d[rowb:rowb + 1, :].partition_broadcast(128), w=[BA])
            V("scalar", lambda h: h.activation(out=gA, in_=gA, func=AF.Copy, scale=ALPHA), r=[GA], w=[GA])
            V("scalar", lambda h: h.activation(out=bA, in_=bA, func=AF.Copy, scale=ALPHA), r=[BA], w=[BA])

        load_bc(0, 1)
        nmr = ar.view("nmr", [NT], F32)
        NMR = Buf()
        V("vector", lambda h: h.scalar_tensor_tensor(out=nmr, in0=mv[:, :, 0], scalar=-1.0, in1=mv[:, :, 1], op0=ALU.mult, op1=ALU.mult), r=MV, w=[NMR])
        for m in range(NT):
            j = m % 2
            fw.dma("sync", xsem2[j], xts[j], x[m * 128:(m + 1) * 128, :], w=[XT[j]])
            V("scalar", lambda h: h.activation(out=xts[j], in_=xts[j], func=AF.Identity, scale=mv[:, m, 1:2], bias=nmr[:, m:m + 1]),
              r=[XT[j], MV[m], NMR], w=[XT[j]])
            V("vector", lambda h: h.tensor_tensor(out=Rr[:, m, :], in0=xts[j], in1=gA, op=ALU.mult), r=[XT[j], GA], w=RB[m])
            V("vector", lambda h: h.tensor_tensor(out=Rr[:, m, :], in0=Rr[:, m, :], in1=bA, op=ALU.add), r=RB[m] + [BA], w=RB[m])
        for half in range(2):
            if half == 1:
                fw.dma("gpsimd", wog_sem, wo_g, w_out_g[:, :, 512:1024], w=[WOG])
                fw.dma("gpsimd", wof_sem, wo_f, w_out_f[:, :, 512:1024], w=[WOF])
            for m in range(NT):
                tok = slice(m * 128, (m + 1) * 128)
                bk = nextbank()
                for kc in range(4):
                    V("tensor", lambda h: h.matmul(banks[bk][:, :], ogT[:, kc, tok], wo_g[:, kc, :], start=(kc == 0), stop=False),
                      r=[OGT[kc], WOG], w=[BK[bk]], inc=False)
                for j4 in range(4):
                    V("tensor", lambda h: h.matmul(banks[bk][:, :], ofox[:, j4, tok], wo_f[:, j4, :], start=False, stop=(j4 == 3)),
                      r=[OFOX[2 * j4], OFOX[2 * j4 + 1], WOF], w=[BK[bk]], inc=(j4 == 3))
                V("vector", lambda h: h.tensor_tensor(out=Rr[:, m, half * 512:(half + 1) * 512], in0=Rr[:, m, half * 512:(half + 1) * 512],
                                                      in1=banks[bk][:, :], op=ALU.add), r=[RB[m][half], BK[bk]], w=[RB[m][half]])
        load_bc(2, 3)

        def c_p1(m):
            j = m % 4
            V("vector", lambda h: h.bn_stats(out=st[:, j, 0:6], in_=Rr[:, m, 0:512]), r=[RB[m][0]], w=[STB[j]])
            V("vector", lambda h: h.bn_stats(out=st[:, j, 6:12], in_=Rr[:, m, 512:1024]), r=[RB[m][1]], w=[STB[j]])
            V("vector", lambda h: h.bn_aggr(out=mv1[:, m, 0:2], in_=st[:, j, 0:12]), r=[STB[j]], w=[MV1[m]])
            V("vector", lambda h: h.tensor_scalar_add(out=mv1[:, m, 1:2], in0=mv1[:, m, 1:2], scalar1=LN_EPS), r=[MV1[m]], w=[MV1[m]])
            V("gpsimd", lambda h: h.tensor_tensor(out=mv1[:, m, 1:2], in0=mv1[:, m, 1:2], in1=m05, op=ALU.pow), r=[MV1[m], CST], w=[MV1[m]])

        def c_p2(m):
            j = m % 4
            V("vector", lambda h: h.scalar_tensor_tensor(out=mv1[:, m, 0:1], in0=mv1[:, m, 0:1], scalar=-1.0, in1=mv1[:, m, 1:2],
                                                         op0=ALU.mult, op1=ALU.mult), r=[MV1[m]], w=[MV1[m]])
            V("scalar", lambda h: h.activation(out=xts[j], in_=Rr[:, m, :], func=AF.Identity, scale=mv1[:, m, 1:2], bias=mv1[:, m, 0:1]),
              r=RB[m] + [MV1[m]], w=[XT[j]])

        def c_p3(m):
            j = m % 4
            V("vector", lambda h: h.tensor_tensor(out=Rr[:, m, :], in0=xts[j], in1=gA, op=ALU.mult), r=[XT[j], GA], w=RB[m])
            V("vector", lambda h: h.tensor_tensor(out=Rr[:, m, :], in0=Rr[:, m, :], in1=bA, op=ALU.add), r=RB[m] + [BA], w=RB[m])

        def c_tr(g):
            for c in range(8):
                bk = nextbank()
                for j in range(4):
                    V("tensor", lambda h: h.transpose(banks[bk][:, j * 128:(j + 1) * 128], xts[j][:, c * 128:(c + 1) * 128], identf),
                      r=[XT[j], CST], w=[BK[bk]], inc=(j == 3))
                V("scalar", lambda h: h.activation(out=h1T[:, c, g * 512:(g + 1) * 512], in_=banks[bk][:, :], func=AF.Identity,
                                                   scale=par[:, P_G1 + c:P_G1 + c + 1], bias=par[:, P_B1 + c:P_B1 + c + 1]),
                  r=[BK[bk], PAR], w=[H1T[c][4 * g + jj] for jj in range(4)])

        for k in range(NT + 2):
            if k < NT:
                c_p1(k)
            if 0 <= k - 1 < NT:
                c_p2(k - 1)
                if (k - 1) % 4 == 3:
                    c_tr((k - 1) // 4)
            if 0 <= k - 2 < NT:
                c_p3(k - 2)
        dumpsrc["h1T"] = (h1T, [b for row in H1T for b in row], "C")
        dumpsrc["R"] = (Rr, [b for p_ in RB for b in p_], "C")
        do_dumps("C")
        if stop_after == "C":
            finish()
            return nc
        fw.barrier()
        ar.release("ofox", "ogT", "wo_g", "wo_f", "xt0", "xt1", "xt2", "xt3", "gA", "bA", "nmr")

        NG = DFF // 512
        wu = [ar.view("wu%d" % i, [8, 512], BF16) for i in range(2)]
        wd = [ar.view("wd%d" % i, [4, D], BF16) for i in range(2)]
        WU = [Buf(), Buf()]
        WD = [Buf(), Buf()]
        wusem = [newd(), newd()]
        wdsem = [newd(), newd()]
        w_up_v = w_up.rearrange("(kc p) f -> p kc f", p=128)
        w_down_v = w_down.rearrange("(fc p) c -> p fc c", p=128)
        def load_ffw(g):
            s_ = g % 2
            fw.dma("gpsimd", wusem[s_], wu[s_], w_up_v[:, :, g * 512:(g + 1) * 512], w=[WU[s_]])
            fw.dma("gpsimd", wdsem[s_], wd[s_], w_down_v[:, g * 4:(g + 1) * 4, :], w=[WD[s_]])

        load_ffw(0)
        pTt = ar.view("pTt", [2, T], BF16)
        PTB = [Buf() for _ in range(NT)]
        ptile = ar.view("ptile", [2, 256], F32)
        PTL = [Buf() for _ in range(2)]
        psem = [newd(), newd()]
        wple = ar.view("wple", [2, D], BF16)
        wgt = ar.view("wgt", [8, D], BF16)
        WPLE, WGT = Buf(), Buf()
        bgf = ar.view("bgf", [D], F32)
        bgh = ar.view("bgh", [D], BF16)
        bgl = ar.view("bgl", [D], BF16)
        ones_b = ar.view("ones_b", [128], BF16)
        BGF, BGH, BGL, ONB = Buf(), Buf(), Buf(), Buf()
        sg = ar.view("sg", [2, 512], F32)
        SG = [Buf(), Buf()]
        fw.dma("gpsimd", newd(), wple, w_ple.rearrange("(kc p) c -> p kc c", p=128), w=[WPLE])
        fw.dma("gpsimd", newd(), wgt, w_gate.rearrange("(kc p) c -> p kc c", p=128), w=[WGT])
        fw.dma("sync", newd(), bgf[0:1, :], vec_d[6:7, :], w=[BGF])
        V("vector", lambda h: h.tensor_copy(out=bgh[0:1, :], in_=bgf[0:1, :]), r=[BGF], w=[BGH])
        V("vector", lambda h: h.tensor_tensor(out=bgf[0:1, :], in0=bgf[0:1, :], in1=bgh[0:1, :], op=ALU.subtract), r=[BGF, BGH], w=[BGF])
        V("vector", lambda h: h.tensor_copy(out=bgl[0:1, :], in_=bgf[0:1, :]), r=[BGF], w=[BGL])
        V("vector", lambda h: h.memset(ones_b, 1.0), w=[ONB])
        for m in range(NT):
            j = m % 2
            tok = slice(m * 128, (m + 1) * 128)
            fw.dma("sync", psem[j], ptile[:, j, :], p_in[tok, :], w=[PTL[j]])
            bk = nextbank()
            for kc in range(2):
                V("tensor", lambda h: h.transpose(banks[bk][:, kc * 128:(kc + 1) * 128], ptile[:, j, kc * 128:(kc + 1) * 128], identf),
                  r=[PTL[j], CST], w=[BK[bk]], inc=(kc == 1))
            V("scalar", lambda h: h.copy(out=pTt[:, :, tok], in_=banks[bk][:, 0:256].rearrange("p (k t) -> p k t", k=2)), r=[BK[bk]], w=[PTB[m]])
            for half in range(2):
                cs = slice(half * 512, (half + 1) * 512)
                bp, bg_ = nextbank(), nextbank()
                for kc in range(2):
                    V("tensor", lambda h: h.matmul(banks[bp][:, :], pTt[:, kc, tok], wple[:, kc, cs], start=(kc == 0), stop=(kc == 1)),
                      r=[PTB[m], WPLE], w=[BK[bp]], inc=(kc == 1))
                for kc in range(8):
                    V("tensor", lambda h: h.matmul(banks[bg_][:, :], h1T[:, kc, tok], wgt[:, kc, cs], start=(kc == 0), stop=False),
                      r=[H1T[kc][m], WGT], w=[BK[bg_]], inc=False)
                V("tensor", lambda h: h.matmul(banks[bg_][:, :], ones_b[0:1, :], bgh[0:1, cs], start=False, stop=False), r=[ONB, BGH], w=[BK[bg_]], inc=False)
                V("tensor", lambda h: h.matmul(banks[bg_][:, :], ones_b[0:1, :], bgl[0:1, cs], start=False, stop=True), r=[ONB, BGL], w=[BK[bg_]])
                V("scalar", lambda h: h.activation(out=sg[:, half, :], in_=banks[bg_][:, :], func=AF.Sigmoid), r=[BK[bg_]], w=[SG[half]])
                V("vector", lambda h: h.tensor_tensor(out=sg[:, half, :], in0=sg[:, half, :], in1=banks[bp][:, :], op=ALU.mult), r=[SG[half], BK[bp]], w=[SG[half]])
                V("gpsimd", lambda h: h.tensor_tensor(out=Rr[:, m, cs], in0=Rr[:, m, cs], in1=sg[:, half, :], op=ALU.add), r=[RB[m][half], SG[half]], w=[RB[m][half]])
        if stop_after == "D1":
            dumpsrc["R1"] = (Rr, [b for p_ in RB for b in p_], "D1")
            do_dumps("D1")
            finish()
            return nc
        fw.barrier()
        ar.release("pTt", "ptile", "wple", "wgt", "bgf", "bgh", "bgl", "ones_b", "sg")

        actT = [ar.view("actT%d" % i, [4, T], BF16) for i in range(2)]
        ACTT = [[[Buf() for _ in range(NBLK)] for _ in range(4)] for _ in range(2)]
        rl = [ar.view("rl%d" % i, [512], F32) for i in range(2)]
        RL = [Buf(), Buf()]

        gA = ar.view("gA", [D], F32)
        bA = ar.view("bA", [D], F32)
        GA, BA = Buf(), Buf()
        fw.dma("sync", bcsem[0], gA, vec_d[4:5, :].partition_broadcast(128), w=[GA])
        fw.dma("sync", bcsem[1], bA, vec_d[5:6, :].partition_broadcast(128), w=[BA])
        yo = ar.view("yo", [2, D], F32)
        YO = [Buf(), Buf()]
        osem = [newd(), newd()]

        def emit_E1(m):
            j = m % 4
            V("vector", lambda h: h.bn_stats(out=st[:, j, 0:6], in_=Rr[:, m, 0:512]), r=[RB[m][0]], w=[STB[j]])
            V("vector", lambda h: h.bn_stats(out=st[:, j, 6:12], in_=Rr[:, m, 512:1024]), r=[RB[m][1]], w=[STB[j]])
            V("vector", lambda h: h.bn_aggr(out=mv1[:, m, 0:2], in_=st[:, j, 0:12]), r=[STB[j]], w=[MV1[m]])
            V("vector", lambda h: h.tensor_scalar_add(out=mv1[:, m, 1:2], in0=mv1[:, m, 1:2], scalar1=LN_EPS), r=[MV1[m]], w=[MV1[m]])
            V("gpsimd", lambda h: h.tensor_tensor(out=mv1[:, m, 1:2], in0=mv1[:, m, 1:2], in1=m05, op=ALU.pow), r=[MV1[m], CST], w=[MV1[m]])

        def emit_E2(m):
            j = m % 2
            V("vector", lambda h: h.scalar_tensor_tensor(out=mv1[:, m, 0:1], in0=mv1[:, m, 0:1], scalar=-1.0, in1=mv1[:, m, 1:2],
                                                         op0=ALU.mult, op1=ALU.mult), r=[MV1[m]], w=[MV1[m]])
            V("scalar", lambda h: h.activation(out=yo[:, j, :], in_=Rr[:, m, :], func=AF.Identity, scale=mv1[:, m, 1:2], bias=mv1[:, m, 0:1]),
              r=RB[m] + [MV1[m]], w=[YO[j]])

        def emit_E3(m):
            j = m % 2
            V("vector", lambda h: h.tensor_tensor(out=yo[:, j, :], in0=yo[:, j, :], in1=gA, op=ALU.mult), r=[YO[j], GA], w=[YO[j]])
            V("vector", lambda h: h.tensor_tensor(out=yo[:, j, :], in0=yo[:, j, :], in1=bA, op=ALU.add), r=[YO[j], BA], w=[YO[j]])
            final_tickets.append(fw.dma("sync", osem[j], out[m * 128:(m + 1) * 128, :], yo[:, j, :], r=[YO[j]]))

        def emit_E_step(k):
            if 0 <= k < NT:
                emit_E1(k)
            if 0 <= k - 1 < NT:
                emit_E2(k - 1)
            if 0 <= k - 2 < NT:
                emit_E3(k - 2)

        rcnt = 0
        for g in range(NG):
            s_ = g % 2
            if g + 1 < NG:
                load_ffw(g + 1)
            for blk in range(NBLK):
                for fc in range(4):
                    bk = nextbank()
                    for kc in range(8):
                        V("tensor", lambda h: h.matmul(banks[bk][:, :], wu[s_][:, kc, fc * 128:(fc + 1) * 128], h1T[:, kc, blk * 512:(blk + 1) * 512],
                                                       start=(kc == 0), stop=(kc == 7)),
                          r=[WU[s_]] + [H1T[kc][mm] for mm in range(blk * 4, blk * 4 + 4)], w=[BK[bk]], inc=(kc == 7))
                    q = rcnt % 2
                    rcnt += 1
                    V("scalar", lambda h: h.activation(out=rl[q], in_=banks[bk][:, :], func=AF.Relu), r=[BK[bk]], w=[RL[q]])
                    V("gpsimd", lambda h: h.tensor_tensor(out=actT[s_][:, fc, blk * 512:(blk + 1) * 512], in0=rl[q], in1=rl[q], op=ALU.mult),
                      r=[RL[q]], w=[ACTT[s_][fc][blk]])
            for m in range(NT):
                tok = slice(m * 128, (m + 1) * 128)
                for half in range(2):
                    cs = slice(half * 512, (half + 1) * 512)
                    bk = nextbank()
                    for fc in range(4):
                        V("tensor", lambda h: h.matmul(banks[bk][:, :], actT[s_][:, fc, tok], wd[s_][:, fc, cs], start=(fc == 0), stop=(fc == 3)),
                          r=[ACTT[s_][fc][m // 4], WD[s_]], w=[BK[bk]], inc=(fc == 3))
                    V("vector", lambda h: h.tensor_tensor(out=Rr[:, m, cs], in0=Rr[:, m, cs], in1=banks[bk][:, :], op=ALU.add), r=[RB[m][half], BK[bk]], w=[RB[m][half]])
                if g == NG - 1:
                    emit_E_step(m - 1)
        for k in range(NT - 1, NT + 2):
            emit_E_step(k)

        finish()
    return nc


_CACHE = {}


def _host_consts():
    c = np.zeros((128, NCST), np.float32)
    i = np.arange(128)
    c[:, C_ID:C_ID + 128] = np.eye(128, dtype=np.float32)
    c[:, C_ONE:C_ONE + 128] = 1.0
    same = (i[:, None] // 64) == (i[None, :] // 64)
    c[:, C_TRI:C_TRI + 128] = (same & (i[:, None] <= i[None, :])).astype(np.float32)
    c[:, C_BLK:C_BLK + 128] = same.astype(np.float32)
    c[:, C_NEGM:C_NEGM + 128] = np.where(same & (i[:, None] >= i[None, :]), 0.0, NEG)
    c[:, C_STR:C_STR + 128] = (same & (i[:, None] > i[None, :])).astype(np.float32)
    c[:, C_IND] = (i < 64)
    c[:, C_IND + 1] = (i >= 64)
    c[0:64, C_OAUG:C_OAUG + 128] = 1.0
    c[64, C_OAUG:C_OAUG + 128] = 64.0 * NORM_EPS
    c[:, C_M05] = -0.5
    c[:, C_M05 + 1] = NORM_EPS
    return c


def _prep_shared(inp):
    par = np.zeros((128, NPAR), np.float32)
    par[:, P_G0:P_G0 + 8] = inp["ln_in_g"].reshape(8, 128).T
    par[:, P_B0:P_B0 + 8] = inp["ln_in_b"].reshape(8, 128).T
    cw = inp["conv_w"][0]
    par[:, P_CW:P_CW + 48] = cw.T.reshape(12, 128, 4).transpose(1, 0, 2).reshape(128, 48)
    par[0:64, P_FOXG] = inp["fox_norm_g"][0]
    par[0:8, P_BF] = inp["b_f"][0]
    par[:, P_DTB:P_DTB + 64] = np.tile(inp["dt_bias"][0], 16)[None, :]
    par[:, P_ALOG:P_ALOG + 64] = np.tile(inp["a_log"][0], 16)[None, :]
    par[:, P_G1:P_G1 + 8] = inp["ln1_g"][0].reshape(8, 128).T
    par[:, P_B1:P_B1 + 8] = inp["ln1_b"][0].reshape(8, 128).T
    vecs = np.zeros((8, D), np.float32)
    vecs[0] = inp["ln_in_g"]
    vecs[1] = inp["ln_in_b"]
    vecs[2] = inp["ln1_g"][0]
    vecs[3] = inp["ln1_b"][0]
    vecs[4] = inp["ln2_g"][0]
    vecs[5] = inp["ln2_b"][0]
    vecs[6] = inp["b_ple_gate"][0]
    vecs[7, 0:128] = inp["gdn_norm_g"][0]
    return {
        "w_in": np.ascontiguousarray(inp["w_in"][0]), "w_out": np.ascontiguousarray(inp["w_out"][0]),
        "w_up": np.ascontiguousarray(inp["w_up"][0]), "w_down": np.ascontiguousarray(inp["w_down"][0]),
        "w_ple": np.ascontiguousarray(inp["w_ple"][0]), "w_gate": np.ascontiguousarray(inp["w_ple_gate"][0]),
        "cst": _host_consts(), "par": par, "vecs": vecs, "ones_rows": np.ones((3, 8 * T), np.float32),
    }


def run(inp, stop_after=None, dumps=(), cores=8):
    key = (stop_after, tuple(dumps))
    if key not in _CACHE:
        _CACHE[key] = build(stop_after, dumps)
    nc = _CACHE[key]
    inp = {k: np.asarray(v, dtype=np.float32) for k, v in inp.items()}
    shared = _prep_shared(inp)
    in_maps = []
    for b in range(cores):
        m = dict(shared)
        m["x"] = np.ascontiguousarray(inp["x"][b])
        m["p"] = np.ascontiguousarray(inp["p"][0, b])
        in_maps.append(m)
    res = run_bass_kernel_spmd(nc, in_maps, core_ids=list(range(cores)))
    return res.results


def kernel(**inputs):
    results = run(inputs)
    return np.stack([r["out"] for r in results], axis=0).astype(np.float32)
```

```python
import numpy as np
from contextlib import ExitStack
import concourse.bass as bass
import concourse.mybir as mybir
from concourse.bass_utils import run_bass_kernel_spmd

F32 = mybir.dt.float32
BF16 = mybir.dt.bfloat16
F32R = mybir.dt.float32r
AF = mybir.ActivationFunctionType
ALU = mybir.AluOpType

T, D, NT, NBLK = 2048, 1024, 16, 4
DFF = 4096
ALPHA = 2.0 ** 0.25
LN_EPS = 1e-5
NORM_EPS = 1e-6
NEG = -30000.0
DSZ = {F32: 4, BF16: 2, F32R: 4}

C_ID, C_ONE, C_TRI, C_BLK, C_NEGM, C_STR, C_IND, C_OAUG, C_M05, NCST = 0, 128, 256, 384, 512, 640, 768, 770, 898, 900
P_G0, P_B0, P_CW, P_FOXG, P_BF, P_DTB, P_ALOG, P_G1, P_B1, NPAR = 0, 8, 16, 64, 65, 66, 130, 194, 202, 210


class Buf:
    __slots__ = ("w", "r", "lock")

    def __init__(self, lock=None):
        self.w = None
        self.r = {}
        self.lock = lock


class Ticket:
    __slots__ = ("sem", "val", "key", "eng")

    def __init__(self, sem, val, key, eng):
        self.sem, self.val, self.key, self.eng = sem, val, key, eng


class Eng:
    def __init__(self, name, h, sem):
        self.name, self.h, self.sem = name, h, sem
        self.count = 0
        self.waited = {}


class FW:
    def __init__(self, nc, es):
        self.nc, self.es = nc, es
        self.E = {}
        for n in ("tensor", "vector", "scalar", "gpsimd", "sync"):
            sem = es.enter_context(nc.semaphore("s_" + n))
            self.E[n] = Eng(n, getattr(nc, n), sem)
        self.dsems = []

    def _wait(self, e, t):
        if e.waited.get(t.key, 0) >= t.val:
            return
        e.h.wait_ge(t.sem, t.val)
        e.waited[t.key] = t.val

    def _deps(self, e, reads, writes):
        for b in reads:
            t = b.w
            if t is not None and not (t.eng == e.name and e.name == "tensor"):
                self._wait(e, t)
        for b in writes:
            t = b.w
            if t is not None and (t.eng != e.name or (e.name != "tensor" and t.val <= e.count)):
                self._wait(e, t)
            for t in b.r.values():
                if t.eng != e.name:
                    self._wait(e, t)

    @staticmethod
    def _mark(t, reads, writes):
        for b in reads:
            b.r[t.key] = t
        for b in writes:
            b.w = t
            b.r = {}

    def op(self, eng, fn, r=(), w=(), inc=True):
        e = self.E[eng]
        locks = []
        for b in list(r) + list(w):
            if b.lock is not None and b.lock not in locks:
                locks.append(b.lock)
        for lk in locks:
            t = lk.w
            if t is not None and t.eng != e.name:
                self._wait(e, t)
        self._deps(e, r, w)
        ins = fn(e.h)
        if inc:
            e.count += 1
            ins.then_inc(e.sem, 1)
            t = Ticket(e.sem, e.count, "e_" + eng, eng)
        else:
            t = Ticket(e.sem, e.count + 1, "e_" + eng, eng)
        self._mark(t, r, w)
        for lk in locks:
            lk.w = t
        return t

    def dsem(self, name):
        sem = self.es.enter_context(self.nc.semaphore(name))
        d = [sem, 0, name]
        self.dsems.append(d)
        return d

    def dma(self, q, d, out, in_, r=(), w=()):
        e = self.E[q]
        self._deps(e, r, w)
        ins = e.h.dma_start(out=out, in_=in_)
        d[1] += 16
        ins.then_inc(d[0], 16)
        t = Ticket(d[0], d[1], "d_" + d[2], "dma")
        self._mark(t, r, w)
        return t

    def barrier(self):
        tl = [Ticket(e.sem, e.count, "e_" + n, n) for n, e in self.E.items() if e.count > 0]
        tl += [Ticket(d[0], d[1], "d_" + d[2], "dma") for d in self.dsems if d[1] > 0]
        for e in self.E.values():
            for t in tl:
                if t.eng != e.name:
                    self._wait(e, t)


class Arena:
    def __init__(self, ap, nbytes):
        self.ap = ap
        self.free = [(0, nbytes)]
        self.live = {}

    def alloc(self, name, nbytes):
        nbytes = (nbytes + 63) // 64 * 64
        for i, (o, s) in enumerate(self.free):
            if s >= nbytes:
                self.free[i] = (o + nbytes, s - nbytes)
                self.live[name] = (o, nbytes)
                return o
        raise RuntimeError("arena full for %s (%d) free=%s" % (name, nbytes, self.free))

    def release(self, *names):
        for name in names:
            o, s = self.live.pop(name)
            self.free.append((o, s))
        self.free.sort()
        m = []
        for o, s in self.free:
            if s == 0:
                continue
            if m and m[-1][0] + m[-1][1] == o:
                m[-1] = (m[-1][0], m[-1][1] + s)
            else:
                m.append((o, s))
        self.free = m

    def view(self, name, shape, dt):
        n = 1
        for s in shape:
            n *= s
        nb = n * DSZ[dt]
        o = self.alloc(name, nb)
        v = self.ap[:, o // 4:(o + nb) // 4]
        if dt != F32:
            v = v.bitcast(dt)
        if len(shape) == 2:
            v = v.rearrange("p (a b) -> p a b", a=shape[0])
        elif len(shape) == 3:
            v = v.rearrange("p (a b c) -> p a b c", a=shape[0], b=shape[1])
        elif len(shape) == 4:
            v = v.rearrange("p (a b c d) -> p a b c d", a=shape[0], b=shape[1], c=shape[2])
        return v


def build(stop_after=None, dumps=()):
    nc = bass.Bass("TRN2", target_bir_lowering=False)

    def din(name, shape):
        return nc.dram_tensor(name, list(shape), F32, kind="ExternalInput").ap()

    x = din("x", [T, D])
    p_in = din("p", [T, 256])
    w_in = din("w_in", [D, 3600])
    w_out = din("w_out", [D, D])
    w_up = din("w_up", [D, DFF])
    w_down = din("w_down", [DFF, D])
    w_ple = din("w_ple", [256, D])
    w_gate = din("w_gate", [D, D])
    cst_d = din("cst", [128, NCST])
    par_d = din("par", [128, NPAR])
    vec_d = din("vecs", [8, D])
    ones_d = din("ones_rows", [3, 8 * T])
    out = nc.dram_tensor("out", [T, D], F32, kind="ExternalOutput").ap()
    dump_out = {}
    for (nm, shp, dt) in dumps:
        dump_out[nm] = nc.dram_tensor("dbg_" + nm, list(shp), dt, kind="ExternalOutput").ap()

    w_in_v = w_in.rearrange("(kc p) c -> p kc c", p=128)

    with ExitStack() as es:
        fw = FW(nc, es)
        ARENA_BYTES = 204 * 1024
        arena_t = es.enter_context(nc.sbuf_tensor("arena", [128, ARENA_BYTES // 4], F32))
        ar = Arena(arena_t[:, :], ARENA_BYTES)
        banks = [es.enter_context(nc.psum_tensor("bank%d" % i, [128, 512], F32)) for i in range(8)]
        LK = [Buf() for _ in range(8)]
        BK = [Buf(LK[i]) for i in range(8)]
        rot = [0]

        def nextbank(lo=0, hi=8):
            b = lo + rot[0] % (hi - lo)
            rot[0] += 1
            return b

        def V(eng, fn, r=(), w=(), inc=True):
            return fw.op(eng, fn, r, w, inc)

        dctr = [0]

        def newd():
            dctr[0] += 1
            return fw.dsem("d%d" % dctr[0])

        final_tickets = []

        def do_dumps(stage):
            for (nm, shp, dt) in dumps:
                if nm in dumpsrc and dumpsrc[nm][2] == stage:
                    src, bufs, _ = dumpsrc[nm]
                    final_tickets.append(fw.dma("sync", newd(), dump_out[nm], src, r=bufs))

        dumpsrc = {}

        def finish():
            for t in final_tickets:
                fw._wait(fw.E["sync"], t)

        cst = ar.view("cst", [NCST], F32)
        par = ar.view("par", [NPAR], F32)
        CST, PAR = Buf(), Buf()
        fw.dma("sync", newd(), cst, cst_d, w=[CST])
        fw.dma("sync", newd(), par, par_d, w=[PAR])
        identf = cst[:, C_ID:C_ID + 128]
        onesf = cst[:, C_ONE:C_ONE + 128]
        m05 = cst[:, C_M05:C_M05 + 1]
        eps_col = cst[:, C_M05 + 1:C_M05 + 2]
        EPSB = CST
        identb = ar.view("identb", [128], BF16)
        IDB = Buf()
        V("vector", lambda h: h.tensor_copy(out=identb, in_=identf), r=[CST], w=[IDB])
        mv = ar.view("mv", [NT, 2], F32)
        MV = [Buf() for _ in range(NT)]

        hT = ar.view("hT", [8, T], BF16)
        HT = [[Buf() for _ in range(NBLK)] for _ in range(8)]
        xt = ar.view("xt", [8, D], F32)
        XT = [Buf() for _ in range(8)]
        xsem = [newd() for _ in range(8)]
        st = ar.view("st", [8, 12], F32)
        STB = [Buf() for _ in range(8)]

        def ln_stats(src, SRC, j, mvap, MVB):
            V("vector", lambda h: h.bn_stats(out=st[:, j, 0:6], in_=src[:, 0:512]), r=[SRC], w=[STB[j]])
            V("vector", lambda h: h.bn_stats(out=st[:, j, 6:12], in_=src[:, 512:1024]), r=[SRC], w=[STB[j]])
            V("vector", lambda h: h.bn_aggr(out=mvap[:, 0:2], in_=st[:, j, 0:12]), r=[STB[j]], w=[MVB])
            V("vector", lambda h: h.tensor_scalar_add(out=mvap[:, 1:2], in0=mvap[:, 1:2], scalar1=LN_EPS), r=[MVB], w=[MVB])
            V("gpsimd", lambda h: h.tensor_tensor(out=mvap[:, 1:2], in0=mvap[:, 1:2], in1=m05, op=ALU.pow), r=[MVB, CST], w=[MVB])

        for g in range(NBLK):
            for j4 in range(4):
                m = 4 * g + j4
                j = m % 8
                fw.dma("sync", xsem[j], xt[:, j, :], x[m * 128:(m + 1) * 128, :], w=[XT[j]])
                ln_stats(xt[:, j, :], XT[j], j, mv[:, m, :], MV[m])
                V("vector", lambda h: h.tensor_scalar(out=xt[:, j, :], in0=xt[:, j, :], scalar1=mv[:, m, 0:1], scalar2=mv[:, m, 1:2],
                                                      op0=ALU.subtract, op1=ALU.mult), r=[XT[j], MV[m]], w=[XT[j]])
            for c in range(8):
                bk = nextbank()
                for j4 in range(4):
                    j = (4 * g + j4) % 8
                    V("tensor", lambda h: h.transpose(banks[bk][:, j4 * 128:(j4 + 1) * 128], xt[:, j, c * 128:(c + 1) * 128], identf),
                      r=[XT[j], CST], w=[BK[bk]], inc=(j4 == 3))
                V("scalar", lambda h: h.activation(out=hT[:, c, g * 512:(g + 1) * 512], in_=banks[bk][:, :], func=AF.Identity,
                                                   scale=par[:, P_G0 + c:P_G0 + c + 1], bias=par[:, P_B0 + c:P_B0 + c + 1]),
                  r=[BK[bk], PAR], w=[HT[c][g]])
        dumpsrc["hT"] = (hT, [b for row in HT for b in row], "A1")
        do_dumps("A1")
        if stop_after == "A1":
            finish()
            return nc

        fw.barrier()
        ar.release("xt", "st")
        wsl = [ar.view("wsl%d" % i, [8, 512], BF16) for i in range(3)]
        WSL = [Buf() for _ in range(3)]
        wsem = [newd() for _ in range(3)]
        wsmall = ar.view("wsmall", [8, 8], BF16)
        WSM = Buf()
        wsm_sem = newd()
        fq = ar.view("fq", [8, T], BF16)
        fk = ar.view("fk", [8, T], BF16)
        FQ = [Buf() for _ in range(8)]
        FK = [Buf() for _ in range(8)]
        vaug = ar.view("vaug", [NT, 8, 65], BF16)
        VAUG = Buf()
        OFF_FOX = 2056
        fw.dma("gpsimd", wsem[0], wsl[0], w_in_v[:, :, OFF_FOX:OFF_FOX + 512], w=[WSL[0]])
        fw.dma("gpsimd", wsem[1], wsl[1], w_in_v[:, :, OFF_FOX + 512:OFF_FOX + 1024], w=[WSL[1]])
        fw.dma("gpsimd", wsem[2], wsl[2], w_in_v[:, :, OFF_FOX + 1024:OFF_FOX + 1536], w=[WSL[2]])
        fw.dma("gpsimd", wsm_sem, wsmall, w_in_v[:, :, 3592:3600], w=[WSM])
        for h8 in range(8):
            pass
        V("gpsimd", lambda h: h.memset(fq[64:128, :, :], 0.0), w=FQ)
        V("vector", lambda h: h.memset(fk[64:128, :, :], 0.0), w=FK)
        fw.dma("gpsimd", newd(), fq[67:70, :, :], ones_d.rearrange("r (h t) -> r h t", h=8), w=FQ)
        fw.dma("gpsimd", newd(), fk[64:67, :, :], ones_d.rearrange("r (h t) -> r h t", h=8), w=FK)
        V("gpsimd", lambda h: h.memset(vaug[:, :, :, 64:65], 1.0), w=[VAUG])

        rowA = ar.view("rowA", [T], F32)
        rowC = ar.view("rowC", [T], F32)
        rparts = ar.view("rparts", [6, T], BF16)
        nbf = ar.view("nbf", [1], F32)
        RA, RC, RP, NBF = Buf(), Buf(), Buf(), Buf()
        V("vector", lambda h: h.tensor_scalar_mul(out=nbf[0:8, :], in0=par[0:8, P_BF:P_BF + 1], scalar1=-1.0), r=[PAR], w=[NBF])
        for blk in range(NBLK):
            bk = nextbank()
            for kc in range(8):
                V("tensor", lambda h: h.matmul(banks[bk][0:8, :], wsmall[:, kc, :], hT[:, kc, blk * 512:(blk + 1) * 512],
                                               start=(kc == 0), stop=(kc == 7)), r=[WSM, HT[kc][blk]], w=[BK[bk]], inc=(kc == 7))
            V("scalar", lambda h: h.activation(out=rowA[0:8, blk * 512:(blk + 1) * 512], in_=banks[bk][0:8, :], func=AF.Exp,
                                               scale=-1.0, bias=nbf[0:8, :]), r=[BK[bk], NBF], w=[RA])
        V("scalar", lambda h: h.activation(out=rowA[0:8, :], in_=rowA[0:8, :], func=AF.Ln, bias=1.0), r=[RA], w=[RA])
        V("vector", lambda h: h.tensor_tensor_scan(out=rowC[0:8, :], data0=cst[0:8, C_ONE:C_ONE + 1].to_broadcast([8, T]),
                                                   data1=rowA[0:8, :], initial=0.0, op0=ALU.mult, op1=ALU.subtract),
          r=[RA, CST], w=[RC])
        dumpsrc["crow"] = (rowC[0:8, :], [RC], "AF")
        V("vector", lambda h: h.tensor_copy(out=rparts[0:8, 0, :], in_=rowC[0:8, :]), r=[RC], w=[RP])
        V("vector", lambda h: h.tensor_tensor(out=rowA[0:8, :], in0=rowC[0:8, :], in1=rparts[0:8, 0, :], op=ALU.subtract), r=[RC, RP], w=[RA])
        V("vector", lambda h: h.tensor_copy(out=rparts[0:8, 1, :], in_=rowA[0:8, :]), r=[RA], w=[RP])
        V("vector", lambda h: h.tensor_tensor(out=rowA[0:8, :], in0=rowA[0:8, :], in1=rparts[0:8, 1, :], op=ALU.subtract), r=[RA, RP], w=[RA])
        V("vector", lambda h: h.tensor_copy(out=rparts[0:8, 2, :], in_=rowA[0:8, :]), r=[RA], w=[RP])
        V("vector", lambda h: h.tensor_scalar_mul(out=rparts[0:8, 3:6, :], in0=rparts[0:8, 0:3, :], scalar1=-1.0), r=[RP], w=[RP])
        for h8 in range(8):
            dq = newd()
            fw.dma("sync", dq, fq[64:67, h8, :], rparts[h8:h8 + 1, 0:3, :], r=[RP], w=[FQ[h8]])
            dk_ = newd()
            fw.dma("sync", dk_, fk[67:70, h8, :], rparts[h8:h8 + 1, 3:6, :], r=[RP], w=[FK[h8]])

        stg = ar.view("stg", [4, 512], BF16)
        STG = [Buf() for _ in range(4)]
        stgsem = [newd() for _ in range(4)]
        cnt = 0
        for (dst, DST, ws, WS, sc) in ((fq, FQ, wsl[0], WSL[0], 0.125), (fk, FK, wsl[1], WSL[1], 1.0)):
            for c4 in range(4):
                for blk in range(NBLK):
                    bk = nextbank()
                    for kc in range(8):
                        V("tensor", lambda h: h.matmul(banks[bk][:, :], ws[:, kc, c4 * 128:(c4 + 1) * 128], hT[:, kc, blk * 512:(blk + 1) * 512],
                                                       start=(kc == 0), stop=(kc == 7)), r=[WS, HT[kc][blk]], w=[BK[bk]], inc=(kc == 7))
                    he, ho = 2 * c4, 2 * c4 + 1
                    sl = cnt % 4
                    cnt += 1
                    cs = slice(blk * 512, (blk + 1) * 512)
                    V("scalar", lambda h: h.activation(out=dst[0:64, he, cs], in_=banks[bk][0:64, :], func=AF.Copy, scale=sc), r=[BK[bk]], w=[DST[he]])
                    V("vector", lambda h: h.tensor_scalar_mul(out=stg[64:128, sl, :], in0=banks[bk][64:128, :], scalar1=sc), r=[BK[bk]], w=[STG[sl]])
                    fw.dma("sync", stgsem[sl], dst[0:64, ho, cs], stg[64:128, sl, :], r=[STG[sl]], w=[DST[ho]])
        for m in range(NT):
            bk = nextbank()
            for kc in range(8):
                V("tensor", lambda h: h.matmul(banks[bk][:, :], hT[:, kc, m * 128:(m + 1) * 128], wsl[2][:, kc, :],
                                               start=(kc == 0), stop=(kc == 7)), r=[WSL[2], HT[kc][m // 4]], w=[BK[bk]], inc=(kc == 7))
            src = banks[bk][:, :].rearrange("p (h d) -> p h d", h=8)
            if m % 2 == 0:
                V("scalar", lambda h: h.copy(out=vaug[:, m, :, 0:64], in_=src), r=[BK[bk]], w=[VAUG])
            else:
                V("vector", lambda h: h.tensor_copy(out=vaug[:, m, :, 0:64], in_=src), r=[BK[bk]], w=[VAUG])
        dumpsrc["fq"] = (fq, FQ, "AF")
        dumpsrc["fk"] = (fk, FK, "AF")
        dumpsrc["vaug"] = (vaug, [VAUG], "AF")
        do_dumps("AF")
        if stop_after == "AF":
            finish()
            return nc

        fw.barrier()
        ar.release("wsl0", "wsl1", "wsl2", "rowA", "rowC", "rparts", "wsmall", "nbf", "stg")
        ofox = ar.view("ofox", [4, T], BF16)
        ostg = ar.view("ostg", [2, 512], BF16)
        OSTG = [Buf(), Buf()]
        ostg_sem = [newd(), newd()]
        ostg_ctr = [0]
        OFOX = [Buf() for _ in range(8)]
        pT = [ar.view("pT%d" % i, [512], BF16) for i in range(4)]
        PT = [Buf() for _ in range(4)]
        sq = [ar.view("sq%d" % i, [512], F32) for i in range(2)]
        rr = [ar.view("rr%d" % i, [512], F32) for i in range(2)]
        oaug = cst[:, C_OAUG:C_OAUG + 128]
        SQ, RR, OAUG = [Buf(), Buf()], [Buf(), Buf()], CST
        def pe_warm(n, bk_, lhs, rhs, RB_):
            for _ in range(n):
                V("tensor", lambda h: h.matmul(banks[bk_][:, :], lhs, rhs, start=True, stop=True), r=RB_, w=[BK[bk_]], inc=False)

        pe_warm(24, 5, fk[0:64, 0, 0:128], fq[0:64, 0, 0:512], [FK[0], FQ[0]])
        NFF, NFW = 0, 256
        pslot = [0]
        gp = [0]
        pend = []
        fsl = [0]

        def fin_step(it):
            stp, h8_, qb_, po_ = it[1], it[2], it[3], it[4]
            if stp == 0:
                k_ = fsl[0] % 2
                fsl[0] += 1
                it.append(k_)
                V("scalar", lambda h: h.activation(out=sq[k_][0:65, :], in_=banks[po_][0:65, :], func=AF.Square), r=[BK[po_]], w=[SQ[k_]])
                V("tensor", lambda h: h.matmul(banks[5][:, :], oaug[0:65, :], sq[k_][0:65, :], start=True, stop=True),
                  r=[OAUG, SQ[k_]], w=[BK[5]])
                it[0] = gp[0] + 2
                it[1] = 1
            else:
                k_ = it[5]
                V("scalar", lambda h: h.activation(out=rr[k_][0:64, :], in_=banks[5][0:64, :], func=AF.Ln, scale=1.0 / 64.0), r=[BK[5]], w=[RR[k_]])
                V("scalar", lambda h: h.activation(out=rr[k_][0:64, :], in_=rr[k_][0:64, :], func=AF.Exp, scale=-0.5), r=[RR[k_]], w=[RR[k_]])
                cs_ = slice(qb_ * 512, (qb_ + 1) * 512)
                if h8_ % 2 == 0:
                    V("vector", lambda h: h.scalar_tensor_tensor(out=ofox[0:64, h8_ // 2, cs_], in0=banks[po_][0:64, :],
                                                                 scalar=par[0:64, P_FOXG:P_FOXG + 1], in1=rr[k_][0:64, :],
                                                                 op0=ALU.mult, op1=ALU.mult), r=[BK[po_], RR[k_], PAR], w=[OFOX[h8_]])
                else:
                    sl_ = ostg_ctr[0] % 2
                    ostg_ctr[0] += 1
                    V("vector", lambda h: h.scalar_tensor_tensor(out=ostg[0:64, sl_, :], in0=banks[po_][0:64, :],
                                                                 scalar=par[0:64, P_FOXG:P_FOXG + 1], in1=rr[k_][0:64, :],
                                                                 op0=ALU.mult, op1=ALU.mult), r=[BK[po_], RR[k_], PAR], w=[OSTG[sl_]])
                    fw.dma("sync", ostg_sem[sl_], ofox[64:128, h8_ // 2, cs_], ostg[0:64, sl_, :], r=[OSTG[sl_]], w=[OFOX[h8_]])
                pend.remove(it)

        for h8 in range(8):
            pairs = []
            for qb in range(NBLK):
                for kb in range(4 * (qb + 1)):
                    pairs.append((qb, kb))
            sbank = {}

            def emit_qk(i):
                qb, kb = pairs[i]
                j = kb - 4 * qb
                qlo = max(j, 0) * 128
                bk = nextbank(0, 4)
                sbank[i] = bk
                V("tensor", lambda h: h.matmul(banks[bk][:, qlo:512], fk[:, h8, kb * 128:(kb + 1) * 128],
                                               fq[:, h8, qb * 512 + qlo:(qb + 1) * 512], start=True, stop=True),
                  r=[FK[h8], FQ[h8]], w=[BK[bk]])

            LA = 3
            for i0 in range(min(LA, len(pairs))):
                emit_qk(i0)
            for i, (qb, kb) in enumerate(pairs):
                if i + LA < len(pairs):
                    emit_qk(i + LA)
                for _ in range(NFF):
                    V("tensor", lambda h: h.matmul(banks[4][:, 0:NFW], identb, fq[:, 0, 0:NFW], start=True, stop=True), r=[IDB], w=[BK[4]], inc=False)
                j = kb - 4 * qb
                qlo = max(j, 0) * 128
                bk = sbank.pop(i)
                s = pslot[0] % 4
                pslot[0] += 1
                po = 6 + (h8 * NBLK + qb) % 2
                V("scalar", lambda h: h.activation(out=pT[s][:, qlo:512], in_=banks[bk][:, qlo:512], func=AF.Exp), r=[BK[bk]], w=[PT[s]])
                if j >= 0:
                    V("gpsimd", lambda h: h.affine_select(out=pT[s][:, qlo:qlo + 128], in_=pT[s][:, qlo:qlo + 128], pattern=[[1, 128]],
                                                          compare_op=ALU.is_ge, fill=0.0, base=0, channel_multiplier=-1),
                      r=[PT[s]], w=[PT[s]])
                last = (kb == 4 * (qb + 1) - 1)
                V("tensor", lambda h: h.matmul(banks[po][0:65, qlo:512], vaug[:, kb, h8, :], pT[s][:, qlo:512],
                                               start=(kb == 0), stop=last), r=[VAUG, PT[s]], w=[BK[po]], inc=last)
                if last:
                    pend.append([gp[0] + 2, 0, h8, qb, po])
                gp[0] += 1
                for it in list(pend):
                    if it[0] <= gp[0]:
                        fin_step(it)
        while pend:
            for it in list(pend):
                fin_step(it)
        dumpsrc["ofox"] = (ofox, OFOX, "F")
        do_dumps("F")
        if stop_after == "F":
            finish()
            return nc
        fw.barrier()
        ar.release("fq", "fk", "vaug", "pT0", "pT1", "pT2", "pT3", "sq0", "sq1", "rr0", "rr1", "ostg")

        wsl = [ar.view("wsl%d" % i, [8, 512], BF16) for i in range(3)]
        WSL = [Buf() for _ in range(3)]
        wg = ar.view("wg", [8, 8], BF16)
        WG = Buf()
        qkvT = ar.view("qkvT", [12, T], BF16)
        QKV = [Buf() for _ in range(12)]
        sz = ar.view("sz", [4, T], BF16)
        SZ = [Buf() for _ in range(4)]
        pcb = ar.view("pcb", [4, 516], F32)
        PC = [Buf() for _ in range(4)]
        accb = ar.view("accb", [4, 512], F32)
        ACC = [Buf() for _ in range(4)]
        gpre = ar.view("gpre", [NT, 8], F32)
        GPRE = Buf()
        beta = ar.view("beta", [NT, 4], F32)
        lg = ar.view("lg", [NT, 4], F32)
        gt1 = ar.view("gt1", [NT, 4], F32)
        gt2 = ar.view("gt2", [NT, 4], F32)
        BETA, LG, GT1, GT2 = Buf(), Buf(), Buf(), Buf()
        for i in range(3):
            fw.dma("gpsimd", wsem[i], wsl[i], w_in_v[:, :, i * 512:(i + 1) * 512], w=[WSL[i]])
        fw.dma("gpsimd", wsm_sem, wg, w_in_v[:, :, 2048:2056], w=[WG])
        cw = par[:, P_CW:P_CW + 48].rearrange("p (c j) -> p c j", j=4)

        def ag_E(u):
            ch, blk = u // 4, u % 4
            grp, hc = ch // 4, ch % 4
            ws, WS = wsl[grp], WSL[grp]
            sl = u % 4
            if u == 16:
                pass
            bk = nextbank()
            for kc in range(8):
                V("tensor", lambda h: h.matmul(banks[bk][:, :], ws[:, kc, hc * 128:(hc + 1) * 128], hT[:, kc, blk * 512:(blk + 1) * 512],
                                               start=(kc == 0), stop=(kc == 7)), r=[WS, HT[kc][blk]], w=[BK[bk]], inc=(kc == 7))
            if blk == 0:
                V("gpsimd", lambda h: h.memset(pcb[:, sl, 0:3], 0.0), w=[PC[sl]])
            else:
                V("gpsimd", lambda h: h.tensor_copy(out=pcb[:, sl, 0:3], in_=pcb[:, (u - 1) % 4, 512:515]), r=[PC[(u - 1) % 4]], w=[PC[sl]])
            V("scalar", lambda h: h.copy(out=pcb[:, sl, 3:515], in_=banks[bk][:, :]), r=[BK[bk]], w=[PC[sl]])

        def ag_T(u):
            ch = u // 4
            sl = u % 4
            V("scalar", lambda h: h.activation(out=accb[:, sl, :], in_=pcb[:, sl, 0:512], func=AF.Copy, scale=cw[:, ch, 0:1]), r=[PC[sl], PAR], w=[ACC[sl]])
            for j in range(1, 4):
                V("vector", lambda h: h.scalar_tensor_tensor(out=accb[:, sl, :], in0=pcb[:, sl, j:j + 512], scalar=cw[:, ch, j:j + 1], in1=accb[:, sl, :],
                                                             op0=ALU.mult, op1=ALU.add), r=[PC[sl], PAR, ACC[sl]], w=[ACC[sl]])

        def ag_S(u):
            ch, blk = u // 4, u % 4
            sl = u % 4
            V("scalar", lambda h: h.activation(out=qkvT[:, ch, blk * 512:(blk + 1) * 512], in_=accb[:, sl, :], func=AF.Silu), r=[ACC[sl]], w=[QKV[ch]])

        NU = 48
        for u in range(NU + 2):
            if u < NU:
                ag_E(u)
                if u == 16:
                    fw.dma("gpsimd", wsem[0], wsl[0], w_in_v[:, :, 1536:2048], w=[WSL[0]])
            if 0 <= u - 1 < NU:
                ag_T(u - 1)
            if 0 <= u - 2 < NU:
                ag_S(u - 2)
        for hc in range(4):
            for blk in range(NBLK):
                bk = nextbank()
                for kc in range(8):
                    V("tensor", lambda h: h.matmul(banks[bk][:, :], wsl[0][:, kc, hc * 128:(hc + 1) * 128], hT[:, kc, blk * 512:(blk + 1) * 512],
                                                   start=(kc == 0), stop=(kc == 7)), r=[WSL[0], HT[kc][blk]], w=[BK[bk]], inc=(kc == 7))
                V("scalar", lambda h: h.activation(out=sz[:, hc, blk * 512:(blk + 1) * 512], in_=banks[bk][:, :], func=AF.Silu), r=[BK[bk]], w=[SZ[hc]])
        for m in range(NT):
            bk = nextbank()
            for kc in range(8):
                V("tensor", lambda h: h.matmul(banks[bk][:, 0:8], hT[:, kc, m * 128:(m + 1) * 128], wg[:, kc, :],
                                               start=(kc == 0), stop=(kc == 7)), r=[WG, HT[kc][m // 4]], w=[BK[bk]], inc=(kc == 7))
            V("scalar", lambda h: h.copy(out=gpre[:, m, :], in_=banks[bk][:, 0:8]), r=[BK[bk]], w=[GPRE])
        dtb = par[:, P_DTB:P_DTB + 64].rearrange("p (m h) -> p m h", h=4)
        alog = par[:, P_ALOG:P_ALOG + 64].rearrange("p (m h) -> p m h", h=4)
        V("scalar", lambda h: h.activation(out=gt1, in_=gpre[:, :, 0:4], func=AF.Exp, scale=-1.0), r=[GPRE], w=[GT1])
        V("vector", lambda h: h.tensor_scalar_add(out=gt1, in0=gt1, scalar1=1.0), r=[GT1], w=[GT1])
        V("vector", lambda h: h.reciprocal(out=beta, in_=gt1), r=[GT1], w=[BETA])
        V("vector", lambda h: h.tensor_tensor(out=gt2, in0=gpre[:, :, 4:8], in1=dtb, op=ALU.add), r=[GPRE, PAR], w=[GT2])
        V("scalar", lambda h: h.activation(out=gt2, in_=gt2, func=AF.Exp), r=[GT2], w=[GT2])
        V("scalar", lambda h: h.activation(out=gt2, in_=gt2, func=AF.Ln, bias=1.0), r=[GT2], w=[GT2])
        V("scalar", lambda h: h.activation(out=gt1, in_=alog, func=AF.Exp), r=[PAR, GT1, BETA], w=[GT1])
        V("vector", lambda h: h.scalar_tensor_tensor(out=lg, in0=gt2, scalar=-1.0, in1=gt1, op0=ALU.mult, op1=ALU.mult), r=[GT1, GT2], w=[LG])
        dumpsrc["qkvT"] = (qkvT, QKV, "AG")
        dumpsrc["sz"] = (sz, SZ, "AG")
        dumpsrc["beta"] = (beta, [BETA], "AG")
        dumpsrc["lg"] = (lg, [LG], "AG")
        do_dumps("AG")
        if stop_after == "AG":
            finish()
            return nc
        fw.barrier()
        ar.release("hT", "wsl0", "wsl1", "wsl2", "wg", "pcb", "accb", "gpre", "gt1", "gt2")

        ogT = ar.view("ogT", [4, T], BF16)
        OGT = [Buf() for _ in range(4)]
        BQ = [[Buf(LK[i]) for _ in range(4)] for i in range(8)]

        def qap(b, q):
            return banks[b][:, q * 128:(q + 1) * 128]

        def flat(v):
            return v.rearrange("p h d -> p (h d)")

        def hb():
            return [Buf() for _ in range(4)]

        gam = ar.view("gam", [NT, 4], F32)
        egam = ar.view("egam", [NT, 4], F32)
        edec = ar.view("edec", [NT, 4], F32)
        glb = ar.view("glb", [2, 64], F32)
        X2 = ar.view("X2", [2, 64], F32)
        GAM, EGAM, EDEC, GLB, X2B = Buf(), Buf(), Buf(), Buf(), Buf()
        gbc = ar.view("gbc", [128], F32)
        GBC = Buf()
        fw.dma("sync", newd(), gbc, vec_d[7:8, 0:128].partition_broadcast(128), w=[GBC])
        negm = cst[:, C_NEGM:C_NEGM + 128]
        negmb = ar.view("negmb", [128], BF16)
        NEGMB = Buf()
        V("vector", lambda h: h.tensor_copy(out=negmb, in_=negm), r=[CST], w=[NEGMB])
        strict = cst[:, C_STR:C_STR + 128]
        lgf = lg.rearrange("p m h -> p (m h)")
        b0 = nextbank()
        V("tensor", lambda h: h.matmul(banks[b0][:, 0:64], cst[:, C_TRI:C_TRI + 128], lgf, start=True, stop=True), r=[CST, LG], w=[BQ[b0][0]])
        V("tensor", lambda h: h.matmul(banks[b0][:, 128:192], cst[:, C_BLK:C_BLK + 128], lgf, start=True, stop=True), r=[CST, LG], w=[BQ[b0][1]])
        V("vector", lambda h: h.tensor_copy(out=gam.rearrange("p m h -> p (m h)"), in_=banks[b0][:, 0:64]), r=[BQ[b0][0]], w=[GAM])
        V("scalar", lambda h: h.activation(out=egam.rearrange("p m h -> p (m h)"), in_=banks[b0][:, 0:64], func=AF.Exp), r=[BQ[b0][0]], w=[EGAM])
        V("vector", lambda h: h.tensor_tensor(out=edec.rearrange("p m h -> p (m h)"), in0=banks[b0][:, 128:192],
                                              in1=gam.rearrange("p m h -> p (m h)"), op=ALU.subtract), r=[BQ[b0][1], GAM], w=[EDEC])
        V("scalar", lambda h: h.activation(out=edec, in_=edec, func=AF.Exp), r=[EDEC], w=[EDEC])
        for hf in range(2):
            V("vector", lambda h: h.tensor_scalar_mul(out=X2[:, hf, :], in0=lgf, scalar1=cst[:, C_IND + hf:C_IND + hf + 1]), r=[LG, CST], w=[X2B])
        V("tensor", lambda h: h.matmul(banks[b0][:, 256:384], onesf, X2.rearrange("p a b -> p (a b)"), start=True, stop=True), r=[CST, X2B], w=[BQ[b0][2]])
        V("scalar", lambda h: h.activation(out=glb.rearrange("p a b -> p (a b)"), in_=banks[b0][:, 256:384], func=AF.Exp), r=[BQ[b0][2]], w=[GLB])

        dumpsrc["gam"] = (gam, [GAM], "G0")
        dumpsrc["glb"] = (glb, [GLB], "G0")
        if stop_after == "G0":
            do_dumps("G0")
            finish()
            return nc
        def hb3(n):
            return [[Buf() for _ in range(4)] for _ in range(n)]

        def v2(name, n, dt):
            return [ar.view("%s%d" % (name, i), [4, 128], dt) for i in range(n)]

        NPI, NSH = 3, 4
        kbt = v2("kbt", NPI, BF16); KBT = hb3(NPI)
        vbt = v2("vbt", NPI, BF16); VBT = hb3(NPI)
        ktv = [ar.view("ktv%d" % i, [8, 128], BF16) for i in range(NPI)]; KTV = hb3(NPI)
        diagc = v2("diagc", NPI, F32); DIAGC = hb3(NPI)
        Dm = v2("Dm", NPI, F32); DM = hb3(NPI)
        DmS = v2("DmS", NPI, F32); DMS = hb3(NPI)
        Am = [v2("Am%d_" % i, NPI, BF16) for i in range(2)]; AMB = [hb3(NPI), hb3(NPI)]
        Bm = [v2("Bm%d_" % i, NPI, BF16) for i in range(2)]; BMB = [hb3(NPI), hb3(NPI)]
        ac = v2("ac", NPI, BF16); AC = hb3(NPI)
        Tt32 = v2("Tt32_", NPI, F32); TT32 = hb3(NPI)
        Ttb = [v2("Ttb%d_" % i, NPI, BF16) for i in range(2)]; TTB = [hb3(NPI), hb3(NPI)]
        kdt = v2("kdt", NSH, BF16); KDT = hb3(NSH)
        acT = v2("acT", NSH, BF16); ACTB = hb3(NSH)
        u_sb = v2("u_sb", NSH, F32); US = hb3(NSH)
        wT_sb = v2("wT_sb", NSH, BF16); WT = hb3(NSH)
        vnew = ar.view("vnew", [4, 128], BF16); VN = hb()
        tmpo = ar.view("tmpo", [4, 128], F32); TMPO = hb()
        otok = ar.view("otok", [4, 128], F32); OTOK = hb()
        on = ar.view("on", [4, 128], BF16); ON = hb()
        junkt = ar.view("junk", [8, 128], F32); JUNKS = [Buf() for _ in range(8)]
        junk2t = ar.view("junk2", [4, 128], F32); JUNK2S = [Buf() for _ in range(4)]
        jctr = [0]

        def njunk():
            jctr[0] += 1
            return jctr[0] % 8
        S32 = ar.view("S32", [4, 128], F32); S32B = hb()
        Sp = ar.view("Sp", [4, 128], F32); SPB = hb()
        Sbf = ar.view("Sbf", [4, 128], BF16); SBF = hb()
        sm = ar.view("sm", [NSH, 12, 4], F32)
        SM = [Buf() for _ in range(NSH)]
        SM2 = [Buf() for _ in range(NSH)]
        V("vector", lambda h: h.memset(flat(S32), 0.0), w=S32B)
        V("vector", lambda h: h.memset(flat(Sbf), 0.0), w=SBF)
        bWS, bOI, bO2, bSD, bOT = 4, 5, 6, 7, 4

        def bq(b, hh):
            return banks[b][:, :].bitcast(BF16)[:, hh * 256:hh * 256 + 128]

        def bqall(b):
            return banks[b][:, :].bitcast(BF16).rearrange("p (h c) -> p h c", h=4)[:, :, 0:128]

        def pbank():
            return nextbank(0, 4)

        def gen_prep(m):
            tok = slice(m * 128, (m + 1) * 128)
            pi = m % NPI
            p3 = m % NSH
            rk, rq, bkk, skb, skd, sa, so, lnk, crow, sso = [sm[:, p3, i, :] for i in range(10)]
            S_ = [SM[p3]]
            X, Y = pbank(), pbank()
            t1 = banks[X][:, :].bitcast(BF16)
            for hh in range(4):
                kq = t1[:, hh * 256:hh * 256 + 128]
                vq = t1[:, hh * 256 + 128:hh * 256 + 256]
                V("tensor", lambda h: h.transpose(kq, qkvT[:, 4 + hh, tok], identb), r=[QKV[4 + hh], IDB], w=[BQ[X][hh]], inc=False)
                V("tensor", lambda h: h.transpose(vq, qkvT[:, 8 + hh, tok], identb), r=[QKV[8 + hh], IDB], w=[BQ[X][hh]], inc=False)
                V("tensor", lambda h: h.transpose(bq(Y, hh), qkvT[:, hh, tok], identb), r=[QKV[hh], IDB], w=[BQ[Y][hh]])
            for hh in range(4):
                kq = t1[:, hh * 256:hh * 256 + 128]
                j1, j2 = njunk(), njunk()
                V("scalar", lambda h: h.activation(out=junkt[:, j1, :], in_=kq, func=AF.Square, accum_out=rk[:, hh:hh + 1]), r=[BQ[X][hh]], w=[JUNKS[j1], SM[p3]])
                V("scalar", lambda h: h.activation(out=junkt[:, j2, :], in_=bq(Y, hh), func=AF.Square, accum_out=rq[:, hh:hh + 1]), r=[BQ[Y][hh]], w=[JUNKS[j2], SM[p3]])
            V("vector", lambda h: h.tensor_copy(out=ktv[pi].rearrange("p a b -> p (a b)"), in_=t1), r=BQ[X], w=KTV[pi])
            yield
            rkq = sm[:, p3, 0:2, :]
            V("scalar", lambda h: h.activation(out=rkq, in_=rkq, func=AF.Ln, bias=eps_col[:, 0:1]), r=S_ + [EPSB], w=S_)
            V("vector", lambda h: h.scalar_tensor_tensor(out=crow, in0=rk, scalar=-0.5, in1=gam[:, m, :], op0=ALU.mult, op1=ALU.subtract), r=S_ + [GAM], w=S_)
            V("scalar", lambda h: h.activation(out=rkq, in_=rkq, func=AF.Exp, scale=-0.5), r=S_, w=S_)
            V("vector", lambda h: h.tensor_tensor(out=bkk, in0=rk, in1=beta[:, m, :], op=ALU.mult), r=S_ + [BETA], w=S_)
            V("vector", lambda h: h.tensor_tensor(out=skb, in0=bkk, in1=egam[:, m, :], op=ALU.mult), r=S_ + [EGAM], w=S_)
            V("vector", lambda h: h.tensor_tensor(out=skd, in0=rk, in1=edec[:, m, :], op=ALU.mult), r=S_ + [EDEC], w=S_)
            V("vector", lambda h: h.tensor_scalar_mul(out=sa, in0=rq, scalar1=128.0 ** -0.5), r=S_, w=S_)
            V("vector", lambda h: h.tensor_tensor(out=so, in0=sa, in1=egam[:, m, :], op=ALU.mult), r=S_ + [EGAM], w=S_)
            yield
            for hh in range(4):
                kq = ktv[pi][:, 2 * hh, :]
                vq = ktv[pi][:, 2 * hh + 1, :]
                V("scalar", lambda h: h.activation(out=diagc[pi][:, hh, :], in_=identf, func=AF.Copy, scale=crow[:, hh:hh + 1]), r=[CST] + S_, w=[DIAGC[pi][hh]])
                V("scalar", lambda h: h.activation(out=kbt[pi][:, hh, :], in_=kq, func=AF.Copy, scale=skb[:, hh:hh + 1]), r=[KTV[pi][hh]] + S_, w=[KBT[pi][hh]])
                V("vector", lambda h: h.tensor_scalar_mul(out=kdt[p3][:, hh, :], in0=kq, scalar1=skd[:, hh:hh + 1]), r=[KTV[pi][hh]] + S_, w=[KDT[p3][hh]])
                V("vector", lambda h: h.tensor_scalar_mul(out=vbt[pi][:, hh, :], in0=vq, scalar1=beta[:, m, hh:hh + 1]), r=[KTV[pi][hh], BETA], w=[VBT[pi][hh]])
            yield
            Y = pbank()
            for hh in range(4):
                V("tensor", lambda h: h.matmul(qap(Y, hh), onesf, diagc[pi][:, hh, :], start=True, stop=False), r=[CST, DIAGC[pi][hh]], w=[BQ[Y][hh]], inc=False)
                V("tensor", lambda h: h.matmul(qap(Y, hh), identb, negmb, start=False, stop=True), r=[IDB, NEGMB], w=[BQ[Y][hh]], inc=(hh == 3))
            for hh in range(4):
                V("scalar", lambda h: h.activation(out=Dm[pi][:, hh, :], in_=qap(Y, hh), func=AF.Exp, bias=gam[:, m, hh:hh + 1]), r=[BQ[Y][hh], GAM], w=[DM[pi][hh]])
            for hh in range(4):
                V("gpsimd", lambda h: h.tensor_tensor(out=DmS[pi][:, hh, :], in0=Dm[pi][:, hh, :], in1=strict, op=ALU.mult), r=[DM[pi][hh], CST], w=[DMS[pi][hh]])
            yield
            X, Y = pbank(), pbank()
            for hh in range(4):
                V("tensor", lambda h: h.matmul(qap(X, hh), qkvT[:, 4 + hh, tok], qkvT[:, 4 + hh, tok], start=True, stop=True), r=[QKV[4 + hh]], w=[BQ[X][hh]], inc=(hh == 3))
            for hh in range(4):
                V("tensor", lambda h: h.matmul(qap(Y, hh), qkvT[:, hh, tok], qkvT[:, 4 + hh, tok], start=True, stop=True), r=[QKV[hh], QKV[4 + hh]], w=[BQ[Y][hh]], inc=(hh == 3))
            for hh in range(4):
                V("vector", lambda h: h.scalar_tensor_tensor(out=Am[0][pi][:, hh, :], in0=qap(X, hh), scalar=bkk[:, hh:hh + 1], in1=DmS[pi][:, hh, :],
                                                             op0=ALU.mult, op1=ALU.mult), r=[BQ[X][hh], DMS[pi][hh]] + S_, w=[AMB[0][pi][hh]])
            for hh in range(4):
                V("vector", lambda h: h.scalar_tensor_tensor(out=ac[pi][:, hh, :], in0=qap(Y, hh), scalar=sa[:, hh:hh + 1], in1=Dm[pi][:, hh, :],
                                                             op0=ALU.mult, op1=ALU.mult), r=[BQ[Y][hh], DM[pi][hh]] + S_, w=[AC[pi][hh]])
            yield
            X, Y = pbank(), pbank()
            for hh in range(4):
                V("tensor", lambda h: h.transpose(bq(X, hh), Am[0][pi][:, hh, :], identb), r=[AMB[0][pi][hh], IDB], w=[BQ[X][hh]], inc=(hh == 3))
            for hh in range(4):
                V("tensor", lambda h: h.transpose(bq(Y, hh), ac[pi][:, hh, :], identb), r=[AC[pi][hh], IDB], w=[BQ[Y][hh]], inc=(hh == 3))
            V("scalar", lambda h: h.copy(out=Bm[0][pi], in_=bqall(X)), r=BQ[X], w=BMB[0][pi])
            for hh in range(4):
                V("vector", lambda h: h.tensor_tensor(out=Ttb[0][pi][:, hh, :], in0=identf, in1=bq(X, hh), op=ALU.subtract), r=[CST, BQ[X][hh]], w=[TTB[0][pi][hh]])
            for hh in range(4):
                V("vector", lambda h: h.tensor_tensor(out=Tt32[pi][:, hh, :], in0=identf, in1=bq(X, hh), op=ALU.subtract), r=[CST, BQ[X][hh]], w=[TT32[pi][hh]])
            V("scalar", lambda h: h.copy(out=acT[p3], in_=bqall(Y)), r=BQ[Y], w=ACTB[p3])
            yield
            cur = 0
            for s_ in range(5):
                X = pbank()
                for hh in range(4):
                    V("tensor", lambda h: h.matmul(qap(X, hh), Bm[cur][pi][:, hh, :], Am[cur][pi][:, hh, :], start=True, stop=True),
                      r=[BMB[cur][pi][hh], AMB[cur][pi][hh]], w=[BQ[X][hh]], inc=(hh == 3))
                V("scalar", lambda h: h.copy(out=flat(Am[1 - cur][pi]), in_=banks[X][:, :]), r=BQ[X], w=AMB[1 - cur][pi])
                if s_ < 4:
                    Y = pbank()
                    for hh in range(4):
                        V("tensor", lambda h: h.matmul(qap(Y, hh), Am[cur][pi][:, hh, :], Bm[cur][pi][:, hh, :], start=True, stop=True),
                          r=[BMB[cur][pi][hh], AMB[cur][pi][hh]], w=[BQ[Y][hh]], inc=(hh == 3))
                    V("scalar", lambda h: h.copy(out=flat(Bm[1 - cur][pi]), in_=banks[Y][:, :]), r=BQ[Y], w=BMB[1 - cur][pi])
                yield
                X = pbank()
                for hh in range(4):
                    V("tensor", lambda h: h.matmul(qap(X, hh), Am[1 - cur][pi][:, hh, :], Ttb[cur][pi][:, hh, :], start=True, stop=True),
                      r=[AMB[1 - cur][pi][hh], TTB[cur][pi][hh]], w=[BQ[X][hh]], inc=(hh == 3))
                V("vector", lambda h: h.tensor_tensor(out=flat(Ttb[1 - cur][pi]), in0=flat(Tt32[pi]), in1=banks[X][:, :], op=ALU.add),
                  r=TT32[pi] + BQ[X], w=TTB[1 - cur][pi])
                V("vector", lambda h: h.tensor_tensor(out=flat(Tt32[pi]), in0=flat(Tt32[pi]), in1=banks[X][:, :], op=ALU.add),
                  r=TT32[pi] + BQ[X], w=TT32[pi])
                cur = 1 - cur
                yield
            X, Y = pbank(), pbank()
            for hh in range(4):
                V("tensor", lambda h: h.matmul(qap(X, hh), Ttb[cur][pi][:, hh, :], vbt[pi][:, hh, :], start=True, stop=True),
                  r=[TTB[cur][pi][hh], VBT[pi][hh]], w=[BQ[X][hh]], inc=(hh == 3))
            for hh in range(4):
                V("tensor", lambda h: h.matmul(qap(Y, hh), kbt[pi][:, hh, :], Ttb[cur][pi][:, hh, :], start=True, stop=True),
                  r=[TTB[cur][pi][hh], KBT[pi][hh]], w=[BQ[Y][hh]], inc=(hh == 3))
            V("scalar", lambda h: h.copy(out=flat(u_sb[p3]), in_=banks[X][:, :]), r=BQ[X], w=US[p3])
            V("vector", lambda h: h.tensor_copy(out=flat(wT_sb[p3]), in_=banks[Y][:, :]), r=BQ[Y], w=WT[p3])
            yield

        def gen_scan(m):
            tok = slice(m * 128, (m + 1) * 128)
            p3 = m % NSH
            so = sm[:, p3, 6, :]
            sso = sm[:, p3, 9, :]
            S_ = [SM[p3]]
            for half in range(2):
                r0 = half * 64
                c0 = m * 128 + r0
                for hh in range(4):
                    V("tensor", lambda h: h.matmul(qap(bWS, hh)[r0:r0 + 64, :], wT_sb[p3][:, hh, r0:r0 + 64], Sbf[:, hh, :], start=True, stop=True),
                      r=[WT[p3][hh], SBF[hh]], w=[BQ[bWS][hh]], inc=(hh == 3))
                for hh in range(4):
                    V("tensor", lambda h: h.matmul(qap(bOI, hh)[r0:r0 + 64, :], qkvT[:, hh, c0:c0 + 64], Sbf[:, hh, :], start=True, stop=True),
                      r=[QKV[hh], SBF[hh]], w=[BQ[bOI][hh]], inc=(hh == 3))
                for hh in range(4):
                    gcol = glb[:, half, m * 4 + hh:m * 4 + hh + 1]
                    V("gpsimd", lambda h: h.tensor_scalar(out=Sp[:, hh, :], in0=S32[:, hh, :], scalar1=gcol, scalar2=0.0, op0=ALU.mult, op1=ALU.add),
                      r=[S32B[hh], GLB], w=[SPB[hh]])
                yield
                V("vector", lambda h: h.tensor_tensor(out=flat(vnew[r0:r0 + 64, :, :]), in0=flat(u_sb[p3][r0:r0 + 64, :, :]),
                                                      in1=banks[bWS][r0:r0 + 64, :], op=ALU.subtract), r=US[p3] + BQ[bWS], w=VN)
                yield
                for hh in range(4):
                    V("tensor", lambda h: h.matmul(qap(bSD, hh), kdt[p3][r0:r0 + 64, hh, :], vnew[r0:r0 + 64, hh, :], start=True, stop=True),
                      r=[KDT[p3][hh], VN[hh]], w=[BQ[bSD][hh]], inc=(hh == 3))
                for hh in range(4):
                    V("tensor", lambda h: h.matmul(qap(bO2, hh)[r0:r0 + 64, :], acT[p3][r0:r0 + 64, hh, r0:r0 + 64], vnew[r0:r0 + 64, hh, :],
                                                   start=True, stop=True), r=[ACTB[p3][hh], VN[hh]], w=[BQ[bO2][hh]], inc=(hh == 3))
                yield
                V("vector", lambda h: h.tensor_tensor(out=flat(S32), in0=flat(Sp), in1=banks[bSD][:, :], op=ALU.add), r=SPB + BQ[bSD], w=S32B)
                V("scalar", lambda h: h.copy(out=flat(Sbf), in_=flat(S32)), r=S32B, w=SBF)
                yield
            for hh in range(4):
                V("scalar", lambda h: h.activation(out=tmpo[:, hh, :], in_=qap(bOI, hh), func=AF.Copy, scale=so[:, hh:hh + 1]), r=[BQ[bOI][hh]] + S_, w=[TMPO[hh]])
            V("vector", lambda h: h.tensor_tensor(out=flat(otok), in0=flat(tmpo), in1=banks[bO2][:, :], op=ALU.add), r=TMPO + BQ[bO2], w=OTOK)
            yield
            for hh in range(4):
                V("scalar", lambda h: h.activation(out=junk2t[:, hh, :], in_=otok[:, hh, :], func=AF.Square, accum_out=sso[:, hh:hh + 1]), r=[OTOK[hh]], w=[JUNK2S[hh], SM2[p3]])
            V("scalar", lambda h: h.activation(out=sso, in_=sso, func=AF.Ln, scale=1.0 / 128.0, bias=eps_col[:, 0:1]), r=[SM2[p3], EPSB], w=[SM2[p3]])
            V("scalar", lambda h: h.activation(out=sso, in_=sso, func=AF.Exp, scale=-0.5), r=[SM2[p3]], w=[SM2[p3]])
            yield
            for hh in range(4):
                V("vector", lambda h: h.scalar_tensor_tensor(out=on[:, hh, :], in0=otok[:, hh, :], scalar=sso[:, hh:hh + 1], in1=gbc,
                                                             op0=ALU.mult, op1=ALU.mult), r=[OTOK[hh], SM2[p3], GBC], w=[ON[hh]])
            for hh in range(4):
                V("tensor", lambda h: h.transpose(bq(bOT, hh), on[:, hh, :], identb), r=[ON[hh], IDB], w=[BQ[bOT][hh]], inc=(hh == 3))
            yield
            V("vector", lambda h: h.tensor_tensor(out=ogT[:, :, tok], in0=bqall(bOT), in1=sz[:, :, tok], op=ALU.mult), r=BQ[bOT] + SZ, w=OGT)
            yield

        preps = {}
        prep_done = set()
        scan_done = set()
        next_prep = 0
        scan_m = 0
        scan_g = None
        while len(scan_done) < NT:
            while len(preps) < NPI and next_prep < NT and (next_prep < NSH or (next_prep - NSH) in scan_done) \
                    and (next_prep < NPI or (next_prep - NPI) in prep_done):
                preps[next_prep] = gen_prep(next_prep)
                next_prep += 1
            progressed = False
            for _rep in range(2):
                if scan_g is None and scan_m < NT and scan_m in prep_done:
                    scan_g = gen_scan(scan_m)
                if scan_g is not None:
                    try:
                        next(scan_g)
                    except StopIteration:
                        scan_done.add(scan_m)
                        scan_m += 1
                        scan_g = None
                    progressed = True
            for t_ in sorted(preps):
                try:
                    next(preps[t_])
                except StopIteration:
                    prep_done.add(t_)
                    del preps[t_]
                progressed = True
            assert progressed
        dumpsrc["ogT"] = (ogT, OGT, "G")
        do_dumps("G")
        if stop_after == "G":
            finish()
            return nc
        fw.barrier()
        ar.release("gam", "egam", "edec", "glb", "X2", "gbc", "negmb", "vnew", "tmpo", "otok", "on", "junk", "junk2", "S32", "Sp", "Sbf", "sm", "qkvT", "sz", "beta", "lg", "kbt0", "kbt1", "kbt2", "vbt0", "vbt1", "vbt2", "ktv0", "ktv1", "ktv2", "diagc0", "diagc1", "diagc2", "Dm0", "Dm1", "Dm2", "DmS0", "DmS1", "DmS2", "Am0_0", "Am0_1", "Am0_2", "Am1_0", "Am1_1", "Am1_2", "Bm0_0", "Bm0_1", "Bm0_2", "Bm1_0", "Bm1_1", "Bm1_2", "ac0", "ac1", "ac2", "Tt32_0", "Tt32_1", "Tt32_2", "Ttb0_0", "Ttb0_1", "Ttb0_2", "Ttb1_0", "Ttb1_1", "Ttb1_2", "kdt0", "kdt1", "kdt2", "kdt3", "acT0", "acT1", "acT2", "acT3", "u_sb0", "u_sb1", "u_sb2", "u_sb3", "wT_sb0", "wT_sb1", "wT_sb2", "wT_sb3")

        Rr = ar.view("R", [NT, D], F32)
        RB = [[Buf(), Buf()] for _ in range(NT)]
        h1T = ar.view("h1T", [8, T], BF16)
        H1T = [[Buf() for _ in range(NT)] for _ in range(8)]
        wo_g = ar.view("wo_g", [4, 512], BF16)
        wo_f = ar.view("wo_f", [4, 512], BF16)
        WOG, WOF = Buf(), Buf()
        wog_sem, wof_sem = newd(), newd()
        xts = [ar.view("xt%d" % i, [D], F32) for i in range(4)]
        XT = [Buf() for _ in range(4)]
        xsem2 = [newd() for _ in range(2)]
        gA = ar.view("gA", [D], F32)
        bA = ar.view("bA", [D], F32)
        GA, BA = Buf(), Buf()
        st = ar.view("st", [4, 12], F32)
        STB = [Buf() for _ in range(4)]
        mv1 = ar.view("mv1", [NT, 2], F32)
        MV1 = [Buf() for _ in range(NT)]
        w_out_g = w_out[0:512, :].rearrange("(kc p) c -> p kc c", p=128)
        w_out_f = w_out[512:1024, :].rearrange("(j p) c -> p j c", p=128)
        fw.dma("gpsimd", wog_sem, wo_g, w_out_g[:, :, 0:512], w=[WOG])
        fw.dma("gpsimd", wof_sem, wo_f, w_out_f[:, :, 0:512], w=[WOF])
        bcsem = [newd(), newd()]

        def load_bc(rowg, rowb):
            fw.dma("sync", bcsem[0], gA, vec_d[rowg:rowg + 1, :].partition_broadcast(128), w=[GA])
            fw.dma("sync", bcsem[1], bA, vec_d[rowb:rowb + 1, :].partition_broadcast(128), w=[BA])
            V("scalar", lambda h: h.activation(out=gA, in_=gA, func=AF.Copy, scale=ALPHA), r=[GA], w=[GA])
            V("scalar", lambda h: h.activation(out=bA, in_=bA, func=AF.Copy, scale=ALPHA), r=[BA], w=[BA])

        load_bc(0, 1)
        nmr = ar.view("nmr", [NT], F32)
        NMR = Buf()
        V("vector", lambda h: h.scalar_tensor_tensor(out=nmr, in0=mv[:, :, 0], scalar=-1.0, in1=mv[:, :, 1], op0=ALU.mult, op1=ALU.mult), r=MV, w=[NMR])
        for m in range(NT):
            j = m % 2
            fw.dma("sync", xsem2[j], xts[j], x[m * 128:(m + 1) * 128, :], w=[XT[j]])
            V("scalar", lambda h: h.activation(out=xts[j], in_=xts[j], func=AF.Identity, scale=mv[:, m, 1:2], bias=nmr[:, m:m + 1]),
              r=[XT[j], MV[m], NMR], w=[XT[j]])
            V("vector", lambda h: h.tensor_tensor(out=Rr[:, m, :], in0=xts[j], in1=gA, op=ALU.mult), r=[XT[j], GA], w=RB[m])
            V("vector", lambda h: h.tensor_tensor(out=Rr[:, m, :], in0=Rr[:, m, :], in1=bA, op=ALU.add), r=RB[m] + [BA], w=RB[m])
        for half in range(2):
            if half == 1:
                fw.dma("gpsimd", wog_sem, wo_g, w_out_g[:, :, 512:1024], w=[WOG])
                fw.dma("gpsimd", wof_sem, wo_f, w_out_f[:, :, 512:1024], w=[WOF])
            for m in range(NT):
                tok = slice(m * 128, (m + 1) * 128)
                bk = nextbank()
                for kc in range(4):
                    V("tensor", lambda h: h.matmul(banks[bk][:, :], ogT[:, kc, tok], wo_g[:, kc, :], start=(kc == 0), stop=False),
                      r=[OGT[kc], WOG], w=[BK[bk]], inc=False)
                for j4 in range(4):
                    V("tensor", lambda h: h.matmul(banks[bk][:, :], ofox[:, j4, tok], wo_f[:, j4, :], start=False, stop=(j4 == 3)),
                      r=[OFOX[2 * j4], OFOX[2 * j4 + 1], WOF], w=[BK[bk]], inc=(j4 == 3))
                V("vector", lambda h: h.tensor_tensor(out=Rr[:, m, half * 512:(half + 1) * 512], in0=Rr[:, m, half * 512:(half + 1) * 512],
                                                      in1=banks[bk][:, :], op=ALU.add), r=[RB[m][half], BK[bk]], w=[RB[m][half]])
        load_bc(2, 3)

        def c_p1(m):
            j = m % 4
            V("vector", lambda h: h.bn_stats(out=st[:, j, 0:6], in_=Rr[:, m, 0:512]), r=[RB[m][0]], w=[STB[j]])
            V("vector", lambda h: h.bn_stats(out=st[:, j, 6:12], in_=Rr[:, m, 512:1024]), r=[RB[m][1]], w=[STB[j]])
            V("vector", lambda h: h.bn_aggr(out=mv1[:, m, 0:2], in_=st[:, j, 0:12]), r=[STB[j]], w=[MV1[m]])
            V("vector", lambda h: h.tensor_scalar_add(out=mv1[:, m, 1:2], in0=mv1[:, m, 1:2], scalar1=LN_EPS), r=[MV1[m]], w=[MV1[m]])
            V("gpsimd", lambda h: h.tensor_tensor(out=mv1[:, m, 1:2], in0=mv1[:, m, 1:2], in1=m05, op=ALU.pow), r=[MV1[m], CST], w=[MV1[m]])

        def c_p2(m):
            j = m % 4
            V("vector", lambda h: h.scalar_tensor_tensor(out=mv1[:, m, 0:1], in0=mv1[:, m, 0:1], scalar=-1.0, in1=mv1[:, m, 1:2],
                                                         op0=ALU.mult, op1=ALU.mult), r=[MV1[m]], w=[MV1[m]])
            V("scalar", lambda h: h.activation(out=xts[j], in_=Rr[:, m, :], func=AF.Identity, scale=mv1[:, m, 1:2], bias=mv1[:, m, 0:1]),
              r=RB[m] + [MV1[m]], w=[XT[j]])

        def c_p3(m):
            j = m % 4
            V("vector", lambda h: h.tensor_tensor(out=Rr[:, m, :], in0=xts[j], in1=gA, op=ALU.mult), r=[XT[j], GA], w=RB[m])
            V("vector", lambda h: h.tensor_tensor(out=Rr[:, m, :], in0=Rr[:, m, :], in1=bA, op=ALU.add), r=RB[m] + [BA], w=RB[m])

        def c_tr(g):
            for c in range(8):
                bk = nextbank()
                for j in range(4):
                    V("tensor", lambda h: h.transpose(banks[bk][:, j * 128:(j + 1) * 128], xts[j][:, c * 128:(c + 1) * 128], identf),
                      r=[XT[j], CST], w=[BK[bk]], inc=(j == 3))
                V("scalar", lambda h: h.activation(out=h1T[:, c, g * 512:(g + 1) * 512], in_=banks[bk][:, :], func=AF.Identity,
                                                   scale=par[:, P_G1 + c:P_G1 + c + 1], bias=par[:, P_B1 + c:P_B1 + c + 1]),
                  r=[BK[bk], PAR], w=[H1T[c][4 * g + jj] for jj in range(4)])

        for k in range(NT + 2):
            if k < NT:
                c_p1(k)
            if 0 <= k - 1 < NT:
                c_p2(k - 1)
                if (k - 1) % 4 == 3:
                    c_tr((k - 1) // 4)
            if 0 <= k - 2 < NT:
                c_p3(k - 2)
        dumpsrc["h1T"] = (h1T, [b for row in H1T for b in row], "C")
        dumpsrc["R"] = (Rr, [b for p_ in RB for b in p_], "C")
        do_dumps("C")
        if stop_after == "C":
            finish()
            return nc
        fw.barrier()
        ar.release("ofox", "ogT", "wo_g", "wo_f", "xt0", "xt1", "xt2", "xt3", "gA", "bA", "nmr")

        NG = DFF // 512
        wu = [ar.view("wu%d" % i, [8, 512], BF16) for i in range(2)]
        wd = [ar.view("wd%d" % i, [4, D], BF16) for i in range(2)]
        WU = [Buf(), Buf()]
        WD = [Buf(), Buf()]
        wusem = [newd(), newd()]
        wdsem = [newd(), newd()]
        w_up_v = w_up.rearrange("(kc p) f -> p kc f", p=128)
        w_down_v = w_down.rearrange("(fc p) c -> p fc c", p=128)
        def load_ffw(g):
            s_ = g % 2
            fw.dma("gpsimd", wusem[s_], wu[s_], w_up_v[:, :, g * 512:(g + 1) * 512], w=[WU[s_]])
            fw.dma("gpsimd", wdsem[s_], wd[s_], w_down_v[:, g * 4:(g + 1) * 4, :], w=[WD[s_]])

        load_ffw(0)
        pTt = ar.view("pTt", [2, T], BF16)
        PTB = [Buf() for _ in range(NT)]
        ptile = ar.view("ptile", [2, 256], F32)
        PTL = [Buf() for _ in range(2)]
        psem = [newd(), newd()]
        wple = ar.view("wple", [2, D], BF16)
        wgt = ar.view("wgt", [8, D], BF16)
        WPLE, WGT = Buf(), Buf()
        bgf = ar.view("bgf", [D], F32)
        bgh = ar.view("bgh", [D], BF16)
        bgl = ar.view("bgl", [D], BF16)
        ones_b = ar.view("ones_b", [128], BF16)
        BGF, BGH, BGL, ONB = Buf(), Buf(), Buf(), Buf()
        sg = ar.view("sg", [2, 512], F32)
        SG = [Buf(), Buf()]
        fw.dma("gpsimd", newd(), wple, w_ple.rearrange("(kc p) c -> p kc c", p=128), w=[WPLE])
        fw.dma("gpsimd", newd(), wgt, w_gate.rearrange("(kc p) c -> p kc c", p=128), w=[WGT])
        fw.dma("sync", newd(), bgf[0:1, :], vec_d[6:7, :], w=[BGF])
        V("vector", lambda h: h.tensor_copy(out=bgh[0:1, :], in_=bgf[0:1, :]), r=[BGF], w=[BGH])
        V("vector", lambda h: h.tensor_tensor(out=bgf[0:1, :], in0=bgf[0:1, :], in1=bgh[0:1, :], op=ALU.subtract), r=[BGF, BGH], w=[BGF])
        V("vector", lambda h: h.tensor_copy(out=bgl[0:1, :], in_=bgf[0:1, :]), r=[BGF], w=[BGL])
        V("vector", lambda h: h.memset(ones_b, 1.0), w=[ONB])
        for m in range(NT):
            j = m % 2
            tok = slice(m * 128, (m + 1) * 128)
            fw.dma("sync", psem[j], ptile[:, j, :], p_in[tok, :], w=[PTL[j]])
            bk = nextbank()
            for kc in range(2):
                V("tensor", lambda h: h.transpose(banks[bk][:, kc * 128:(kc + 1) * 128], ptile[:, j, kc * 128:(kc + 1) * 128], identf),
                  r=[PTL[j], CST], w=[BK[bk]], inc=(kc == 1))
            V("scalar", lambda h: h.copy(out=pTt[:, :, tok], in_=banks[bk][:, 0:256].rearrange("p (k t) -> p k t", k=2)), r=[BK[bk]], w=[PTB[m]])
            for half in range(2):
                cs = slice(half * 512, (half + 1) * 512)
                bp, bg_ = nextbank(), nextbank()
                for kc in range(2):
                    V("tensor", lambda h: h.matmul(banks[bp][:, :], pTt[:, kc, tok], wple[:, kc, cs], start=(kc == 0), stop=(kc == 1)),
                      r=[PTB[m], WPLE], w=[BK[bp]], inc=(kc == 1))
                for kc in range(8):
                    V("tensor", lambda h: h.matmul(banks[bg_][:, :], h1T[:, kc, tok], wgt[:, kc, cs], start=(kc == 0), stop=False),
                      r=[H1T[kc][m], WGT], w=[BK[bg_]], inc=False)
                V("tensor", lambda h: h.matmul(banks[bg_][:, :], ones_b[0:1, :], bgh[0:1, cs], start=False, stop=False), r=[ONB, BGH], w=[BK[bg_]], inc=False)
                V("tensor", lambda h: h.matmul(banks[bg_][:, :], ones_b[0:1, :], bgl[0:1, cs], start=False, stop=True), r=[ONB, BGL], w=[BK[bg_]])
                V("scalar", lambda h: h.activation(out=sg[:, half, :], in_=banks[bg_][:, :], func=AF.Sigmoid), r=[BK[bg_]], w=[SG[half]])
                V("vector", lambda h: h.tensor_tensor(out=sg[:, half, :], in0=sg[:, half, :], in1=banks[bp][:, :], op=ALU.mult), r=[SG[half], BK[bp]], w=[SG[half]])
                V("gpsimd", lambda h: h.tensor_tensor(out=Rr[:, m, cs], in0=Rr[:, m, cs], in1=sg[:, half, :], op=ALU.add), r=[RB[m][half], SG[half]], w=[RB[m][half]])
        if stop_after == "D1":
            dumpsrc["R1"] = (Rr, [b for p_ in RB for b in p_], "D1")
            do_dumps("D1")
            finish()
            return nc
        fw.barrier()
        ar.release("pTt", "ptile", "wple", "wgt", "bgf", "bgh", "bgl", "ones_b", "sg")

        actT = [ar.view("actT%d" % i, [4, T], BF16) for i in range(2)]
        ACTT = [[[Buf() for _ in range(NBLK)] for _ in range(4)] for _ in range(2)]
        rl = [ar.view("rl%d" % i, [512], F32) for i in range(2)]
        RL = [Buf(), Buf()]

        gA = ar.view("gA", [D], F32)
        bA = ar.view("bA", [D], F32)
        GA, BA = Buf(), Buf()
        fw.dma("sync", bcsem[0], gA, vec_d[4:5, :].partition_broadcast(128), w=[GA])
        fw.dma("sync", bcsem[1], bA, vec_d[5:6, :].partition_broadcast(128), w=[BA])
        yo = ar.view("yo", [2, D], F32)
        YO = [Buf(), Buf()]
        osem = [newd(), newd()]

        def emit_E1(m):
            j = m % 4
            V("vector", lambda h: h.bn_stats(out=st[:, j, 0:6], in_=Rr[:, m, 0:512]), r=[RB[m][0]], w=[STB[j]])
            V("vector", lambda h: h.bn_stats(out=st[:, j, 6:12], in_=Rr[:, m, 512:1024]), r=[RB[m][1]], w=[STB[j]])
            V("vector", lambda h: h.bn_aggr(out=mv1[:, m, 0:2], in_=st[:, j, 0:12]), r=[STB[j]], w=[MV1[m]])
            V("vector", lambda h: h.tensor_scalar_add(out=mv1[:, m, 1:2], in0=mv1[:, m, 1:2], scalar1=LN_EPS), r=[MV1[m]], w=[MV1[m]])
            V("gpsimd", lambda h: h.tensor_tensor(out=mv1[:, m, 1:2], in0=mv1[:, m, 1:2], in1=m05, op=ALU.pow), r=[MV1[m], CST], w=[MV1[m]])

        def emit_E2(m):
            j = m % 2
            V("vector", lambda h: h.scalar_tensor_tensor(out=mv1[:, m, 0:1], in0=mv1[:, m, 0:1], scalar=-1.0, in1=mv1[:, m, 1:2],
                                                         op0=ALU.mult, op1=ALU.mult), r=[MV1[m]], w=[MV1[m]])
            V("scalar", lambda h: h.activation(out=yo[:, j, :], in_=Rr[:, m, :], func=AF.Identity, scale=mv1[:, m, 1:2], bias=mv1[:, m, 0:1]),
              r=RB[m] + [MV1[m]], w=[YO[j]])

        def emit_E3(m):
            j = m % 2
            V("vector", lambda h: h.tensor_tensor(out=yo[:, j, :], in0=yo[:, j, :], in1=gA, op=ALU.mult), r=[YO[j], GA], w=[YO[j]])
            V("vector", lambda h: h.tensor_tensor(out=yo[:, j, :], in0=yo[:, j, :], in1=bA, op=ALU.add), r=[YO[j], BA], w=[YO[j]])
            final_tickets.append(fw.dma("sync", osem[j], out[m * 128:(m + 1) * 128, :], yo[:, j, :], r=[YO[j]]))

        def emit_E_step(k):
            if 0 <= k < NT:
                emit_E1(k)
            if 0 <= k - 1 < NT:
                emit_E2(k - 1)
            if 0 <= k - 2 < NT:
                emit_E3(k - 2)

        rcnt = 0
        for g in range(NG):
            s_ = g % 2
            if g + 1 < NG:
                load_ffw(g + 1)
            for blk in range(NBLK):
                for fc in range(4):
                    bk = nextbank()
                    for kc in range(8):
                        V("tensor", lambda h: h.matmul(banks[bk][:, :], wu[s_][:, kc, fc * 128:(fc + 1) * 128], h1T[:, kc, blk * 512:(blk + 1) * 512],
                                                       start=(kc == 0), stop=(kc == 7)),
                          r=[WU[s_]] + [H1T[kc][mm] for mm in range(blk * 4, blk * 4 + 4)], w=[BK[bk]], inc=(kc == 7))
                    q = rcnt % 2
                    rcnt += 1
                    V("scalar", lambda h: h.activation(out=rl[q], in_=banks[bk][:, :], func=AF.Relu), r=[BK[bk]], w=[RL[q]])
                    V("gpsimd", lambda h: h.tensor_tensor(out=actT[s_][:, fc, blk * 512:(blk + 1) * 512], in0=rl[q], in1=rl[q], op=ALU.mult),
                      r=[RL[q]], w=[ACTT[s_][fc][blk]])
            for m in range(NT):
                tok = slice(m * 128, (m + 1) * 128)
                for half in range(2):
                    cs = slice(half * 512, (half + 1) * 512)
                    bk = nextbank()
                    for fc in range(4):
                        V("tensor", lambda h: h.matmul(banks[bk][:, :], actT[s_][:, fc, tok], wd[s_][:, fc, cs], start=(fc == 0), stop=(fc == 3)),
                          r=[ACTT[s_][fc][m // 4], WD[s_]], w=[BK[bk]], inc=(fc == 3))
                    V("vector", lambda h: h.tensor_tensor(out=Rr[:, m, cs], in0=Rr[:, m, cs], in1=banks[bk][:, :], op=ALU.add), r=[RB[m][half], BK[bk]], w=[RB[m][half]])
                if g == NG - 1:
                    emit_E_step(m - 1)
        for k in range(NT - 1, NT + 2):
            emit_E_step(k)

        finish()
    return nc


_CACHE = {}


def _host_consts():
    c = np.zeros((128, NCST), np.float32)
    i = np.arange(128)
    c[:, C_ID:C_ID + 128] = np.eye(128, dtype=np.float32)
    c[:, C_ONE:C_ONE + 128] = 1.0
    same = (i[:, None] // 64) == (i[None, :] // 64)
    c[:, C_TRI:C_TRI + 128] = (same & (i[:, None] <= i[None, :])).astype(np.float32)
    c[:, C_BLK:C_BLK + 128] = same.astype(np.float32)
    c[:, C_NEGM:C_NEGM + 128] = np.where(same & (i[:, None] >= i[None, :]), 0.0, NEG)
    c[:, C_STR:C_STR + 128] = (same & (i[:, None] > i[None, :])).astype(np.float32)
    c[:, C_IND] = (i < 64)
    c[:, C_IND + 1] = (i >= 64)
    c[0:64, C_OAUG:C_OAUG + 128] = 1.0
    c[64, C_OAUG:C_OAUG + 128] = 64.0 * NORM_EPS
    c[:, C_M05] = -0.5
    c[:, C_M05 + 1] = NORM_EPS
    return c


def _prep_shared(inp):
    par = np.zeros((128, NPAR), np.float32)
    par[:, P_G0:P_G0 + 8] = inp["ln_in_g"].reshape(8, 128).T
    par[:, P_B0:P_B0 + 8] = inp["ln_in_b"].reshape(8, 128).T
    cw = inp["conv_w"][0]
    par[:, P_CW:P_CW + 48] = cw.T.reshape(12, 128, 4).transpose(1, 0, 2).reshape(128, 48)
    par[0:64, P_FOXG] = inp["fox_norm_g"][0]
    par[0:8, P_BF] = inp["b_f"][0]
    par[:, P_DTB:P_DTB + 64] = np.tile(inp["dt_bias"][0], 16)[None, :]
    par[:, P_ALOG:P_ALOG + 64] = np.tile(inp["a_log"][0], 16)[None, :]
    par[:, P_G1:P_G1 + 8] = inp["ln1_g"][0].reshape(8, 128).T
    par[:, P_B1:P_B1 + 8] = inp["ln1_b"][0].reshape(8, 128).T
    vecs = np.zeros((8, D), np.float32)
    vecs[0] = inp["ln_in_g"]
    vecs[1] = inp["ln_in_b"]
    vecs[2] = inp["ln1_g"][0]
    vecs[3] = inp["ln1_b"][0]
    vecs[4] = inp["ln2_g"][0]
    vecs[5] = inp["ln2_b"][0]
    vecs[6] = inp["b_ple_gate"][0]
    vecs[7, 0:128] = inp["gdn_norm_g"][0]
    return {
        "w_in": np.ascontiguousarray(inp["w_in"][0]), "w_out": np.ascontiguousarray(inp["w_out"][0]),
        "w_up": np.ascontiguousarray(inp["w_up"][0]), "w_down": np.ascontiguousarray(inp["w_down"][0]),
        "w_ple": np.ascontiguousarray(inp["w_ple"][0]), "w_gate": np.ascontiguousarray(inp["w_ple_gate"][0]),
        "cst": _host_consts(), "par": par, "vecs": vecs, "ones_rows": np.ones((3, 8 * T), np.float32),
    }


def run(inp, stop_after=None, dumps=(), cores=8):
    key = (stop_after, tuple(dumps))
    if key not in _CACHE:
        _CACHE[key] = build(stop_after, dumps)
    nc = _CACHE[key]
    inp = {k: np.asarray(v, dtype=np.float32) for k, v in inp.items()}
    shared = _prep_shared(inp)
    in_maps = []
    for b in range(cores):
        m = dict(shared)
        m["x"] = np.ascontiguousarray(inp["x"][b])
        m["p"] = np.ascontiguousarray(inp["p"][0, b])
        in_maps.append(m)
    res = run_bass_kernel_spmd(nc, in_maps, core_ids=list(range(cores)))
    return res.results


def kernel(**inputs):
    results = run(inputs)
    return np.stack([r["out"] for r in results], axis=0).astype(np.float32)
```

```python
import numpy as np
from contextlib import ExitStack
import concourse.bass as bass
import concourse.mybir as mybir
from concourse.bass_utils import run_bass_kernel_spmd

F32 = mybir.dt.float32
BF16 = mybir.dt.bfloat16
F32R = mybir.dt.float32r
AF = mybir.ActivationFunctionType
ALU = mybir.AluOpType

T, D, NT, NBLK = 2048, 1024, 16, 4
DFF = 4096
ALPHA = 2.0 ** 0.25
LN_EPS = 1e-5
NORM_EPS = 1e-6
NEG = -30000.0
DSZ = {F32: 4, BF16: 2, F32R: 4}

C_ID, C_ONE, C_TRI, C_BLK, C_NEGM, C_STR, C_IND, C_OAUG, C_M05, NCST = 0, 128, 256, 384, 512, 640, 768, 770, 898, 900
P_G0, P_B0, P_CW, P_FOXG, P_BF, P_DTB, P_ALOG, P_G1, P_B1, NPAR = 0, 8, 16, 64, 65, 66, 130, 194, 202, 210


class Buf:
    __slots__ = ("w", "r", "lock")

    def __init__(self, lock=None):
        self.w = None
        self.r = {}
        self.lock = lock


class Ticket:
    __slots__ = ("sem", "val", "key", "eng")

    def __init__(self, sem, val, key, eng):
        self.sem, self.val, self.key, self.eng = sem, val, key, eng


class Eng:
    def __init__(self, name, h, sem):
        self.name, self.h, self.sem = name, h, sem
        self.count = 0
        self.waited = {}


class FW:
    def __init__(self, nc, es):
        self.nc, self.es = nc, es
        self.E = {}
        for n in ("tensor", "vector", "scalar", "gpsimd", "sync"):
            sem = es.enter_context(nc.semaphore("s_" + n))
            self.E[n] = Eng(n, getattr(nc, n), sem)
        self.dsems = []

    def _wait(self, e, t):
        if e.waited.get(t.key, 0) >= t.val:
            return
        e.h.wait_ge(t.sem, t.val)
        e.waited[t.key] = t.val

    def _deps(self, e, reads, writes):
        for b in reads:
            t = b.w
            if t is not None and not (t.eng == e.name and e.name == "tensor"):
                self._wait(e, t)
        for b in writes:
            t = b.w
            if t is not None and (t.eng != e.name or (e.name != "tensor" and t.val <= e.count)):
                self._wait(e, t)
            for t in b.r.values():
                if t.eng != e.name:
                    self._wait(e, t)

    @staticmethod
    def _mark(t, reads, writes):
        for b in reads:
            b.r[t.key] = t
        for b in writes:
            b.w = t
            b.r = {}

    def op(self, eng, fn, r=(), w=(), inc=True):
        e = self.E[eng]
        locks = []
        for b in list(r) + list(w):
            if b.lock is not None and b.lock not in locks:
                locks.append(b.lock)
        for lk in locks:
            t = lk.w
            if t is not None and t.eng != e.name:
                self._wait(e, t)
        self._deps(e, r, w)
        ins = fn(e.h)
        if inc:
            e.count += 1
            ins.then_inc(e.sem, 1)
            t = Ticket(e.sem, e.count, "e_" + eng, eng)
        else:
            t = Ticket(e.sem, e.count + 1, "e_" + eng, eng)
        self._mark(t, r, w)
        for lk in locks:
            lk.w = t
        return t

    def dsem(self, name):
        sem = self.es.enter_context(self.nc.semaphore(name))
        d = [sem, 0, name]
        self.dsems.append(d)
        return d

    def dma(self, q, d, out, in_, r=(), w=()):
        e = self.E[q]
        self._deps(e, r, w)
        ins = e.h.dma_start(out=out, in_=in_)
        d[1] += 16
        ins.then_inc(d[0], 16)
        t = Ticket(d[0], d[1], "d_" + d[2], "dma")
        self._mark(t, r, w)
        return t

    def barrier(self):
        tl = [Ticket(e.sem, e.count, "e_" + n, n) for n, e in self.E.items() if e.count > 0]
        tl += [Ticket(d[0], d[1], "d_" + d[2], "dma") for d in self.dsems if d[1] > 0]
        for e in self.E.values():
            for t in tl:
                if t.eng != e.name:
                    self._wait(e, t)


class Arena:
    def __init__(self, ap, nbytes):
        self.ap = ap
        self.free = [(0, nbytes)]
        self.live = {}

    def alloc(self, name, nbytes):
        nbytes = (nbytes + 63) // 64 * 64
        for i, (o, s) in enumerate(self.free):
            if s >= nbytes:
                self.free[i] = (o + nbytes, s - nbytes)
                self.live[name] = (o, nbytes)
                return o
        raise RuntimeError("arena full for %s (%d) free=%s" % (name, nbytes, self.free))

    def release(self, *names):
        for name in names:
            o, s = self.live.pop(name)
            self.free.append((o, s))
        self.free.sort()
        m = []
        for o, s in self.free:
            if s == 0:
                continue
            if m and m[-1][0] + m[-1][1] == o:
                m[-1] = (m[-1][0], m[-1][1] + s)
            else:
                m.append((o, s))
        self.free = m

    def view(self, name, shape, dt):
        n = 1
        for s in shape:
            n *= s
        nb = n * DSZ[dt]
        o = self.alloc(name, nb)
        v = self.ap[:, o // 4:(o + nb) // 4]
        if dt != F32:
            v = v.bitcast(dt)
        if len(shape) == 2:
            v = v.rearrange("p (a b) -> p a b", a=shape[0])
        elif len(shape) == 3:
            v = v.rearrange("p (a b c) -> p a b c", a=shape[0], b=shape[1])
        elif len(shape) == 4:
            v = v.rearrange("p (a b c d) -> p a b c d", a=shape[0], b=shape[1], c=shape[2])
        return v


def build(stop_after=None, dumps=()):
    nc = bass.Bass("TRN2", target_bir_lowering=False)

    def din(name, shape):
        return nc.dram_tensor(name, list(shape), F32, kind="ExternalInput").ap()

    x = din("x", [T, D])
    p_in = din("p", [T, 256])
    w_in = din("w_in", [D, 3600])
    w_out = din("w_out", [D, D])
    w_up = din("w_up", [D, DFF])
    w_down = din("w_down", [DFF, D])
    w_ple = din("w_ple", [256, D])
    w_gate = din("w_gate", [D, D])
    cst_d = din("cst", [128, NCST])
    par_d = din("par", [128, NPAR])
    vec_d = din("vecs", [8, D])
    ones_d = din("ones_rows", [3, 8 * T])
    out = nc.dram_tensor("out", [T, D], F32, kind="ExternalOutput").ap()
    dump_out = {}
    for (nm, shp, dt) in dumps:
        dump_out[nm] = nc.dram_tensor("dbg_" + nm, list(shp), dt, kind="ExternalOutput").ap()

    w_in_v = w_in.rearrange("(kc p) c -> p kc c", p=128)

    with ExitStack() as es:
        fw = FW(nc, es)
        ARENA_BYTES = 204 * 1024
        arena_t = es.enter_context(nc.sbuf_tensor("arena", [128, ARENA_BYTES // 4], F32))
        ar = Arena(arena_t[:, :], ARENA_BYTES)
        banks = [es.enter_context(nc.psum_tensor("bank%d" % i, [128, 512], F32)) for i in range(8)]
        LK = [Buf() for _ in range(8)]
        BK = [Buf(LK[i]) for i in range(8)]
        rot = [0]

        def nextbank(lo=0, hi=8):
            b = lo + rot[0] % (hi - lo)
            rot[0] += 1
            return b

        def V(eng, fn, r=(), w=(), inc=True):
            return fw.op(eng, fn, r, w, inc)

        dctr = [0]

        def newd():
            dctr[0] += 1
            return fw.dsem("d%d" % dctr[0])

        final_tickets = []

        def do_dumps(stage):
            for (nm, shp, dt) in dumps:
                if nm in dumpsrc and dumpsrc[nm][2] == stage:
                    src, bufs, _ = dumpsrc[nm]
                    final_tickets.append(fw.dma("sync", newd(), dump_out[nm], src, r=bufs))

        dumpsrc = {}

        def finish():
            for t in final_tickets:
                fw._wait(fw.E["sync"], t)

        cst = ar.view("cst", [NCST], F32)
        par = ar.view("par", [NPAR], F32)
        CST, PAR = Buf(), Buf()
        fw.dma("sync", newd(), cst, cst_d, w=[CST])
        fw.dma("sync", newd(), par, par_d, w=[PAR])
        identf = cst[:, C_ID:C_ID + 128]
        onesf = cst[:, C_ONE:C_ONE + 128]
        m05 = cst[:, C_M05:C_M05 + 1]
        eps_col = cst[:, C_M05 + 1:C_M05 + 2]
        EPSB = CST
        identb = ar.view("identb", [128], BF16)
        IDB = Buf()
        V("vector", lambda h: h.tensor_copy(out=identb, in_=identf), r=[CST], w=[IDB])
        mv = ar.view("mv", [NT, 2], F32)
        MV = [Buf() for _ in range(NT)]

        OFF_FOX = 2056
        wsl = [ar.view("wsl%d" % i, [8, 512], BF16) for i in range(3)]
        WSL = [Buf() for _ in range(3)]
        wsem = [newd() for _ in range(3)]
        wsmall = ar.view("wsmall", [8, 8], BF16)
        WSM = Buf()
        wsm_sem = newd()
        fw.dma("gpsimd", wsem[0], wsl[0], w_in_v[:, :, OFF_FOX:OFF_FOX + 512], w=[WSL[0]])
        fw.dma("gpsimd", wsem[1], wsl[1], w_in_v[:, :, OFF_FOX + 512:OFF_FOX + 1024], w=[WSL[1]])
        fw.dma("gpsimd", wsem[2], wsl[2], w_in_v[:, :, OFF_FOX + 1024:OFF_FOX + 1536], w=[WSL[2]])
        fw.dma("gpsimd", wsm_sem, wsmall, w_in_v[:, :, 3592:3600], w=[WSM])
        hT = ar.view("hT", [8, T], BF16)
        HT = [[Buf() for _ in range(NBLK)] for _ in range(8)]
        xt = ar.view("xt", [8, D], F32)
        XT = [Buf() for _ in range(8)]
        xsem = [newd() for _ in range(8)]
        st = ar.view("st", [8, 12], F32)
        STB = [Buf() for _ in range(8)]

        def ln_stats(src, SRC, j, mvap, MVB):
            V("vector", lambda h: h.bn_stats(out=st[:, j, 0:6], in_=src[:, 0:512]), r=[SRC], w=[STB[j]])
            V("vector", lambda h: h.bn_stats(out=st[:, j, 6:12], in_=src[:, 512:1024]), r=[SRC], w=[STB[j]])
            V("vector", lambda h: h.bn_aggr(out=mvap[:, 0:2], in_=st[:, j, 0:12]), r=[STB[j]], w=[MVB])
            V("vector", lambda h: h.tensor_scalar_add(out=mvap[:, 1:2], in0=mvap[:, 1:2], scalar1=LN_EPS), r=[MVB], w=[MVB])
            V("gpsimd", lambda h: h.tensor_tensor(out=mvap[:, 1:2], in0=mvap[:, 1:2], in1=m05, op=ALU.pow), r=[MVB, CST], w=[MVB])

        for g in range(NBLK):
            for j4 in range(4):
                m = 4 * g + j4
                j = m % 8
                fw.dma("sync", xsem[j], xt[:, j, :], x[m * 128:(m + 1) * 128, :], w=[XT[j]])
                ln_stats(xt[:, j, :], XT[j], j, mv[:, m, :], MV[m])
                V("vector", lambda h: h.tensor_scalar(out=xt[:, j, :], in0=xt[:, j, :], scalar1=mv[:, m, 0:1], scalar2=mv[:, m, 1:2],
                                                      op0=ALU.subtract, op1=ALU.mult), r=[XT[j], MV[m]], w=[XT[j]])
            for c in range(8):
                bk = nextbank()
                for j4 in range(4):
                    j = (4 * g + j4) % 8
                    V("tensor", lambda h: h.transpose(banks[bk][:, j4 * 128:(j4 + 1) * 128], xt[:, j, c * 128:(c + 1) * 128], identf),
                      r=[XT[j], CST], w=[BK[bk]], inc=(j4 == 3))
                V("scalar", lambda h: h.activation(out=hT[:, c, g * 512:(g + 1) * 512], in_=banks[bk][:, :], func=AF.Identity,
                                                   scale=par[:, P_G0 + c:P_G0 + c + 1], bias=par[:, P_B0 + c:P_B0 + c + 1]),
                  r=[BK[bk], PAR], w=[HT[c][g]])
        dumpsrc["hT"] = (hT, [b for row in HT for b in row], "A1")
        do_dumps("A1")
        if stop_after == "A1":
            finish()
            return nc

        fw.barrier()
        ar.release("xt", "st")
        fq = ar.view("fq", [8, T], BF16)
        fk = ar.view("fk", [8, T], BF16)
        FQ = [Buf() for _ in range(8)]
        FK = [Buf() for _ in range(8)]
        vaug = ar.view("vaug", [NT, 8, 65], BF16)
        VAUG = Buf()
        for h8 in range(8):
            pass
        V("gpsimd", lambda h: h.memset(fq[64:128, :, :], 0.0), w=FQ)
        V("vector", lambda h: h.memset(fk[64:128, :, :], 0.0), w=FK)
        fw.dma("gpsimd", newd(), fq[67:70, :, :], ones_d.rearrange("r (h t) -> r h t", h=8), w=FQ)
        fw.dma("gpsimd", newd(), fk[64:67, :, :], ones_d.rearrange("r (h t) -> r h t", h=8), w=FK)
        V("gpsimd", lambda h: h.memset(vaug[:, :, :, 64:65], 1.0), w=[VAUG])

        rowA = ar.view("rowA", [T], F32)
        rowC = ar.view("rowC", [T], F32)
        rparts = ar.view("rparts", [6, T], BF16)
        nbf = ar.view("nbf", [1], F32)
        RA, RC, RP, NBF = Buf(), Buf(), Buf(), Buf()
        V("vector", lambda h: h.tensor_scalar_mul(out=nbf[0:8, :], in0=par[0:8, P_BF:P_BF + 1], scalar1=-1.0), r=[PAR], w=[NBF])
        for blk in range(NBLK):
            bk = nextbank()
            for kc in range(8):
                V("tensor", lambda h: h.matmul(banks[bk][0:8, :], wsmall[:, kc, :], hT[:, kc, blk * 512:(blk + 1) * 512],
                                               start=(kc == 0), stop=(kc == 7)), r=[WSM, HT[kc][blk]], w=[BK[bk]], inc=(kc == 7))
            V("scalar", lambda h: h.activation(out=rowA[0:8, blk * 512:(blk + 1) * 512], in_=banks[bk][0:8, :], func=AF.Exp,
                                               scale=-1.0, bias=nbf[0:8, :]), r=[BK[bk], NBF], w=[RA])
        V("scalar", lambda h: h.activation(out=rowA[0:8, :], in_=rowA[0:8, :], func=AF.Ln, bias=1.0), r=[RA], w=[RA])
        V("vector", lambda h: h.tensor_tensor_scan(out=rowC[0:8, :], data0=cst[0:8, C_ONE:C_ONE + 1].to_broadcast([8, T]),
                                                   data1=rowA[0:8, :], initial=0.0, op0=ALU.mult, op1=ALU.subtract),
          r=[RA, CST], w=[RC])
        dumpsrc["crow"] = (rowC[0:8, :], [RC], "AF")
        V("vector", lambda h: h.tensor_copy(out=rparts[0:8, 0, :], in_=rowC[0:8, :]), r=[RC], w=[RP])
        V("vector", lambda h: h.tensor_tensor(out=rowA[0:8, :], in0=rowC[0:8, :], in1=rparts[0:8, 0, :], op=ALU.subtract), r=[RC, RP], w=[RA])
        V("vector", lambda h: h.tensor_copy(out=rparts[0:8, 1, :], in_=rowA[0:8, :]), r=[RA], w=[RP])
        V("vector", lambda h: h.tensor_tensor(out=rowA[0:8, :], in0=rowA[0:8, :], in1=rparts[0:8, 1, :], op=ALU.subtract), r=[RA, RP], w=[RA])
        V("vector", lambda h: h.tensor_copy(out=rparts[0:8, 2, :], in_=rowA[0:8, :]), r=[RA], w=[RP])
        V("vector", lambda h: h.tensor_scalar_mul(out=rparts[0:8, 3:6, :], in0=rparts[0:8, 0:3, :], scalar1=-1.0), r=[RP], w=[RP])
        for h8 in range(8):
            dq = newd()
            fw.dma("sync", dq, fq[64:67, h8, :], rparts[h8:h8 + 1, 0:3, :], r=[RP], w=[FQ[h8]])
            dk_ = newd()
            fw.dma("sync", dk_, fk[67:70, h8, :], rparts[h8:h8 + 1, 3:6, :], r=[RP], w=[FK[h8]])

        stg = ar.view("stg", [4, 512], BF16)
        STG = [Buf() for _ in range(4)]
        stgsem = [newd() for _ in range(4)]
        cnt = 0
        for (dst, DST, ws, WS, sc) in ((fq, FQ, wsl[0], WSL[0], 0.125), (fk, FK, wsl[1], WSL[1], 1.0)):
            for c4 in range(4):
                for blk in range(NBLK):
                    bk = nextbank()
                    for kc in range(8):
                        V("tensor", lambda h: h.matmul(banks[bk][:, :], ws[:, kc, c4 * 128:(c4 + 1) * 128], hT[:, kc, blk * 512:(blk + 1) * 512],
                                                       start=(kc == 0), stop=(kc == 7)), r=[WS, HT[kc][blk]], w=[BK[bk]], inc=(kc == 7))
                    he, ho = 2 * c4, 2 * c4 + 1
                    sl = cnt % 4
                    cnt += 1
                    cs = slice(blk * 512, (blk + 1) * 512)
                    V("scalar", lambda h: h.activation(out=dst[0:64, he, cs], in_=banks[bk][0:64, :], func=AF.Copy, scale=sc), r=[BK[bk]], w=[DST[he]])
                    V("vector", lambda h: h.tensor_scalar_mul(out=stg[64:128, sl, :], in0=banks[bk][64:128, :], scalar1=sc), r=[BK[bk]], w=[STG[sl]])
                    fw.dma("sync", stgsem[sl], dst[0:64, ho, cs], stg[64:128, sl, :], r=[STG[sl]], w=[DST[ho]])
        for m in range(NT):
            bk = nextbank()
            for kc in range(8):
                V("tensor", lambda h: h.matmul(banks[bk][:, :], hT[:, kc, m * 128:(m + 1) * 128], wsl[2][:, kc, :],
                                               start=(kc == 0), stop=(kc == 7)), r=[WSL[2], HT[kc][m // 4]], w=[BK[bk]], inc=(kc == 7))
            src = banks[bk][:, :].rearrange("p (h d) -> p h d", h=8)
            if m % 2 == 0:
                V("scalar", lambda h: h.copy(out=vaug[:, m, :, 0:64], in_=src), r=[BK[bk]], w=[VAUG])
            else:
                V("vector", lambda h: h.tensor_copy(out=vaug[:, m, :, 0:64], in_=src), r=[BK[bk]], w=[VAUG])
        dumpsrc["fq"] = (fq, FQ, "AF")
        dumpsrc["fk"] = (fk, FK, "AF")
        dumpsrc["vaug"] = (vaug, [VAUG], "AF")
        do_dumps("AF")
        if stop_after == "AF":
            finish()
            return nc

        fw.barrier()
        ar.release("rowA", "rowC", "rparts", "wsmall", "nbf", "stg")
        wg = ar.view("wg", [8, 8], BF16)
        WG = Buf()
        for i in range(3):
            fw.dma("gpsimd", wsem[i], wsl[i], w_in_v[:, :, i * 512:(i + 1) * 512], w=[WSL[i]])
        fw.dma("gpsimd", wsm_sem, wg, w_in_v[:, :, 2048:2056], w=[WG])
        ofox = ar.view("ofox", [4, T], BF16)
        ostg = ar.view("ostg", [2, 512], BF16)
        OSTG = [Buf(), Buf()]
        ostg_sem = [newd(), newd()]
        ostg_ctr = [0]
        OFOX = [Buf() for _ in range(8)]
        pT = [ar.view("pT%d" % i, [512], BF16) for i in range(4)]
        PT = [Buf() for _ in range(4)]
        sq = [ar.view("sq%d" % i, [512], F32) for i in range(2)]
        rr = [ar.view("rr%d" % i, [512], F32) for i in range(2)]
        oaug = cst[:, C_OAUG:C_OAUG + 128]
        SQ, RR, OAUG = [Buf(), Buf()], [Buf(), Buf()], CST
        def pe_warm(n, bk_, lhs, rhs, RB_):
            for _ in range(n):
                V("tensor", lambda h: h.matmul(banks[bk_][:, :], lhs, rhs, start=True, stop=True), r=RB_, w=[BK[bk_]], inc=False)

        pe_warm(24, 5, fk[0:64, 0, 0:128], fq[0:64, 0, 0:512], [FK[0], FQ[0]])
        NFF, NFW = 0, 256
        pslot = [0]
        gp = [0]
        pend = []
        fsl = [0]

        def fin_step(it):
            stp, h8_, qb_, po_ = it[1], it[2], it[3], it[4]
            if stp == 0:
                k_ = fsl[0] % 2
                fsl[0] += 1
                it.append(k_)
                V("scalar", lambda h: h.activation(out=sq[k_][0:65, :], in_=banks[po_][0:65, :], func=AF.Square), r=[BK[po_]], w=[SQ[k_]])
                V("tensor", lambda h: h.matmul(banks[5][:, :], oaug[0:65, :], sq[k_][0:65, :], start=True, stop=True),
                  r=[OAUG, SQ[k_]], w=[BK[5]])
                it[0] = gp[0] + 2
                it[1] = 1
            else:
                k_ = it[5]
                V("scalar", lambda h: h.activation(out=rr[k_][0:64, :], in_=banks[5][0:64, :], func=AF.Ln, scale=1.0 / 64.0), r=[BK[5]], w=[RR[k_]])
                V("scalar", lambda h: h.activation(out=rr[k_][0:64, :], in_=rr[k_][0:64, :], func=AF.Exp, scale=-0.5), r=[RR[k_]], w=[RR[k_]])
                cs_ = slice(qb_ * 512, (qb_ + 1) * 512)
                if h8_ % 2 == 0:
                    V("vector", lambda h: h.scalar_tensor_tensor(out=ofox[0:64, h8_ // 2, cs_], in0=banks[po_][0:64, :],
                                                                 scalar=par[0:64, P_FOXG:P_FOXG + 1], in1=rr[k_][0:64, :],
                                                                 op0=ALU.mult, op1=ALU.mult), r=[BK[po_], RR[k_], PAR], w=[OFOX[h8_]])
                else:
                    sl_ = ostg_ctr[0] % 2
                    ostg_ctr[0] += 1
                    V("vector", lambda h: h.scalar_tensor_tensor(out=ostg[0:64, sl_, :], in0=banks[po_][0:64, :],
                                                                 scalar=par[0:64, P_FOXG:P_FOXG + 1], in1=rr[k_][0:64, :],
                                                                 op0=ALU.mult, op1=ALU.mult), r=[BK[po_], RR[k_], PAR], w=[OSTG[sl_]])
                    fw.dma("sync", ostg_sem[sl_], ofox[64:128, h8_ // 2, cs_], ostg[0:64, sl_, :], r=[OSTG[sl_]], w=[OFOX[h8_]])
                pend.remove(it)

        for h8 in range(8):
            pairs = []
            for qb in range(NBLK):
                for kb in range(4 * (qb + 1)):
                    pairs.append((qb, kb))
            sbank = {}

            def emit_qk(i):
                qb, kb = pairs[i]
                j = kb - 4 * qb
                qlo = max(j, 0) * 128
                bk = nextbank(0, 4)
                sbank[i] = bk
                V("tensor", lambda h: h.matmul(banks[bk][:, qlo:512], fk[:, h8, kb * 128:(kb + 1) * 128],
                                               fq[:, h8, qb * 512 + qlo:(qb + 1) * 512], start=True, stop=True),
                  r=[FK[h8], FQ[h8]], w=[BK[bk]])

            LA = 3
            for i0 in range(min(LA, len(pairs))):
                emit_qk(i0)
            for i, (qb, kb) in enumerate(pairs):
                if i + LA < len(pairs):
                    emit_qk(i + LA)
                for _ in range(NFF):
                    V("tensor", lambda h: h.matmul(banks[4][:, 0:NFW], identb, fq[:, 0, 0:NFW], start=True, stop=True), r=[IDB], w=[BK[4]], inc=False)
                j = kb - 4 * qb
                qlo = max(j, 0) * 128
                bk = sbank.pop(i)
                s = pslot[0] % 4
                pslot[0] += 1
                po = 6 + (h8 * NBLK + qb) % 2
                V("scalar", lambda h: h.activation(out=pT[s][:, qlo:512], in_=banks[bk][:, qlo:512], func=AF.Exp), r=[BK[bk]], w=[PT[s]])
                if j >= 0:
                    V("gpsimd", lambda h: h.affine_select(out=pT[s][:, qlo:qlo + 128], in_=pT[s][:, qlo:qlo + 128], pattern=[[1, 128]],
                                                          compare_op=ALU.is_ge, fill=0.0, base=0, channel_multiplier=-1),
                      r=[PT[s]], w=[PT[s]])
                last = (kb == 4 * (qb + 1) - 1)
                V("tensor", lambda h: h.matmul(banks[po][0:65, qlo:512], vaug[:, kb, h8, :], pT[s][:, qlo:512],
                                               start=(kb == 0), stop=last), r=[VAUG, PT[s]], w=[BK[po]], inc=last)
                if last:
                    pend.append([gp[0] + 2, 0, h8, qb, po])
                gp[0] += 1
                for it in list(pend):
                    if it[0] <= gp[0]:
                        fin_step(it)
        while pend:
            for it in list(pend):
                fin_step(it)
        dumpsrc["ofox"] = (ofox, OFOX, "F")
        do_dumps("F")
        if stop_after == "F":
            finish()
            return nc
        fw.barrier()
        ar.release("fq", "fk", "vaug", "pT0", "pT1", "pT2", "pT3", "sq0", "sq1", "rr0", "rr1", "ostg")

        qkvT = ar.view("qkvT", [12, T], BF16)
        QKV = [Buf() for _ in range(12)]
        sz = ar.view("sz", [4, T], BF16)
        SZ = [Buf() for _ in range(4)]
        pcb = ar.view("pcb", [4, 516], F32)
        PC = [Buf() for _ in range(4)]
        accb = ar.view("accb", [4, 512], F32)
        ACC = [Buf() for _ in range(4)]
        gpre = ar.view("gpre", [NT, 8], F32)
        GPRE = Buf()
        beta = ar.view("beta", [NT, 4], F32)
        lg = ar.view("lg", [NT, 4], F32)
        gt1 = ar.view("gt1", [NT, 4], F32)
        gt2 = ar.view("gt2", [NT, 4], F32)
        BETA, LG, GT1, GT2 = Buf(), Buf(), Buf(), Buf()
        cw = par[:, P_CW:P_CW + 48].rearrange("p (c j) -> p c j", j=4)

        def ag_E(u):
            ch, blk = u // 4, u % 4
            grp, hc = ch // 4, ch % 4
            ws, WS = wsl[grp], WSL[grp]
            sl = u % 4
            if u == 16:
                pass
            bk = nextbank()
            for kc in range(8):
                V("tensor", lambda h: h.matmul(banks[bk][:, :], ws[:, kc, hc * 128:(hc + 1) * 128], hT[:, kc, blk * 512:(blk + 1) * 512],
                                               start=(kc == 0), stop=(kc == 7)), r=[WS, HT[kc][blk]], w=[BK[bk]], inc=(kc == 7))
            if blk == 0:
                V("gpsimd", lambda h: h.memset(pcb[:, sl, 0:3], 0.0), w=[PC[sl]])
            else:
                V("gpsimd", lambda h: h.tensor_copy(out=pcb[:, sl, 0:3], in_=pcb[:, (u - 1) % 4, 512:515]), r=[PC[(u - 1) % 4]], w=[PC[sl]])
            V("scalar", lambda h: h.copy(out=pcb[:, sl, 3:515], in_=banks[bk][:, :]), r=[BK[bk]], w=[PC[sl]])

        def ag_T(u):
            ch = u // 4
            sl = u % 4
            V("scalar", lambda h: h.activation(out=accb[:, sl, :], in_=pcb[:, sl, 0:512], func=AF.Copy, scale=cw[:, ch, 0:1]), r=[PC[sl], PAR], w=[ACC[sl]])
            for j in range(1, 4):
                V("vector", lambda h: h.scalar_tensor_tensor(out=accb[:, sl, :], in0=pcb[:, sl, j:j + 512], scalar=cw[:, ch, j:j + 1], in1=accb[:, sl, :],
                                                             op0=ALU.mult, op1=ALU.add), r=[PC[sl], PAR, ACC[sl]], w=[ACC[sl]])

        def ag_S(u):
            ch, blk = u // 4, u % 4
            sl = u % 4
            V("scalar", lambda h: h.activation(out=qkvT[:, ch, blk * 512:(blk + 1) * 512], in_=accb[:, sl, :], func=AF.Silu), r=[ACC[sl]], w=[QKV[ch]])

        NU = 48
        for u in range(NU + 2):
            if u < NU:
                ag_E(u)
                if u == 16:
                    fw.dma("gpsimd", wsem[0], wsl[0], w_in_v[:, :, 1536:2048], w=[WSL[0]])
            if 0 <= u - 1 < NU:
                ag_T(u - 1)
            if 0 <= u - 2 < NU:
                ag_S(u - 2)
        for hc in range(4):
            for blk in range(NBLK):
                bk = nextbank()
                for kc in range(8):
                    V("tensor", lambda h: h.matmul(banks[bk][:, :], wsl[0][:, kc, hc * 128:(hc + 1) * 128], hT[:, kc, blk * 512:(blk + 1) * 512],
                                                   start=(kc == 0), stop=(kc == 7)), r=[WSL[0], HT[kc][blk]], w=[BK[bk]], inc=(kc == 7))
                V("scalar", lambda h: h.activation(out=sz[:, hc, blk * 512:(blk + 1) * 512], in_=banks[bk][:, :], func=AF.Silu), r=[BK[bk]], w=[SZ[hc]])
        for m in range(NT):
            bk = nextbank()
            for kc in range(8):
                V("tensor", lambda h: h.matmul(banks[bk][:, 0:8], hT[:, kc, m * 128:(m + 1) * 128], wg[:, kc, :],
                                               start=(kc == 0), stop=(kc == 7)), r=[WG, HT[kc][m // 4]], w=[BK[bk]], inc=(kc == 7))
            V("scalar", lambda h: h.copy(out=gpre[:, m, :], in_=banks[bk][:, 0:8]), r=[BK[bk]], w=[GPRE])
        dtb = par[:, P_DTB:P_DTB + 64].rearrange("p (m h) -> p m h", h=4)
        alog = par[:, P_ALOG:P_ALOG + 64].rearrange("p (m h) -> p m h", h=4)
        V("scalar", lambda h: h.activation(out=gt1, in_=gpre[:, :, 0:4], func=AF.Exp, scale=-1.0), r=[GPRE], w=[GT1])
        V("vector", lambda h: h.tensor_scalar_add(out=gt1, in0=gt1, scalar1=1.0), r=[GT1], w=[GT1])
        V("vector", lambda h: h.reciprocal(out=beta, in_=gt1), r=[GT1], w=[BETA])
        V("vector", lambda h: h.tensor_tensor(out=gt2, in0=gpre[:, :, 4:8], in1=dtb, op=ALU.add), r=[GPRE, PAR], w=[GT2])
        V("scalar", lambda h: h.activation(out=gt2, in_=gt2, func=AF.Exp), r=[GT2], w=[GT2])
        V("scalar", lambda h: h.activation(out=gt2, in_=gt2, func=AF.Ln, bias=1.0), r=[GT2], w=[GT2])
        V("scalar", lambda h: h.activation(out=gt1, in_=alog, func=AF.Exp), r=[PAR, GT1, BETA], w=[GT1])
        V("vector", lambda h: h.scalar_tensor_tensor(out=lg, in0=gt2, scalar=-1.0, in1=gt1, op0=ALU.mult, op1=ALU.mult), r=[GT1, GT2], w=[LG])
        dumpsrc["qkvT"] = (qkvT, QKV, "AG")
        dumpsrc["sz"] = (sz, SZ, "AG")
        dumpsrc["beta"] = (beta, [BETA], "AG")
        dumpsrc["lg"] = (lg, [LG], "AG")
        do_dumps("AG")
        if stop_after == "AG":
            finish()
            return nc
        fw.barrier()
        ar.release("hT", "wsl0", "wsl1", "wsl2", "wg", "pcb", "accb", "gpre", "gt1", "gt2")

        ogT = ar.view("ogT", [4, T], BF16)
        OGT = [Buf() for _ in range(4)]
        BQ = [[Buf(LK[i]) for _ in range(4)] for i in range(8)]

        def qap(b, q):
            return banks[b][:, q * 128:(q + 1) * 128]

        def flat(v):
            return v.rearrange("p h d -> p (h d)")

        def hb():
            return [Buf() for _ in range(4)]

        gam = ar.view("gam", [NT, 4], F32)
        egam = ar.view("egam", [NT, 4], F32)
        edec = ar.view("edec", [NT, 4], F32)
        glb = ar.view("glb", [2, 64], F32)
        X2 = ar.view("X2", [2, 64], F32)
        GAM, EGAM, EDEC, GLB, X2B = Buf(), Buf(), Buf(), Buf(), Buf()
        gbc = ar.view("gbc", [128], F32)
        GBC = Buf()
        fw.dma("sync", newd(), gbc, vec_d[7:8, 0:128].partition_broadcast(128), w=[GBC])
        negm = cst[:, C_NEGM:C_NEGM + 128]
        negmb = ar.view("negmb", [128], BF16)
        NEGMB = Buf()
        V("vector", lambda h: h.tensor_copy(out=negmb, in_=negm), r=[CST], w=[NEGMB])
        strict = cst[:, C_STR:C_STR + 128]
        lgf = lg.rearrange("p m h -> p (m h)")
        b0 = nextbank()
        V("tensor", lambda h: h.matmul(banks[b0][:, 0:64], cst[:, C_TRI:C_TRI + 128], lgf, start=True, stop=True), r=[CST, LG], w=[BQ[b0][0]])
        V("tensor", lambda h: h.matmul(banks[b0][:, 128:192], cst[:, C_BLK:C_BLK + 128], lgf, start=True, stop=True), r=[CST, LG], w=[BQ[b0][1]])
        V("vector", lambda h: h.tensor_copy(out=gam.rearrange("p m h -> p (m h)"), in_=banks[b0][:, 0:64]), r=[BQ[b0][0]], w=[GAM])
        V("scalar", lambda h: h.activation(out=egam.rearrange("p m h -> p (m h)"), in_=banks[b0][:, 0:64], func=AF.Exp), r=[BQ[b0][0]], w=[EGAM])
        V("vector", lambda h: h.tensor_tensor(out=edec.rearrange("p m h -> p (m h)"), in0=banks[b0][:, 128:192],
                                              in1=gam.rearrange("p m h -> p (m h)"), op=ALU.subtract), r=[BQ[b0][1], GAM], w=[EDEC])
        V("scalar", lambda h: h.activation(out=edec, in_=edec, func=AF.Exp), r=[EDEC], w=[EDEC])
        for hf in range(2):
            V("vector", lambda h: h.tensor_scalar_mul(out=X2[:, hf, :], in0=lgf, scalar1=cst[:, C_IND + hf:C_IND + hf + 1]), r=[LG, CST], w=[X2B])
        V("tensor", lambda h: h.matmul(banks[b0][:, 256:384], onesf, X2.rearrange("p a b -> p (a b)"), start=True, stop=True), r=[CST, X2B], w=[BQ[b0][2]])
        V("scalar", lambda h: h.activation(out=glb.rearrange("p a b -> p (a b)"), in_=banks[b0][:, 256:384], func=AF.Exp), r=[BQ[b0][2]], w=[GLB])

        dumpsrc["gam"] = (gam, [GAM], "G0")
        dumpsrc["glb"] = (glb, [GLB], "G0")
        if stop_after == "G0":
            do_dumps("G0")
            finish()
            return nc
        def hb3(n):
            return [[Buf() for _ in range(4)] for _ in range(n)]

        def v2(name, n, dt):
            return [ar.view("%s%d" % (name, i), [4, 128], dt) for i in range(n)]

        NPI, NSH = 3, 4
        kbt = v2("kbt", NPI, BF16); KBT = hb3(NPI)
        vbt = v2("vbt", NPI, BF16); VBT = hb3(NPI)
        ktv = [ar.view("ktv%d" % i, [8, 128], BF16) for i in range(NPI)]; KTV = hb3(NPI)
        diagc = v2("diagc", NPI, F32); DIAGC = hb3(NPI)
        Dm = v2("Dm", NPI, F32); DM = hb3(NPI)
        DmS = v2("DmS", NPI, F32); DMS = hb3(NPI)
        Am = [v2("Am%d_" % i, NPI, BF16) for i in range(2)]; AMB = [hb3(NPI), hb3(NPI)]
        Bm = [v2("Bm%d_" % i, NPI, BF16) for i in range(2)]; BMB = [hb3(NPI), hb3(NPI)]
        ac = v2("ac", NPI, BF16); AC = hb3(NPI)
        Tt32 = v2("Tt32_", NPI, F32); TT32 = hb3(NPI)
        Ttb = [v2("Ttb%d_" % i, NPI, BF16) for i in range(2)]; TTB = [hb3(NPI), hb3(NPI)]
        kdt = v2("kdt", NSH, BF16); KDT = hb3(NSH)
        acT = v2("acT", NSH, BF16); ACTB = hb3(NSH)
        u_sb = v2("u_sb", NSH, F32); US = hb3(NSH)
        wT_sb = v2("wT_sb", NSH, BF16); WT = hb3(NSH)
        vnew = ar.view("vnew", [4, 128], BF16); VN = hb()
        tmpo = ar.view("tmpo", [4, 128], F32); TMPO = hb()
        otok = ar.view("otok", [4, 128], F32); OTOK = hb()
        on = ar.view("on", [4, 128], BF16); ON = hb()
        junkt = ar.view("junk", [8, 128], F32); JUNKS = [Buf() for _ in range(8)]
        junk2t = ar.view("junk2", [4, 128], F32); JUNK2S = [Buf() for _ in range(4)]
        jctr = [0]

        def njunk():
            jctr[0] += 1
            return jctr[0] % 8
        S32 = ar.view("S32", [4, 128], F32); S32B = hb()
        Sp = ar.view("Sp", [4, 128], F32); SPB = hb()
        Sbf = ar.view("Sbf", [4, 128], BF16); SBF = hb()
        sm = ar.view("sm", [NSH, 12, 4], F32)
        SM = [Buf() for _ in range(NSH)]
        SM2 = [Buf() for _ in range(NSH)]
        V("vector", lambda h: h.memset(flat(S32), 0.0), w=S32B)
        V("vector", lambda h: h.memset(flat(Sbf), 0.0), w=SBF)
        bWS, bOI, bO2, bSD, bOT = 4, 5, 6, 7, 4

        def bq(b, hh):
            return banks[b][:, :].bitcast(BF16)[:, hh * 256:hh * 256 + 128]

        def bqall(b):
            return banks[b][:, :].bitcast(BF16).rearrange("p (h c) -> p h c", h=4)[:, :, 0:128]

        def pbank():
            return nextbank(0, 4)

        def gen_prep(m):
            tok = slice(m * 128, (m + 1) * 128)
            pi = m % NPI
            p3 = m % NSH
            rk, rq, bkk, skb, skd, sa, so, lnk, crow, sso = [sm[:, p3, i, :] for i in range(10)]
            S_ = [SM[p3]]
            X, Y = pbank(), pbank()
            t1 = banks[X][:, :].bitcast(BF16)
            for hh in range(4):
                kq = t1[:, hh * 256:hh * 256 + 128]
                vq = t1[:, hh * 256 + 128:hh * 256 + 256]
                V("tensor", lambda h: h.transpose(kq, qkvT[:, 4 + hh, tok], identb), r=[QKV[4 + hh], IDB], w=[BQ[X][hh]], inc=False)
                V("tensor", lambda h: h.transpose(vq, qkvT[:, 8 + hh, tok], identb), r=[QKV[8 + hh], IDB], w=[BQ[X][hh]], inc=False)
                V("tensor", lambda h: h.transpose(bq(Y, hh), qkvT[:, hh, tok], identb), r=[QKV[hh], IDB], w=[BQ[Y][hh]])
            for hh in range(4):
                kq = t1[:, hh * 256:hh * 256 + 128]
                j1, j2 = njunk(), njunk()
                V("scalar", lambda h: h.activation(out=junkt[:, j1, :], in_=kq, func=AF.Square, accum_out=rk[:, hh:hh + 1]), r=[BQ[X][hh]], w=[JUNKS[j1], SM[p3]])
                V("scalar", lambda h: h.activation(out=junkt[:, j2, :], in_=bq(Y, hh), func=AF.Square, accum_out=rq[:, hh:hh + 1]), r=[BQ[Y][hh]], w=[JUNKS[j2], SM[p3]])
            V("vector", lambda h: h.tensor_copy(out=ktv[pi].rearrange("p a b -> p (a b)"), in_=t1), r=BQ[X], w=KTV[pi])
            yield
            rkq = sm[:, p3, 0:2, :]
            V("scalar", lambda h: h.activation(out=rkq, in_=rkq, func=AF.Ln, bias=eps_col[:, 0:1]), r=S_ + [EPSB], w=S_)
            V("vector", lambda h: h.scalar_tensor_tensor(out=crow, in0=rk, scalar=-0.5, in1=gam[:, m, :], op0=ALU.mult, op1=ALU.subtract), r=S_ + [GAM], w=S_)
            V("scalar", lambda h: h.activation(out=rkq, in_=rkq, func=AF.Exp, scale=-0.5), r=S_, w=S_)
            V("vector", lambda h: h.tensor_tensor(out=bkk, in0=rk, in1=beta[:, m, :], op=ALU.mult), r=S_ + [BETA], w=S_)
            V("vector", lambda h: h.tensor_tensor(out=skb, in0=bkk, in1=egam[:, m, :], op=ALU.mult), r=S_ + [EGAM], w=S_)
            V("vector", lambda h: h.tensor_tensor(out=skd, in0=rk, in1=edec[:, m, :], op=ALU.mult), r=S_ + [EDEC], w=S_)
            V("vector", lambda h: h.tensor_scalar_mul(out=sa, in0=rq, scalar1=128.0 ** -0.5), r=S_, w=S_)
            V("vector", lambda h: h.tensor_tensor(out=so, in0=sa, in1=egam[:, m, :], op=ALU.mult), r=S_ + [EGAM], w=S_)
            yield
            for hh in range(4):
                kq = ktv[pi][:, 2 * hh, :]
                vq = ktv[pi][:, 2 * hh + 1, :]
                V("scalar", lambda h: h.activation(out=diagc[pi][:, hh, :], in_=identf, func=AF.Copy, scale=crow[:, hh:hh + 1]), r=[CST] + S_, w=[DIAGC[pi][hh]])
                V("scalar", lambda h: h.activation(out=kbt[pi][:, hh, :], in_=kq, func=AF.Copy, scale=skb[:, hh:hh + 1]), r=[KTV[pi][hh]] + S_, w=[KBT[pi][hh]])
                V("vector", lambda h: h.tensor_scalar_mul(out=kdt[p3][:, hh, :], in0=kq, scalar1=skd[:, hh:hh + 1]), r=[KTV[pi][hh]] + S_, w=[KDT[p3][hh]])
                V("vector", lambda h: h.tensor_scalar_mul(out=vbt[pi][:, hh, :], in0=vq, scalar1=beta[:, m, hh:hh + 1]), r=[KTV[pi][hh], BETA], w=[VBT[pi][hh]])
            yield
            Y = pbank()
            for hh in range(4):
                V("tensor", lambda h: h.matmul(qap(Y, hh), onesf, diagc[pi][:, hh, :], start=True, stop=False), r=[CST, DIAGC[pi][hh]], w=[BQ[Y][hh]], inc=False)
                V("tensor", lambda h: h.matmul(qap(Y, hh), identb, negmb, start=False, stop=True), r=[IDB, NEGMB], w=[BQ[Y][hh]], inc=(hh == 3))
            for hh in range(4):
                V("scalar", lambda h: h.activation(out=Dm[pi][:, hh, :], in_=qap(Y, hh), func=AF.Exp, bias=gam[:, m, hh:hh + 1]), r=[BQ[Y][hh], GAM], w=[DM[pi][hh]])
            for hh in range(4):
                V("gpsimd", lambda h: h.tensor_tensor(out=DmS[pi][:, hh, :], in0=Dm[pi][:, hh, :], in1=strict, op=ALU.mult), r=[DM[pi][hh], CST], w=[DMS[pi][hh]])
            yield
            X, Y = pbank(), pbank()
            for hh in range(4):
                V("tensor", lambda h: h.matmul(qap(X, hh), qkvT[:, 4 + hh, tok], qkvT[:, 4 + hh, tok], start=True, stop=True), r=[QKV[4 + hh]], w=[BQ[X][hh]], inc=(hh == 3))
            for hh in range(4):
                V("tensor", lambda h: h.matmul(qap(Y, hh), qkvT[:, hh, tok], qkvT[:, 4 + hh, tok], start=True, stop=True), r=[QKV[hh], QKV[4 + hh]], w=[BQ[Y][hh]], inc=(hh == 3))
            for hh in range(4):
                V("vector", lambda h: h.scalar_tensor_tensor(out=Am[0][pi][:, hh, :], in0=qap(X, hh), scalar=bkk[:, hh:hh + 1], in1=DmS[pi][:, hh, :],
                                                             op0=ALU.mult, op1=ALU.mult), r=[BQ[X][hh], DMS[pi][hh]] + S_, w=[AMB[0][pi][hh]])
            for hh in range(4):
                V("vector", lambda h: h.scalar_tensor_tensor(out=ac[pi][:, hh, :], in0=qap(Y, hh), scalar=sa[:, hh:hh + 1], in1=Dm[pi][:, hh, :],
                                                             op0=ALU.mult, op1=ALU.mult), r=[BQ[Y][hh], DM[pi][hh]] + S_, w=[AC[pi][hh]])
            yield
            X, Y = pbank(), pbank()
            for hh in range(4):
                V("tensor", lambda h: h.transpose(bq(X, hh), Am[0][pi][:, hh, :], identb), r=[AMB[0][pi][hh], IDB], w=[BQ[X][hh]], inc=(hh == 3))
            for hh in range(4):
                V("tensor", lambda h: h.transpose(bq(Y, hh), ac[pi][:, hh, :], identb), r=[AC[pi][hh], IDB], w=[BQ[Y][hh]], inc=(hh == 3))
            V("scalar", lambda h: h.copy(out=Bm[0][pi], in_=bqall(X)), r=BQ[X], w=BMB[0][pi])
            for hh in range(4):
                V("vector", lambda h: h.tensor_tensor(out=Ttb[0][pi][:, hh, :], in0=identf, in1=bq(X, hh), op=ALU.subtract), r=[CST, BQ[X][hh]], w=[TTB[0][pi][hh]])
            for hh in range(4):
                V("vector", lambda h: h.tensor_tensor(out=Tt32[pi][:, hh, :], in0=identf, in1=bq(X, hh), op=ALU.subtract), r=[CST, BQ[X][hh]], w=[TT32[pi][hh]])
            V("scalar", lambda h: h.copy(out=acT[p3], in_=bqall(Y)), r=BQ[Y], w=ACTB[p3])
            yield
            cur = 0
            for s_ in range(5):
                X = pbank()
                for hh in range(4):
                    V("tensor", lambda h: h.matmul(qap(X, hh), Bm[cur][pi][:, hh, :], Am[cur][pi][:, hh, :], start=True, stop=True),
                      r=[BMB[cur][pi][hh], AMB[cur][pi][hh]], w=[BQ[X][hh]], inc=(hh == 3))
                V("scalar", lambda h: h.copy(out=flat(Am[1 - cur][pi]), in_=banks[X][:, :]), r=BQ[X], w=AMB[1 - cur][pi])
                if s_ < 4:
                    Y = pbank()
                    for hh in range(4):
                        V("tensor", lambda h: h.matmul(qap(Y, hh), Am[cur][pi][:, hh, :], Bm[cur][pi][:, hh, :], start=True, stop=True),
                          r=[BMB[cur][pi][hh], AMB[cur][pi][hh]], w=[BQ[Y][hh]], inc=(hh == 3))
                    V("scalar", lambda h: h.copy(out=flat(Bm[1 - cur][pi]), in_=banks[Y][:, :]), r=BQ[Y], w=BMB[1 - cur][pi])
                yield
                X = pbank()
                for hh in range(4):
                    V("tensor", lambda h: h.matmul(qap(X, hh), Am[1 - cur][pi][:, hh, :], Ttb[cur][pi][:, hh, :], start=True, stop=True),
                      r=[AMB[1 - cur][pi][hh], TTB[cur][pi][hh]], w=[BQ[X][hh]], inc=(hh == 3))
                V("vector", lambda h: h.tensor_tensor(out=flat(Ttb[1 - cur][pi]), in0=flat(Tt32[pi]), in1=banks[X][:, :], op=ALU.add),
                  r=TT32[pi] + BQ[X], w=TTB[1 - cur][pi])
                V("vector", lambda h: h.tensor_tensor(out=flat(Tt32[pi]), in0=flat(Tt32[pi]), in1=banks[X][:, :], op=ALU.add),
                  r=TT32[pi] + BQ[X], w=TT32[pi])
                cur = 1 - cur
                yield
            X, Y = pbank(), pbank()
            for hh in range(4):
                V("tensor", lambda h: h.matmul(qap(X, hh), Ttb[cur][pi][:, hh, :], vbt[pi][:, hh, :], start=True, stop=True),
                  r=[TTB[cur][pi][hh], VBT[pi][hh]], w=[BQ[X][hh]], inc=(hh == 3))
            for hh in range(4):
                V("tensor", lambda h: h.matmul(qap(Y, hh), kbt[pi][:, hh, :], Ttb[cur][pi][:, hh, :], start=True, stop=True),
                  r=[TTB[cur][pi][hh], KBT[pi][hh]], w=[BQ[Y][hh]], inc=(hh == 3))
            V("scalar", lambda h: h.copy(out=flat(u_sb[p3]), in_=banks[X][:, :]), r=BQ[X], w=US[p3])
            V("vector", lambda h: h.tensor_copy(out=flat(wT_sb[p3]), in_=banks[Y][:, :]), r=BQ[Y], w=WT[p3])
            yield

        def gen_scan(m):
            tok = slice(m * 128, (m + 1) * 128)
            p3 = m % NSH
            so = sm[:, p3, 6, :]
            sso = sm[:, p3, 9, :]
            S_ = [SM[p3]]
            for half in range(2):
                r0 = half * 64
                c0 = m * 128 + r0
                for hh in range(4):
                    V("tensor", lambda h: h.matmul(qap(bWS, hh)[r0:r0 + 64, :], wT_sb[p3][:, hh, r0:r0 + 64], Sbf[:, hh, :], start=True, stop=True),
                      r=[WT[p3][hh], SBF[hh]], w=[BQ[bWS][hh]], inc=(hh == 3))
                for hh in range(4):
                    V("tensor", lambda h: h.matmul(qap(bOI, hh)[r0:r0 + 64, :], qkvT[:, hh, c0:c0 + 64], Sbf[:, hh, :], start=True, stop=True),
                      r=[QKV[hh], SBF[hh]], w=[BQ[bOI][hh]], inc=(hh == 3))
                for hh in range(4):
                    gcol = glb[:, half, m * 4 + hh:m * 4 + hh + 1]
                    V("gpsimd", lambda h: h.tensor_scalar(out=Sp[:, hh, :], in0=S32[:, hh, :], scalar1=gcol, scalar2=0.0, op0=ALU.mult, op1=ALU.add),
                      r=[S32B[hh], GLB], w=[SPB[hh]])
                yield
                V("vector", lambda h: h.tensor_tensor(out=flat(vnew[r0:r0 + 64, :, :]), in0=flat(u_sb[p3][r0:r0 + 64, :, :]),
                                                      in1=banks[bWS][r0:r0 + 64, :], op=ALU.subtract), r=US[p3] + BQ[bWS], w=VN)
                yield
                for hh in range(4):
                    V("tensor", lambda h: h.matmul(qap(bSD, hh), kdt[p3][r0:r0 + 64, hh, :], vnew[r0:r0 + 64, hh, :], start=True, stop=True),
                      r=[KDT[p3][hh], VN[hh]], w=[BQ[bSD][hh]], inc=(hh == 3))
                for hh in range(4):
                    V("tensor", lambda h: h.matmul(qap(bO2, hh)[r0:r0 + 64, :], acT[p3][r0:r0 + 64, hh, r0:r0 + 64], vnew[r0:r0 + 64, hh, :],
                                                   start=True, stop=True), r=[ACTB[p3][hh], VN[hh]], w=[BQ[bO2][hh]], inc=(hh == 3))
                yield
                V("vector", lambda h: h.tensor_tensor(out=flat(S32), in0=flat(Sp), in1=banks[bSD][:, :], op=ALU.add), r=SPB + BQ[bSD], w=S32B)
                V("scalar", lambda h: h.copy(out=flat(Sbf), in_=flat(S32)), r=S32B, w=SBF)
                yield
            for hh in range(4):
                V("scalar", lambda h: h.activation(out=tmpo[:, hh, :], in_=qap(bOI, hh), func=AF.Copy, scale=so[:, hh:hh + 1]), r=[BQ[bOI][hh]] + S_, w=[TMPO[hh]])
            V("vector", lambda h: h.tensor_tensor(out=flat(otok), in0=flat(tmpo), in1=banks[bO2][:, :], op=ALU.add), r=TMPO + BQ[bO2], w=OTOK)
            yield
            for hh in range(4):
                V("scalar", lambda h: h.activation(out=junk2t[:, hh, :], in_=otok[:, hh, :], func=AF.Square, accum_out=sso[:, hh:hh + 1]), r=[OTOK[hh]], w=[JUNK2S[hh], SM2[p3]])
            V("scalar", lambda h: h.activation(out=sso, in_=sso, func=AF.Ln, scale=1.0 / 128.0, bias=eps_col[:, 0:1]), r=[SM2[p3], EPSB], w=[SM2[p3]])
            V("scalar", lambda h: h.activation(out=sso, in_=sso, func=AF.Exp, scale=-0.5), r=[SM2[p3]], w=[SM2[p3]])
            yield
            for hh in range(4):
                V("vector", lambda h: h.scalar_tensor_tensor(out=on[:, hh, :], in0=otok[:, hh, :], scalar=sso[:, hh:hh + 1], in1=gbc,
                                                             op0=ALU.mult, op1=ALU.mult), r=[OTOK[hh], SM2[p3], GBC], w=[ON[hh]])
            for hh in range(4):
                V("tensor", lambda h: h.transpose(bq(bOT, hh), on[:, hh, :], identb), r=[ON[hh], IDB], w=[BQ[bOT][hh]], inc=(hh == 3))
            yield
            V("vector", lambda h: h.tensor_tensor(out=ogT[:, :, tok], in0=bqall(bOT), in1=sz[:, :, tok], op=ALU.mult), r=BQ[bOT] + SZ, w=OGT)
            yield

        preps = {}
        prep_done = set()
        scan_done = set()
        next_prep = 0
        scan_m = 0
        scan_g = None
        while len(scan_done) < NT:
            while len(preps) < NPI and next_prep < NT and (next_prep < NSH or (next_prep - NSH) in scan_done) \
                    and (next_prep < NPI or (next_prep - NPI) in prep_done):
                preps[next_prep] = gen_prep(next_prep)
                next_prep += 1
            progressed = False
            for _rep in range(2):
                if scan_g is None and scan_m < NT and scan_m in prep_done:
                    scan_g = gen_scan(scan_m)
                if scan_g is not None:
                    try:
                        next(scan_g)
                    except StopIteration:
                        scan_done.add(scan_m)
                        scan_m += 1
                        scan_g = None
                    progressed = True
            for t_ in sorted(preps):
                try:
                    next(preps[t_])
                except StopIteration:
                    prep_done.add(t_)
                    del preps[t_]
                progressed = True
            assert progressed
        dumpsrc["ogT"] = (ogT, OGT, "G")
        do_dumps("G")
        if stop_after == "G":
            finish()
            return nc
        fw.barrier()
        ar.release("gam", "egam", "edec", "glb", "X2", "gbc", "negmb", "vnew", "tmpo", "otok", "on", "junk", "junk2", "S32", "Sp", "Sbf", "sm", "qkvT", "sz", "beta", "lg", "kbt0", "kbt1", "kbt2", "vbt0", "vbt1", "vbt2", "ktv0", "ktv1", "ktv2", "diagc0", "diagc1", "diagc2", "Dm0", "Dm1", "Dm2", "DmS0", "DmS1", "DmS2", "Am0_0", "Am0_1", "Am0_2", "Am1_0", "Am1_1", "Am1_2", "Bm0_0", "Bm0_1", "Bm0_2", "Bm1_0", "Bm1_1", "Bm1_2", "ac0", "ac1", "ac2", "Tt32_0", "Tt32_1", "Tt32_2", "Ttb0_0", "Ttb0_1", "Ttb0_2", "Ttb1_0", "Ttb1_1", "Ttb1_2", "kdt0", "kdt1", "kdt2", "kdt3", "acT0", "acT1", "acT2", "acT3", "u_sb0", "u_sb1", "u_sb2", "u_sb3", "wT_sb0", "wT_sb1", "wT_sb2", "wT_sb3")

        Rr = ar.view("R", [NT, D], F32)
        RB = [[Buf(), Buf()] for _ in range(NT)]
        h1T = ar.view("h1T", [8, T], BF16)
        H1T = [[Buf() for _ in range(NT)] for _ in range(8)]
        wo_g = ar.view("wo_g", [4, 512], BF16)
        wo_f = ar.view("wo_f", [4, 512], BF16)
        WOG, WOF = Buf(), Buf()
        wog_sem, wof_sem = newd(), newd()
        xts = [ar.view("xt%d" % i, [D], F32) for i in range(4)]
        XT = [Buf() for _ in range(4)]
        xsem2 = [newd() for _ in range(2)]
        gA = ar.view("gA", [D], F32)
        bA = ar.view("bA", [D], F32)
        GA, BA = Buf(), Buf()
        st = ar.view("st", [4, 12], F32)
        STB = [Buf() for _ in range(4)]
        mv1 = ar.view("mv1", [NT, 2], F32)
        MV1 = [Buf() for _ in range(NT)]
        w_out_g = w_out[0:512, :].rearrange("(kc p) c -> p kc c", p=128)
        w_out_f = w_out[512:1024, :].rearrange("(j p) c -> p j c", p=128)
        fw.dma("gpsimd", wog_sem, wo_g, w_out_g[:, :, 0:512], w=[WOG])
        fw.dma("gpsimd", wof_sem, wo_f, w_out_f[:, :, 0:512], w=[WOF])
        bcsem = [newd(), newd()]

        def load_bc(rowg, rowb):
            fw.dma("sync", bcsem[0], gA, vec_d[rowg:rowg + 1, :].partition_broadcast(128), w=[GA])
            fw.dma("sync", bcsem[1], bA, vec_d[rowb:rowb + 1, :].partition_broadcast(128), w=[BA])
            V("scalar", lambda h: h.activation(out=gA, in_=gA, func=AF.Copy, scale=ALPHA), r=[GA], w=[GA])
            V("scalar", lambda h: h.activation(out=bA, in_=bA, func=AF.Copy, scale=ALPHA), r=[BA], w=[BA])

        wple = ar.view("wple", [2, D], BF16)
        wgt = ar.view("wgt", [8, D], BF16)
        WPLE, WGT = Buf(), Buf()
        fw.dma("gpsimd", newd(), wple, w_ple.rearrange("(kc p) c -> p kc c", p=128), w=[WPLE])
        fw.dma("gpsimd", newd(), wgt, w_gate.rearrange("(kc p) c -> p kc c", p=128), w=[WGT])
        load_bc(0, 1)
        nmr = ar.view("nmr", [NT], F32)
        NMR = Buf()
        V("vector", lambda h: h.scalar_tensor_tensor(out=nmr, in0=mv[:, :, 0], scalar=-1.0, in1=mv[:, :, 1], op0=ALU.mult, op1=ALU.mult), r=MV, w=[NMR])
        for m in range(NT):
            j = m % 2
            fw.dma("sync", xsem2[j], xts[j], x[m * 128:(m + 1) * 128, :], w=[XT[j]])
            V("scalar", lambda h: h.activation(out=xts[j], in_=xts[j], func=AF.Identity, scale=mv[:, m, 1:2], bias=nmr[:, m:m + 1]),
              r=[XT[j], MV[m], NMR], w=[XT[j]])
            V("vector", lambda h: h.tensor_tensor(out=Rr[:, m, :], in0=xts[j], in1=gA, op=ALU.mult), r=[XT[j], GA], w=RB[m])
            V("vector", lambda h: h.tensor_tensor(out=Rr[:, m, :], in0=Rr[:, m, :], in1=bA, op=ALU.add), r=RB[m] + [BA], w=RB[m])
        for half in range(2):
            if half == 1:
                fw.dma("gpsimd", wog_sem, wo_g, w_out_g[:, :, 512:1024], w=[WOG])
                fw.dma("gpsimd", wof_sem, wo_f, w_out_f[:, :, 512:1024], w=[WOF])
            for m in range(NT):
                tok = slice(m * 128, (m + 1) * 128)
                bk = nextbank()
                for kc in range(4):
                    V("tensor", lambda h: h.matmul(banks[bk][:, :], ogT[:, kc, tok], wo_g[:, kc, :], start=(kc == 0), stop=False),
                      r=[OGT[kc], WOG], w=[BK[bk]], inc=False)
                for j4 in range(4):
                    V("tensor", lambda h: h.matmul(banks[bk][:, :], ofox[:, j4, tok], wo_f[:, j4, :], start=False, stop=(j4 == 3)),
                      r=[OFOX[2 * j4], OFOX[2 * j4 + 1], WOF], w=[BK[bk]], inc=(j4 == 3))
                V("vector", lambda h: h.tensor_tensor(out=Rr[:, m, half * 512:(half + 1) * 512], in0=Rr[:, m, half * 512:(half + 1) * 512],
                                                      in1=banks[bk][:, :], op=ALU.add), r=[RB[m][half], BK[bk]], w=[RB[m][half]])
        load_bc(2, 3)

        def c_p1(m):
            j = m % 4
            V("vector", lambda h: h.bn_stats(out=st[:, j, 0:6], in_=Rr[:, m, 0:512]), r=[RB[m][0]], w=[STB[j]])
            V("vector", lambda h: h.bn_stats(out=st[:, j, 6:12], in_=Rr[:, m, 512:1024]), r=[RB[m][1]], w=[STB[j]])
            V("vector", lambda h: h.bn_aggr(out=mv1[:, m, 0:2], in_=st[:, j, 0:12]), r=[STB[j]], w=[MV1[m]])
            V("vector", lambda h: h.tensor_scalar_add(out=mv1[:, m, 1:2], in0=mv1[:, m, 1:2], scalar1=LN_EPS), r=[MV1[m]], w=[MV1[m]])
            V("gpsimd", lambda h: h.tensor_tensor(out=mv1[:, m, 1:2], in0=mv1[:, m, 1:2], in1=m05, op=ALU.pow), r=[MV1[m], CST], w=[MV1[m]])

        def c_p2(m):
            j = m % 4
            V("vector", lambda h: h.scalar_tensor_tensor(out=mv1[:, m, 0:1], in0=mv1[:, m, 0:1], scalar=-1.0, in1=mv1[:, m, 1:2],
                                                         op0=ALU.mult, op1=ALU.mult), r=[MV1[m]], w=[MV1[m]])
            V("scalar", lambda h: h.activation(out=xts[j], in_=Rr[:, m, :], func=AF.Identity, scale=mv1[:, m, 1:2], bias=mv1[:, m, 0:1]),
              r=RB[m] + [MV1[m]], w=[XT[j]])

        def c_p3(m):
            j = m % 4
            V("vector", lambda h: h.tensor_tensor(out=Rr[:, m, :], in0=xts[j], in1=gA, op=ALU.mult), r=[XT[j], GA], w=RB[m])
            V("vector", lambda h: h.tensor_tensor(out=Rr[:, m, :], in0=Rr[:, m, :], in1=bA, op=ALU.add), r=RB[m] + [BA], w=RB[m])

        def c_tr(g):
            for c in range(8):
                bk = nextbank()
                for j in range(4):
                    V("tensor", lambda h: h.transpose(banks[bk][:, j * 128:(j + 1) * 128], xts[j][:, c * 128:(c + 1) * 128], identf),
                      r=[XT[j], CST], w=[BK[bk]], inc=(j == 3))
                V("scalar", lambda h: h.activation(out=h1T[:, c, g * 512:(g + 1) * 512], in_=banks[bk][:, :], func=AF.Identity,
                                                   scale=par[:, P_G1 + c:P_G1 + c + 1], bias=par[:, P_B1 + c:P_B1 + c + 1]),
                  r=[BK[bk], PAR], w=[H1T[c][4 * g + jj] for jj in range(4)])

        for k in range(NT + 2):
            if k < NT:
                c_p1(k)
            if 0 <= k - 1 < NT:
                c_p2(k - 1)
                if (k - 1) % 4 == 3:
                    c_tr((k - 1) // 4)
            if 0 <= k - 2 < NT:
                c_p3(k - 2)
        dumpsrc["h1T"] = (h1T, [b for row in H1T for b in row], "C")
        dumpsrc["R"] = (Rr, [b for p_ in RB for b in p_], "C")
        do_dumps("C")
        if stop_after == "C":
            finish()
            return nc
        fw.barrier()
        ar.release("ofox", "ogT", "wo_g", "wo_f", "xt0", "xt1", "xt2", "xt3", "gA", "bA", "nmr")

        NG = DFF // 512
        wu = [ar.view("wu%d" % i, [8, 512], BF16) for i in range(2)]
        wd = [ar.view("wd%d" % i, [4, D], BF16) for i in range(2)]
        WU = [Buf(), Buf()]
        WD = [Buf(), Buf()]
        wusem = [newd(), newd()]
        wdsem = [newd(), newd()]
        w_up_v = w_up.rearrange("(kc p) f -> p kc f", p=128)
        w_down_v = w_down.rearrange("(fc p) c -> p fc c", p=128)
        def load_ffw(g):
            s_ = g % 2
            fw.dma("gpsimd", wusem[s_], wu[s_], w_up_v[:, :, g * 512:(g + 1) * 512], w=[WU[s_]])
            fw.dma("gpsimd", wdsem[s_], wd[s_], w_down_v[:, g * 4:(g + 1) * 4, :], w=[WD[s_]])

        load_ffw(0)
        pTt = ar.view("pTt", [2, T], BF16)
        PTB = [Buf() for _ in range(NT)]
        ptile = ar.view("ptile", [2, 256], F32)
        PTL = [Buf() for _ in range(2)]
        psem = [newd(), newd()]
        bgf = ar.view("bgf", [D], F32)
        bgh = ar.view("bgh", [D], BF16)
        bgl = ar.view("bgl", [D], BF16)
        ones_b = ar.view("ones_b", [128], BF16)
        BGF, BGH, BGL, ONB = Buf(), Buf(), Buf(), Buf()
        sg = ar.view("sg", [2, 512], F32)
        SG = [Buf(), Buf()]
        fw.dma("sync", newd(), bgf[0:1, :], vec_d[6:7, :], w=[BGF])
        V("vector", lambda h: h.tensor_copy(out=bgh[0:1, :], in_=bgf[0:1, :]), r=[BGF], w=[BGH])
        V("vector", lambda h: h.tensor_tensor(out=bgf[0:1, :], in0=bgf[0:1, :], in1=bgh[0:1, :], op=ALU.subtract), r=[BGF, BGH], w=[BGF])
        V("vector", lambda h: h.tensor_copy(out=bgl[0:1, :], in_=bgf[0:1, :]), r=[BGF], w=[BGL])
        V("vector", lambda h: h.memset(ones_b, 1.0), w=[ONB])
        for m in range(NT):
            j = m % 2
            tok = slice(m * 128, (m + 1) * 128)
            fw.dma("sync", psem[j], ptile[:, j, :], p_in[tok, :], w=[PTL[j]])
            bk = nextbank()
            for kc in range(2):
                V("tensor", lambda h: h.transpose(banks[bk][:, kc * 128:(kc + 1) * 128], ptile[:, j, kc * 128:(kc + 1) * 128], identf),
                  r=[PTL[j], CST], w=[BK[bk]], inc=(kc == 1))
            V("scalar", lambda h: h.copy(out=pTt[:, :, tok], in_=banks[bk][:, 0:256].rearrange("p (k t) -> p k t", k=2)), r=[BK[bk]], w=[PTB[m]])
            for half in range(2):
                cs = slice(half * 512, (half + 1) * 512)
                bp, bg_ = nextbank(), nextbank()
                for kc in range(2):
                    V("tensor", lambda h: h.matmul(banks[bp][:, :], pTt[:, kc, tok], wple[:, kc, cs], start=(kc == 0), stop=(kc == 1)),
                      r=[PTB[m], WPLE], w=[BK[bp]], inc=(kc == 1))
                for kc in range(8):
                    V("tensor", lambda h: h.matmul(banks[bg_][:, :], h1T[:, kc, tok], wgt[:, kc, cs], start=(kc == 0), stop=False),
                      r=[H1T[kc][m], WGT], w=[BK[bg_]], inc=False)
                V("tensor", lambda h: h.matmul(banks[bg_][:, :], ones_b[0:1, :], bgh[0:1, cs], start=False, stop=False), r=[ONB, BGH], w=[BK[bg_]], inc=False)
                V("tensor", lambda h: h.matmul(banks[bg_][:, :], ones_b[0:1, :], bgl[0:1, cs], start=False, stop=True), r=[ONB, BGL], w=[BK[bg_]])
                V("scalar", lambda h: h.activation(out=sg[:, half, :], in_=banks[bg_][:, :], func=AF.Sigmoid), r=[BK[bg_]], w=[SG[half]])
                V("vector", lambda h: h.tensor_tensor(out=sg[:, half, :], in0=sg[:, half, :], in1=banks[bp][:, :], op=ALU.mult), r=[SG[half], BK[bp]], w=[SG[half]])
                V("gpsimd", lambda h: h.tensor_tensor(out=Rr[:, m, cs], in0=Rr[:, m, cs], in1=sg[:, half, :], op=ALU.add), r=[RB[m][half], SG[half]], w=[RB[m][half]])
        if stop_after == "D1":
            dumpsrc["R1"] = (Rr, [b for p_ in RB for b in p_], "D1")
            do_dumps("D1")
            finish()
            return nc
        fw.barrier()
        ar.release("pTt", "ptile", "wple", "wgt", "bgf", "bgh", "bgl", "ones_b", "sg")

        actT = [ar.view("actT%d" % i, [4, T], BF16) for i in range(2)]
        ACTT = [[[Buf() for _ in range(NBLK)] for _ in range(4)] for _ in range(2)]
        rl = [ar.view("rl%d" % i, [512], F32) for i in range(2)]
        RL = [Buf(), Buf()]

        gA = ar.view("gA", [D], F32)
        bA = ar.view("bA", [D], F32)
        GA, BA = Buf(), Buf()
        fw.dma("sync", bcsem[0], gA, vec_d[4:5, :].partition_broadcast(128), w=[GA])
        fw.dma("sync", bcsem[1], bA, vec_d[5:6, :].partition_broadcast(128), w=[BA])
        yo = ar.view("yo", [2, D], F32)
        YO = [Buf(), Buf()]
        osem = [newd(), newd()]

        def emit_E1(m):
            j = m % 4
            V("vector", lambda h: h.bn_stats(out=st[:, j, 0:6], in_=Rr[:, m, 0:512]), r=[RB[m][0]], w=[STB[j]])
            V("vector", lambda h: h.bn_stats(out=st[:, j, 6:12], in_=Rr[:, m, 512:1024]), r=[RB[m][1]], w=[STB[j]])
            V("vector", lambda h: h.bn_aggr(out=mv1[:, m, 0:2], in_=st[:, j, 0:12]), r=[STB[j]], w=[MV1[m]])
            V("vector", lambda h: h.tensor_scalar_add(out=mv1[:, m, 1:2], in0=mv1[:, m, 1:2], scalar1=LN_EPS), r=[MV1[m]], w=[MV1[m]])
            V("gpsimd", lambda h: h.tensor_tensor(out=mv1[:, m, 1:2], in0=mv1[:, m, 1:2], in1=m05, op=ALU.pow), r=[MV1[m], CST], w=[MV1[m]])

        def emit_E2(m):
            j = m % 2
            V("vector", lambda h: h.scalar_tensor_tensor(out=mv1[:, m, 0:1], in0=mv1[:, m, 0:1], scalar=-1.0, in1=mv1[:, m, 1:2],
                                                         op0=ALU.mult, op1=ALU.mult), r=[MV1[m]], w=[MV1[m]])
            V("scalar", lambda h: h.activation(out=yo[:, j, :], in_=Rr[:, m, :], func=AF.Identity, scale=mv1[:, m, 1:2], bias=mv1[:, m, 0:1]),
              r=RB[m] + [MV1[m]], w=[YO[j]])

        def emit_E3(m):
            j = m % 2
            V("vector", lambda h: h.tensor_tensor(out=yo[:, j, :], in0=yo[:, j, :], in1=gA, op=ALU.mult), r=[YO[j], GA], w=[YO[j]])
            V("vector", lambda h: h.tensor_tensor(out=yo[:, j, :], in0=yo[:, j, :], in1=bA, op=ALU.add), r=[YO[j], BA], w=[YO[j]])
            final_tickets.append(fw.dma("sync", osem[j], out[m * 128:(m + 1) * 128, :], yo[:, j, :], r=[YO[j]]))

        def emit_E_step(k):
            if 0 <= k < NT:
                emit_E1(k)
            if 0 <= k - 1 < NT:
                emit_E2(k - 1)
            if 0 <= k - 2 < NT:
                emit_E3(k - 2)

        rcnt = 0
        for g in range(NG):
            s_ = g % 2
            if g + 1 < NG:
                load_ffw(g + 1)
            for blk in range(NBLK):
                for fc in range(4):
                    bk = nextbank()
                    for kc in range(8):
                        V("tensor", lambda h: h.matmul(banks[bk][:, :], wu[s_][:, kc, fc * 128:(fc + 1) * 128], h1T[:, kc, blk * 512:(blk + 1) * 512],
                                                       start=(kc == 0), stop=(kc == 7)),
                          r=[WU[s_]] + [H1T[kc][mm] for mm in range(blk * 4, blk * 4 + 4)], w=[BK[bk]], inc=(kc == 7))
                    q = rcnt % 2
                    rcnt += 1
                    V("scalar", lambda h: h.activation(out=rl[q], in_=banks[bk][:, :], func=AF.Relu), r=[BK[bk]], w=[RL[q]])
                    V("gpsimd", lambda h: h.tensor_tensor(out=actT[s_][:, fc, blk * 512:(blk + 1) * 512], in0=rl[q], in1=rl[q], op=ALU.mult),
                      r=[RL[q]], w=[ACTT[s_][fc][blk]])
            for m in range(NT):
                tok = slice(m * 128, (m + 1) * 128)
                for half in range(2):
                    cs = slice(half * 512, (half + 1) * 512)
                    bk = nextbank()
                    for fc in range(4):
                        V("tensor", lambda h: h.matmul(banks[bk][:, :], actT[s_][:, fc, tok], wd[s_][:, fc, cs], start=(fc == 0), stop=(fc == 3)),
                          r=[ACTT[s_][fc][m // 4], WD[s_]], w=[BK[bk]], inc=(fc == 3))
                    V("vector", lambda h: h.tensor_tensor(out=Rr[:, m, cs], in0=Rr[:, m, cs], in1=banks[bk][:, :], op=ALU.add), r=[RB[m][half], BK[bk]], w=[RB[m][half]])
                if g == NG - 1:
                    emit_E_step(m - 1)
        for k in range(NT - 1, NT + 2):
            emit_E_step(k)

        finish()
    return nc


_CACHE = {}


def _host_consts():
    c = np.zeros((128, NCST), np.float32)
    i = np.arange(128)
    c[:, C_ID:C_ID + 128] = np.eye(128, dtype=np.float32)
    c[:, C_ONE:C_ONE + 128] = 1.0
    same = (i[:, None] // 64) == (i[None, :] // 64)
    c[:, C_TRI:C_TRI + 128] = (same & (i[:, None] <= i[None, :])).astype(np.float32)
    c[:, C_BLK:C_BLK + 128] = same.astype(np.float32)
    c[:, C_NEGM:C_NEGM + 128] = np.where(same & (i[:, None] >= i[None, :]), 0.0, NEG)
    c[:, C_STR:C_STR + 128] = (same & (i[:, None] > i[None, :])).astype(np.float32)
    c[:, C_IND] = (i < 64)
    c[:, C_IND + 1] = (i >= 64)
    c[0:64, C_OAUG:C_OAUG + 128] = 1.0
    c[64, C_OAUG:C_OAUG + 128] = 64.0 * NORM_EPS
    c[:, C_M05] = -0.5
    c[:, C_M05 + 1] = NORM_EPS
    return c


def _prep_shared(inp):
    par = np.zeros((128, NPAR), np.float32)
    par[:, P_G0:P_G0 + 8] = inp["ln_in_g"].reshape(8, 128).T
    par[:, P_B0:P_B0 + 8] = inp["ln_in_b"].reshape(8, 128).T
    cw = inp["conv_w"][0]
    par[:, P_CW:P_CW + 48] = cw.T.reshape(12, 128, 4).transpose(1, 0, 2).reshape(128, 48)
    par[0:64, P_FOXG] = inp["fox_norm_g"][0]
    par[0:8, P_BF] = inp["b_f"][0]
    par[:, P_DTB:P_DTB + 64] = np.tile(inp["dt_bias"][0], 16)[None, :]
    par[:, P_ALOG:P_ALOG + 64] = np.tile(inp["a_log"][0], 16)[None, :]
    par[:, P_G1:P_G1 + 8] = inp["ln1_g"][0].reshape(8, 128).T
    par[:, P_B1:P_B1 + 8] = inp["ln1_b"][0].reshape(8, 128).T
    vecs = np.zeros((8, D), np.float32)
    vecs[0] = inp["ln_in_g"]
    vecs[1] = inp["ln_in_b"]
    vecs[2] = inp["ln1_g"][0]
    vecs[3] = inp["ln1_b"][0]
    vecs[4] = inp["ln2_g"][0]
    vecs[5] = inp["ln2_b"][0]
    vecs[6] = inp["b_ple_gate"][0]
    vecs[7, 0:128] = inp["gdn_norm_g"][0]
    return {
        "w_in": np.ascontiguousarray(inp["w_in"][0]), "w_out": np.ascontiguousarray(inp["w_out"][0]),
        "w_up": np.ascontiguousarray(inp["w_up"][0]), "w_down": np.ascontiguousarray(inp["w_down"][0]),
        "w_ple": np.ascontiguousarray(inp["w_ple"][0]), "w_gate": np.ascontiguousarray(inp["w_ple_gate"][0]),
        "cst": _host_consts(), "par": par, "vecs": vecs, "ones_rows": np.ones((3, 8 * T), np.float32),
    }


def run(inp, stop_after=None, dumps=(), cores=8):
    key = (stop_after, tuple(dumps))
    if key not in _CACHE:
        _CACHE[key] = build(stop_after, dumps)
    nc = _CACHE[key]
    inp = {k: np.asarray(v, dtype=np.float32) for k, v in inp.items()}
    shared = _prep_shared(inp)
    in_maps = []
    for b in range(cores):
        m = dict(shared)
        m["x"] = np.ascontiguousarray(inp["x"][b])
        m["p"] = np.ascontiguousarray(inp["p"][0, b])
        in_maps.append(m)
    res = run_bass_kernel_spmd(nc, in_maps, core_ids=list(range(cores)))
    return res.results


def kernel(**inputs):
    results = run(inputs)
    return np.stack([r["out"] for r in results], axis=0).astype(np.float32)
```

```python
import numpy as np
from contextlib import ExitStack
import concourse.bass as bass
import concourse.mybir as mybir
from concourse.bass_utils import run_bass_kernel_spmd

F32 = mybir.dt.float32
BF16 = mybir.dt.bfloat16
F32R = mybir.dt.float32r
AF = mybir.ActivationFunctionType
ALU = mybir.AluOpType

T, D, NT, NBLK = 2048, 1024, 16, 4
DFF = 4096
ALPHA = 2.0 ** 0.25
LN_EPS = 1e-5
NORM_EPS = 1e-6
NEG = -30000.0
DSZ = {F32: 4, BF16: 2, F32R: 4}

C_ID, C_ONE, C_TRI, C_BLK, C_NEGM, C_STR, C_IND, C_OAUG, C_M05, NCST = 0, 128, 256, 384, 512, 640, 768, 770, 898, 900
P_G0, P_B0, P_CW, P_FOXG, P_BF, P_DTB, P_ALOG, P_G1, P_B1, NPAR = 0, 8, 16, 64, 65, 66, 130, 194, 202, 210


class Buf:
    __slots__ = ("w", "r", "lock")

    def __init__(self, lock=None):
        self.w = None
        self.r = {}
        self.lock = lock


class Ticket:
    __slots__ = ("sem", "val", "key", "eng")

    def __init__(self, sem, val, key, eng):
        self.sem, self.val, self.key, self.eng = sem, val, key, eng


class Eng:
    def __init__(self, name, h, sem):
        self.name, self.h, self.sem = name, h, sem
        self.count = 0
        self.waited = {}


class FW:
    def __init__(self, nc, es):
        self.nc, self.es = nc, es
        self.E = {}
        for n in ("tensor", "vector", "scalar", "gpsimd", "sync"):
            sem = es.enter_context(nc.semaphore("s_" + n))
            self.E[n] = Eng(n, getattr(nc, n), sem)
        self.dsems = []

    def _wait(self, e, t):
        if e.waited.get(t.key, 0) >= t.val:
            return
        e.h.wait_ge(t.sem, t.val)
        e.waited[t.key] = t.val

    def _deps(self, e, reads, writes):
        for b in reads:
            t = b.w
            if t is not None and not (t.eng == e.name and e.name == "tensor"):
                self._wait(e, t)
        for b in writes:
            t = b.w
            if t is not None and (t.eng != e.name or (e.name != "tensor" and t.val <= e.count)):
                self._wait(e, t)
            for t in b.r.values():
                if t.eng != e.name:
                    self._wait(e, t)

    @staticmethod
    def _mark(t, reads, writes):
        for b in reads:
            b.r[t.key] = t
        for b in writes:
            b.w = t
            b.r = {}

    def op(self, eng, fn, r=(), w=(), inc=True):
        e = self.E[eng]
        locks = []
        for b in list(r) + list(w):
            if b.lock is not None and b.lock not in locks:
                locks.append(b.lock)
        for lk in locks:
            t = lk.w
            if t is not None and t.eng != e.name:
                self._wait(e, t)
        self._deps(e, r, w)
        ins = fn(e.h)
        if inc:
            e.count += 1
            ins.then_inc(e.sem, 1)
            t = Ticket(e.sem, e.count, "e_" + eng, eng)
        else:
            t = Ticket(e.sem, e.count + 1, "e_" + eng, eng)
        self._mark(t, r, w)
        for lk in locks:
            lk.w = t
        return t

    def dsem(self, name):
        sem = self.es.enter_context(self.nc.semaphore(name))
        d = [sem, 0, name]
        self.dsems.append(d)
        return d

    def dma(self, q, d, out, in_, r=(), w=()):
        e = self.E[q]
        self._deps(e, r, w)
        ins = e.h.dma_start(out=out, in_=in_)
        d[1] += 16
        ins.then_inc(d[0], 16)
        t = Ticket(d[0], d[1], "d_" + d[2], "dma")
        self._mark(t, r, w)
        return t

    def barrier(self):
        tl = [Ticket(e.sem, e.count, "e_" + n, n) for n, e in self.E.items() if e.count > 0]
        tl += [Ticket(d[0], d[1], "d_" + d[2], "dma") for d in self.dsems if d[1] > 0]
        for e in self.E.values():
            for t in tl:
                if t.eng != e.name:
                    self._wait(e, t)


class Arena:
    def __init__(self, ap, nbytes):
        self.ap = ap
        self.free = [(0, nbytes)]
        self.live = {}

    def alloc(self, name, nbytes):
        nbytes = (nbytes + 63) // 64 * 64
        for i, (o, s) in enumerate(self.free):
            if s >= nbytes:
                self.free[i] = (o + nbytes, s - nbytes)
                self.live[name] = (o, nbytes)
                return o
        raise RuntimeError("arena full for %s (%d) free=%s" % (name, nbytes, self.free))

    def release(self, *names):
        for name in names:
            o, s = self.live.pop(name)
            self.free.append((o, s))
        self.free.sort()
        m = []
        for o, s in self.free:
            if s == 0:
                continue
            if m and m[-1][0] + m[-1][1] == o:
                m[-1] = (m[-1][0], m[-1][1] + s)
            else:
                m.append((o, s))
        self.free = m

    def view(self, name, shape, dt):
        n = 1
        for s in shape:
            n *= s
        nb = n * DSZ[dt]
        o = self.alloc(name, nb)
        v = self.ap[:, o // 4:(o + nb) // 4]
        if dt != F32:
            v = v.bitcast(dt)
        if len(shape) == 2:
            v = v.rearrange("p (a b) -> p a b", a=shape[0])
        elif len(shape) == 3:
            v = v.rearrange("p (a b c) -> p a b c", a=shape[0], b=shape[1])
        elif len(shape) == 4:
            v = v.rearrange("p (a b c d) -> p a b c d", a=shape[0], b=shape[1], c=shape[2])
        return v


def build(stop_after=None, dumps=()):
    nc = bass.Bass("TRN2", target_bir_lowering=False)

    def din(name, shape):
        return nc.dram_tensor(name, list(shape), F32, kind="ExternalInput").ap()

    x = din("x", [T, D])
    p_in = din("p", [T, 256])
    w_in = din("w_in", [D, 3600])
    w_out = din("w_out", [D, D])
    w_up = din("w_up", [D, DFF])
    w_down = din("w_down", [DFF, D])
    w_ple = din("w_ple", [256, D])
    w_gate = din("w_gate", [D, D])
    cst_d = din("cst", [128, NCST])
    par_d = din("par", [128, NPAR])
    vec_d = din("vecs", [8, D])
    ones_d = din("ones_rows", [3, 8 * T])
    out = nc.dram_tensor("out", [T, D], F32, kind="ExternalOutput").ap()
    dump_out = {}
    for (nm, shp, dt) in dumps:
        dump_out[nm] = nc.dram_tensor("dbg_" + nm, list(shp), dt, kind="ExternalOutput").ap()

    w_in_v = w_in.rearrange("(kc p) c -> p kc c", p=128)

    with ExitStack() as es:
        fw = FW(nc, es)
        ARENA_BYTES = 204 * 1024
        arena_t = es.enter_context(nc.sbuf_tensor("arena", [128, ARENA_BYTES // 4], F32))
        ar = Arena(arena_t[:, :], ARENA_BYTES)
        banks = [es.enter_context(nc.psum_tensor("bank%d" % i, [128, 512], F32)) for i in range(8)]
        LK = [Buf() for _ in range(8)]
        BK = [Buf(LK[i]) for i in range(8)]
        rot = [0]

        def nextbank(lo=0, hi=8):
            b = lo + rot[0] % (hi - lo)
            rot[0] += 1
            return b

        def V(eng, fn, r=(), w=(), inc=True):
            return fw.op(eng, fn, r, w, inc)

        dctr = [0]

        def newd():
            dctr[0] += 1
            return fw.dsem("d%d" % dctr[0])

        final_tickets = []

        def do_dumps(stage):
            for (nm, shp, dt) in dumps:
                if nm in dumpsrc and dumpsrc[nm][2] == stage:
                    src, bufs, _ = dumpsrc[nm]
                    final_tickets.append(fw.dma("sync", newd(), dump_out[nm], src, r=bufs))

        dumpsrc = {}

        def finish():
            for t in final_tickets:
                fw._wait(fw.E["sync"], t)

        cst = ar.view("cst", [NCST], F32)
        par = ar.view("par", [NPAR], F32)
        CST, PAR = Buf(), Buf()
        fw.dma("sync", newd(), cst, cst_d, w=[CST])
        fw.dma("sync", newd(), par, par_d, w=[PAR])
        identf = cst[:, C_ID:C_ID + 128]
        onesf = cst[:, C_ONE:C_ONE + 128]
        m05 = cst[:, C_M05:C_M05 + 1]
        eps_col = cst[:, C_M05 + 1:C_M05 + 2]
        EPSB = CST
        identb = ar.view("identb", [128], BF16)
        IDB = Buf()
        V("vector", lambda h: h.tensor_copy(out=identb, in_=identf), r=[CST], w=[IDB])
        mv = ar.view("mv", [NT, 2], F32)
        MV = [Buf() for _ in range(NT)]

        OFF_FOX = 2056
        wsl = [ar.view("wsl%d" % i, [8, 512], BF16) for i in range(3)]
        WSL = [Buf() for _ in range(3)]
        wsem = [newd() for _ in range(3)]
        wsmall = ar.view("wsmall", [8, 8], BF16)
        WSM = Buf()
        wsm_sem = newd()
        fw.dma("gpsimd", wsem[0], wsl[0], w_in_v[:, :, OFF_FOX:OFF_FOX + 512], w=[WSL[0]])
        fw.dma("gpsimd", wsem[1], wsl[1], w_in_v[:, :, OFF_FOX + 512:OFF_FOX + 1024], w=[WSL[1]])
        fw.dma("gpsimd", wsem[2], wsl[2], w_in_v[:, :, OFF_FOX + 1024:OFF_FOX + 1536], w=[WSL[2]])
        fw.dma("gpsimd", wsm_sem, wsmall, w_in_v[:, :, 3592:3600], w=[WSM])
        hT = ar.view("hT", [8, T], BF16)
        HT = [[Buf() for _ in range(NBLK)] for _ in range(8)]
        xt = ar.view("xt", [8, D], F32)
        XT = [Buf() for _ in range(8)]
        xsem = [newd() for _ in range(8)]
        st = ar.view("st", [8, 12], F32)
        STB = [Buf() for _ in range(8)]

        def ln_stats(src, SRC, j, mvap, MVB):
            V("vector", lambda h: h.bn_stats(out=st[:, j, 0:6], in_=src[:, 0:512]), r=[SRC], w=[STB[j]])
            V("vector", lambda h: h.bn_stats(out=st[:, j, 6:12], in_=src[:, 512:1024]), r=[SRC], w=[STB[j]])
            V("vector", lambda h: h.bn_aggr(out=mvap[:, 0:2], in_=st[:, j, 0:12]), r=[STB[j]], w=[MVB])
            V("vector", lambda h: h.tensor_scalar_add(out=mvap[:, 1:2], in0=mvap[:, 1:2], scalar1=LN_EPS), r=[MVB], w=[MVB])
            V("gpsimd", lambda h: h.tensor_tensor(out=mvap[:, 1:2], in0=mvap[:, 1:2], in1=m05, op=ALU.pow), r=[MVB, CST], w=[MVB])

        for g in range(NBLK):
            for j4 in range(4):
                m = 4 * g + j4
                j = m % 8
                fw.dma("sync", xsem[j], xt[:, j, :], x[m * 128:(m + 1) * 128, :], w=[XT[j]])
                ln_stats(xt[:, j, :], XT[j], j, mv[:, m, :], MV[m])
                V("vector", lambda h: h.tensor_scalar(out=xt[:, j, :], in0=xt[:, j, :], scalar1=mv[:, m, 0:1], scalar2=mv[:, m, 1:2],
                                                      op0=ALU.subtract, op1=ALU.mult), r=[XT[j], MV[m]], w=[XT[j]])
            for c in range(8):
                bk = nextbank()
                for j4 in range(4):
                    j = (4 * g + j4) % 8
                    V("tensor", lambda h: h.transpose(banks[bk][:, j4 * 128:(j4 + 1) * 128], xt[:, j, c * 128:(c + 1) * 128], identf),
                      r=[XT[j], CST], w=[BK[bk]], inc=(j4 == 3))
                V("scalar", lambda h: h.activation(out=hT[:, c, g * 512:(g + 1) * 512], in_=banks[bk][:, :], func=AF.Identity,
                                                   scale=par[:, P_G0 + c:P_G0 + c + 1], bias=par[:, P_B0 + c:P_B0 + c + 1]),
                  r=[BK[bk], PAR], w=[HT[c][g]])
        dumpsrc["hT"] = (hT, [b for row in HT for b in row], "A1")
        do_dumps("A1")
        if stop_after == "A1":
            finish()
            return nc

        fw.barrier()
        ar.release("xt", "st")
        fq = ar.view("fq", [8, T], BF16)
        fk = ar.view("fk", [8, T], BF16)
        FQ = [Buf() for _ in range(8)]
        FK = [Buf() for _ in range(8)]
        vaug = ar.view("vaug", [NT, 8, 65], BF16)
        VAUG = Buf()
        for h8 in range(8):
            pass
        V("gpsimd", lambda h: h.memset(fq[64:128, :, :], 0.0), w=FQ)
        V("vector", lambda h: h.memset(fk[64:128, :, :], 0.0), w=FK)
        fw.dma("gpsimd", newd(), fq[67:70, :, :], ones_d.rearrange("r (h t) -> r h t", h=8), w=FQ)
        fw.dma("gpsimd", newd(), fk[64:67, :, :], ones_d.rearrange("r (h t) -> r h t", h=8), w=FK)
        V("gpsimd", lambda h: h.memset(vaug[:, :, :, 64:65], 1.0), w=[VAUG])

        rowA = ar.view("rowA", [T], F32)
        rowC = ar.view("rowC", [T], F32)
        rparts = ar.view("rparts", [6, T], BF16)
        nbf = ar.view("nbf", [1], F32)
        RA, RC, RP, NBF = Buf(), Buf(), Buf(), Buf()
        V("vector", lambda h: h.tensor_scalar_mul(out=nbf[0:8, :], in0=par[0:8, P_BF:P_BF + 1], scalar1=-1.0), r=[PAR], w=[NBF])
        for blk in range(NBLK):
            bk = nextbank()
            for kc in range(8):
                V("tensor", lambda h: h.matmul(banks[bk][0:8, :], wsmall[:, kc, :], hT[:, kc, blk * 512:(blk + 1) * 512],
                                               start=(kc == 0), stop=(kc == 7)), r=[WSM, HT[kc][blk]], w=[BK[bk]], inc=(kc == 7))
            V("scalar", lambda h: h.activation(out=rowA[0:8, blk * 512:(blk + 1) * 512], in_=banks[bk][0:8, :], func=AF.Exp,
                                               scale=-1.0, bias=nbf[0:8, :]), r=[BK[bk], NBF], w=[RA])
        V("scalar", lambda h: h.activation(out=rowA[0:8, :], in_=rowA[0:8, :], func=AF.Ln, bias=1.0), r=[RA], w=[RA])
        V("vector", lambda h: h.tensor_tensor_scan(out=rowC[0:8, :], data0=cst[0:8, C_ONE:C_ONE + 1].to_broadcast([8, T]),
                                                   data1=rowA[0:8, :], initial=0.0, op0=ALU.mult, op1=ALU.subtract),
          r=[RA, CST], w=[RC])
        dumpsrc["crow"] = (rowC[0:8, :], [RC], "AF")
        V("vector", lambda h: h.tensor_copy(out=rparts[0:8, 0, :], in_=rowC[0:8, :]), r=[RC], w=[RP])
        V("vector", lambda h: h.tensor_tensor(out=rowA[0:8, :], in0=rowC[0:8, :], in1=rparts[0:8, 0, :], op=ALU.subtract), r=[RC, RP], w=[RA])
        V("vector", lambda h: h.tensor_copy(out=rparts[0:8, 1, :], in_=rowA[0:8, :]), r=[RA], w=[RP])
        V("vector", lambda h: h.tensor_tensor(out=rowA[0:8, :], in0=rowA[0:8, :], in1=rparts[0:8, 1, :], op=ALU.subtract), r=[RA, RP], w=[RA])
        V("vector", lambda h: h.tensor_copy(out=rparts[0:8, 2, :], in_=rowA[0:8, :]), r=[RA], w=[RP])
        V("vector", lambda h: h.tensor_scalar_mul(out=rparts[0:8, 3:6, :], in0=rparts[0:8, 0:3, :], scalar1=-1.0), r=[RP], w=[RP])
        for h8 in range(8):
            dq = newd()
            fw.dma("sync", dq, fq[64:67, h8, :], rparts[h8:h8 + 1, 0:3, :], r=[RP], w=[FQ[h8]])
            dk_ = newd()
            fw.dma("sync", dk_, fk[67:70, h8, :], rparts[h8:h8 + 1, 3:6, :], r=[RP], w=[FK[h8]])

        stg = ar.view("stg", [4, 512], BF16)
        STG = [Buf() for _ in range(4)]
        stgsem = [newd() for _ in range(4)]
        cnt = 0
        for (dst, DST, ws, WS, sc) in ((fq, FQ, wsl[0], WSL[0], 0.125), (fk, FK, wsl[1], WSL[1], 1.0)):
            for c4 in range(4):
                for blk in range(NBLK):
                    bk = nextbank()
                    for kc in range(8):
                        V("tensor", lambda h: h.matmul(banks[bk][:, :], ws[:, kc, c4 * 128:(c4 + 1) * 128], hT[:, kc, blk * 512:(blk + 1) * 512],
                                                       start=(kc == 0), stop=(kc == 7)), r=[WS, HT[kc][blk]], w=[BK[bk]], inc=(kc == 7))
                    he, ho = 2 * c4, 2 * c4 + 1
                    sl = cnt % 4
                    cnt += 1
                    cs = slice(blk * 512, (blk + 1) * 512)
                    V("scalar", lambda h: h.activation(out=dst[0:64, he, cs], in_=banks[bk][0:64, :], func=AF.Copy, scale=sc), r=[BK[bk]], w=[DST[he]])
                    V("vector", lambda h: h.tensor_scalar_mul(out=stg[64:128, sl, :], in0=banks[bk][64:128, :], scalar1=sc), r=[BK[bk]], w=[STG[sl]])
                    fw.dma("sync", stgsem[sl], dst[0:64, ho, cs], stg[64:128, sl, :], r=[STG[sl]], w=[DST[ho]])
        for m in range(NT):
            bk = nextbank()
            for kc in range(8):
                V("tensor", lambda h: h.matmul(banks[bk][:, :], hT[:, kc, m * 128:(m + 1) * 128], wsl[2][:, kc, :],
                                               start=(kc == 0), stop=(kc == 7)), r=[WSL[2], HT[kc][m // 4]], w=[BK[bk]], inc=(kc == 7))
            src = banks[bk][:, :].rearrange("p (h d) -> p h d", h=8)
            if m % 2 == 0:
                V("scalar", lambda h: h.copy(out=vaug[:, m, :, 0:64], in_=src), r=[BK[bk]], w=[VAUG])
            else:
                V("vector", lambda h: h.tensor_copy(out=vaug[:, m, :, 0:64], in_=src), r=[BK[bk]], w=[VAUG])
        dumpsrc["fq"] = (fq, FQ, "AF")
        dumpsrc["fk"] = (fk, FK, "AF")
        dumpsrc["vaug"] = (vaug, [VAUG], "AF")
        do_dumps("AF")
        if stop_after == "AF":
            finish()
            return nc

        fw.barrier()
        ar.release("rowA", "rowC", "rparts", "wsmall", "nbf", "stg")
        wg = ar.view("wg", [8, 8], BF16)
        WG = Buf()
        for i in range(3):
            fw.dma("gpsimd", wsem[i], wsl[i], w_in_v[:, :, i * 512:(i + 1) * 512], w=[WSL[i]])
        fw.dma("gpsimd", wsm_sem, wg, w_in_v[:, :, 2048:2056], w=[WG])
        ofox = ar.view("ofox", [4, T], BF16)
        ostg = ar.view("ostg", [2, 512], BF16)
        OSTG = [Buf(), Buf()]
        ostg_sem = [newd(), newd()]
        ostg_ctr = [0]
        OFOX = [Buf() for _ in range(8)]
        pT = [ar.view("pT%d" % i, [512], BF16) for i in range(4)]
        PT = [Buf() for _ in range(4)]
        sq = [ar.view("sq%d" % i, [512], F32) for i in range(2)]
        rr = [ar.view("rr%d" % i, [512], F32) for i in range(2)]
        oaug = cst[:, C_OAUG:C_OAUG + 128]
        SQ, RR, OAUG = [Buf(), Buf()], [Buf(), Buf()], CST
        def pe_warm(n, bk_, lhs, rhs, RB_):
            for _ in range(n):
                V("tensor", lambda h: h.matmul(banks[bk_][:, :], lhs, rhs, start=True, stop=True), r=RB_, w=[BK[bk_]], inc=False)

        pe_warm(0, 5, fk[0:64, 0, 0:128], fq[0:64, 0, 0:512], [FK[0], FQ[0]])
        NFF, NFW = 0, 256
        pslot = [0]
        gp = [0]
        pend = []
        fsl = [0]

        def fin_step(it):
            stp, h8_, qb_, po_ = it[1], it[2], it[3], it[4]
            if stp == 0:
                k_ = fsl[0] % 2
                fsl[0] += 1
                it.append(k_)
                V("scalar", lambda h: h.activation(out=sq[k_][0:65, :], in_=banks[po_][0:65, :], func=AF.Square), r=[BK[po_]], w=[SQ[k_]])
                V("tensor", lambda h: h.matmul(banks[5][:, :], oaug[0:65, :], sq[k_][0:65, :], start=True, stop=True),
                  r=[OAUG, SQ[k_]], w=[BK[5]])
                it[0] = gp[0] + 2
                it[1] = 1
            else:
                k_ = it[5]
                V("scalar", lambda h: h.activation(out=rr[k_][0:64, :], in_=banks[5][0:64, :], func=AF.Ln, scale=1.0 / 64.0), r=[BK[5]], w=[RR[k_]])
                V("scalar", lambda h: h.activation(out=rr[k_][0:64, :], in_=rr[k_][0:64, :], func=AF.Exp, scale=-0.5), r=[RR[k_]], w=[RR[k_]])
                cs_ = slice(qb_ * 512, (qb_ + 1) * 512)
                if h8_ % 2 == 0:
                    V("vector", lambda h: h.scalar_tensor_tensor(out=ofox[0:64, h8_ // 2, cs_], in0=banks[po_][0:64, :],
                                                                 scalar=par[0:64, P_FOXG:P_FOXG + 1], in1=rr[k_][0:64, :],
                                                                 op0=ALU.mult, op1=ALU.mult), r=[BK[po_], RR[k_], PAR], w=[OFOX[h8_]])
                else:
                    sl_ = ostg_ctr[0] % 2
                    ostg_ctr[0] += 1
                    V("vector", lambda h: h.scalar_tensor_tensor(out=ostg[0:64, sl_, :], in0=banks[po_][0:64, :],
                                                                 scalar=par[0:64, P_FOXG:P_FOXG + 1], in1=rr[k_][0:64, :],
                                                                 op0=ALU.mult, op1=ALU.mult), r=[BK[po_], RR[k_], PAR], w=[OSTG[sl_]])
                    fw.dma("sync", ostg_sem[sl_], ofox[64:128, h8_ // 2, cs_], ostg[0:64, sl_, :], r=[OSTG[sl_]], w=[OFOX[h8_]])
                pend.remove(it)

        for h8 in range(8):
            pairs = []
            for qb in range(NBLK):
                for kb in range(4 * (qb + 1)):
                    pairs.append((qb, kb))
            sbank = {}

            def emit_qk(i):
                qb, kb = pairs[i]
                j = kb - 4 * qb
                qlo = max(j, 0) * 128
                bk = nextbank(0, 4)
                sbank[i] = bk
                V("tensor", lambda h: h.matmul(banks[bk][:, qlo:512], fk[:, h8, kb * 128:(kb + 1) * 128],
                                               fq[:, h8, qb * 512 + qlo:(qb + 1) * 512], start=True, stop=True),
                  r=[FK[h8], FQ[h8]], w=[BK[bk]])

            LA = 3
            for i0 in range(min(LA, len(pairs))):
                emit_qk(i0)
            for i, (qb, kb) in enumerate(pairs):
                if i + LA < len(pairs):
                    emit_qk(i + LA)
                for _ in range(NFF):
                    V("tensor", lambda h: h.matmul(banks[4][:, 0:NFW], identb, fq[:, 0, 0:NFW], start=True, stop=True), r=[IDB], w=[BK[4]], inc=False)
                j = kb - 4 * qb
                qlo = max(j, 0) * 128
                bk = sbank.pop(i)
                s = pslot[0] % 4
                pslot[0] += 1
                po = 6 + (h8 * NBLK + qb) % 2
                V("scalar", lambda h: h.activation(out=pT[s][:, qlo:512], in_=banks[bk][:, qlo:512], func=AF.Exp), r=[BK[bk]], w=[PT[s]])
                if j >= 0:
                    V("gpsimd", lambda h: h.affine_select(out=pT[s][:, qlo:qlo + 128], in_=pT[s][:, qlo:qlo + 128], pattern=[[1, 128]],
                                                          compare_op=ALU.is_ge, fill=0.0, base=0, channel_multiplier=-1),
                      r=[PT[s]], w=[PT[s]])
                last = (kb == 4 * (qb + 1) - 1)
                V("tensor", lambda h: h.matmul(banks[po][0:65, qlo:512], vaug[:, kb, h8, :], pT[s][:, qlo:512],
                                               start=(kb == 0), stop=last), r=[VAUG, PT[s]], w=[BK[po]], inc=last)
                if last:
                    pend.append([gp[0] + 2, 0, h8, qb, po])
                gp[0] += 1
                for it in list(pend):
                    if it[0] <= gp[0]:
                        fin_step(it)
        while pend:
            for it in list(pend):
                fin_step(it)
        dumpsrc["ofox"] = (ofox, OFOX, "F")
        do_dumps("F")
        if stop_after == "F":
            finish()
            return nc
        fw.barrier()
        ar.release("fq", "fk", "vaug", "pT0", "pT1", "pT2", "pT3", "sq0", "sq1", "rr0", "rr1", "ostg")

        qkvT = ar.view("qkvT", [12, T], BF16)
        QKV = [Buf() for _ in range(12)]
        sz = ar.view("sz", [4, T], BF16)
        SZ = [Buf() for _ in range(4)]
        pcb = ar.view("pcb", [4, 516], F32)
        PC = [Buf() for _ in range(4)]
        accb = ar.view("accb", [4, 512], F32)
        ACC = [Buf() for _ in range(4)]
        gpre = ar.view("gpre", [NT, 8], F32)
        GPRE = Buf()
        beta = ar.view("beta", [NT, 4], F32)
        lg = ar.view("lg", [NT, 4], F32)
        gt1 = ar.view("gt1", [NT, 4], F32)
        gt2 = ar.view("gt2", [NT, 4], F32)
        BETA, LG, GT1, GT2 = Buf(), Buf(), Buf(), Buf()
        cw = par[:, P_CW:P_CW + 48].rearrange("p (c j) -> p c j", j=4)

        def ag_E(u):
            ch, blk = u // 4, u % 4
            grp, hc = ch // 4, ch % 4
            ws, WS = wsl[grp], WSL[grp]
            sl = u % 4
            if u == 16:
                pass
            bk = nextbank()
            for kc in range(8):
                V("tensor", lambda h: h.matmul(banks[bk][:, :], ws[:, kc, hc * 128:(hc + 1) * 128], hT[:, kc, blk * 512:(blk + 1) * 512],
                                               start=(kc == 0), stop=(kc == 7)), r=[WS, HT[kc][blk]], w=[BK[bk]], inc=(kc == 7))
            if blk == 0:
                V("gpsimd", lambda h: h.memset(pcb[:, sl, 0:3], 0.0), w=[PC[sl]])
            else:
                V("gpsimd", lambda h: h.tensor_copy(out=pcb[:, sl, 0:3], in_=pcb[:, (u - 1) % 4, 512:515]), r=[PC[(u - 1) % 4]], w=[PC[sl]])
            V("scalar", lambda h: h.copy(out=pcb[:, sl, 3:515], in_=banks[bk][:, :]), r=[BK[bk]], w=[PC[sl]])

        def ag_T(u):
            ch = u // 4
            sl = u % 4
            V("scalar", lambda h: h.activation(out=accb[:, sl, :], in_=pcb[:, sl, 0:512], func=AF.Copy, scale=cw[:, ch, 0:1]), r=[PC[sl], PAR], w=[ACC[sl]])
            for j in range(1, 4):
                V("vector", lambda h: h.scalar_tensor_tensor(out=accb[:, sl, :], in0=pcb[:, sl, j:j + 512], scalar=cw[:, ch, j:j + 1], in1=accb[:, sl, :],
                                                             op0=ALU.mult, op1=ALU.add), r=[PC[sl], PAR, ACC[sl]], w=[ACC[sl]])

        def ag_S(u):
            ch, blk = u // 4, u % 4
            sl = u % 4
            V("scalar", lambda h: h.activation(out=qkvT[:, ch, blk * 512:(blk + 1) * 512], in_=accb[:, sl, :], func=AF.Silu), r=[ACC[sl]], w=[QKV[ch]])

        NU = 48
        for u in range(NU + 2):
            if u < NU:
                ag_E(u)
                if u == 16:
                    fw.dma("gpsimd", wsem[0], wsl[0], w_in_v[:, :, 1536:2048], w=[WSL[0]])
            if 0 <= u - 1 < NU:
                ag_T(u - 1)
            if 0 <= u - 2 < NU:
                ag_S(u - 2)
        for hc in range(4):
            for blk in range(NBLK):
                bk = nextbank()
                for kc in range(8):
                    V("tensor", lambda h: h.matmul(banks[bk][:, :], wsl[0][:, kc, hc * 128:(hc + 1) * 128], hT[:, kc, blk * 512:(blk + 1) * 512],
                                                   start=(kc == 0), stop=(kc == 7)), r=[WSL[0], HT[kc][blk]], w=[BK[bk]], inc=(kc == 7))
                V("scalar", lambda h: h.activation(out=sz[:, hc, blk * 512:(blk + 1) * 512], in_=banks[bk][:, :], func=AF.Silu), r=[BK[bk]], w=[SZ[hc]])
        for m in range(NT):
            bk = nextbank()
            for kc in range(8):
                V("tensor", lambda h: h.matmul(banks[bk][:, 0:8], hT[:, kc, m * 128:(m + 1) * 128], wg[:, kc, :],
                                               start=(kc == 0), stop=(kc == 7)), r=[WG, HT[kc][m // 4]], w=[BK[bk]], inc=(kc == 7))
            V("scalar", lambda h: h.copy(out=gpre[:, m, :], in_=banks[bk][:, 0:8]), r=[BK[bk]], w=[GPRE])
        dtb = par[:, P_DTB:P_DTB + 64].rearrange("p (m h) -> p m h", h=4)
        alog = par[:, P_ALOG:P_ALOG + 64].rearrange("p (m h) -> p m h", h=4)
        V("scalar", lambda h: h.activation(out=gt1, in_=gpre[:, :, 0:4], func=AF.Exp, scale=-1.0), r=[GPRE], w=[GT1])
        V("vector", lambda h: h.tensor_scalar_add(out=gt1, in0=gt1, scalar1=1.0), r=[GT1], w=[GT1])
        V("vector", lambda h: h.reciprocal(out=beta, in_=gt1), r=[GT1], w=[BETA])
        V("vector", lambda h: h.tensor_tensor(out=gt2, in0=gpre[:, :, 4:8], in1=dtb, op=ALU.add), r=[GPRE, PAR], w=[GT2])
        V("scalar", lambda h: h.activation(out=gt2, in_=gt2, func=AF.Exp), r=[GT2], w=[GT2])
        V("scalar", lambda h: h.activation(out=gt2, in_=gt2, func=AF.Ln, bias=1.0), r=[GT2], w=[GT2])
        V("scalar", lambda h: h.activation(out=gt1, in_=alog, func=AF.Exp), r=[PAR, GT1, BETA], w=[GT1])
        V("vector", lambda h: h.scalar_tensor_tensor(out=lg, in0=gt2, scalar=-1.0, in1=gt1, op0=ALU.mult, op1=ALU.mult), r=[GT1, GT2], w=[LG])
        dumpsrc["qkvT"] = (qkvT, QKV, "AG")
        dumpsrc["sz"] = (sz, SZ, "AG")
        dumpsrc["beta"] = (beta, [BETA], "AG")
        dumpsrc["lg"] = (lg, [LG], "AG")
        do_dumps("AG")
        if stop_after == "AG":
            finish()
            return nc
        fw.barrier()
        ar.release("hT", "wsl0", "wsl1", "wsl2", "wg", "pcb", "accb", "gpre", "gt1", "gt2")

        ogT = ar.view("ogT", [4, T], BF16)
        OGT = [Buf() for _ in range(4)]
        BQ = [[Buf(LK[i]) for _ in range(4)] for i in range(8)]

        def qap(b, q):
            return banks[b][:, q * 128:(q + 1) * 128]

        def flat(v):
            return v.rearrange("p h d -> p (h d)")

        def hb():
            return [Buf() for _ in range(4)]

        gam = ar.view("gam", [NT, 4], F32)
        egam = ar.view("egam", [NT, 4], F32)
        edec = ar.view("edec", [NT, 4], F32)
        glb = ar.view("glb", [2, 64], F32)
        X2 = ar.view("X2", [2, 64], F32)
        GAM, EGAM, EDEC, GLB, X2B = Buf(), Buf(), Buf(), Buf(), Buf()
        gbc = ar.view("gbc", [128], F32)
        GBC = Buf()
        fw.dma("sync", newd(), gbc, vec_d[7:8, 0:128].partition_broadcast(128), w=[GBC])
        negm = cst[:, C_NEGM:C_NEGM + 128]
        negmb = ar.view("negmb", [128], BF16)
        NEGMB = Buf()
        V("vector", lambda h: h.tensor_copy(out=negmb, in_=negm), r=[CST], w=[NEGMB])
        strict = cst[:, C_STR:C_STR + 128]
        lgf = lg.rearrange("p m h -> p (m h)")
        b0 = nextbank()
        V("tensor", lambda h: h.matmul(banks[b0][:, 0:64], cst[:, C_TRI:C_TRI + 128], lgf, start=True, stop=True), r=[CST, LG], w=[BQ[b0][0]])
        V("tensor", lambda h: h.matmul(banks[b0][:, 128:192], cst[:, C_BLK:C_BLK + 128], lgf, start=True, stop=True), r=[CST, LG], w=[BQ[b0][1]])
        V("vector", lambda h: h.tensor_copy(out=gam.rearrange("p m h -> p (m h)"), in_=banks[b0][:, 0:64]), r=[BQ[b0][0]], w=[GAM])
        V("scalar", lambda h: h.activation(out=egam.rearrange("p m h -> p (m h)"), in_=banks[b0][:, 0:64], func=AF.Exp), r=[BQ[b0][0]], w=[EGAM])
        V("vector", lambda h: h.tensor_tensor(out=edec.rearrange("p m h -> p (m h)"), in0=banks[b0][:, 128:192],
                                              in1=gam.rearrange("p m h -> p (m h)"), op=ALU.subtract), r=[BQ[b0][1], GAM], w=[EDEC])
        V("scalar", lambda h: h.activation(out=edec, in_=edec, func=AF.Exp), r=[EDEC], w=[EDEC])
        for hf in range(2):
            V("vector", lambda h: h.tensor_scalar_mul(out=X2[:, hf, :], in0=lgf, scalar1=cst[:, C_IND + hf:C_IND + hf + 1]), r=[LG, CST], w=[X2B])
        V("tensor", lambda h: h.matmul(banks[b0][:, 256:384], onesf, X2.rearrange("p a b -> p (a b)"), start=True, stop=True), r=[CST, X2B], w=[BQ[b0][2]])
        V("scalar", lambda h: h.activation(out=glb.rearrange("p a b -> p (a b)"), in_=banks[b0][:, 256:384], func=AF.Exp), r=[BQ[b0][2]], w=[GLB])

        dumpsrc["gam"] = (gam, [GAM], "G0")
        dumpsrc["glb"] = (glb, [GLB], "G0")
        if stop_after == "G0":
            do_dumps("G0")
            finish()
            return nc
        def hb3(n):
            return [[Buf() for _ in range(4)] for _ in range(n)]

        def v2(name, n, dt):
            return [ar.view("%s%d" % (name, i), [4, 128], dt) for i in range(n)]

        NPI, NSH = 3, 4
        kbt = v2("kbt", NPI, BF16); KBT = hb3(NPI)
        vbt = v2("vbt", NPI, BF16); VBT = hb3(NPI)
        ktv = [ar.view("ktv%d" % i, [8, 128], BF16) for i in range(NPI)]; KTV = hb3(NPI)
        diagc = v2("diagc", NPI, F32); DIAGC = hb3(NPI)
        Dm = v2("Dm", NPI, F32); DM = hb3(NPI)
        DmS = v2("DmS", NPI, F32); DMS = hb3(NPI)
        Am = [v2("Am%d_" % i, NPI, BF16) for i in range(2)]; AMB = [hb3(NPI), hb3(NPI)]
        Bm = [v2("Bm%d_" % i, NPI, BF16) for i in range(2)]; BMB = [hb3(NPI), hb3(NPI)]
        ac = v2("ac", NPI, BF16); AC = hb3(NPI)
        Tt32 = v2("Tt32_", NPI, F32); TT32 = hb3(NPI)
        Ttb = [v2("Ttb%d_" % i, NPI, BF16) for i in range(2)]; TTB = [hb3(NPI), hb3(NPI)]
        kdt = v2("kdt", NSH, BF16); KDT = hb3(NSH)
        acT = v2("acT", NSH, BF16); ACTB = hb3(NSH)
        u_sb = v2("u_sb", NSH, F32); US = hb3(NSH)
        wT_sb = v2("wT_sb", NSH, BF16); WT = hb3(NSH)
        vnew = ar.view("vnew", [4, 128], BF16); VN = hb()
        tmpo = ar.view("tmpo", [4, 128], F32); TMPO = hb()
        otok = ar.view("otok", [4, 128], F32); OTOK = hb()
        on = ar.view("on", [4, 128], BF16); ON = hb()
        junkt = ar.view("junk", [8, 128], F32); JUNKS = [Buf() for _ in range(8)]
        junk2t = ar.view("junk2", [4, 128], F32); JUNK2S = [Buf() for _ in range(4)]
        jctr = [0]

        def njunk():
            jctr[0] += 1
            return jctr[0] % 8
        S32 = ar.view("S32", [4, 128], F32); S32B = hb()
        Sp = ar.view("Sp", [4, 128], F32); SPB = hb()
        Sbf = ar.view("Sbf", [4, 128], BF16); SBF = hb()
        sm = ar.view("sm", [NSH, 12, 4], F32)
        SM = [Buf() for _ in range(NSH)]
        SM2 = [Buf() for _ in range(NSH)]
        V("vector", lambda h: h.memset(flat(S32), 0.0), w=S32B)
        V("vector", lambda h: h.memset(flat(Sbf), 0.0), w=SBF)
        bWS, bOI, bO2, bSD, bOT = 4, 5, 6, 7, 4

        def bq(b, hh):
            return banks[b][:, :].bitcast(BF16)[:, hh * 256:hh * 256 + 128]

        def bqall(b):
            return banks[b][:, :].bitcast(BF16).rearrange("p (h c) -> p h c", h=4)[:, :, 0:128]

        def pbank():
            return nextbank(0, 4)

        def gen_prep(m):
            tok = slice(m * 128, (m + 1) * 128)
            pi = m % NPI
            p3 = m % NSH
            rk, rq, bkk, skb, skd, sa, so, lnk, crow, sso = [sm[:, p3, i, :] for i in range(10)]
            S_ = [SM[p3]]
            X, Y = pbank(), pbank()
            t1 = banks[X][:, :].bitcast(BF16)
            for hh in range(4):
                kq = t1[:, hh * 256:hh * 256 + 128]
                vq = t1[:, hh * 256 + 128:hh * 256 + 256]
                V("tensor", lambda h: h.transpose(kq, qkvT[:, 4 + hh, tok], identb), r=[QKV[4 + hh], IDB], w=[BQ[X][hh]], inc=False)
                V("tensor", lambda h: h.transpose(vq, qkvT[:, 8 + hh, tok], identb), r=[QKV[8 + hh], IDB], w=[BQ[X][hh]], inc=False)
                V("tensor", lambda h: h.transpose(bq(Y, hh), qkvT[:, hh, tok], identb), r=[QKV[hh], IDB], w=[BQ[Y][hh]])
            for hh in range(4):
                kq = t1[:, hh * 256:hh * 256 + 128]
                j1, j2 = njunk(), njunk()
                V("scalar", lambda h: h.activation(out=junkt[:, j1, :], in_=kq, func=AF.Square, accum_out=rk[:, hh:hh + 1]), r=[BQ[X][hh]], w=[JUNKS[j1], SM[p3]])
                V("scalar", lambda h: h.activation(out=junkt[:, j2, :], in_=bq(Y, hh), func=AF.Square, accum_out=rq[:, hh:hh + 1]), r=[BQ[Y][hh]], w=[JUNKS[j2], SM[p3]])
            V("vector", lambda h: h.tensor_copy(out=ktv[pi].rearrange("p a b -> p (a b)"), in_=t1), r=BQ[X], w=KTV[pi])
            yield
            rkq = sm[:, p3, 0:2, :]
            V("scalar", lambda h: h.activation(out=rkq, in_=rkq, func=AF.Ln, bias=eps_col[:, 0:1]), r=S_ + [EPSB], w=S_)
            V("vector", lambda h: h.scalar_tensor_tensor(out=crow, in0=rk, scalar=-0.5, in1=gam[:, m, :], op0=ALU.mult, op1=ALU.subtract), r=S_ + [GAM], w=S_)
            V("scalar", lambda h: h.activation(out=rkq, in_=rkq, func=AF.Exp, scale=-0.5), r=S_, w=S_)
            V("vector", lambda h: h.tensor_tensor(out=bkk, in0=rk, in1=beta[:, m, :], op=ALU.mult), r=S_ + [BETA], w=S_)
            V("vector", lambda h: h.tensor_tensor(out=skb, in0=bkk, in1=egam[:, m, :], op=ALU.mult), r=S_ + [EGAM], w=S_)
            V("vector", lambda h: h.tensor_tensor(out=skd, in0=rk, in1=edec[:, m, :], op=ALU.mult), r=S_ + [EDEC], w=S_)
            V("vector", lambda h: h.tensor_scalar_mul(out=sa, in0=rq, scalar1=128.0 ** -0.5), r=S_, w=S_)
            V("vector", lambda h: h.tensor_tensor(out=so, in0=sa, in1=egam[:, m, :], op=ALU.mult), r=S_ + [EGAM], w=S_)
            yield
            for hh in range(4):
                kq = ktv[pi][:, 2 * hh, :]
                vq = ktv[pi][:, 2 * hh + 1, :]
                V("scalar", lambda h: h.activation(out=diagc[pi][:, hh, :], in_=identf, func=AF.Copy, scale=crow[:, hh:hh + 1]), r=[CST] + S_, w=[DIAGC[pi][hh]])
                V("scalar", lambda h: h.activation(out=kbt[pi][:, hh, :], in_=kq, func=AF.Copy, scale=skb[:, hh:hh + 1]), r=[KTV[pi][hh]] + S_, w=[KBT[pi][hh]])
                V("vector", lambda h: h.tensor_scalar_mul(out=kdt[p3][:, hh, :], in0=kq, scalar1=skd[:, hh:hh + 1]), r=[KTV[pi][hh]] + S_, w=[KDT[p3][hh]])
                V("vector", lambda h: h.tensor_scalar_mul(out=vbt[pi][:, hh, :], in0=vq, scalar1=beta[:, m, hh:hh + 1]), r=[KTV[pi][hh], BETA], w=[VBT[pi][hh]])
            yield
            Y = pbank()
            for hh in range(4):
                V("tensor", lambda h: h.matmul(qap(Y, hh), onesf, diagc[pi][:, hh, :], start=True, stop=False), r=[CST, DIAGC[pi][hh]], w=[BQ[Y][hh]], inc=False)
                V("tensor", lambda h: h.matmul(qap(Y, hh), identb, negmb, start=False, stop=True), r=[IDB, NEGMB], w=[BQ[Y][hh]], inc=(hh == 3))
            for hh in range(4):
                V("scalar", lambda h: h.activation(out=Dm[pi][:, hh, :], in_=qap(Y, hh), func=AF.Exp, bias=gam[:, m, hh:hh + 1]), r=[BQ[Y][hh], GAM], w=[DM[pi][hh]])
            for hh in range(4):
                V("gpsimd", lambda h: h.tensor_tensor(out=DmS[pi][:, hh, :], in0=Dm[pi][:, hh, :], in1=strict, op=ALU.mult), r=[DM[pi][hh], CST], w=[DMS[pi][hh]])
            yield
            X, Y = pbank(), pbank()
            for hh in range(4):
                V("tensor", lambda h: h.matmul(qap(X, hh), qkvT[:, 4 + hh, tok], qkvT[:, 4 + hh, tok], start=True, stop=True), r=[QKV[4 + hh]], w=[BQ[X][hh]], inc=(hh == 3))
            for hh in range(4):
                V("tensor", lambda h: h.matmul(qap(Y, hh), qkvT[:, hh, tok], qkvT[:, 4 + hh, tok], start=True, stop=True), r=[QKV[hh], QKV[4 + hh]], w=[BQ[Y][hh]], inc=(hh == 3))
            for hh in range(4):
                V("vector", lambda h: h.scalar_tensor_tensor(out=Am[0][pi][:, hh, :], in0=qap(X, hh), scalar=bkk[:, hh:hh + 1], in1=DmS[pi][:, hh, :],
                                                             op0=ALU.mult, op1=ALU.mult), r=[BQ[X][hh], DMS[pi][hh]] + S_, w=[AMB[0][pi][hh]])
            for hh in range(4):
                V("vector", lambda h: h.scalar_tensor_tensor(out=ac[pi][:, hh, :], in0=qap(Y, hh), scalar=sa[:, hh:hh + 1], in1=Dm[pi][:, hh, :],
                                                             op0=ALU.mult, op1=ALU.mult), r=[BQ[Y][hh], DM[pi][hh]] + S_, w=[AC[pi][hh]])
            yield
            X, Y = pbank(), pbank()
            for hh in range(4):
                V("tensor", lambda h: h.transpose(bq(X, hh), Am[0][pi][:, hh, :], identb), r=[AMB[0][pi][hh], IDB], w=[BQ[X][hh]], inc=(hh == 3))
            for hh in range(4):
                V("tensor", lambda h: h.transpose(bq(Y, hh), ac[pi][:, hh, :], identb), r=[AC[pi][hh], IDB], w=[BQ[Y][hh]], inc=(hh == 3))
            V("scalar", lambda h: h.copy(out=Bm[0][pi], in_=bqall(X)), r=BQ[X], w=BMB[0][pi])
            for hh in range(4):
                V("vector", lambda h: h.tensor_tensor(out=Ttb[0][pi][:, hh, :], in0=identf, in1=bq(X, hh), op=ALU.subtract), r=[CST, BQ[X][hh]], w=[TTB[0][pi][hh]])
            for hh in range(4):
                V("vector", lambda h: h.tensor_tensor(out=Tt32[pi][:, hh, :], in0=identf, in1=bq(X, hh), op=ALU.subtract), r=[CST, BQ[X][hh]], w=[TT32[pi][hh]])
            V("scalar", lambda h: h.copy(out=acT[p3], in_=bqall(Y)), r=BQ[Y], w=ACTB[p3])
            yield
            cur = 0
            for s_ in range(5):
                X = pbank()
                for hh in range(4):
                    V("tensor", lambda h: h.matmul(qap(X, hh), Bm[cur][pi][:, hh, :], Am[cur][pi][:, hh, :], start=True, stop=True),
                      r=[BMB[cur][pi][hh], AMB[cur][pi][hh]], w=[BQ[X][hh]], inc=(hh == 3))
                V("scalar", lambda h: h.copy(out=flat(Am[1 - cur][pi]), in_=banks[X][:, :]), r=BQ[X], w=AMB[1 - cur][pi])
                if s_ < 4:
                    Y = pbank()
                    for hh in range(4):
                        V("tensor", lambda h: h.matmul(qap(Y, hh), Am[cur][pi][:, hh, :], Bm[cur][pi][:, hh, :], start=True, stop=True),
                          r=[BMB[cur][pi][hh], AMB[cur][pi][hh]], w=[BQ[Y][hh]], inc=(hh == 3))
                    V("scalar", lambda h: h.copy(out=flat(Bm[1 - cur][pi]), in_=banks[Y][:, :]), r=BQ[Y], w=BMB[1 - cur][pi])
                yield
                X = pbank()
                for hh in range(4):
                    V("tensor", lambda h: h.matmul(qap(X, hh), Am[1 - cur][pi][:, hh, :], Ttb[cur][pi][:, hh, :], start=True, stop=True),
                      r=[AMB[1 - cur][pi][hh], TTB[cur][pi][hh]], w=[BQ[X][hh]], inc=(hh == 3))
                V("vector", lambda h: h.tensor_tensor(out=flat(Ttb[1 - cur][pi]), in0=flat(Tt32[pi]), in1=banks[X][:, :], op=ALU.add),
                  r=TT32[pi] + BQ[X], w=TTB[1 - cur][pi])
                V("vector", lambda h: h.tensor_tensor(out=flat(Tt32[pi]), in0=flat(Tt32[pi]), in1=banks[X][:, :], op=ALU.add),
                  r=TT32[pi] + BQ[X], w=TT32[pi])
                cur = 1 - cur
                yield
            X, Y = pbank(), pbank()
            for hh in range(4):
                V("tensor", lambda h: h.matmul(qap(X, hh), Ttb[cur][pi][:, hh, :], vbt[pi][:, hh, :], start=True, stop=True),
                  r=[TTB[cur][pi][hh], VBT[pi][hh]], w=[BQ[X][hh]], inc=(hh == 3))
            for hh in range(4):
                V("tensor", lambda h: h.matmul(qap(Y, hh), kbt[pi][:, hh, :], Ttb[cur][pi][:, hh, :], start=True, stop=True),
                  r=[TTB[cur][pi][hh], KBT[pi][hh]], w=[BQ[Y][hh]], inc=(hh == 3))
            V("scalar", lambda h: h.copy(out=flat(u_sb[p3]), in_=banks[X][:, :]), r=BQ[X], w=US[p3])
            V("vector", lambda h: h.tensor_copy(out=flat(wT_sb[p3]), in_=banks[Y][:, :]), r=BQ[Y], w=WT[p3])
            yield

        def gen_scan(m):
            tok = slice(m * 128, (m + 1) * 128)
            p3 = m % NSH
            so = sm[:, p3, 6, :]
            sso = sm[:, p3, 9, :]
            S_ = [SM[p3]]
            for half in range(2):
                r0 = half * 64
                c0 = m * 128 + r0
                for hh in range(4):
                    V("tensor", lambda h: h.matmul(qap(bWS, hh)[r0:r0 + 64, :], wT_sb[p3][:, hh, r0:r0 + 64], Sbf[:, hh, :], start=True, stop=True),
                      r=[WT[p3][hh], SBF[hh]], w=[BQ[bWS][hh]], inc=(hh == 3))
                for hh in range(4):
                    V("tensor", lambda h: h.matmul(qap(bOI, hh)[r0:r0 + 64, :], qkvT[:, hh, c0:c0 + 64], Sbf[:, hh, :], start=True, stop=True),
                      r=[QKV[hh], SBF[hh]], w=[BQ[bOI][hh]], inc=(hh == 3))
                for hh in range(4):
                    gcol = glb[:, half, m * 4 + hh:m * 4 + hh + 1]
                    V("gpsimd", lambda h: h.tensor_scalar(out=Sp[:, hh, :], in0=S32[:, hh, :], scalar1=gcol, scalar2=0.0, op0=ALU.mult, op1=ALU.add),
                      r=[S32B[hh], GLB], w=[SPB[hh]])
                yield
                V("vector", lambda h: h.tensor_tensor(out=flat(vnew[r0:r0 + 64, :, :]), in0=flat(u_sb[p3][r0:r0 + 64, :, :]),
                                                      in1=banks[bWS][r0:r0 + 64, :], op=ALU.subtract), r=US[p3] + BQ[bWS], w=VN)
                yield
                for hh in range(4):
                    V("tensor", lambda h: h.matmul(qap(bSD, hh), kdt[p3][r0:r0 + 64, hh, :], vnew[r0:r0 + 64, hh, :], start=True, stop=True),
                      r=[KDT[p3][hh], VN[hh]], w=[BQ[bSD][hh]], inc=(hh == 3))
                for hh in range(4):
                    V("tensor", lambda h: h.matmul(qap(bO2, hh)[r0:r0 + 64, :], acT[p3][r0:r0 + 64, hh, r0:r0 + 64], vnew[r0:r0 + 64, hh, :],
                                                   start=True, stop=True), r=[ACTB[p3][hh], VN[hh]], w=[BQ[bO2][hh]], inc=(hh == 3))
                yield
                V("vector", lambda h: h.tensor_tensor(out=flat(S32), in0=flat(Sp), in1=banks[bSD][:, :], op=ALU.add), r=SPB + BQ[bSD], w=S32B)
                V("scalar", lambda h: h.copy(out=flat(Sbf), in_=flat(S32)), r=S32B, w=SBF)
                yield
            for hh in range(4):
                V("scalar", lambda h: h.activation(out=tmpo[:, hh, :], in_=qap(bOI, hh), func=AF.Copy, scale=so[:, hh:hh + 1]), r=[BQ[bOI][hh]] + S_, w=[TMPO[hh]])
            V("vector", lambda h: h.tensor_tensor(out=flat(otok), in0=flat(tmpo), in1=banks[bO2][:, :], op=ALU.add), r=TMPO + BQ[bO2], w=OTOK)
            yield
            for hh in range(4):
                V("scalar", lambda h: h.activation(out=junk2t[:, hh, :], in_=otok[:, hh, :], func=AF.Square, accum_out=sso[:, hh:hh + 1]), r=[OTOK[hh]], w=[JUNK2S[hh], SM2[p3]])
            V("scalar", lambda h: h.activation(out=sso, in_=sso, func=AF.Ln, scale=1.0 / 128.0, bias=eps_col[:, 0:1]), r=[SM2[p3], EPSB], w=[SM2[p3]])
            V("scalar", lambda h: h.activation(out=sso, in_=sso, func=AF.Exp, scale=-0.5), r=[SM2[p3]], w=[SM2[p3]])
            yield
            for hh in range(4):
                V("vector", lambda h: h.scalar_tensor_tensor(out=on[:, hh, :], in0=otok[:, hh, :], scalar=sso[:, hh:hh + 1], in1=gbc,
                                                             op0=ALU.mult, op1=ALU.mult), r=[OTOK[hh], SM2[p3], GBC], w=[ON[hh]])
            for hh in range(4):
                V("tensor", lambda h: h.transpose(bq(bOT, hh), on[:, hh, :], identb), r=[ON[hh], IDB], w=[BQ[bOT][hh]], inc=(hh == 3))
            yield
            V("vector", lambda h: h.tensor_tensor(out=ogT[:, :, tok], in0=bqall(bOT), in1=sz[:, :, tok], op=ALU.mult), r=BQ[bOT] + SZ, w=OGT)
            yield

        preps = {}
        prep_done = set()
        scan_done = set()
        next_prep = 0
        scan_m = 0
        scan_g = None
        while len(scan_done) < NT:
            while len(preps) < NPI and next_prep < NT and (next_prep < NSH or (next_prep - NSH) in scan_done) \
                    and (next_prep < NPI or (next_prep - NPI) in prep_done):
                preps[next_prep] = gen_prep(next_prep)
                next_prep += 1
            progressed = False
            for _rep in range(2):
                if scan_g is None and scan_m < NT and scan_m in prep_done:
                    scan_g = gen_scan(scan_m)
                if scan_g is not None:
                    try:
                        next(scan_g)
                    except StopIteration:
                        scan_done.add(scan_m)
                        scan_m += 1
                        scan_g = None
                    progressed = True
            for t_ in sorted(preps):
                try:
                    next(preps[t_])
                except StopIteration:
                    prep_done.add(t_)
                    del preps[t_]
                progressed = True
            assert progressed
        dumpsrc["ogT"] = (ogT, OGT, "G")
        do_dumps("G")
        if stop_after == "G":
            finish()
            return nc
        fw.barrier()
        ar.release("gam", "egam", "edec", "glb", "X2", "gbc", "negmb", "vnew", "tmpo", "otok", "on", "junk", "junk2", "S32", "Sp", "Sbf", "sm", "qkvT", "sz", "beta", "lg", "kbt0", "kbt1", "kbt2", "vbt0", "vbt1", "vbt2", "ktv0", "ktv1", "ktv2", "diagc0", "diagc1", "diagc2", "Dm0", "Dm1", "Dm2", "DmS0", "DmS1", "DmS2", "Am0_0", "Am0_1", "Am0_2", "Am1_0", "Am1_1", "Am1_2", "Bm0_0", "Bm0_1", "Bm0_2", "Bm1_0", "Bm1_1", "Bm1_2", "ac0", "ac1", "ac2", "Tt32_0", "Tt32_1", "Tt32_2", "Ttb0_0", "Ttb0_1", "Ttb0_2", "Ttb1_0", "Ttb1_1", "Ttb1_2", "kdt0", "kdt1", "kdt2", "kdt3", "acT0", "acT1", "acT2", "acT3", "u_sb0", "u_sb1", "u_sb2", "u_sb3", "wT_sb0", "wT_sb1", "wT_sb2", "wT_sb3")

        Rr = ar.view("R", [NT, D], F32)
        RB = [[Buf(), Buf()] for _ in range(NT)]
        h1T = ar.view("h1T", [8, T], BF16)
        H1T = [[Buf() for _ in range(NT)] for _ in range(8)]
        wo_g = ar.view("wo_g", [4, 512], BF16)
        wo_f = ar.view("wo_f", [4, 512], BF16)
        WOG, WOF = Buf(), Buf()
        wog_sem, wof_sem = newd(), newd()
        xts = [ar.view("xt%d" % i, [D], F32) for i in range(4)]
        XT = [Buf() for _ in range(4)]
        xsem2 = [newd() for _ in range(2)]
        gA = ar.view("gA", [D], F32)
        bA = ar.view("bA", [D], F32)
        GA, BA = Buf(), Buf()
        st = ar.view("st", [4, 12], F32)
        STB = [Buf() for _ in range(4)]
        mv1 = ar.view("mv1", [NT, 2], F32)
        MV1 = [Buf() for _ in range(NT)]
        w_out_g = w_out[0:512, :].rearrange("(kc p) c -> p kc c", p=128)
        w_out_f = w_out[512:1024, :].rearrange("(j p) c -> p j c", p=128)
        fw.dma("gpsimd", wog_sem, wo_g, w_out_g[:, :, 0:512], w=[WOG])
        fw.dma("gpsimd", wof_sem, wo_f, w_out_f[:, :, 0:512], w=[WOF])
        bcsem = [newd(), newd()]

        def load_bc(rowg, rowb):
            fw.dma("sync", bcsem[0], gA, vec_d[rowg:rowg + 1, :].partition_broadcast(128), w=[GA])
            fw.dma("sync", bcsem[1], bA, vec_d[rowb:rowb + 1, :].partition_broadcast(128), w=[BA])
            V("scalar", lambda h: h.activation(out=gA, in_=gA, func=AF.Copy, scale=ALPHA), r=[GA], w=[GA])
            V("scalar", lambda h: h.activation(out=bA, in_=bA, func=AF.Copy, scale=ALPHA), r=[BA], w=[BA])

        wple = ar.view("wple", [2, D], BF16)
        wgt = ar.view("wgt", [8, D], BF16)
        WPLE, WGT = Buf(), Buf()
        fw.dma("gpsimd", newd(), wple, w_ple.rearrange("(kc p) c -> p kc c", p=128), w=[WPLE])
        fw.dma("gpsimd", newd(), wgt, w_gate.rearrange("(kc p) c -> p kc c", p=128), w=[WGT])
        load_bc(0, 1)
        nmr = ar.view("nmr", [NT], F32)
        NMR = Buf()
        V("vector", lambda h: h.scalar_tensor_tensor(out=nmr, in0=mv[:, :, 0], scalar=-1.0, in1=mv[:, :, 1], op0=ALU.mult, op1=ALU.mult), r=MV, w=[NMR])
        for m in range(NT):
            j = m % 2
            fw.dma("sync", xsem2[j], xts[j], x[m * 128:(m + 1) * 128, :], w=[XT[j]])
            V("scalar", lambda h: h.activation(out=xts[j], in_=xts[j], func=AF.Identity, scale=mv[:, m, 1:2], bias=nmr[:, m:m + 1]),
              r=[XT[j], MV[m], NMR], w=[XT[j]])
            V("vector", lambda h: h.tensor_tensor(out=Rr[:, m, :], in0=xts[j], in1=gA, op=ALU.mult), r=[XT[j], GA], w=RB[m])
            V("vector", lambda h: h.tensor_tensor(out=Rr[:, m, :], in0=Rr[:, m, :], in1=bA, op=ALU.add), r=RB[m] + [BA], w=RB[m])
        for half in range(2):
            if half == 1:
                fw.dma("gpsimd", wog_sem, wo_g, w_out_g[:, :, 512:1024], w=[WOG])
                fw.dma("gpsimd", wof_sem, wo_f, w_out_f[:, :, 512:1024], w=[WOF])
            for m in range(NT):
                tok = slice(m * 128, (m + 1) * 128)
                bk = nextbank()
                for kc in range(4):
                    V("tensor", lambda h: h.matmul(banks[bk][:, :], ogT[:, kc, tok], wo_g[:, kc, :], start=(kc == 0), stop=False),
                      r=[OGT[kc], WOG], w=[BK[bk]], inc=False)
                for j4 in range(4):
                    V("tensor", lambda h: h.matmul(banks[bk][:, :], ofox[:, j4, tok], wo_f[:, j4, :], start=False, stop=(j4 == 3)),
                      r=[OFOX[2 * j4], OFOX[2 * j4 + 1], WOF], w=[BK[bk]], inc=(j4 == 3))
                V("vector", lambda h: h.tensor_tensor(out=Rr[:, m, half * 512:(half + 1) * 512], in0=Rr[:, m, half * 512:(half + 1) * 512],
                                                      in1=banks[bk][:, :], op=ALU.add), r=[RB[m][half], BK[bk]], w=[RB[m][half]])
        load_bc(2, 3)

        def c_p1(m):
            j = m % 4
            V("vector", lambda h: h.bn_stats(out=st[:, j, 0:6], in_=Rr[:, m, 0:512]), r=[RB[m][0]], w=[STB[j]])
            V("vector", lambda h: h.bn_stats(out=st[:, j, 6:12], in_=Rr[:, m, 512:1024]), r=[RB[m][1]], w=[STB[j]])
            V("vector", lambda h: h.bn_aggr(out=mv1[:, m, 0:2], in_=st[:, j, 0:12]), r=[STB[j]], w=[MV1[m]])
            V("vector", lambda h: h.tensor_scalar_add(out=mv1[:, m, 1:2], in0=mv1[:, m, 1:2], scalar1=LN_EPS), r=[MV1[m]], w=[MV1[m]])
            V("gpsimd", lambda h: h.tensor_tensor(out=mv1[:, m, 1:2], in0=mv1[:, m, 1:2], in1=m05, op=ALU.pow), r=[MV1[m], CST], w=[MV1[m]])

        def c_p2(m):
            j = m % 4
            V("vector", lambda h: h.scalar_tensor_tensor(out=mv1[:, m, 0:1], in0=mv1[:, m, 0:1], scalar=-1.0, in1=mv1[:, m, 1:2],
                                                         op0=ALU.mult, op1=ALU.mult), r=[MV1[m]], w=[MV1[m]])
            V("scalar", lambda h: h.activation(out=xts[j], in_=Rr[:, m, :], func=AF.Identity, scale=mv1[:, m, 1:2], bias=mv1[:, m, 0:1]),
              r=RB[m] + [MV1[m]], w=[XT[j]])

        def c_p3(m):
            j = m % 4
            V("vector", lambda h: h.tensor_tensor(out=Rr[:, m, :], in0=xts[j], in1=gA, op=ALU.mult), r=[XT[j], GA], w=RB[m])
            V("vector", lambda h: h.tensor_tensor(out=Rr[:, m, :], in0=Rr[:, m, :], in1=bA, op=ALU.add), r=RB[m] + [BA], w=RB[m])

        def c_tr(g):
            for c in range(8):
                bk = nextbank()
                for j in range(4):
                    V("tensor", lambda h: h.transpose(banks[bk][:, j * 128:(j + 1) * 128], xts[j][:, c * 128:(c + 1) * 128], identf),
                      r=[XT[j], CST], w=[BK[bk]], inc=(j == 3))
                V("scalar", lambda h: h.activation(out=h1T[:, c, g * 512:(g + 1) * 512], in_=banks[bk][:, :], func=AF.Identity,
                                                   scale=par[:, P_G1 + c:P_G1 + c + 1], bias=par[:, P_B1 + c:P_B1 + c + 1]),
                  r=[BK[bk], PAR], w=[H1T[c][4 * g + jj] for jj in range(4)])

        for k in range(NT + 2):
            if k < NT:
                c_p1(k)
            if 0 <= k - 1 < NT:
                c_p2(k - 1)
                if (k - 1) % 4 == 3:
                    c_tr((k - 1) // 4)
            if 0 <= k - 2 < NT:
                c_p3(k - 2)
        dumpsrc["h1T"] = (h1T, [b for row in H1T for b in row], "C")
        dumpsrc["R"] = (Rr, [b for p_ in RB for b in p_], "C")
        do_dumps("C")
        if stop_after == "C":
            finish()
            return nc
        fw.barrier()
        ar.release("ofox", "ogT", "wo_g", "wo_f", "xt0", "xt1", "xt2", "xt3", "gA", "bA", "nmr")

        NG = DFF // 512
        wu = [ar.view("wu%d" % i, [8, 512], BF16) for i in range(2)]
        wd = [ar.view("wd%d" % i, [4, D], BF16) for i in range(2)]
        WU = [Buf(), Buf()]
        WD = [Buf(), Buf()]
        wusem = [newd(), newd()]
        wdsem = [newd(), newd()]
        w_up_v = w_up.rearrange("(kc p) f -> p kc f", p=128)
        w_down_v = w_down.rearrange("(fc p) c -> p fc c", p=128)
        def load_ffw(g):
            s_ = g % 2
            fw.dma("gpsimd", wusem[s_], wu[s_], w_up_v[:, :, g * 512:(g + 1) * 512], w=[WU[s_]])
            fw.dma("gpsimd", wdsem[s_], wd[s_], w_down_v[:, g * 4:(g + 1) * 4, :], w=[WD[s_]])

        load_ffw(0)
        pTt = ar.view("pTt", [2, T], BF16)
        PTB = [Buf() for _ in range(NT)]
        ptile = ar.view("ptile", [2, 256], F32)
        PTL = [Buf() for _ in range(2)]
        psem = [newd(), newd()]
        bgf = ar.view("bgf", [D], F32)
        bgh = ar.view("bgh", [D], BF16)
        bgl = ar.view("bgl", [D], BF16)
        ones_b = ar.view("ones_b", [128], BF16)
        BGF, BGH, BGL, ONB = Buf(), Buf(), Buf(), Buf()
        sg = ar.view("sg", [2, 512], F32)
        SG = [Buf(), Buf()]
        fw.dma("sync", newd(), bgf[0:1, :], vec_d[6:7, :], w=[BGF])
        V("vector", lambda h: h.tensor_copy(out=bgh[0:1, :], in_=bgf[0:1, :]), r=[BGF], w=[BGH])
        V("vector", lambda h: h.tensor_tensor(out=bgf[0:1, :], in0=bgf[0:1, :], in1=bgh[0:1, :], op=ALU.subtract), r=[BGF, BGH], w=[BGF])
        V("vector", lambda h: h.tensor_copy(out=bgl[0:1, :], in_=bgf[0:1, :]), r=[BGF], w=[BGL])
        V("vector", lambda h: h.memset(ones_b, 1.0), w=[ONB])
        for m in range(NT):
            j = m % 2
            tok = slice(m * 128, (m + 1) * 128)
            fw.dma("sync", psem[j], ptile[:, j, :], p_in[tok, :], w=[PTL[j]])
            bk = nextbank()
            for kc in range(2):
                V("tensor", lambda h: h.transpose(banks[bk][:, kc * 128:(kc + 1) * 128], ptile[:, j, kc * 128:(kc + 1) * 128], identf),
                  r=[PTL[j], CST], w=[BK[bk]], inc=(kc == 1))
            V("scalar", lambda h: h.copy(out=pTt[:, :, tok], in_=banks[bk][:, 0:256].rearrange("p (k t) -> p k t", k=2)), r=[BK[bk]], w=[PTB[m]])
            for half in range(2):
                cs = slice(half * 512, (half + 1) * 512)
                bp, bg_ = nextbank(), nextbank()
                for kc in range(2):
                    V("tensor", lambda h: h.matmul(banks[bp][:, :], pTt[:, kc, tok], wple[:, kc, cs], start=(kc == 0), stop=(kc == 1)),
                      r=[PTB[m], WPLE], w=[BK[bp]], inc=(kc == 1))
                for kc in range(8):
                    V("tensor", lambda h: h.matmul(banks[bg_][:, :], h1T[:, kc, tok], wgt[:, kc, cs], start=(kc == 0), stop=False),
                      r=[H1T[kc][m], WGT], w=[BK[bg_]], inc=False)
                V("tensor", lambda h: h.matmul(banks[bg_][:, :], ones_b[0:1, :], bgh[0:1, cs], start=False, stop=False), r=[ONB, BGH], w=[BK[bg_]], inc=False)
                V("tensor", lambda h: h.matmul(banks[bg_][:, :], ones_b[0:1, :], bgl[0:1, cs], start=False, stop=True), r=[ONB, BGL], w=[BK[bg_]])
                V("scalar", lambda h: h.activation(out=sg[:, half, :], in_=banks[bg_][:, :], func=AF.Sigmoid), r=[BK[bg_]], w=[SG[half]])
                V("vector", lambda h: h.tensor_tensor(out=sg[:, half, :], in0=sg[:, half, :], in1=banks[bp][:, :], op=ALU.mult), r=[SG[half], BK[bp]], w=[SG[half]])
                V("gpsimd", lambda h: h.tensor_tensor(out=Rr[:, m, cs], in0=Rr[:, m, cs], in1=sg[:, half, :], op=ALU.add), r=[RB[m][half], SG[half]], w=[RB[m][half]])
        if stop_after == "D1":
            dumpsrc["R1"] = (Rr, [b for p_ in RB for b in p_], "D1")
            do_dumps("D1")
            finish()
            return nc
        fw.barrier()
        ar.release("pTt", "ptile", "wple", "wgt", "bgf", "bgh", "bgl", "ones_b", "sg")

        actT = [ar.view("actT%d" % i, [4, T], BF16) for i in range(2)]
        ACTT = [[[Buf() for _ in range(NBLK)] for _ in range(4)] for _ in range(2)]
        rl = [ar.view("rl%d" % i, [512], F32) for i in range(2)]
        RL = [Buf(), Buf()]

        gA = ar.view("gA", [D], F32)
        bA = ar.view("bA", [D], F32)
        GA, BA = Buf(), Buf()
        fw.dma("sync", bcsem[0], gA, vec_d[4:5, :].partition_broadcast(128), w=[GA])
        fw.dma("sync", bcsem[1], bA, vec_d[5:6, :].partition_broadcast(128), w=[BA])
        yo = ar.view("yo", [2, D], F32)
        YO = [Buf(), Buf()]
        osem = [newd(), newd()]

        def emit_E1(m):
            j = m % 4
            V("vector", lambda h: h.bn_stats(out=st[:, j, 0:6], in_=Rr[:, m, 0:512]), r=[RB[m][0]], w=[STB[j]])
            V("vector", lambda h: h.bn_stats(out=st[:, j, 6:12], in_=Rr[:, m, 512:1024]), r=[RB[m][1]], w=[STB[j]])
            V("vector", lambda h: h.bn_aggr(out=mv1[:, m, 0:2], in_=st[:, j, 0:12]), r=[STB[j]], w=[MV1[m]])
            V("vector", lambda h: h.tensor_scalar_add(out=mv1[:, m, 1:2], in0=mv1[:, m, 1:2], scalar1=LN_EPS), r=[MV1[m]], w=[MV1[m]])
            V("gpsimd", lambda h: h.tensor_tensor(out=mv1[:, m, 1:2], in0=mv1[:, m, 1:2], in1=m05, op=ALU.pow), r=[MV1[m], CST], w=[MV1[m]])

        def emit_E2(m):
            j = m % 2
            V("vector", lambda h: h.scalar_tensor_tensor(out=mv1[:, m, 0:1], in0=mv1[:, m, 0:1], scalar=-1.0, in1=mv1[:, m, 1:2],
                                                         op0=ALU.mult, op1=ALU.mult), r=[MV1[m]], w=[MV1[m]])
            V("scalar", lambda h: h.activation(out=yo[:, j, :], in_=Rr[:, m, :], func=AF.Identity, scale=mv1[:, m, 1:2], bias=mv1[:, m, 0:1]),
              r=RB[m] + [MV1[m]], w=[YO[j]])

        def emit_E3(m):
            j = m % 2
            V("vector", lambda h: h.tensor_tensor(out=yo[:, j, :], in0=yo[:, j, :], in1=gA, op=ALU.mult), r=[YO[j], GA], w=[YO[j]])
            V("vector", lambda h: h.tensor_tensor(out=yo[:, j, :], in0=yo[:, j, :], in1=bA, op=ALU.add), r=[YO[j], BA], w=[YO[j]])
            final_tickets.append(fw.dma("sync", osem[j], out[m * 128:(m + 1) * 128, :], yo[:, j, :], r=[YO[j]]))

        def emit_E_step(k):
            if 0 <= k < NT:
                emit_E1(k)
            if 0 <= k - 1 < NT:
                emit_E2(k - 1)
            if 0 <= k - 2 < NT:
                emit_E3(k - 2)

        rcnt = 0
        for g in range(NG):
            s_ = g % 2
            if g + 1 < NG:
                load_ffw(g + 1)
            for blk in range(NBLK):
                for fc in range(4):
                    bk = nextbank()
                    for kc in range(8):
                        V("tensor", lambda h: h.matmul(banks[bk][:, :], wu[s_][:, kc, fc * 128:(fc + 1) * 128], h1T[:, kc, blk * 512:(blk + 1) * 512],
                                                       start=(kc == 0), stop=(kc == 7)),
                          r=[WU[s_]] + [H1T[kc][mm] for mm in range(blk * 4, blk * 4 + 4)], w=[BK[bk]], inc=(kc == 7))
                    q = rcnt % 2
                    rcnt += 1
                    V("scalar", lambda h: h.activation(out=rl[q], in_=banks[bk][:, :], func=AF.Relu), r=[BK[bk]], w=[RL[q]])
                    V("gpsimd", lambda h: h.tensor_tensor(out=actT[s_][:, fc, blk * 512:(blk + 1) * 512], in0=rl[q], in1=rl[q], op=ALU.mult),
                      r=[RL[q]], w=[ACTT[s_][fc][blk]])
            for m in range(NT):
                tok = slice(m * 128, (m + 1) * 128)
                for half in range(2):
                    cs = slice(half * 512, (half + 1) * 512)
                    bk = nextbank()
                    for fc in range(4):
                        V("tensor", lambda h: h.matmul(banks[bk][:, :], actT[s_][:, fc, tok], wd[s_][:, fc, cs], start=(fc == 0), stop=(fc == 3)),
                          r=[ACTT[s_][fc][m // 4], WD[s_]], w=[BK[bk]], inc=(fc == 3))
                    V("vector", lambda h: h.tensor_tensor(out=Rr[:, m, cs], in0=Rr[:, m, cs], in1=banks[bk][:, :], op=ALU.add), r=[RB[m][half], BK[bk]], w=[RB[m][half]])
                if g == NG - 1:
                    emit_E_step(m - 1)
        for k in range(NT - 1, NT + 2):
            emit_E_step(k)

        finish()
    return nc


_CACHE = {}


def _host_consts():
    c = np.zeros((128, NCST), np.float32)
    i = np.arange(128)
    c[:, C_ID:C_ID + 128] = np.eye(128, dtype=np.float32)
    c[:, C_ONE:C_ONE + 128] = 1.0
    same = (i[:, None] // 64) == (i[None, :] // 64)
    c[:, C_TRI:C_TRI + 128] = (same & (i[:, None] <= i[None, :])).astype(np.float32)
    c[:, C_BLK:C_BLK + 128] = same.astype(np.float32)
    c[:, C_NEGM:C_NEGM + 128] = np.where(same & (i[:, None] >= i[None, :]), 0.0, NEG)
    c[:, C_STR:C_STR + 128] = (same & (i[:, None] > i[None, :])).astype(np.float32)
    c[:, C_IND] = (i < 64)
    c[:, C_IND + 1] = (i >= 64)
    c[0:64, C_OAUG:C_OAUG + 128] = 1.0
    c[64, C_OAUG:C_OAUG + 128] = 64.0 * NORM_EPS
    c[:, C_M05] = -0.5
    c[:, C_M05 + 1] = NORM_EPS
    return c


def _prep_shared(inp):
    par = np.zeros((128, NPAR), np.float32)
    par[:, P_G0:P_G0 + 8] = inp["ln_in_g"].reshape(8, 128).T
    par[:, P_B0:P_B0 + 8] = inp["ln_in_b"].reshape(8, 128).T
    cw = inp["conv_w"][0]
    par[:, P_CW:P_CW + 48] = cw.T.reshape(12, 128, 4).transpose(1, 0, 2).reshape(128, 48)
    par[0:64, P_FOXG] = inp["fox_norm_g"][0]
    par[0:8, P_BF] = inp["b_f"][0]
    par[:, P_DTB:P_DTB + 64] = np.tile(inp["dt_bias"][0], 16)[None, :]
    par[:, P_ALOG:P_ALOG + 64] = np.tile(inp["a_log"][0], 16)[None, :]
    par[:, P_G1:P_G1 + 8] = inp["ln1_g"][0].reshape(8, 128).T
    par[:, P_B1:P_B1 + 8] = inp["ln1_b"][0].reshape(8, 128).T
    vecs = np.zeros((8, D), np.float32)
    vecs[0] = inp["ln_in_g"]
    vecs[1] = inp["ln_in_b"]
    vecs[2] = inp["ln1_g"][0]
    vecs[3] = inp["ln1_b"][0]
    vecs[4] = inp["ln2_g"][0]
    vecs[5] = inp["ln2_b"][0]
    vecs[6] = inp["b_ple_gate"][0]
    vecs[7, 0:128] = inp["gdn_norm_g"][0]
    return {
        "w_in": np.ascontiguousarray(inp["w_in"][0]), "w_out": np.ascontiguousarray(inp["w_out"][0]),
        "w_up": np.ascontiguousarray(inp["w_up"][0]), "w_down": np.ascontiguousarray(inp["w_down"][0]),
        "w_ple": np.ascontiguousarray(inp["w_ple"][0]), "w_gate": np.ascontiguousarray(inp["w_ple_gate"][0]),
        "cst": _host_consts(), "par": par, "vecs": vecs, "ones_rows": np.ones((3, 8 * T), np.float32),
    }


def run(inp, stop_after=None, dumps=(), cores=8):
    key = (stop_after, tuple(dumps))
    if key not in _CACHE:
        _CACHE[key] = build(stop_after, dumps)
    nc = _CACHE[key]
    inp = {k: np.asarray(v, dtype=np.float32) for k, v in inp.items()}
    shared = _prep_shared(inp)
    in_maps = []
    for b in range(cores):
        m = dict(shared)
        m["x"] = np.ascontiguousarray(inp["x"][b])
        m["p"] = np.ascontiguousarray(inp["p"][0, b])
        in_maps.append(m)
    res = run_bass_kernel_spmd(nc, in_maps, core_ids=list(range(cores)))
    return res.results


def kernel(**inputs):
    results = run(inputs)
    return np.stack([r["out"] for r in results], axis=0).astype(np.float32)
```

```python
import numpy as np
from contextlib import ExitStack
import concourse.bass as bass
import concourse.mybir as mybir
from concourse.bass_utils import run_bass_kernel_spmd

F32 = mybir.dt.float32
BF16 = mybir.dt.bfloat16
F32R = mybir.dt.float32r
AF = mybir.ActivationFunctionType
ALU = mybir.AluOpType

T, D, NT, NBLK = 2048, 1024, 16, 4
DFF = 4096
ALPHA = 2.0 ** 0.25
LN_EPS = 1e-5
NORM_EPS = 1e-6
NEG = -30000.0
DSZ = {F32: 4, BF16: 2, F32R: 4}

C_ID, C_ONE, C_TRI, C_BLK, C_NEGM, C_STR, C_IND, C_OAUG, C_M05, NCST = 0, 128, 256, 384, 512, 640, 768, 770, 898, 900
P_G0, P_B0, P_CW, P_FOXG, P_BF, P_DTB, P_ALOG, P_G1, P_B1, NPAR = 0, 8, 16, 64, 65, 66, 130, 194, 202, 210


class Buf:
    __slots__ = ("w", "r", "lock")

    def __init__(self, lock=None):
        self.w = None
        self.r = {}
        self.lock = lock


class Ticket:
    __slots__ = ("sem", "val", "key", "eng")

    def __init__(self, sem, val, key, eng):
        self.sem, self.val, self.key, self.eng = sem, val, key, eng


class Eng:
    def __init__(self, name, h, sem):
        self.name, self.h, self.sem = name, h, sem
        self.count = 0
        self.waited = {}


class FW:
    def __init__(self, nc, es):
        self.nc, self.es = nc, es
        self.E = {}
        for n in ("tensor", "vector", "scalar", "gpsimd", "sync"):
            sem = es.enter_context(nc.semaphore("s_" + n))
            self.E[n] = Eng(n, getattr(nc, n), sem)
        self.dsems = []

    def _wait(self, e, t):
        if e.waited.get(t.key, 0) >= t.val:
            return
        e.h.wait_ge(t.sem, t.val)
        e.waited[t.key] = t.val

    def _deps(self, e, reads, writes):
        for b in reads:
            t = b.w
            if t is not None and not (t.eng == e.name and e.name == "tensor"):
                self._wait(e, t)
        for b in writes:
            t = b.w
            if t is not None and (t.eng != e.name or (e.name != "tensor" and t.val <= e.count)):
                self._wait(e, t)
            for t in b.r.values():
                if t.eng != e.name:
                    self._wait(e, t)

    @staticmethod
    def _mark(t, reads, writes):
        for b in reads:
            b.r[t.key] = t
        for b in writes:
            b.w = t
            b.r = {}

    def op(self, eng, fn, r=(), w=(), inc=True):
        e = self.E[eng]
        locks = []
        for b in list(r) + list(w):
            if b.lock is not None and b.lock not in locks:
                locks.append(b.lock)
        for lk in locks:
            t = lk.w
            if t is not None and t.eng != e.name:
                self._wait(e, t)
        self._deps(e, r, w)
        ins = fn(e.h)
        if inc:
            e.count += 1
            ins.then_inc(e.sem, 1)
            t = Ticket(e.sem, e.count, "e_" + eng, eng)
        else:
            t = Ticket(e.sem, e.count + 1, "e_" + eng, eng)
        self._mark(t, r, w)
        for lk in locks:
            lk.w = t
        return t

    def dsem(self, name):
        sem = self.es.enter_context(self.nc.semaphore(name))
        d = [sem, 0, name]
        self.dsems.append(d)
        return d

    def dma(self, q, d, out, in_, r=(), w=()):
        e = self.E[q]
        self._deps(e, r, w)
        ins = e.h.dma_start(out=out, in_=in_)
        d[1] += 16
        ins.then_inc(d[0], 16)
        t = Ticket(d[0], d[1], "d_" + d[2], "dma")
        self._mark(t, r, w)
        return t

    def barrier(self):
        tl = [Ticket(e.sem, e.count, "e_" + n, n) for n, e in self.E.items() if e.count > 0]
        tl += [Ticket(d[0], d[1], "d_" + d[2], "dma") for d in self.dsems if d[1] > 0]
        for e in self.E.values():
            for t in tl:
                if t.eng != e.name:
                    self._wait(e, t)


class Arena:
    def __init__(self, ap, nbytes):
        self.ap = ap
        self.free = [(0, nbytes)]
        self.live = {}

    def alloc(self, name, nbytes):
        nbytes = (nbytes + 63) // 64 * 64
        for i, (o, s) in enumerate(self.free):
            if s >= nbytes:
                self.free[i] = (o + nbytes, s - nbytes)
                self.live[name] = (o, nbytes)
                return o
        raise RuntimeError("arena full for %s (%d) free=%s" % (name, nbytes, self.free))

    def release(self, *names):
        for name in names:
            o, s = self.live.pop(name)
            self.free.append((o, s))
        self.free.sort()
        m = []
        for o, s in self.free:
            if s == 0:
                continue
            if m and m[-1][0] + m[-1][1] == o:
                m[-1] = (m[-1][0], m[-1][1] + s)
            else:
                m.append((o, s))
        self.free = m

    def view(self, name, shape, dt):
        n = 1
        for s in shape:
            n *= s
        nb = n * DSZ[dt]
        o = self.alloc(name, nb)
        v = self.ap[:, o // 4:(o + nb) // 4]
        if dt != F32:
            v = v.bitcast(dt)
        if len(shape) == 2:
            v = v.rearrange("p (a b) -> p a b", a=shape[0])
        elif len(shape) == 3:
            v = v.rearrange("p (a b c) -> p a b c", a=shape[0], b=shape[1])
        elif len(shape) == 4:
            v = v.rearrange("p (a b c d) -> p a b c d", a=shape[0], b=shape[1], c=shape[2])
        return v


def build(stop_after=None, dumps=()):
    nc = bass.Bass("TRN2", target_bir_lowering=False)

    def din(name, shape):
        return nc.dram_tensor(name, list(shape), F32, kind="ExternalInput").ap()

    x = din("x", [T, D])
    p_in = din("p", [T, 256])
    w_in = din("w_in", [D, 3600])
    w_out = din("w_out", [D, D])
    w_up = din("w_up", [D, DFF])
    w_down = din("w_down", [DFF, D])
    w_ple = din("w_ple", [256, D])
    w_gate = din("w_gate", [D, D])
    cst_d = din("cst", [128, NCST])
    par_d = din("par", [128, NPAR])
    vec_d = din("vecs", [8, D])
    ones_d = din("ones_rows", [3, 8 * T])
    out = nc.dram_tensor("out", [T, D], F32, kind="ExternalOutput").ap()
    dump_out = {}
    for (nm, shp, dt) in dumps:
        dump_out[nm] = nc.dram_tensor("dbg_" + nm, list(shp), dt, kind="ExternalOutput").ap()

    w_in_v = w_in.rearrange("(kc p) c -> p kc c", p=128)

    with ExitStack() as es:
        fw = FW(nc, es)
        ARENA_BYTES = 204 * 1024
        arena_t = es.enter_context(nc.sbuf_tensor("arena", [128, ARENA_BYTES // 4], F32))
        ar = Arena(arena_t[:, :], ARENA_BYTES)
        banks = [es.enter_context(nc.psum_tensor("bank%d" % i, [128, 512], F32)) for i in range(8)]
        LK = [Buf() for _ in range(8)]
        BK = [Buf(LK[i]) for i in range(8)]
        rot = [0]

        def nextbank(lo=0, hi=8):
            b = lo + rot[0] % (hi - lo)
            rot[0] += 1
            return b

        def V(eng, fn, r=(), w=(), inc=True):
            return fw.op(eng, fn, r, w, inc)

        dctr = [0]

        def newd():
            dctr[0] += 1
            return fw.dsem("d%d" % dctr[0])

        final_tickets = []

        def do_dumps(stage):
            for (nm, shp, dt) in dumps:
                if nm in dumpsrc and dumpsrc[nm][2] == stage:
                    src, bufs, _ = dumpsrc[nm]
                    final_tickets.append(fw.dma("sync", newd(), dump_out[nm], src, r=bufs))

        dumpsrc = {}

        def finish():
            for t in final_tickets:
                fw._wait(fw.E["sync"], t)

        cst = ar.view("cst", [NCST], F32)
        par = ar.view("par", [NPAR], F32)
        CST, PAR = Buf(), Buf()
        fw.dma("sync", newd(), cst, cst_d, w=[CST])
        fw.dma("sync", newd(), par, par_d, w=[PAR])
        identf = cst[:, C_ID:C_ID + 128]
        onesf = cst[:, C_ONE:C_ONE + 128]
        m05 = cst[:, C_M05:C_M05 + 1]
        eps_col = cst[:, C_M05 + 1:C_M05 + 2]
        EPSB = CST
        identb = ar.view("identb", [128], BF16)
        IDB = Buf()
        V("vector", lambda h: h.tensor_copy(out=identb, in_=identf), r=[CST], w=[IDB])
        mv = ar.view("mv", [NT, 2], F32)
        MV = [Buf() for _ in range(NT)]

        OFF_FOX = 2056
        wsl = [ar.view("wsl%d" % i, [8, 512], BF16) for i in range(3)]
        WSL = [Buf() for _ in range(3)]
        wsem = [newd() for _ in range(3)]
        wsmall = ar.view("wsmall", [8, 8], BF16)
        WSM = Buf()
        wsm_sem = newd()
        hT = ar.view("hT", [8, T], BF16)
        HT = [[Buf() for _ in range(NBLK)] for _ in range(8)]
        xt = ar.view("xt", [8, D], F32)
        XT = [Buf() for _ in range(8)]
        xsem = [newd() for _ in range(8)]
        st = ar.view("st", [8, 12], F32)
        STB = [Buf() for _ in range(8)]

        def ln_stats(src, SRC, j, mvap, MVB):
            V("vector", lambda h: h.bn_stats(out=st[:, j, 0:6], in_=src[:, 0:512]), r=[SRC], w=[STB[j]])
            V("vector", lambda h: h.bn_stats(out=st[:, j, 6:12], in_=src[:, 512:1024]), r=[SRC], w=[STB[j]])
            V("vector", lambda h: h.bn_aggr(out=mvap[:, 0:2], in_=st[:, j, 0:12]), r=[STB[j]], w=[MVB])
            V("vector", lambda h: h.tensor_scalar_add(out=mvap[:, 1:2], in0=mvap[:, 1:2], scalar1=LN_EPS), r=[MVB], w=[MVB])
            V("gpsimd", lambda h: h.tensor_tensor(out=mvap[:, 1:2], in0=mvap[:, 1:2], in1=m05, op=ALU.pow), r=[MVB, CST], w=[MVB])

        for g in range(NBLK):
            for j4 in range(4):
                m = 4 * g + j4
                j = m % 8
                fw.dma("sync", xsem[j], xt[:, j, :], x[m * 128:(m + 1) * 128, :], w=[XT[j]])
                ln_stats(xt[:, j, :], XT[j], j, mv[:, m, :], MV[m])
                V("vector", lambda h: h.tensor_scalar(out=xt[:, j, :], in0=xt[:, j, :], scalar1=mv[:, m, 0:1], scalar2=mv[:, m, 1:2],
                                                      op0=ALU.subtract, op1=ALU.mult), r=[XT[j], MV[m]], w=[XT[j]])
            if g == 0:
                fw.dma("gpsimd", wsem[0], wsl[0], w_in_v[:, :, OFF_FOX:OFF_FOX + 512], w=[WSL[0]])
                fw.dma("gpsimd", wsem[1], wsl[1], w_in_v[:, :, OFF_FOX + 512:OFF_FOX + 1024], w=[WSL[1]])
                fw.dma("gpsimd", wsem[2], wsl[2], w_in_v[:, :, OFF_FOX + 1024:OFF_FOX + 1536], w=[WSL[2]])
                fw.dma("gpsimd", wsm_sem, wsmall, w_in_v[:, :, 3592:3600], w=[WSM])
            for c in range(8):
                bk = nextbank()
                for j4 in range(4):
                    j = (4 * g + j4) % 8
                    V("tensor", lambda h: h.transpose(banks[bk][:, j4 * 128:(j4 + 1) * 128], xt[:, j, c * 128:(c + 1) * 128], identf),
                      r=[XT[j], CST], w=[BK[bk]], inc=(j4 == 3))
                V("scalar", lambda h: h.activation(out=hT[:, c, g * 512:(g + 1) * 512], in_=banks[bk][:, :], func=AF.Identity,
                                                   scale=par[:, P_G0 + c:P_G0 + c + 1], bias=par[:, P_B0 + c:P_B0 + c + 1]),
                  r=[BK[bk], PAR], w=[HT[c][g]])
        dumpsrc["hT"] = (hT, [b for row in HT for b in row], "A1")
        do_dumps("A1")
        if stop_after == "A1":
            finish()
            return nc

        fw.barrier()
        ar.release("xt", "st")
        fq = ar.view("fq", [8, T], BF16)
        fk = ar.view("fk", [8, T], BF16)
        FQ = [Buf() for _ in range(8)]
        FK = [Buf() for _ in range(8)]
        vaug = ar.view("vaug", [NT, 8, 65], BF16)
        VAUG = Buf()
        for h8 in range(8):
            pass
        V("gpsimd", lambda h: h.memset(fq[64:128, :, :], 0.0), w=FQ)
        V("vector", lambda h: h.memset(fk[64:128, :, :], 0.0), w=FK)
        fw.dma("gpsimd", newd(), fq[67:70, :, :], ones_d.rearrange("r (h t) -> r h t", h=8), w=FQ)
        fw.dma("gpsimd", newd(), fk[64:67, :, :], ones_d.rearrange("r (h t) -> r h t", h=8), w=FK)
        V("gpsimd", lambda h: h.memset(vaug[:, :, :, 64:65], 1.0), w=[VAUG])

        rowA = ar.view("rowA", [T], F32)
        rowC = ar.view("rowC", [T], F32)
        rparts = ar.view("rparts", [6, T], BF16)
        nbf = ar.view("nbf", [1], F32)
        RA, RC, RP, NBF = Buf(), Buf(), Buf(), Buf()
        V("vector", lambda h: h.tensor_scalar_mul(out=nbf[0:8, :], in0=par[0:8, P_BF:P_BF + 1], scalar1=-1.0), r=[PAR], w=[NBF])
        for blk in range(NBLK):
            bk = nextbank()
            for kc in range(8):
                V("tensor", lambda h: h.matmul(banks[bk][0:8, :], wsmall[:, kc, :], hT[:, kc, blk * 512:(blk + 1) * 512],
                                               start=(kc == 0), stop=(kc == 7)), r=[WSM, HT[kc][blk]], w=[BK[bk]], inc=(kc == 7))
            V("scalar", lambda h: h.activation(out=rowA[0:8, blk * 512:(blk + 1) * 512], in_=banks[bk][0:8, :], func=AF.Exp,
                                               scale=-1.0, bias=nbf[0:8, :]), r=[BK[bk], NBF], w=[RA])
        V("scalar", lambda h: h.activation(out=rowA[0:8, :], in_=rowA[0:8, :], func=AF.Ln, bias=1.0), r=[RA], w=[RA])
        V("vector", lambda h: h.tensor_tensor_scan(out=rowC[0:8, :], data0=cst[0:8, C_ONE:C_ONE + 1].to_broadcast([8, T]),
                                                   data1=rowA[0:8, :], initial=0.0, op0=ALU.mult, op1=ALU.subtract),
          r=[RA, CST], w=[RC])
        dumpsrc["crow"] = (rowC[0:8, :], [RC], "AF")
        V("vector", lambda h: h.tensor_copy(out=rparts[0:8, 0, :], in_=rowC[0:8, :]), r=[RC], w=[RP])
        V("vector", lambda h: h.tensor_tensor(out=rowA[0:8, :], in0=rowC[0:8, :], in1=rparts[0:8, 0, :], op=ALU.subtract), r=[RC, RP], w=[RA])
        V("vector", lambda h: h.tensor_copy(out=rparts[0:8, 1, :], in_=rowA[0:8, :]), r=[RA], w=[RP])
        V("vector", lambda h: h.tensor_tensor(out=rowA[0:8, :], in0=rowA[0:8, :], in1=rparts[0:8, 1, :], op=ALU.subtract), r=[RA, RP], w=[RA])
        V("vector", lambda h: h.tensor_copy(out=rparts[0:8, 2, :], in_=rowA[0:8, :]), r=[RA], w=[RP])
        V("vector", lambda h: h.tensor_scalar_mul(out=rparts[0:8, 3:6, :], in0=rparts[0:8, 0:3, :], scalar1=-1.0), r=[RP], w=[RP])
        for h8 in range(8):
            dq = newd()
            fw.dma("sync", dq, fq[64:67, h8, :], rparts[h8:h8 + 1, 0:3, :], r=[RP], w=[FQ[h8]])
            dk_ = newd()
            fw.dma("sync", dk_, fk[67:70, h8, :], rparts[h8:h8 + 1, 3:6, :], r=[RP], w=[FK[h8]])

        stg = ar.view("stg", [4, 512], BF16)
        STG = [Buf() for _ in range(4)]
        stgsem = [newd() for _ in range(4)]
        cnt = 0
        for (dst, DST, ws, WS, sc) in ((fq, FQ, wsl[0], WSL[0], 0.125), (fk, FK, wsl[1], WSL[1], 1.0)):
            for c4 in range(4):
                for blk in range(NBLK):
                    bk = nextbank()
                    for kc in range(8):
                        V("tensor", lambda h: h.matmul(banks[bk][:, :], ws[:, kc, c4 * 128:(c4 + 1) * 128], hT[:, kc, blk * 512:(blk + 1) * 512],
                                                       start=(kc == 0), stop=(kc == 7)), r=[WS, HT[kc][blk]], w=[BK[bk]], inc=(kc == 7))
                    he, ho = 2 * c4, 2 * c4 + 1
                    sl = cnt % 4
                    cnt += 1
                    cs = slice(blk * 512, (blk + 1) * 512)
                    V("scalar", lambda h: h.activation(out=dst[0:64, he, cs], in_=banks[bk][0:64, :], func=AF.Copy, scale=sc), r=[BK[bk]], w=[DST[he]])
                    V("vector", lambda h: h.tensor_scalar_mul(out=stg[64:128, sl, :], in0=banks[bk][64:128, :], scalar1=sc), r=[BK[bk]], w=[STG[sl]])
                    fw.dma("sync", stgsem[sl], dst[0:64, ho, cs], stg[64:128, sl, :], r=[STG[sl]], w=[DST[ho]])
        for m in range(NT):
            bk = nextbank()
            for kc in range(8):
                V("tensor", lambda h: h.matmul(banks[bk][:, :], hT[:, kc, m * 128:(m + 1) * 128], wsl[2][:, kc, :],
                                               start=(kc == 0), stop=(kc == 7)), r=[WSL[2], HT[kc][m // 4]], w=[BK[bk]], inc=(kc == 7))
            src = banks[bk][:, :].rearrange("p (h d) -> p h d", h=8)
            if m % 2 == 0:
                V("scalar", lambda h: h.copy(out=vaug[:, m, :, 0:64], in_=src), r=[BK[bk]], w=[VAUG])
            else:
                V("vector", lambda h: h.tensor_copy(out=vaug[:, m, :, 0:64], in_=src), r=[BK[bk]], w=[VAUG])
        dumpsrc["fq"] = (fq, FQ, "AF")
        dumpsrc["fk"] = (fk, FK, "AF")
        dumpsrc["vaug"] = (vaug, [VAUG], "AF")
        do_dumps("AF")
        if stop_after == "AF":
            finish()
            return nc

        fw.barrier()
        ar.release("rowA", "rowC", "rparts", "wsmall", "nbf", "stg")
        wg = ar.view("wg", [8, 8], BF16)
        WG = Buf()
        for i in range(3):
            fw.dma("gpsimd", wsem[i], wsl[i], w_in_v[:, :, i * 512:(i + 1) * 512], w=[WSL[i]])
        fw.dma("gpsimd", wsm_sem, wg, w_in_v[:, :, 2048:2056], w=[WG])
        ofox = ar.view("ofox", [4, T], BF16)
        ostg = ar.view("ostg", [2, 512], BF16)
        OSTG = [Buf(), Buf()]
        ostg_sem = [newd(), newd()]
        ostg_ctr = [0]
        OFOX = [Buf() for _ in range(8)]
        pT = [ar.view("pT%d" % i, [512], BF16) for i in range(4)]
        PT = [Buf() for _ in range(4)]
        sq = [ar.view("sq%d" % i, [512], F32) for i in range(2)]
        rr = [ar.view("rr%d" % i, [512], F32) for i in range(2)]
        oaug = cst[:, C_OAUG:C_OAUG + 128]
        SQ, RR, OAUG = [Buf(), Buf()], [Buf(), Buf()], CST
        def pe_warm(n, bk_, lhs, rhs, RB_):
            for _ in range(n):
                V("tensor", lambda h: h.matmul(banks[bk_][:, :], lhs, rhs, start=True, stop=True), r=RB_, w=[BK[bk_]], inc=False)

        pe_warm(0, 5, fk[0:64, 0, 0:128], fq[0:64, 0, 0:512], [FK[0], FQ[0]])
        NFF, NFW = 0, 256
        pslot = [0]
        gp = [0]
        pend = []
        fsl = [0]

        def fin_step(it):
            stp, h8_, qb_, po_ = it[1], it[2], it[3], it[4]
            if stp == 0:
                k_ = fsl[0] % 2
                fsl[0] += 1
                it.append(k_)
                V("scalar", lambda h: h.activation(out=sq[k_][0:65, :], in_=banks[po_][0:65, :], func=AF.Square), r=[BK[po_]], w=[SQ[k_]])
                V("tensor", lambda h: h.matmul(banks[5][:, :], oaug[0:65, :], sq[k_][0:65, :], start=True, stop=True),
                  r=[OAUG, SQ[k_]], w=[BK[5]])
                it[0] = gp[0] + 2
                it[1] = 1
            else:
                k_ = it[5]
                V("scalar", lambda h: h.activation(out=rr[k_][0:64, :], in_=banks[5][0:64, :], func=AF.Ln, scale=1.0 / 64.0), r=[BK[5]], w=[RR[k_]])
                V("scalar", lambda h: h.activation(out=rr[k_][0:64, :], in_=rr[k_][0:64, :], func=AF.Exp, scale=-0.5), r=[RR[k_]], w=[RR[k_]])
                cs_ = slice(qb_ * 512, (qb_ + 1) * 512)
                if h8_ % 2 == 0:
                    V("vector", lambda h: h.scalar_tensor_tensor(out=ofox[0:64, h8_ // 2, cs_], in0=banks[po_][0:64, :],
                                                                 scalar=par[0:64, P_FOXG:P_FOXG + 1], in1=rr[k_][0:64, :],
                                                                 op0=ALU.mult, op1=ALU.mult), r=[BK[po_], RR[k_], PAR], w=[OFOX[h8_]])
                else:
                    sl_ = ostg_ctr[0] % 2
                    ostg_ctr[0] += 1
                    V("vector", lambda h: h.scalar_tensor_tensor(out=ostg[0:64, sl_, :], in0=banks[po_][0:64, :],
                                                                 scalar=par[0:64, P_FOXG:P_FOXG + 1], in1=rr[k_][0:64, :],
                                                                 op0=ALU.mult, op1=ALU.mult), r=[BK[po_], RR[k_], PAR], w=[OSTG[sl_]])
                    fw.dma("sync", ostg_sem[sl_], ofox[64:128, h8_ // 2, cs_], ostg[0:64, sl_, :], r=[OSTG[sl_]], w=[OFOX[h8_]])
                pend.remove(it)

        for h8 in range(8):
            pairs = []
            for qb in range(NBLK):
                for kb in range(4 * (qb + 1)):
                    pairs.append((qb, kb))
            sbank = {}

            def emit_qk(i):
                qb, kb = pairs[i]
                j = kb - 4 * qb
                qlo = max(j, 0) * 128
                bk = nextbank(0, 4)
                sbank[i] = bk
                V("tensor", lambda h: h.matmul(banks[bk][:, qlo:512], fk[:, h8, kb * 128:(kb + 1) * 128],
                                               fq[:, h8, qb * 512 + qlo:(qb + 1) * 512], start=True, stop=True),
                  r=[FK[h8], FQ[h8]], w=[BK[bk]])

            LA = 3
            for i0 in range(min(LA, len(pairs))):
                emit_qk(i0)
            for i, (qb, kb) in enumerate(pairs):
                if i + LA < len(pairs):
                    emit_qk(i + LA)
                for _ in range(NFF):
                    V("tensor", lambda h: h.matmul(banks[4][:, 0:NFW], identb, fq[:, 0, 0:NFW], start=True, stop=True), r=[IDB], w=[BK[4]], inc=False)
                j = kb - 4 * qb
                qlo = max(j, 0) * 128
                bk = sbank.pop(i)
                s = pslot[0] % 4
                pslot[0] += 1
                po = 6 + (h8 * NBLK + qb) % 2
                V("scalar", lambda h: h.activation(out=pT[s][:, qlo:512], in_=banks[bk][:, qlo:512], func=AF.Exp), r=[BK[bk]], w=[PT[s]])
                if j >= 0:
                    V("gpsimd", lambda h: h.affine_select(out=pT[s][:, qlo:qlo + 128], in_=pT[s][:, qlo:qlo + 128], pattern=[[1, 128]],
                                                          compare_op=ALU.is_ge, fill=0.0, base=0, channel_multiplier=-1),
                      r=[PT[s]], w=[PT[s]])
                last = (kb == 4 * (qb + 1) - 1)
                V("tensor", lambda h: h.matmul(banks[po][0:65, qlo:512], vaug[:, kb, h8, :], pT[s][:, qlo:512],
                                               start=(kb == 0), stop=last), r=[VAUG, PT[s]], w=[BK[po]], inc=last)
                if last:
                    pend.append([gp[0] + 2, 0, h8, qb, po])
                gp[0] += 1
                for it in list(pend):
                    if it[0] <= gp[0]:
                        fin_step(it)
        while pend:
            for it in list(pend):
                fin_step(it)
        dumpsrc["ofox"] = (ofox, OFOX, "F")
        do_dumps("F")
        if stop_after == "F":
            finish()
            return nc
        fw.barrier()
        ar.release("fq", "fk", "vaug", "pT0", "pT1", "pT2", "pT3", "sq0", "sq1", "rr0", "rr1", "ostg")

        qkvT = ar.view("qkvT", [12, T], BF16)
        QKV = [Buf() for _ in range(12)]
        sz = ar.view("sz", [4, T], BF16)
        SZ = [Buf() for _ in range(4)]
        pcb = ar.view("pcb", [4, 516], F32)
        PC = [Buf() for _ in range(4)]
        accb = ar.view("accb", [4, 512], F32)
        ACC = [Buf() for _ in range(4)]
        gpre = ar.view("gpre", [NT, 8], F32)
        GPRE = Buf()
        beta = ar.view("beta", [NT, 4], F32)
        lg = ar.view("lg", [NT, 4], F32)
        gt1 = ar.view("gt1", [NT, 4], F32)
        gt2 = ar.view("gt2", [NT, 4], F32)
        BETA, LG, GT1, GT2 = Buf(), Buf(), Buf(), Buf()
        cw = par[:, P_CW:P_CW + 48].rearrange("p (c j) -> p c j", j=4)

        def ag_E(u):
            ch, blk = u // 4, u % 4
            grp, hc = ch // 4, ch % 4
            ws, WS = wsl[grp], WSL[grp]
            sl = u % 4
            if u == 16:
                pass
            bk = nextbank()
            for kc in range(8):
                V("tensor", lambda h: h.matmul(banks[bk][:, :], ws[:, kc, hc * 128:(hc + 1) * 128], hT[:, kc, blk * 512:(blk + 1) * 512],
                                               start=(kc == 0), stop=(kc == 7)), r=[WS, HT[kc][blk]], w=[BK[bk]], inc=(kc == 7))
            if blk == 0:
                V("gpsimd", lambda h: h.memset(pcb[:, sl, 0:3], 0.0), w=[PC[sl]])
            else:
                V("gpsimd", lambda h: h.tensor_copy(out=pcb[:, sl, 0:3], in_=pcb[:, (u - 1) % 4, 512:515]), r=[PC[(u - 1) % 4]], w=[PC[sl]])
            V("scalar", lambda h: h.copy(out=pcb[:, sl, 3:515], in_=banks[bk][:, :]), r=[BK[bk]], w=[PC[sl]])

        def ag_T(u):
            ch = u // 4
            sl = u % 4
            V("scalar", lambda h: h.activation(out=accb[:, sl, :], in_=pcb[:, sl, 0:512], func=AF.Copy, scale=cw[:, ch, 0:1]), r=[PC[sl], PAR], w=[ACC[sl]])
            for j in range(1, 4):
                V("vector", lambda h: h.scalar_tensor_tensor(out=accb[:, sl, :], in0=pcb[:, sl, j:j + 512], scalar=cw[:, ch, j:j + 1], in1=accb[:, sl, :],
                                                             op0=ALU.mult, op1=ALU.add), r=[PC[sl], PAR, ACC[sl]], w=[ACC[sl]])

        def ag_S(u):
            ch, blk = u // 4, u % 4
            sl = u % 4
            V("scalar", lambda h: h.activation(out=qkvT[:, ch, blk * 512:(blk + 1) * 512], in_=accb[:, sl, :], func=AF.Silu), r=[ACC[sl]], w=[QKV[ch]])

        NU = 48
        for u in range(NU + 2):
            if u < NU:
                ag_E(u)
                if u == 16:
                    fw.dma("gpsimd", wsem[0], wsl[0], w_in_v[:, :, 1536:2048], w=[WSL[0]])
            if 0 <= u - 1 < NU:
                ag_T(u - 1)
            if 0 <= u - 2 < NU:
                ag_S(u - 2)
        for hc in range(4):
            for blk in range(NBLK):
                bk = nextbank()
                for kc in range(8):
                    V("tensor", lambda h: h.matmul(banks[bk][:, :], wsl[0][:, kc, hc * 128:(hc + 1) * 128], hT[:, kc, blk * 512:(blk + 1) * 512],
                                                   start=(kc == 0), stop=(kc == 7)), r=[WSL[0], HT[kc][blk]], w=[BK[bk]], inc=(kc == 7))
                V("scalar", lambda h: h.activation(out=sz[:, hc, blk * 512:(blk + 1) * 512], in_=banks[bk][:, :], func=AF.Silu), r=[BK[bk]], w=[SZ[hc]])
        for m in range(NT):
            bk = nextbank()
            for kc in range(8):
                V("tensor", lambda h: h.matmul(banks[bk][:, 0:8], hT[:, kc, m * 128:(m + 1) * 128], wg[:, kc, :],
                                               start=(kc == 0), stop=(kc == 7)), r=[WG, HT[kc][m // 4]], w=[BK[bk]], inc=(kc == 7))
            V("scalar", lambda h: h.copy(out=gpre[:, m, :], in_=banks[bk][:, 0:8]), r=[BK[bk]], w=[GPRE])
        dtb = par[:, P_DTB:P_DTB + 64].rearrange("p (m h) -> p m h", h=4)
        alog = par[:, P_ALOG:P_ALOG + 64].rearrange("p (m h) -> p m h", h=4)
        V("scalar", lambda h: h.activation(out=gt1, in_=gpre[:, :, 0:4], func=AF.Exp, scale=-1.0), r=[GPRE], w=[GT1])
        V("vector", lambda h: h.tensor_scalar_add(out=gt1, in0=gt1, scalar1=1.0), r=[GT1], w=[GT1])
        V("vector", lambda h: h.reciprocal(out=beta, in_=gt1), r=[GT1], w=[BETA])
        V("vector", lambda h: h.tensor_tensor(out=gt2, in0=gpre[:, :, 4:8], in1=dtb, op=ALU.add), r=[GPRE, PAR], w=[GT2])
        V("scalar", lambda h: h.activation(out=gt2, in_=gt2, func=AF.Exp), r=[GT2], w=[GT2])
        V("scalar", lambda h: h.activation(out=gt2, in_=gt2, func=AF.Ln, bias=1.0), r=[GT2], w=[GT2])
        V("scalar", lambda h: h.activation(out=gt1, in_=alog, func=AF.Exp), r=[PAR, GT1, BETA], w=[GT1])
        V("vector", lambda h: h.scalar_tensor_tensor(out=lg, in0=gt2, scalar=-1.0, in1=gt1, op0=ALU.mult, op1=ALU.mult), r=[GT1, GT2], w=[LG])
        dumpsrc["qkvT"] = (qkvT, QKV, "AG")
        dumpsrc["sz"] = (sz, SZ, "AG")
        dumpsrc["beta"] = (beta, [BETA], "AG")
        dumpsrc["lg"] = (lg, [LG], "AG")
        do_dumps("AG")
        if stop_after == "AG":
            finish()
            return nc
        fw.barrier()
        ar.release("hT", "wsl0", "wsl1", "wsl2", "wg", "pcb", "accb", "gpre", "gt1", "gt2")

        ogT = ar.view("ogT", [4, T], BF16)
        OGT = [Buf() for _ in range(4)]
        BQ = [[Buf(LK[i]) for _ in range(4)] for i in range(8)]

        def qap(b, q):
            return banks[b][:, q * 128:(q + 1) * 128]

        def flat(v):
            return v.rearrange("p h d -> p (h d)")

        def hb():
            return [Buf() for _ in range(4)]

        gam = ar.view("gam", [NT, 4], F32)
        egam = ar.view("egam", [NT, 4], F32)
        edec = ar.view("edec", [NT, 4], F32)
        glb = ar.view("glb", [2, 64], F32)
        X2 = ar.view("X2", [2, 64], F32)
        GAM, EGAM, EDEC, GLB, X2B = Buf(), Buf(), Buf(), Buf(), Buf()
        gbc = ar.view("gbc", [128], F32)
        GBC = Buf()
        fw.dma("sync", newd(), gbc, vec_d[7:8, 0:128].partition_broadcast(128), w=[GBC])
        negm = cst[:, C_NEGM:C_NEGM + 128]
        negmb = ar.view("negmb", [128], BF16)
        NEGMB = Buf()
        V("vector", lambda h: h.tensor_copy(out=negmb, in_=negm), r=[CST], w=[NEGMB])
        strict = cst[:, C_STR:C_STR + 128]
        lgf = lg.rearrange("p m h -> p (m h)")
        b0 = nextbank()
        V("tensor", lambda h: h.matmul(banks[b0][:, 0:64], cst[:, C_TRI:C_TRI + 128], lgf, start=True, stop=True), r=[CST, LG], w=[BQ[b0][0]])
        V("tensor", lambda h: h.matmul(banks[b0][:, 128:192], cst[:, C_BLK:C_BLK + 128], lgf, start=True, stop=True), r=[CST, LG], w=[BQ[b0][1]])
        V("vector", lambda h: h.tensor_copy(out=gam.rearrange("p m h -> p (m h)"), in_=banks[b0][:, 0:64]), r=[BQ[b0][0]], w=[GAM])
        V("scalar", lambda h: h.activation(out=egam.rearrange("p m h -> p (m h)"), in_=banks[b0][:, 0:64], func=AF.Exp), r=[BQ[b0][0]], w=[EGAM])
        V("vector", lambda h: h.tensor_tensor(out=edec.rearrange("p m h -> p (m h)"), in0=banks[b0][:, 128:192],
                                              in1=gam.rearrange("p m h -> p (m h)"), op=ALU.subtract), r=[BQ[b0][1], GAM], w=[EDEC])
        V("scalar", lambda h: h.activation(out=edec, in_=edec, func=AF.Exp), r=[EDEC], w=[EDEC])
        for hf in range(2):
            V("vector", lambda h: h.tensor_scalar_mul(out=X2[:, hf, :], in0=lgf, scalar1=cst[:, C_IND + hf:C_IND + hf + 1]), r=[LG, CST], w=[X2B])
        V("tensor", lambda h: h.matmul(banks[b0][:, 256:384], onesf, X2.rearrange("p a b -> p (a b)"), start=True, stop=True), r=[CST, X2B], w=[BQ[b0][2]])
        V("scalar", lambda h: h.activation(out=glb.rearrange("p a b -> p (a b)"), in_=banks[b0][:, 256:384], func=AF.Exp), r=[BQ[b0][2]], w=[GLB])

        dumpsrc["gam"] = (gam, [GAM], "G0")
        dumpsrc["glb"] = (glb, [GLB], "G0")
        if stop_after == "G0":
            do_dumps("G0")
            finish()
            return nc
        def hb3(n):
            return [[Buf() for _ in range(4)] for _ in range(n)]

        def v2(name, n, dt):
            return [ar.view("%s%d" % (name, i), [4, 128], dt) for i in range(n)]

        NPI, NSH = 3, 4
        kbt = v2("kbt", NPI, BF16); KBT = hb3(NPI)
        vbt = v2("vbt", NPI, BF16); VBT = hb3(NPI)
        ktv = [ar.view("ktv%d" % i, [8, 128], BF16) for i in range(NPI)]; KTV = hb3(NPI)
        diagc = v2("diagc", NPI, F32); DIAGC = hb3(NPI)
        Dm = v2("Dm", NPI, F32); DM = hb3(NPI)
        DmS = v2("DmS", NPI, F32); DMS = hb3(NPI)
        Am = [v2("Am%d_" % i, NPI, BF16) for i in range(2)]; AMB = [hb3(NPI), hb3(NPI)]
        Bm = [v2("Bm%d_" % i, NPI, BF16) for i in range(2)]; BMB = [hb3(NPI), hb3(NPI)]
        ac = v2("ac", NPI, BF16); AC = hb3(NPI)
        Tt32 = v2("Tt32_", NPI, F32); TT32 = hb3(NPI)
        Ttb = [v2("Ttb%d_" % i, NPI, BF16) for i in range(2)]; TTB = [hb3(NPI), hb3(NPI)]
        kdt = v2("kdt", NSH, BF16); KDT = hb3(NSH)
        acT = v2("acT", NSH, BF16); ACTB = hb3(NSH)
        u_sb = v2("u_sb", NSH, F32); US = hb3(NSH)
        wT_sb = v2("wT_sb", NSH, BF16); WT = hb3(NSH)
        vnew = ar.view("vnew", [4, 128], BF16); VN = hb()
        tmpo = ar.view("tmpo", [4, 128], F32); TMPO = hb()
        otok = ar.view("otok", [4, 128], F32); OTOK = hb()
        on = ar.view("on", [4, 128], BF16); ON = hb()
        junkt = ar.view("junk", [8, 128], F32); JUNKS = [Buf() for _ in range(8)]
        junk2t = ar.view("junk2", [4, 128], F32); JUNK2S = [Buf() for _ in range(4)]
        jctr = [0]

        def njunk():
            jctr[0] += 1
            return jctr[0] % 8
        S32 = ar.view("S32", [4, 128], F32); S32B = hb()
        Sp = ar.view("Sp", [4, 128], F32); SPB = hb()
        Sbf = ar.view("Sbf", [4, 128], BF16); SBF = hb()
        sm = ar.view("sm", [NSH, 12, 4], F32)
        SM = [Buf() for _ in range(NSH)]
        SM2 = [Buf() for _ in range(NSH)]
        V("vector", lambda h: h.memset(flat(S32), 0.0), w=S32B)
        V("vector", lambda h: h.memset(flat(Sbf), 0.0), w=SBF)
        bWS, bOI, bO2, bSD, bOT = 4, 5, 6, 7, 4

        def bq(b, hh):
            return banks[b][:, :].bitcast(BF16)[:, hh * 256:hh * 256 + 128]

        def bqall(b):
            return banks[b][:, :].bitcast(BF16).rearrange("p (h c) -> p h c", h=4)[:, :, 0:128]

        def pbank():
            return nextbank(0, 4)

        def gen_prep(m):
            tok = slice(m * 128, (m + 1) * 128)
            pi = m % NPI
            p3 = m % NSH
            rk, rq, bkk, skb, skd, sa, so, lnk, crow, sso = [sm[:, p3, i, :] for i in range(10)]
            S_ = [SM[p3]]
            X, Y = pbank(), pbank()
            t1 = banks[X][:, :].bitcast(BF16)
            for hh in range(4):
                kq = t1[:, hh * 256:hh * 256 + 128]
                vq = t1[:, hh * 256 + 128:hh * 256 + 256]
                V("tensor", lambda h: h.transpose(kq, qkvT[:, 4 + hh, tok], identb), r=[QKV[4 + hh], IDB], w=[BQ[X][hh]], inc=False)
                V("tensor", lambda h: h.transpose(vq, qkvT[:, 8 + hh, tok], identb), r=[QKV[8 + hh], IDB], w=[BQ[X][hh]], inc=False)
                V("tensor", lambda h: h.transpose(bq(Y, hh), qkvT[:, hh, tok], identb), r=[QKV[hh], IDB], w=[BQ[Y][hh]])
            for hh in range(4):
                kq = t1[:, hh * 256:hh * 256 + 128]
                j1, j2 = njunk(), njunk()
                V("scalar", lambda h: h.activation(out=junkt[:, j1, :], in_=kq, func=AF.Square, accum_out=rk[:, hh:hh + 1]), r=[BQ[X][hh]], w=[JUNKS[j1], SM[p3]])
                V("scalar", lambda h: h.activation(out=junkt[:, j2, :], in_=bq(Y, hh), func=AF.Square, accum_out=rq[:, hh:hh + 1]), r=[BQ[Y][hh]], w=[JUNKS[j2], SM[p3]])
            V("vector", lambda h: h.tensor_copy(out=ktv[pi].rearrange("p a b -> p (a b)"), in_=t1), r=BQ[X], w=KTV[pi])
            yield
            rkq = sm[:, p3, 0:2, :]
            V("scalar", lambda h: h.activation(out=rkq, in_=rkq, func=AF.Ln, bias=eps_col[:, 0:1]), r=S_ + [EPSB], w=S_)
            V("vector", lambda h: h.scalar_tensor_tensor(out=crow, in0=rk, scalar=-0.5, in1=gam[:, m, :], op0=ALU.mult, op1=ALU.subtract), r=S_ + [GAM], w=S_)
            V("scalar", lambda h: h.activation(out=rkq, in_=rkq, func=AF.Exp, scale=-0.5), r=S_, w=S_)
            V("vector", lambda h: h.tensor_tensor(out=bkk, in0=rk, in1=beta[:, m, :], op=ALU.mult), r=S_ + [BETA], w=S_)
            V("vector", lambda h: h.tensor_tensor(out=skb, in0=bkk, in1=egam[:, m, :], op=ALU.mult), r=S_ + [EGAM], w=S_)
            V("vector", lambda h: h.tensor_tensor(out=skd, in0=rk, in1=edec[:, m, :], op=ALU.mult), r=S_ + [EDEC], w=S_)
            V("vector", lambda h: h.tensor_scalar_mul(out=sa, in0=rq, scalar1=128.0 ** -0.5), r=S_, w=S_)
            V("vector", lambda h: h.tensor_tensor(out=so, in0=sa, in1=egam[:, m, :], op=ALU.mult), r=S_ + [EGAM], w=S_)
            yield
            for hh in range(4):
                kq = ktv[pi][:, 2 * hh, :]
                vq = ktv[pi][:, 2 * hh + 1, :]
                V("scalar", lambda h: h.activation(out=diagc[pi][:, hh, :], in_=identf, func=AF.Copy, scale=crow[:, hh:hh + 1]), r=[CST] + S_, w=[DIAGC[pi][hh]])
                V("scalar", lambda h: h.activation(out=kbt[pi][:, hh, :], in_=kq, func=AF.Copy, scale=skb[:, hh:hh + 1]), r=[KTV[pi][hh]] + S_, w=[KBT[pi][hh]])
                V("vector", lambda h: h.tensor_scalar_mul(out=kdt[p3][:, hh, :], in0=kq, scalar1=skd[:, hh:hh + 1]), r=[KTV[pi][hh]] + S_, w=[KDT[p3][hh]])
                V("vector", lambda h: h.tensor_scalar_mul(out=vbt[pi][:, hh, :], in0=vq, scalar1=beta[:, m, hh:hh + 1]), r=[KTV[pi][hh], BETA], w=[VBT[pi][hh]])
            yield
            Y = pbank()
            for hh in range(4):
                V("tensor", lambda h: h.matmul(qap(Y, hh), onesf, diagc[pi][:, hh, :], start=True, stop=False), r=[CST, DIAGC[pi][hh]], w=[BQ[Y][hh]], inc=False)
                V("tensor", lambda h: h.matmul(qap(Y, hh), identb, negmb, start=False, stop=True), r=[IDB, NEGMB], w=[BQ[Y][hh]], inc=(hh == 3))
            for hh in range(4):
                V("scalar", lambda h: h.activation(out=Dm[pi][:, hh, :], in_=qap(Y, hh), func=AF.Exp, bias=gam[:, m, hh:hh + 1]), r=[BQ[Y][hh], GAM], w=[DM[pi][hh]])
            for hh in range(4):
                V("gpsimd", lambda h: h.tensor_tensor(out=DmS[pi][:, hh, :], in0=Dm[pi][:, hh, :], in1=strict, op=ALU.mult), r=[DM[pi][hh], CST], w=[DMS[pi][hh]])
            yield
            X, Y = pbank(), pbank()
            for hh in range(4):
                V("tensor", lambda h: h.matmul(qap(X, hh), qkvT[:, 4 + hh, tok], qkvT[:, 4 + hh, tok], start=True, stop=True), r=[QKV[4 + hh]], w=[BQ[X][hh]], inc=(hh == 3))
            for hh in range(4):
                V("tensor", lambda h: h.matmul(qap(Y, hh), qkvT[:, hh, tok], qkvT[:, 4 + hh, tok], start=True, stop=True), r=[QKV[hh], QKV[4 + hh]], w=[BQ[Y][hh]], inc=(hh == 3))
            for hh in range(4):
                V("vector", lambda h: h.scalar_tensor_tensor(out=Am[0][pi][:, hh, :], in0=qap(X, hh), scalar=bkk[:, hh:hh + 1], in1=DmS[pi][:, hh, :],
                                                             op0=ALU.mult, op1=ALU.mult), r=[BQ[X][hh], DMS[pi][hh]] + S_, w=[AMB[0][pi][hh]])
            for hh in range(4):
                V("vector", lambda h: h.scalar_tensor_tensor(out=ac[pi][:, hh, :], in0=qap(Y, hh), scalar=sa[:, hh:hh + 1], in1=Dm[pi][:, hh, :],
                                                             op0=ALU.mult, op1=ALU.mult), r=[BQ[Y][hh], DM[pi][hh]] + S_, w=[AC[pi][hh]])
            yield
            X, Y = pbank(), pbank()
            for hh in range(4):
                V("tensor", lambda h: h.transpose(bq(X, hh), Am[0][pi][:, hh, :], identb), r=[AMB[0][pi][hh], IDB], w=[BQ[X][hh]], inc=(hh == 3))
            for hh in range(4):
                V("tensor", lambda h: h.transpose(bq(Y, hh), ac[pi][:, hh, :], identb), r=[AC[pi][hh], IDB], w=[BQ[Y][hh]], inc=(hh == 3))
            V("scalar", lambda h: h.copy(out=Bm[0][pi], in_=bqall(X)), r=BQ[X], w=BMB[0][pi])
            for hh in range(4):
                V("vector", lambda h: h.tensor_tensor(out=Ttb[0][pi][:, hh, :], in0=identf, in1=bq(X, hh), op=ALU.subtract), r=[CST, BQ[X][hh]], w=[TTB[0][pi][hh]])
            for hh in range(4):
                V("vector", lambda h: h.tensor_tensor(out=Tt32[pi][:, hh, :], in0=identf, in1=bq(X, hh), op=ALU.subtract), r=[CST, BQ[X][hh]], w=[TT32[pi][hh]])
            V("scalar", lambda h: h.copy(out=acT[p3], in_=bqall(Y)), r=BQ[Y], w=ACTB[p3])
            yield
            cur = 0
            for s_ in range(5):
                X = pbank()
                for hh in range(4):
                    V("tensor", lambda h: h.matmul(qap(X, hh), Bm[cur][pi][:, hh, :], Am[cur][pi][:, hh, :], start=True, stop=True),
                      r=[BMB[cur][pi][hh], AMB[cur][pi][hh]], w=[BQ[X][hh]], inc=(hh == 3))
                V("scalar", lambda h: h.copy(out=flat(Am[1 - cur][pi]), in_=banks[X][:, :]), r=BQ[X], w=AMB[1 - cur][pi])
                if s_ < 4:
                    Y = pbank()
                    for hh in range(4):
                        V("tensor", lambda h: h.matmul(qap(Y, hh), Am[cur][pi][:, hh, :], Bm[cur][pi][:, hh, :], start=True, stop=True),
                          r=[BMB[cur][pi][hh], AMB[cur][pi][hh]], w=[BQ[Y][hh]], inc=(hh == 3))
                    V("scalar", lambda h: h.copy(out=flat(Bm[1 - cur][pi]), in_=banks[Y][:, :]), r=BQ[Y], w=BMB[1 - cur][pi])
                yield
                X = pbank()
                for hh in range(4):
                    V("tensor", lambda h: h.matmul(qap(X, hh), Am[1 - cur][pi][:, hh, :], Ttb[cur][pi][:, hh, :], start=True, stop=True),
                      r=[AMB[1 - cur][pi][hh], TTB[cur][pi][hh]], w=[BQ[X][hh]], inc=(hh == 3))
                V("vector", lambda h: h.tensor_tensor(out=flat(Ttb[1 - cur][pi]), in0=flat(Tt32[pi]), in1=banks[X][:, :], op=ALU.add),
                  r=TT32[pi] + BQ[X], w=TTB[1 - cur][pi])
                V("vector", lambda h: h.tensor_tensor(out=flat(Tt32[pi]), in0=flat(Tt32[pi]), in1=banks[X][:, :], op=ALU.add),
                  r=TT32[pi] + BQ[X], w=TT32[pi])
                cur = 1 - cur
                yield
            X, Y = pbank(), pbank()
            for hh in range(4):
                V("tensor", lambda h: h.matmul(qap(X, hh), Ttb[cur][pi][:, hh, :], vbt[pi][:, hh, :], start=True, stop=True),
                  r=[TTB[cur][pi][hh], VBT[pi][hh]], w=[BQ[X][hh]], inc=(hh == 3))
            for hh in range(4):
                V("tensor", lambda h: h.matmul(qap(Y, hh), kbt[pi][:, hh, :], Ttb[cur][pi][:, hh, :], start=True, stop=True),
                  r=[TTB[cur][pi][hh], KBT[pi][hh]], w=[BQ[Y][hh]], inc=(hh == 3))
            V("scalar", lambda h: h.copy(out=flat(u_sb[p3]), in_=banks[X][:, :]), r=BQ[X], w=US[p3])
            V("vector", lambda h: h.tensor_copy(out=flat(wT_sb[p3]), in_=banks[Y][:, :]), r=BQ[Y], w=WT[p3])
            yield

        def gen_scan(m):
            tok = slice(m * 128, (m + 1) * 128)
            p3 = m % NSH
            so = sm[:, p3, 6, :]
            sso = sm[:, p3, 9, :]
            S_ = [SM[p3]]
            for half in range(2):
                r0 = half * 64
                c0 = m * 128 + r0
                for hh in range(4):
                    V("tensor", lambda h: h.matmul(qap(bWS, hh)[r0:r0 + 64, :], wT_sb[p3][:, hh, r0:r0 + 64], Sbf[:, hh, :], start=True, stop=True),
                      r=[WT[p3][hh], SBF[hh]], w=[BQ[bWS][hh]], inc=(hh == 3))
                for hh in range(4):
                    V("tensor", lambda h: h.matmul(qap(bOI, hh)[r0:r0 + 64, :], qkvT[:, hh, c0:c0 + 64], Sbf[:, hh, :], start=True, stop=True),
                      r=[QKV[hh], SBF[hh]], w=[BQ[bOI][hh]], inc=(hh == 3))
                for hh in range(4):
                    gcol = glb[:, half, m * 4 + hh:m * 4 + hh + 1]
                    V("gpsimd", lambda h: h.tensor_scalar(out=Sp[:, hh, :], in0=S32[:, hh, :], scalar1=gcol, scalar2=0.0, op0=ALU.mult, op1=ALU.add),
                      r=[S32B[hh], GLB], w=[SPB[hh]])
                yield
                V("vector", lambda h: h.tensor_tensor(out=flat(vnew[r0:r0 + 64, :, :]), in0=flat(u_sb[p3][r0:r0 + 64, :, :]),
                                                      in1=banks[bWS][r0:r0 + 64, :], op=ALU.subtract), r=US[p3] + BQ[bWS], w=VN)
                yield
                for hh in range(4):
                    V("tensor", lambda h: h.matmul(qap(bSD, hh), kdt[p3][r0:r0 + 64, hh, :], vnew[r0:r0 + 64, hh, :], start=True, stop=True),
                      r=[KDT[p3][hh], VN[hh]], w=[BQ[bSD][hh]], inc=(hh == 3))
                for hh in range(4):
                    V("tensor", lambda h: h.matmul(qap(bO2, hh)[r0:r0 + 64, :], acT[p3][r0:r0 + 64, hh, r0:r0 + 64], vnew[r0:r0 + 64, hh, :],
                                                   start=True, stop=True), r=[ACTB[p3][hh], VN[hh]], w=[BQ[bO2][hh]], inc=(hh == 3))
                yield
                V("vector", lambda h: h.tensor_tensor(out=flat(S32), in0=flat(Sp), in1=banks[bSD][:, :], op=ALU.add), r=SPB + BQ[bSD], w=S32B)
                V("scalar", lambda h: h.copy(out=flat(Sbf), in_=flat(S32)), r=S32B, w=SBF)
                yield
            for hh in range(4):
                V("scalar", lambda h: h.activation(out=tmpo[:, hh, :], in_=qap(bOI, hh), func=AF.Copy, scale=so[:, hh:hh + 1]), r=[BQ[bOI][hh]] + S_, w=[TMPO[hh]])
            V("vector", lambda h: h.tensor_tensor(out=flat(otok), in0=flat(tmpo), in1=banks[bO2][:, :], op=ALU.add), r=TMPO + BQ[bO2], w=OTOK)
            yield
            for hh in range(4):
                V("scalar", lambda h: h.activation(out=junk2t[:, hh, :], in_=otok[:, hh, :], func=AF.Square, accum_out=sso[:, hh:hh + 1]), r=[OTOK[hh]], w=[JUNK2S[hh], SM2[p3]])
            V("scalar", lambda h: h.activation(out=sso, in_=sso, func=AF.Ln, scale=1.0 / 128.0, bias=eps_col[:, 0:1]), r=[SM2[p3], EPSB], w=[SM2[p3]])
            V("scalar", lambda h: h.activation(out=sso, in_=sso, func=AF.Exp, scale=-0.5), r=[SM2[p3]], w=[SM2[p3]])
            yield
            for hh in range(4):
                V("vector", lambda h: h.scalar_tensor_tensor(out=on[:, hh, :], in0=otok[:, hh, :], scalar=sso[:, hh:hh + 1], in1=gbc,
                                                             op0=ALU.mult, op1=ALU.mult), r=[OTOK[hh], SM2[p3], GBC], w=[ON[hh]])
            for hh in range(4):
                V("tensor", lambda h: h.transpose(bq(bOT, hh), on[:, hh, :], identb), r=[ON[hh], IDB], w=[BQ[bOT][hh]], inc=(hh == 3))
            yield
            V("vector", lambda h: h.tensor_tensor(out=ogT[:, :, tok], in0=bqall(bOT), in1=sz[:, :, tok], op=ALU.mult), r=BQ[bOT] + SZ, w=OGT)
            yield

        preps = {}
        prep_done = set()
        scan_done = set()
        next_prep = 0
        scan_m = 0
        scan_g = None
        while len(scan_done) < NT:
            while len(preps) < NPI and next_prep < NT and (next_prep < NSH or (next_prep - NSH) in scan_done) \
                    and (next_prep < NPI or (next_prep - NPI) in prep_done):
                preps[next_prep] = gen_prep(next_prep)
                next_prep += 1
            progressed = False
            for _rep in range(2):
                if scan_g is None and scan_m < NT and scan_m in prep_done:
                    scan_g = gen_scan(scan_m)
                if scan_g is not None:
                    try:
                        next(scan_g)
                    except StopIteration:
                        scan_done.add(scan_m)
                        scan_m += 1
                        scan_g = None
                    progressed = True
            for t_ in sorted(preps):
                try:
                    next(preps[t_])
                except StopIteration:
                    prep_done.add(t_)
                    del preps[t_]
                progressed = True
            assert progressed
        dumpsrc["ogT"] = (ogT, OGT, "G")
        do_dumps("G")
        if stop_after == "G":
            finish()
            return nc
        fw.barrier()
        ar.release("gam", "egam", "edec", "glb", "X2", "gbc", "negmb", "vnew", "tmpo", "otok", "on", "junk", "junk2", "S32", "Sp", "Sbf", "sm", "qkvT", "sz", "beta", "lg", "kbt0", "kbt1", "kbt2", "vbt0", "vbt1", "vbt2", "ktv0", "ktv1", "ktv2", "diagc0", "diagc1", "diagc2", "Dm0", "Dm1", "Dm2", "DmS0", "DmS1", "DmS2", "Am0_0", "Am0_1", "Am0_2", "Am1_0", "Am1_1", "Am1_2", "Bm0_0", "Bm0_1", "Bm0_2", "Bm1_0", "Bm1_1", "Bm1_2", "ac0", "ac1", "ac2", "Tt32_0", "Tt32_1", "Tt32_2", "Ttb0_0", "Ttb0_1", "Ttb0_2", "Ttb1_0", "Ttb1_1", "Ttb1_2", "kdt0", "kdt1", "kdt2", "kdt3", "acT0", "acT1", "acT2", "acT3", "u_sb0", "u_sb1", "u_sb2", "u_sb3", "wT_sb0", "wT_sb1", "wT_sb2", "wT_sb3")

        Rr = ar.view("R", [NT, D], F32)
        RB = [[Buf(), Buf()] for _ in range(NT)]
        h1T = ar.view("h1T", [8, T], BF16)
        H1T = [[Buf() for _ in range(NT)] for _ in range(8)]
        wo_g = ar.view("wo_g", [4, 512], BF16)
        wo_f = ar.view("wo_f", [4, 512], BF16)
        WOG, WOF = Buf(), Buf()
        wog_sem, wof_sem = newd(), newd()
        xts = [ar.view("xt%d" % i, [D], F32) for i in range(4)]
        XT = [Buf() for _ in range(4)]
        xsem2 = [newd() for _ in range(2)]
        gA = ar.view("gA", [D], F32)
        bA = ar.view("bA", [D], F32)
        GA, BA = Buf(), Buf()
        st = ar.view("st", [4, 12], F32)
        STB = [Buf() for _ in range(4)]
        mv1 = ar.view("mv1", [NT, 2], F32)
        MV1 = [Buf() for _ in range(NT)]
        w_out_g = w_out[0:512, :].rearrange("(kc p) c -> p kc c", p=128)
        w_out_f = w_out[512:1024, :].rearrange("(j p) c -> p j c", p=128)
        fw.dma("gpsimd", wog_sem, wo_g, w_out_g[:, :, 0:512], w=[WOG])
        fw.dma("gpsimd", wof_sem, wo_f, w_out_f[:, :, 0:512], w=[WOF])
        bcsem = [newd(), newd()]

        def load_bc(rowg, rowb):
            fw.dma("sync", bcsem[0], gA, vec_d[rowg:rowg + 1, :].partition_broadcast(128), w=[GA])
            fw.dma("sync", bcsem[1], bA, vec_d[rowb:rowb + 1, :].partition_broadcast(128), w=[BA])
            V("scalar", lambda h: h.activation(out=gA, in_=gA, func=AF.Copy, scale=ALPHA), r=[GA], w=[GA])
            V("scalar", lambda h: h.activation(out=bA, in_=bA, func=AF.Copy, scale=ALPHA), r=[BA], w=[BA])

        wple = ar.view("wple", [2, D], BF16)
        wgt = ar.view("wgt", [8, D], BF16)
        WPLE, WGT = Buf(), Buf()
        fw.dma("gpsimd", newd(), wple, w_ple.rearrange("(kc p) c -> p kc c", p=128), w=[WPLE])
        fw.dma("gpsimd", newd(), wgt, w_gate.rearrange("(kc p) c -> p kc c", p=128), w=[WGT])
        load_bc(0, 1)
        nmr = ar.view("nmr", [NT], F32)
        NMR = Buf()
        V("vector", lambda h: h.scalar_tensor_tensor(out=nmr, in0=mv[:, :, 0], scalar=-1.0, in1=mv[:, :, 1], op0=ALU.mult, op1=ALU.mult), r=MV, w=[NMR])
        for m in range(NT):
            j = m % 2
            fw.dma("sync", xsem2[j], xts[j], x[m * 128:(m + 1) * 128, :], w=[XT[j]])
            V("scalar", lambda h: h.activation(out=xts[j], in_=xts[j], func=AF.Identity, scale=mv[:, m, 1:2], bias=nmr[:, m:m + 1]),
              r=[XT[j], MV[m], NMR], w=[XT[j]])
            V("vector", lambda h: h.tensor_tensor(out=Rr[:, m, :], in0=xts[j], in1=gA, op=ALU.mult), r=[XT[j], GA], w=RB[m])
            V("vector", lambda h: h.tensor_tensor(out=Rr[:, m, :], in0=Rr[:, m, :], in1=bA, op=ALU.add), r=RB[m] + [BA], w=RB[m])
        for half in range(2):
            if half == 1:
                fw.dma("gpsimd", wog_sem, wo_g, w_out_g[:, :, 512:1024], w=[WOG])
                fw.dma("gpsimd", wof_sem, wo_f, w_out_f[:, :, 512:1024], w=[WOF])
            for m in range(NT):
                tok = slice(m * 128, (m + 1) * 128)
                bk = nextbank()
                for kc in range(4):
                    V("tensor", lambda h: h.matmul(banks[bk][:, :], ogT[:, kc, tok], wo_g[:, kc, :], start=(kc == 0), stop=False),
                      r=[OGT[kc], WOG], w=[BK[bk]], inc=False)
                for j4 in range(4):
                    V("tensor", lambda h: h.matmul(banks[bk][:, :], ofox[:, j4, tok], wo_f[:, j4, :], start=False, stop=(j4 == 3)),
                      r=[OFOX[2 * j4], OFOX[2 * j4 + 1], WOF], w=[BK[bk]], inc=(j4 == 3))
                V("vector", lambda h: h.tensor_tensor(out=Rr[:, m, half * 512:(half + 1) * 512], in0=Rr[:, m, half * 512:(half + 1) * 512],
                                                      in1=banks[bk][:, :], op=ALU.add), r=[RB[m][half], BK[bk]], w=[RB[m][half]])
        load_bc(2, 3)

        def c_p1(m):
            j = m % 4
            V("vector", lambda h: h.bn_stats(out=st[:, j, 0:6], in_=Rr[:, m, 0:512]), r=[RB[m][0]], w=[STB[j]])
            V("vector", lambda h: h.bn_stats(out=st[:, j, 6:12], in_=Rr[:, m, 512:1024]), r=[RB[m][1]], w=[STB[j]])
            V("vector", lambda h: h.bn_aggr(out=mv1[:, m, 0:2], in_=st[:, j, 0:12]), r=[STB[j]], w=[MV1[m]])
            V("vector", lambda h: h.tensor_scalar_add(out=mv1[:, m, 1:2], in0=mv1[:, m, 1:2], scalar1=LN_EPS), r=[MV1[m]], w=[MV1[m]])
            V("gpsimd", lambda h: h.tensor_tensor(out=mv1[:, m, 1:2], in0=mv1[:, m, 1:2], in1=m05, op=ALU.pow), r=[MV1[m], CST], w=[MV1[m]])

        def c_p2(m):
            j = m % 4
            V("vector", lambda h: h.scalar_tensor_tensor(out=mv1[:, m, 0:1], in0=mv1[:, m, 0:1], scalar=-1.0, in1=mv1[:, m, 1:2],
                                                         op0=ALU.mult, op1=ALU.mult), r=[MV1[m]], w=[MV1[m]])
            V("scalar", lambda h: h.activation(out=xts[j], in_=Rr[:, m, :], func=AF.Identity, scale=mv1[:, m, 1:2], bias=mv1[:, m, 0:1]),
              r=RB[m] + [MV1[m]], w=[XT[j]])

        def c_p3(m):
            j = m % 4
            V("vector", lambda h: h.tensor_tensor(out=Rr[:, m, :], in0=xts[j], in1=gA, op=ALU.mult), r=[XT[j], GA], w=RB[m])
            V("vector", lambda h: h.tensor_tensor(out=Rr[:, m, :], in0=Rr[:, m, :], in1=bA, op=ALU.add), r=RB[m] + [BA], w=RB[m])

        def c_tr(g):
            for c in range(8):
                bk = nextbank()
                for j in range(4):
                    V("tensor", lambda h: h.transpose(banks[bk][:, j * 128:(j + 1) * 128], xts[j][:, c * 128:(c + 1) * 128], identf),
                      r=[XT[j], CST], w=[BK[bk]], inc=(j == 3))
                V("scalar", lambda h: h.activation(out=h1T[:, c, g * 512:(g + 1) * 512], in_=banks[bk][:, :], func=AF.Identity,
                                                   scale=par[:, P_G1 + c:P_G1 + c + 1], bias=par[:, P_B1 + c:P_B1 + c + 1]),
                  r=[BK[bk], PAR], w=[H1T[c][4 * g + jj] for jj in range(4)])

        for k in range(NT + 2):
            if k < NT:
                c_p1(k)
            if 0 <= k - 1 < NT:
                c_p2(k - 1)
                if (k - 1) % 4 == 3:
                    c_tr((k - 1) // 4)
            if 0 <= k - 2 < NT:
                c_p3(k - 2)
        dumpsrc["h1T"] = (h1T, [b for row in H1T for b in row], "C")
        dumpsrc["R"] = (Rr, [b for p_ in RB for b in p_], "C")
        do_dumps("C")
        if stop_after == "C":
            finish()
            return nc
        fw.barrier()
        ar.release("ofox", "ogT", "wo_g", "wo_f", "xt0", "xt1", "xt2", "xt3", "gA", "bA", "nmr")

        NG = DFF // 512
        wu = [ar.view("wu%d" % i, [8, 512], BF16) for i in range(2)]
        wd = [ar.view("wd%d" % i, [4, D], BF16) for i in range(2)]
        WU = [Buf(), Buf()]
        WD = [Buf(), Buf()]
        wusem = [newd(), newd()]
        wdsem = [newd(), newd()]
        w_up_v = w_up.rearrange("(kc p) f -> p kc f", p=128)
        w_down_v = w_down.rearrange("(fc p) c -> p fc c", p=128)
        def load_ffw(g):
            s_ = g % 2
            fw.dma("gpsimd", wusem[s_], wu[s_], w_up_v[:, :, g * 512:(g + 1) * 512], w=[WU[s_]])
            fw.dma("gpsimd", wdsem[s_], wd[s_], w_down_v[:, g * 4:(g + 1) * 4, :], w=[WD[s_]])

        load_ffw(0)
        pTt = ar.view("pTt", [2, T], BF16)
        PTB = [Buf() for _ in range(NT)]
        ptile = ar.view("ptile", [2, 256], F32)
        PTL = [Buf() for _ in range(2)]
        psem = [newd(), newd()]
        bgf = ar.view("bgf", [D], F32)
        bgh = ar.view("bgh", [D], BF16)
        bgl = ar.view("bgl", [D], BF16)
        ones_b = ar.view("ones_b", [128], BF16)
        BGF, BGH, BGL, ONB = Buf(), Buf(), Buf(), Buf()
        sg = ar.view("sg", [2, 512], F32)
        SG = [Buf(), Buf()]
        fw.dma("sync", newd(), bgf[0:1, :], vec_d[6:7, :], w=[BGF])
        V("vector", lambda h: h.tensor_copy(out=bgh[0:1, :], in_=bgf[0:1, :]), r=[BGF], w=[BGH])
        V("vector", lambda h: h.tensor_tensor(out=bgf[0:1, :], in0=bgf[0:1, :], in1=bgh[0:1, :], op=ALU.subtract), r=[BGF, BGH], w=[BGF])
        V("vector", lambda h: h.tensor_copy(out=bgl[0:1, :], in_=bgf[0:1, :]), r=[BGF], w=[BGL])
        V("vector", lambda h: h.memset(ones_b, 1.0), w=[ONB])
        for m in range(NT):
            j = m % 2
            tok = slice(m * 128, (m + 1) * 128)
            fw.dma("sync", psem[j], ptile[:, j, :], p_in[tok, :], w=[PTL[j]])
            bk = nextbank()
            for kc in range(2):
                V("tensor", lambda h: h.transpose(banks[bk][:, kc * 128:(kc + 1) * 128], ptile[:, j, kc * 128:(kc + 1) * 128], identf),
                  r=[PTL[j], CST], w=[BK[bk]], inc=(kc == 1))
            V("scalar", lambda h: h.copy(out=pTt[:, :, tok], in_=banks[bk][:, 0:256].rearrange("p (k t) -> p k t", k=2)), r=[BK[bk]], w=[PTB[m]])
            for half in range(2):
                cs = slice(half * 512, (half + 1) * 512)
                bp, bg_ = nextbank(), nextbank()
                for kc in range(2):
                    V("tensor", lambda h: h.matmul(banks[bp][:, :], pTt[:, kc, tok], wple[:, kc, cs], start=(kc == 0), stop=(kc == 1)),
                      r=[PTB[m], WPLE], w=[BK[bp]], inc=(kc == 1))
                for kc in range(8):
                    V("tensor", lambda h: h.matmul(banks[bg_][:, :], h1T[:, kc, tok], wgt[:, kc, cs], start=(kc == 0), stop=False),
                      r=[H1T[kc][m], WGT], w=[BK[bg_]], inc=False)
                V("tensor", lambda h: h.matmul(banks[bg_][:, :], ones_b[0:1, :], bgh[0:1, cs], start=False, stop=False), r=[ONB, BGH], w=[BK[bg_]], inc=False)
                V("tensor", lambda h: h.matmul(banks[bg_][:, :], ones_b[0:1, :], bgl[0:1, cs], start=False, stop=True), r=[ONB, BGL], w=[BK[bg_]])
                V("scalar", lambda h: h.activation(out=sg[:, half, :], in_=banks[bg_][:, :], func=AF.Sigmoid), r=[BK[bg_]], w=[SG[half]])
                V("vector", lambda h: h.tensor_tensor(out=sg[:, half, :], in0=sg[:, half, :], in1=banks[bp][:, :], op=ALU.mult), r=[SG[half], BK[bp]], w=[SG[half]])
                V("gpsimd", lambda h: h.tensor_tensor(out=Rr[:, m, cs], in0=Rr[:, m, cs], in1=sg[:, half, :], op=ALU.add), r=[RB[m][half], SG[half]], w=[RB[m][half]])
        if stop_after == "D1":
            dumpsrc["R1"] = (Rr, [b for p_ in RB for b in p_], "D1")
            do_dumps("D1")
            finish()
            return nc
        fw.barrier()
        ar.release("pTt", "ptile", "wple", "wgt", "bgf", "bgh", "bgl", "ones_b", "sg")

        actT = [ar.view("actT%d" % i, [4, T], BF16) for i in range(2)]
        ACTT = [[[Buf() for _ in range(NBLK)] for _ in range(4)] for _ in range(2)]
        rl = [ar.view("rl%d" % i, [512], F32) for i in range(2)]
        RL = [Buf(), Buf()]

        gA = ar.view("gA", [D], F32)
        bA = ar.view("bA", [D], F32)
        GA, BA = Buf(), Buf()
        fw.dma("sync", bcsem[0], gA, vec_d[4:5, :].partition_broadcast(128), w=[GA])
        fw.dma("sync", bcsem[1], bA, vec_d[5:6, :].partition_broadcast(128), w=[BA])
        yo = ar.view("yo", [2, D], F32)
        YO = [Buf(), Buf()]
        osem = [newd(), newd()]

        def emit_E1(m):
            j = m % 4
            V("vector", lambda h: h.bn_stats(out=st[:, j, 0:6], in_=Rr[:, m, 0:512]), r=[RB[m][0]], w=[STB[j]])
            V("vector", lambda h: h.bn_stats(out=st[:, j, 6:12], in_=Rr[:, m, 512:1024]), r=[RB[m][1]], w=[STB[j]])
            V("vector", lambda h: h.bn_aggr(out=mv1[:, m, 0:2], in_=st[:, j, 0:12]), r=[STB[j]], w=[MV1[m]])
            V("vector", lambda h: h.tensor_scalar_add(out=mv1[:, m, 1:2], in0=mv1[:, m, 1:2], scalar1=LN_EPS), r=[MV1[m]], w=[MV1[m]])
            V("gpsimd", lambda h: h.tensor_tensor(out=mv1[:, m, 1:2], in0=mv1[:, m, 1:2], in1=m05, op=ALU.pow), r=[MV1[m], CST], w=[MV1[m]])

        def emit_E2(m):
            j = m % 2
            V("vector", lambda h: h.scalar_tensor_tensor(out=mv1[:, m, 0:1], in0=mv1[:, m, 0:1], scalar=-1.0, in1=mv1[:, m, 1:2],
                                                         op0=ALU.mult, op1=ALU.mult), r=[MV1[m]], w=[MV1[m]])
            V("scalar", lambda h: h.activation(out=yo[:, j, :], in_=Rr[:, m, :], func=AF.Identity, scale=mv1[:, m, 1:2], bias=mv1[:, m, 0:1]),
              r=RB[m] + [MV1[m]], w=[YO[j]])

        def emit_E3(m):
            j = m % 2
            V("vector", lambda h: h.tensor_tensor(out=yo[:, j, :], in0=yo[:, j, :], in1=gA, op=ALU.mult), r=[YO[j], GA], w=[YO[j]])
            V("vector", lambda h: h.tensor_tensor(out=yo[:, j, :], in0=yo[:, j, :], in1=bA, op=ALU.add), r=[YO[j], BA], w=[YO[j]])
            final_tickets.append(fw.dma("sync", osem[j], out[m * 128:(m + 1) * 128, :], yo[:, j, :], r=[YO[j]]))

        def emit_E_step(k):
            if 0 <= k < NT:
                emit_E1(k)
            if 0 <= k - 1 < NT:
                emit_E2(k - 1)
            if 0 <= k - 2 < NT:
                emit_E3(k - 2)

        rcnt = 0
        for g in range(NG):
            s_ = g % 2
            if g + 1 < NG:
                load_ffw(g + 1)
            for blk in range(NBLK):
                for fc in range(4):
                    bk = nextbank()
                    for kc in range(8):
                        V("tensor", lambda h: h.matmul(banks[bk][:, :], wu[s_][:, kc, fc * 128:(fc + 1) * 128], h1T[:, kc, blk * 512:(blk + 1) * 512],
                                                       start=(kc == 0), stop=(kc == 7)),
                          r=[WU[s_]] + [H1T[kc][mm] for mm in range(blk * 4, blk * 4 + 4)], w=[BK[bk]], inc=(kc == 7))
                    q = rcnt % 2
                    rcnt += 1
                    V("scalar", lambda h: h.activation(out=rl[q], in_=banks[bk][:, :], func=AF.Relu), r=[BK[bk]], w=[RL[q]])
                    V("gpsimd", lambda h: h.tensor_tensor(out=actT[s_][:, fc, blk * 512:(blk + 1) * 512], in0=rl[q], in1=rl[q], op=ALU.mult),
                      r=[RL[q]], w=[ACTT[s_][fc][blk]])
            for m in range(NT):
                tok = slice(m * 128, (m + 1) * 128)
                for half in range(2):
                    cs = slice(half * 512, (half + 1) * 512)
                    bk = nextbank()
                    for fc in range(4):
                        V("tensor", lambda h: h.matmul(banks[bk][:, :], actT[s_][:, fc, tok], wd[s_][:, fc, cs], start=(fc == 0), stop=(fc == 3)),
                          r=[ACTT[s_][fc][m // 4], WD[s_]], w=[BK[bk]], inc=(fc == 3))
                    V("vector", lambda h: h.tensor_tensor(out=Rr[:, m, cs], in0=Rr[:, m, cs], in1=banks[bk][:, :], op=ALU.add), r=[RB[m][half], BK[bk]], w=[RB[m][half]])
                if g == NG - 1:
                    emit_E_step(m - 1)
        for k in range(NT - 1, NT + 2):
            emit_E_step(k)

        finish()
    return nc


_CACHE = {}


def _host_consts():
    c = np.zeros((128, NCST), np.float32)
    i = np.arange(128)
    c[:, C_ID:C_ID + 128] = np.eye(128, dtype=np.float32)
    c[:, C_ONE:C_ONE + 128] = 1.0
    same = (i[:, None] // 64) == (i[None, :] // 64)
    c[:, C_TRI:C_TRI + 128] = (same & (i[:, None] <= i[None, :])).astype(np.float32)
    c[:, C_BLK:C_BLK + 128] = same.astype(np.float32)
    c[:, C_NEGM:C_NEGM + 128] = np.where(same & (i[:, None] >= i[None, :]), 0.0, NEG)
    c[:, C_STR:C_STR + 128] = (same & (i[:, None] > i[None, :])).astype(np.float32)
    c[:, C_IND] = (i < 64)
    c[:, C_IND + 1] = (i >= 64)
    c[0:64, C_OAUG:C_OAUG + 128] = 1.0
    c[64, C_OAUG:C_OAUG + 128] = 64.0 * NORM_EPS
    c[:, C_M05] = -0.5
    c[:, C_M05 + 1] = NORM_EPS
    return c


def _prep_shared(inp):
    par = np.zeros((128, NPAR), np.float32)
    par[:, P_G0:P_G0 + 8] = inp["ln_in_g"].reshape(8, 128).T
    par[:, P_B0:P_B0 + 8] = inp["ln_in_b"].reshape(8, 128).T
    cw = inp["conv_w"][0]
    par[:, P_CW:P_CW + 48] = cw.T.reshape(12, 128, 4).transpose(1, 0, 2).reshape(128, 48)
    par[0:64, P_FOXG] = inp["fox_norm_g"][0]
    par[0:8, P_BF] = inp["b_f"][0]
    par[:, P_DTB:P_DTB + 64] = np.tile(inp["dt_bias"][0], 16)[None, :]
    par[:, P_ALOG:P_ALOG + 64] = np.tile(inp["a_log"][0], 16)[None, :]
    par[:, P_G1:P_G1 + 8] = inp["ln1_g"][0].reshape(8, 128).T
    par[:, P_B1:P_B1 + 8] = inp["ln1_b"][0].reshape(8, 128).T
    vecs = np.zeros((8, D), np.float32)
    vecs[0] = inp["ln_in_g"]
    vecs[1] = inp["ln_in_b"]
    vecs[2] = inp["ln1_g"][0]
    vecs[3] = inp["ln1_b"][0]
    vecs[4] = inp["ln2_g"][0]
    vecs[5] = inp["ln2_b"][0]
    vecs[6] = inp["b_ple_gate"][0]
    vecs[7, 0:128] = inp["gdn_norm_g"][0]
    return {
        "w_in": np.ascontiguousarray(inp["w_in"][0]), "w_out": np.ascontiguousarray(inp["w_out"][0]),
        "w_up": np.ascontiguousarray(inp["w_up"][0]), "w_down": np.ascontiguousarray(inp["w_down"][0]),
        "w_ple": np.ascontiguousarray(inp["w_ple"][0]), "w_gate": np.ascontiguousarray(inp["w_ple_gate"][0]),
        "cst": _host_consts(), "par": par, "vecs": vecs, "ones_rows": np.ones((3, 8 * T), np.float32),
    }


def run(inp, stop_after=None, dumps=(), cores=8):
    key = (stop_after, tuple(dumps))
    if key not in _CACHE:
        _CACHE[key] = build(stop_after, dumps)
    nc = _CACHE[key]
    inp = {k: np.asarray(v, dtype=np.float32) for k, v in inp.items()}
    shared = _prep_shared(inp)
    in_maps = []
    for b in range(cores):
        m = dict(shared)
        m["x"] = np.ascontiguousarray(inp["x"][b])
        m["p"] = np.ascontiguousarray(inp["p"][0, b])
        in_maps.append(m)
    res = run_bass_kernel_spmd(nc, in_maps, core_ids=list(range(cores)))
    return res.results


def kernel(**inputs):
    results = run(inputs)
    return np.stack([r["out"] for r in results], axis=0).astype(np.float32)
```
